# Optimizing a Trainium2 kernel written in Bass

```python
import functools
import numpy as np
import jax
import jax.numpy as jnp
from jax import lax

D_MODEL = 1024
BATCH = 4
SEQ = 4096
DEPTH = 1
DEC_BATCH = 128
DEC_SEQ = 1
PAST_LEN = 2048
PAGE_SIZE = 128

NSA_HEADS = 16
NSA_HEAD_DIM = 64
NSA_KV_HEADS = 4
NSA_GROUP = NSA_HEADS // NSA_KV_HEADS
NSA_WIDTH = NSA_HEADS * NSA_HEAD_DIM
NSA_KV_WIDTH = NSA_KV_HEADS * NSA_HEAD_DIM
CMP_STRIDE = 16
CMP_LEN = 2 * CMP_STRIDE
CMP_HIDDEN = 2 * NSA_HEAD_DIM
SLC_BLOCK = 64
SLC_TOPN = 16
WINDOW = 512
Q_BLOCK = 64

DN_HEADS = 8
DN_DK = 128
DN_DV = 128
DN_WIDTH = DN_HEADS * DN_DV
DN_QKV = 2 * DN_HEADS * DN_DK + DN_WIDTH
CONV_W = 4
DN_CHUNK = 64

D_FF = 4 * D_MODEL
N_MOD = 6
EPS = 1e-6
IN_SPLITS = (NSA_WIDTH, 6 * NSA_KV_WIDTH, 3 * NSA_HEADS, DN_QKV, DN_WIDTH, DN_HEADS, DN_HEADS, 2 * D_MODEL)
IN_WIDTH = sum(IN_SPLITS)

kernel_name = 'nsa_gated_deltanet_hybrid_step'


def _rmsnorm(x, g):
    xf = x.astype(jnp.float32)
    y = xf * lax.rsqrt(jnp.mean(xf * xf, axis=-1, keepdims=True) + EPS)
    return (y * g.astype(jnp.float32)).astype(x.dtype)


def _l2norm(x):
    xf = x.astype(jnp.float32)
    return xf * lax.rsqrt(jnp.sum(xf * xf, axis=-1, keepdims=True) + EPS)


def _masked_softmax(s, mask):
    s = jnp.where(mask, s.astype(jnp.float32), -jnp.inf)
    m = jnp.max(s, axis=-1, keepdims=True)
    m = jnp.where(jnp.isfinite(m), m, 0.0)
    p = jnp.exp(s - m)
    return p / jnp.maximum(jnp.sum(p, axis=-1, keepdims=True), 1e-30)


def _compress(k, pe, w1, w2):
    n, lp, g, d = k.shape
    kc = k.reshape(n, lp // CMP_STRIDE, CMP_STRIDE, g, d)
    first = jnp.einsum('ncpgd,pde->ncge', kc, w1[:CMP_STRIDE])
    second = jnp.einsum('ncpgd,pde->ncge', kc, w1[CMP_STRIDE:])
    pe_term = jnp.einsum('pd,pde->e', pe, w1)
    hid = jax.nn.gelu(first[:, :-1] + second[:, 1:] + pe_term)
    return jnp.einsum('ncge,ed->ncgd', hid, w2)


def _block_overlap(n_cmp, n_blk):
    c0 = np.arange(n_cmp)[:, None] * CMP_STRIDE
    b0 = np.arange(n_blk)[None, :] * SLC_BLOCK
    inter = np.minimum(c0 + CMP_LEN, b0 + SLC_BLOCK) - np.maximum(c0, b0)
    return jnp.asarray(np.clip(inter, 0, None) / CMP_LEN, dtype=jnp.float32)


def _to_blocks(k):
    n, lp, g, d = k.shape
    return k.reshape(n, lp // SLC_BLOCK, SLC_BLOCK, g, d).transpose(0, 3, 1, 2, 4)


def _nsa_core(q, q_pos, kc, vc, kc_end, ks_blk, vs_blk, kw, vw, kw_pos, gates, overlap):
    scale = NSA_HEAD_DIM ** -0.5
    n, nq, g, hg, _ = q.shape
    n_blk = ks_blk.shape[2]
    top_n = min(SLC_TOPN, n_blk)
    qp = q_pos[:, None]
    s = jnp.einsum('nqghd,ncgd->nqghc', q, kc) * scale
    p_cmp = _masked_softmax(s, (kc_end[None, :] <= qp)[None, :, None, None, :])
    o_cmp = jnp.einsum('nqghc,ncgd->nqghd', p_cmp, vc)
    imp = jnp.einsum('nqghc,cj->nqgj', p_cmp, overlap)
    blk = jnp.arange(n_blk)[None, :]
    forced = (blk == (q_pos // SLC_BLOCK)[:, None]) | (blk == 0)
    causal = blk * SLC_BLOCK <= qp
    imp = jnp.where(forced[None, :, None, :], jnp.inf,
                    jnp.where(causal[None, :, None, :], imp, -jnp.inf))
    _, idx = lax.top_k(imp, top_n)
    n_i = jnp.arange(n)[:, None, None, None]
    g_i = jnp.arange(g)[None, None, :, None]
    ksel = ks_blk[n_i, g_i, idx]
    vsel = vs_blk[n_i, g_i, idx]
    kpos = idx[..., None] * SLC_BLOCK + jnp.arange(SLC_BLOCK)
    s = jnp.einsum('nqghd,nqgksd->nqghks', q, ksel) * scale
    s = s.reshape(n, nq, g, hg, top_n * SLC_BLOCK)
    smask = (kpos <= q_pos[None, :, None, None, None]).reshape(n, nq, g, 1, top_n * SLC_BLOCK)
    p = _masked_softmax(s, smask).reshape(n, nq, g, hg, top_n, SLC_BLOCK)
    o_slc = jnp.einsum('nqghks,nqgksd->nqghd', p, vsel)
    s = jnp.einsum('nqghd,nlgd->nqghl', q, kw) * scale
    kp = kw_pos[None, :]
    wmask = (kp <= qp) & (kp > qp - WINDOW) & (kp >= 0)
    p = _masked_softmax(s, wmask[None, :, None, None, :])
    o_win = jnp.einsum('nqghl,nlgd->nqghd', p, vw)
    out = gates[..., 0:1] * o_cmp + gates[..., 1:2] * o_slc + gates[..., 2:3] * o_win
    return out.astype(q.dtype)


def _nsa_prompt(q, kv, gates, cmp_params):
    k_cmp, v_cmp, k_slc, v_slc, k_win, v_win = kv
    pe_k, w1_k, w2_k, pe_v, w1_v, w2_v = cmp_params
    b, s = q.shape[:2]
    kc = _compress(k_cmp, pe_k, w1_k, w2_k)
    vc = _compress(v_cmp, pe_v, w1_v, w2_v)
    n_cmp = kc.shape[1]
    kc_end = jnp.arange(n_cmp) * CMP_STRIDE + (CMP_LEN - 1)
    overlap = _block_overlap(n_cmp, s // SLC_BLOCK)
    ks_blk = _to_blocks(k_slc)
    vs_blk = _to_blocks(v_slc)
    pad = ((0, 0), (WINDOW, 0), (0, 0), (0, 0))
    kw_pad = jnp.pad(k_win, pad)
    vw_pad = jnp.pad(v_win, pad)

    def body(i):
        s0 = i * Q_BLOCK
        qb = lax.dynamic_slice_in_dim(q, s0, Q_BLOCK, axis=1)
        gb = lax.dynamic_slice_in_dim(gates, s0, Q_BLOCK, axis=1)
        kw = lax.dynamic_slice_in_dim(kw_pad, s0, WINDOW + Q_BLOCK, axis=1)
        vw = lax.dynamic_slice_in_dim(vw_pad, s0, WINDOW + Q_BLOCK, axis=1)
        q_pos = s0 + jnp.arange(Q_BLOCK)
        kw_pos = s0 - WINDOW + jnp.arange(WINDOW + Q_BLOCK)
        return _nsa_core(qb, q_pos, kc, vc, kc_end, ks_blk, vs_blk, kw, vw, kw_pos, gb, overlap)

    o = lax.map(body, jnp.arange(s // Q_BLOCK))
    o = jnp.moveaxis(o, 0, 1).reshape(b, s, NSA_WIDTH)
    wlen = min(WINDOW, s)
    return o, (k_cmp, v_cmp, k_slc, v_slc, k_win[:, s - wlen:], v_win[:, s - wlen:])


def _nsa_sample(q, kv, gates, cmp_params, caches, page_table):
    k_cmp, v_cmp, k_slc, v_slc, k_win, v_win = kv
    pe_k, w1_k, w2_k, pe_v, w1_v, w2_v = cmp_params
    pool_k_cmp, pool_v_cmp, pool_k_slc, pool_v_slc, buf_k_win, buf_v_win = caches
    n, t = q.shape[:2]
    past = page_table.shape[1] * pool_k_cmp.shape[1]
    total = past + t
    lp = -(-total // SLC_BLOCK) * SLC_BLOCK

    def full(pool, new):
        rows = pool[page_table].reshape(n, past, NSA_KV_HEADS, NSA_HEAD_DIM)
        seq = jnp.concatenate([rows.astype(new.dtype), new], axis=1)
        return jnp.pad(seq, ((0, 0), (0, lp - total), (0, 0), (0, 0)))

    kc = _compress(full(pool_k_cmp, k_cmp), pe_k, w1_k, w2_k)
    vc = _compress(full(pool_v_cmp, v_cmp), pe_v, w1_v, w2_v)
    n_cmp = kc.shape[1]
    kc_end = jnp.arange(n_cmp) * CMP_STRIDE + (CMP_LEN - 1)
    overlap = _block_overlap(n_cmp, lp // SLC_BLOCK)
    ks_blk = _to_blocks(full(pool_k_slc, k_slc))
    vs_blk = _to_blocks(full(pool_v_slc, v_slc))
    wbuf = buf_k_win.shape[1]
    kw = jnp.concatenate([buf_k_win.astype(k_win.dtype), k_win], axis=1)
    vw = jnp.concatenate([buf_v_win.astype(v_win.dtype), v_win], axis=1)
    kw_pos = past - wbuf + jnp.arange(wbuf + t)
    q_pos = past + jnp.arange(t)
    o = _nsa_core(q, q_pos, kc, vc, kc_end, ks_blk, vs_blk, kw, vw, kw_pos, gates, overlap)
    o = o.reshape(n, t, NSA_WIDTH)
    return o, (k_cmp, v_cmp, k_slc, v_slc, kw[:, t:], vw[:, t:])


def _dn_features(qkv, conv_buf, conv_w, a, b, a_log, dt_bias):
    n, l = qkv.shape[:2]
    xc = jnp.concatenate([conv_buf.astype(qkv.dtype), qkv], axis=1)
    y = lax.conv_general_dilated(xc, conv_w[:, None, :].astype(xc.dtype), window_strides=(1,),
                                 padding='VALID', dimension_numbers=('NWC', 'WIO', 'NWC'),
                                 feature_group_count=DN_QKV)
    y = jax.nn.silu(y)
    q, k, v = jnp.split(y, [DN_HEADS * DN_DK, 2 * DN_HEADS * DN_DK], axis=-1)
    q = _l2norm(q.reshape(n, l, DN_HEADS, DN_DK))
    k = _l2norm(k.reshape(n, l, DN_HEADS, DN_DK))
    v = v.reshape(n, l, DN_HEADS, DN_DV).astype(jnp.float32)
    g = -jnp.exp(a_log.astype(jnp.float32)) * jax.nn.softplus(a.astype(jnp.float32) + dt_bias.astype(jnp.float32))
    beta = jax.nn.sigmoid(b.astype(jnp.float32))
    return q, k, v, g, beta, xc[:, xc.shape[1] - (CONV_W - 1):]


def _gated_delta(q, k, v, g, beta, s0):
    n, l = q.shape[:2]
    c = min(DN_CHUNK, l)
    lp = -(-l // c) * c

    def chunks(x):
        x = jnp.pad(x.astype(jnp.float32), [(0, 0), (0, lp - l)] + [(0, 0)] * (x.ndim - 2))
        x = x.reshape((n, lp // c, c) + x.shape[2:])
        return jnp.swapaxes(jnp.swapaxes(x, 2, 3), 0, 1)

    q = chunks(q) * DN_DK ** -0.5
    k, v, g, beta = chunks(k), chunks(v), chunks(g), chunks(beta)
    gc = jnp.cumsum(g, axis=-1)
    idx = jnp.arange(c)
    incl = idx[:, None] >= idx[None, :]
    strict = idx[:, None] > idx[None, :]
    decay = jnp.exp(jnp.where(incl, gc[..., :, None] - gc[..., None, :], -jnp.inf))
    kb = k * beta[..., None]
    lmat = jnp.where(strict, jnp.einsum('...id,...jd->...ij', kb, k) * decay, 0.0)
    eye = jnp.eye(c, dtype=jnp.float32)
    tmat = lax.linalg.triangular_solve(eye + lmat, jnp.broadcast_to(eye, lmat.shape),
                                       left_side=True, lower=True, unit_diagonal=True)
    u = tmat @ (v * beta[..., None])
    w = tmat @ (kb * jnp.exp(gc)[..., None])
    a_intra = jnp.einsum('...id,...jd->...ij', q, k) * decay
    q_dec = q * jnp.exp(gc)[..., None]
    g_last = gc[..., -1]
    k_dec = k * jnp.exp(g_last[..., None] - gc)[..., None]

    def step(state, xs):
        q_c, k_c, u_c, w_c, a_c, gl_c = xs
        v_new = u_c - w_c @ state
        o = q_c @ state + a_c @ v_new
        state = state * jnp.exp(gl_c)[..., None, None] + jnp.swapaxes(k_c, -1, -2) @ v_new
        return state, o

    s_fin, o = lax.scan(step, s0.astype(jnp.float32), (q_dec, k_dec, u, w, a_intra, g_last))
    o = jnp.swapaxes(jnp.swapaxes(o, 0, 1), 2, 3).reshape(n, lp, DN_HEADS, DN_DV)[:, :l]
    return o, s_fin


def _dn_out(o, z, norm_g):
    n, l = o.shape[:2]
    on = o * lax.rsqrt(jnp.mean(o * o, axis=-1, keepdims=True) + EPS) * norm_g.astype(jnp.float32)
    zf = z.reshape(n, l, DN_HEADS, DN_DV).astype(jnp.float32)
    return (on * jax.nn.silu(zf)).reshape(n, l, DN_WIDTH).astype(z.dtype)


def _layer(x, c, nsa_fn, conv_buf, dn_state, w_ada, b_ada, norm1_g, norm2_g, w_in, dn_conv_w,
           dn_a_log, dn_dt_bias, dn_norm_g, w_out, w_up, w_down):
    n, l, _ = x.shape
    mod = (c @ w_ada + b_ada)[:, None, :]
    sh1, sc1, gt1, sh2, sc2, gt2 = jnp.split(mod, N_MOD, axis=-1)
    h = _rmsnorm(x, norm1_g) * (1.0 + sc1) + sh1
    proj = h @ w_in
    cuts = [int(v) for v in np.cumsum(IN_SPLITS)[:-1]]
    q, kv, nsa_g, qkv, z, a, b, merge = jnp.split(proj, cuts, axis=-1)
    q = q.reshape(n, l, NSA_KV_HEADS, NSA_GROUP, NSA_HEAD_DIM)
    kv = [t.reshape(n, l, NSA_KV_HEADS, NSA_HEAD_DIM) for t in jnp.split(kv, 6, axis=-1)]
    nsa_g = jax.nn.sigmoid(nsa_g.reshape(n, l, NSA_KV_HEADS, NSA_GROUP, 3))
    o_nsa, nsa_state = nsa_fn(q, kv, nsa_g)
    dq, dk, dv, g, beta, conv_new = _dn_features(qkv, conv_buf, dn_conv_w, a, b, dn_a_log, dn_dt_bias)
    o_dn, dn_new = _gated_delta(dq, dk, dv, g, beta, dn_state)
    o_dn = _dn_out(o_dn, z, dn_norm_g)
    gate_a, gate_b = jnp.split(jax.nn.sigmoid(merge), 2, axis=-1)
    mixed = (gate_a * o_nsa.astype(x.dtype) + gate_b * o_dn.astype(x.dtype)) @ w_out
    x = x + gt1 * mixed
    h = _rmsnorm(x, norm2_g) * (1.0 + sc2) + sh2
    x = x + gt2 * (jnp.square(jax.nn.relu(h @ w_up)) @ w_down)
    return x, nsa_state, conv_new, dn_new


def setup_inputs(seed: int = 0) -> dict:
    key = jax.random.key(seed)
    ks = iter(jax.random.split(key, 48))

    def nrm(shape, scale):
        return jax.random.normal(next(ks), shape, jnp.float32) * scale

    n_pages = PAST_LEN // PAGE_SIZE
    n_pool = (DEC_BATCH * n_pages * 5) // 4
    wbuf = min(WINDOW, PAST_LEN)
    pool_shape = (DEPTH, n_pool, PAGE_SIZE, NSA_KV_HEADS, NSA_HEAD_DIM)
    win_shape = (DEPTH, DEC_BATCH, wbuf, NSA_KV_HEADS, NSA_HEAD_DIM)
    page_table = jax.random.permutation(next(ks), n_pool)[:DEC_BATCH * n_pages]
    page_table = page_table.reshape(DEC_BATCH, n_pages).astype(jnp.int32)
    a_log = jnp.log(jax.random.uniform(next(ks), (DEPTH, DN_HEADS), jnp.float32, 1.0, 16.0))
    dt = jnp.exp(jax.random.uniform(next(ks), (DEPTH, DN_HEADS), jnp.float32, np.log(1e-3), np.log(1e-1)))
    dt_bias = jnp.log(jnp.expm1(dt))
    return {
        'x_prompt': nrm((BATCH, SEQ, D_MODEL), 1.0),
        'x_sample': nrm((DEC_BATCH, DEC_SEQ, D_MODEL), 1.0),
        'c_prompt': nrm((BATCH, D_MODEL), 1.0),
        'c_sample': nrm((DEC_BATCH, D_MODEL), 1.0),
        'cache_k_cmp': nrm(pool_shape, 1.0),
        'cache_v_cmp': nrm(pool_shape, 1.0),
        'cache_k_slc': nrm(pool_shape, 1.0),
        'cache_v_slc': nrm(pool_shape, 1.0),
        'cache_k_win': nrm(win_shape, 1.0),
        'cache_v_win': nrm(win_shape, 1.0),
        'state_conv': nrm((DEPTH, DEC_BATCH, CONV_W - 1, DN_QKV), 1.0),
        'state_dn': nrm((DEPTH, DEC_BATCH, DN_HEADS, DN_DK, DN_DV), 0.3),
        'page_table': page_table,
        'w_ada': nrm((DEPTH, D_MODEL, N_MOD * D_MODEL), 0.5 * D_MODEL ** -0.5),
        'b_ada': nrm((DEPTH, N_MOD * D_MODEL), 0.01),
        'norm1_g': 1.0 + nrm((DEPTH, D_MODEL), 0.02),
        'norm2_g': 1.0 + nrm((DEPTH, D_MODEL), 0.02),
        'w_in': nrm((DEPTH, D_MODEL, IN_WIDTH), D_MODEL ** -0.5),
        'cmp_pe_k': nrm((DEPTH, CMP_LEN, NSA_HEAD_DIM), 0.1),
        'cmp_w1_k': nrm((DEPTH, CMP_LEN, NSA_HEAD_DIM, CMP_HIDDEN), (CMP_LEN * NSA_HEAD_DIM) ** -0.5),
        'cmp_w2_k': nrm((DEPTH, CMP_HIDDEN, NSA_HEAD_DIM), CMP_HIDDEN ** -0.5),
        'cmp_pe_v': nrm((DEPTH, CMP_LEN, NSA_HEAD_DIM), 0.1),
        'cmp_w1_v': nrm((DEPTH, CMP_LEN, NSA_HEAD_DIM, CMP_HIDDEN), (CMP_LEN * NSA_HEAD_DIM) ** -0.5),
        'cmp_w2_v': nrm((DEPTH, CMP_HIDDEN, NSA_HEAD_DIM), CMP_HIDDEN ** -0.5),
        'dn_conv_w': nrm((DEPTH, CONV_W, DN_QKV), CONV_W ** -0.5),
        'dn_a_log': a_log,
        'dn_dt_bias': dt_bias,
        'dn_norm_g': 1.0 + nrm((DEPTH, DN_DV), 0.02),
        'w_out': nrm((DEPTH, D_MODEL, D_MODEL), D_MODEL ** -0.5),
        'w_up': nrm((DEPTH, D_MODEL, D_FF), D_MODEL ** -0.5),
        'w_down': nrm((DEPTH, D_FF, D_MODEL), D_FF ** -0.5),
        'final_g': 1.0 + nrm((D_MODEL,), 0.02),
    }


def reference(x_prompt, x_sample, c_prompt, c_sample, cache_k_cmp, cache_v_cmp, cache_k_slc, cache_v_slc,
              cache_k_win, cache_v_win, state_conv, state_dn, page_table, w_ada, b_ada, norm1_g, norm2_g,
              w_in, cmp_pe_k, cmp_w1_k, cmp_w2_k, cmp_pe_v, cmp_w1_v, cmp_w2_v, dn_conv_w, dn_a_log,
              dn_dt_bias, dn_norm_g, w_out, w_up, w_down, final_g):
    xp, xs = x_prompt, x_sample
    p_new = [[] for _ in range(8)]
    s_new = [[] for _ in range(8)]
    for l in range(DEPTH):
        cmp_params = (cmp_pe_k[l], cmp_w1_k[l], cmp_w2_k[l], cmp_pe_v[l], cmp_w1_v[l], cmp_w2_v[l])
        shared = (w_ada[l], b_ada[l], norm1_g[l], norm2_g[l], w_in[l], dn_conv_w[l], dn_a_log[l],
                  dn_dt_bias[l], dn_norm_g[l], w_out[l], w_up[l], w_down[l])
        conv0 = jnp.zeros((xp.shape[0], CONV_W - 1, DN_QKV), xp.dtype)
        dn0 = jnp.zeros((xp.shape[0], DN_HEADS, DN_DK, DN_DV), jnp.float32)
        prompt_nsa = functools.partial(_nsa_prompt, cmp_params=cmp_params)
        xp, nsa_p, conv_p, dn_p = _layer(xp, c_prompt, prompt_nsa, conv0, dn0, *shared)
        caches = (cache_k_cmp[l], cache_v_cmp[l], cache_k_slc[l], cache_v_slc[l], cache_k_win[l], cache_v_win[l])
        sample_nsa = functools.partial(_nsa_sample, cmp_params=cmp_params, caches=caches, page_table=page_table)
        xs, nsa_s, conv_s, dn_s = _layer(xs, c_sample, sample_nsa, state_conv[l], state_dn[l], *shared)
        for i, t in enumerate(nsa_p + (conv_p, dn_p)):
            p_new[i].append(t)
        for i, t in enumerate(nsa_s + (conv_s, dn_s)):
            s_new[i].append(t)
    y_prompt = _rmsnorm(xp, final_g)
    y_sample = _rmsnorm(xs, final_g)
    return (y_prompt, y_sample,
            jnp.stack(p_new[0]), jnp.stack(p_new[1]), jnp.stack(p_new[2]), jnp.stack(p_new[3]),
            jnp.stack(p_new[4]), jnp.stack(p_new[5]), jnp.stack(p_new[6]), jnp.stack(p_new[7]),
            jnp.stack(s_new[0]), jnp.stack(s_new[1]), jnp.stack(s_new[2]), jnp.stack(s_new[3]),
            jnp.stack(s_new[4]), jnp.stack(s_new[5]), jnp.stack(s_new[6]), jnp.stack(s_new[7]))
```

```python
import contextlib
import os
STOP = int(os.environ.get('DN_STOP', '9'))
import numpy as np
import concourse.bass as bass
import concourse.mybir as mybir
from concourse.bass_utils import run_bass_kernel_spmd

F32 = mybir.dt.float32
BF16 = mybir.dt.bfloat16
I32 = mybir.dt.int32
AF = mybir.ActivationFunctionType
ALU = mybir.AluOpType

S = 4096
D = 1024
NS = 16
INW = 8768
C_KV = 1024
C_QKV = 2608
NT = S // 128


class Buf:
    __slots__ = ("w", "r", "name")

    def __init__(self, name=""):
        self.w = None
        self.r = {}
        self.name = name


class _Rec:
    def __getattr__(self, name):
        def f(*a, **kw):
            self.call = (name, a, kw)
            return self
        return f


class KB:
    NDMASEM = 24

    def __init__(self, nc):
        self.nc = nc
        self.engs = {"pe": nc.tensor, "act": nc.scalar, "dve": nc.vector,
                     "pool": nc.gpsimd, "sp": nc.sync}
        self.prog = {e: [] for e in self.engs}
        self.cnt = {}
        self.waited = {e: {} for e in self.engs}
        self.ndma = 0
        self.sems = {}
        for e in self.engs:
            self.sems[e] = nc.alloc_semaphore("s_" + e)
        for i in range(self.NDMASEM):
            self.sems["dma%d" % i] = nc.alloc_semaphore("s_dma%d" % i)

    def op(self, eng, fn, R=(), W=(), dma=False):
        if fn is None:
            fn = self._raw
        else:
            rec = _Rec()
            fn(rec)
            name_, a_, kw_ = rec.call
            fn = (lambda e, name_=name_, a_=a_, kw_=kw_: getattr(e, name_)(*a_, **kw_))
        deps = {}

        def add(s, v):
            if deps.get(s, 0) < v:
                deps[s] = v
        for b in R:
            if b.w:
                add(*b.w)
        for b in W:
            if b.w:
                add(*b.w)
            for s, v in b.r.items():
                add(s, v)
        if dma:
            sname = "dma%d" % (self.ndma % self.NDMASEM)
            prev = self.cnt.get(sname, 0)
            if prev:
                add(sname, prev)
            val = prev + 16
            self.ndma += 1
        else:
            sname = eng
            val = self.cnt.get(eng, 0) + 1
        self.cnt[sname] = val
        waits = []
        for s, v in deps.items():
            if s == "pe" and eng == "pe" and not dma:
                continue
            if self.waited[eng].get(s, 0) >= v:
                continue
            self.waited[eng][s] = v
            waits.append((s, v))
        self.prog[eng].append((waits, fn, sname, 16 if dma else 1))
        for b in R:
            if b.r.get(sname, 0) < val:
                b.r[sname] = val
        for b in W:
            b.w = (sname, val)
            b.r = {}

    def raw(self, eng, fn, R=(), W=(), dma=False):
        self._raw = fn
        self.op(eng, None, R=R, W=W, dma=dma)

    def dma(self, eng, out, in_, R=(), W=(), **kw):
        self.op(eng, lambda e: e.dma_start(out=out, in_=in_, **kw), R=R, W=W, dma=True)

    def barrier(self):
        for e in self.engs:
            waits = []
            for s, v in self.cnt.items():
                if self.waited[e].get(s, 0) >= v:
                    continue
                self.waited[e][s] = v
                waits.append((s, v))
            if waits:
                self.prog[e].append((waits, None, None, 0))

    def emit(self):
        nc = self.nc
        self.barrier()
        with nc.Block() as block:
            def mk(ename):
                def body(e):
                    for waits, fn, sname, inc in self.prog[ename]:
                        for s, v in waits:
                            e.wait_ge(self.sems[s], v)
                        if fn is not None:
                            fn(e).then_inc(self.sems[sname], inc)
                return body
            block.tensor(mk("pe"))
            block.scalar(mk("act"))
            block.vector(mk("dve"))
            block.gpsimd(mk("pool"))
            block.sync(mk("sp"))


def build_nc():
    nc = bass.Bass("TRN2", target_bir_lowering=False)

    def din(name, shape, dt=F32):
        return nc.dram_tensor(name, list(shape), dt, kind="ExternalInput").ap()

    def dout(name, shape, dt=F32):
        return nc.dram_tensor(name, list(shape), dt, kind="ExternalOutput").ap()

    xp = din("xp", [S, D])
    xs = din("xs", [NS, D])
    cp = din("cp", [1, D])
    cs = din("cs", [NS, D])
    w_ada = din("w_ada", [D, 6 * D])
    b_ada = din("b_ada", [1, 6 * D])
    g1 = din("g1", [1, D])
    g2 = din("g2", [1, D])
    gf = din("gf", [1, D])
    w_in = din("w_in", [D, INW])
    ckwin = din("ckwin", [NS, 512, 256])
    cvwin = din("cvwin", [NS, 512, 256])
    sconv = din("sconv", [NS, 3, 3072])
    conv_w = din("conv_w", [4, 3072])
    a_log = din("a_log", [1, 8])
    dt_bias = din("dt_bias", [1, 8])
    dn_ng = din("dn_ng", [1, 128])
    state_dn = din("state_dn", [NS, 8, 128, 128])
    pk_cmp = din("pk_cmp", [2560, 128, 256])
    pv_cmp = din("pv_cmp", [2560, 128, 256])
    pk_slc = din("pk_slc", [2560, 128, 256])
    pv_slc = din("pv_slc", [2560, 128, 256])
    ptbl = din("ptbl", [1, NS * 16], I32)
    w_out = din("w_out", [D, D])
    w_up = din("w_up", [D, 4 * D])
    w_down = din("w_down", [4 * D, D])
    w1k = din("w1k", [32, 64, 128])
    w2k = din("w2k", [128, 64])
    pek = din("pek", [32, 64])
    w1v = din("w1v", [32, 64, 128])
    w2v = din("w2v", [128, 64])
    pev = din("pev", [32, 64])

    o_pkv = dout("o_pkv", [6, S, 256])
    o_pconv = dout("o_pconv", [3, 3072])
    o_skv = dout("o_skv", [NS, 1536])
    o_skwin = dout("o_skwin", [NS, 512, 256])
    o_svwin = dout("o_svwin", [NS, 512, 256])
    o_sconv = dout("o_sconv", [NS, 3, 3072])
    o_pdn = dout("o_pdn", [8, 128, 128])
    o_sdn = dout("o_sdn", [NS, 8, 128, 128])
    o_yp = dout("o_yp", [S, D])
    o_ys = dout("o_ys", [NS, D])
    odn_scr = nc.dram_tensor("odn_scr", [S, D], F32, kind="Internal").ap()

    k = KB(nc)
    wv = w_in.rearrange("(kc p) n -> p kc n", p=128)
    st0 = contextlib.ExitStack()

    def sb(stack, name, shape, dt):
        return stack.enter_context(nc.sbuf_tensor(name, list(shape), dt))

    PS = [nc.alloc_psum_tensor("ps%d" % i, [128, 512], F32) for i in range(8)]
    PSB = [Buf("ps%d" % i) for i in range(8)]
    DOUT = Buf("dram_out")

    identf = sb(st0, "identf", [128, 128], F32)
    ident = sb(st0, "ident", [128, 128], BF16)
    ones = sb(st0, "ones", [1, 128], F32)
    B_identf, B_ident, B_ones = Buf(), Buf(), Buf()
    k.op("pool", lambda e: e.memset(identf[:, :], 0.0), W=[B_identf])
    k.op("pool", lambda e: e.affine_select(out=identf[:, :], in_=identf[:, :], pattern=[[-1, 128]],
                                           compare_op=ALU.not_equal, fill=1.0, base=0, channel_multiplier=1),
         R=[B_identf], W=[B_identf])
    k.op("dve", lambda e: e.tensor_copy(out=ident[:, :], in_=identf[:, :]), R=[B_identf], W=[B_ident])
    k.op("pool", lambda e: e.memset(ones[:, :], 1.0), W=[B_ones])
    zcol = sb(st0, "zcol", [128, 1], F32)
    k.op("pool", lambda e: e.memset(zcol[:, :], 0.0), W=[Buf()])

    vecT = sb(st0, "vecT", [128, 72], F32)
    gm1 = sb(st0, "gm1", [128, 8], F32)
    st1 = contextlib.ExitStack()
    hT = sb(st1, "hT", [128, 8, S], BF16)
    B_hT = [Buf("hT%d" % i) for i in range(NT)]
    hTs = sb(st1, "hTs", [128, 8, NS], BF16)
    B_hTs = Buf()
    stAB = contextlib.ExitStack()
    mods = sb(stAB, "mods", [NS, 6 * D], F32)
    B_vecT, B_gm1, B_mods = Buf(), Buf(), Buf()
    rows = sb(stAB, "rows", [1, 9 * D], F32)
    B_rows = Buf()
    B_projs = Buf()
    mods_scr = nc.dram_tensor("mods_scr", [NS, 6 * D], F32, kind="Internal").ap()
    projs_scr = nc.dram_tensor("projs_scr", [NS, INW], F32, kind="Internal").ap()
    szt_scr = nc.dram_tensor("szt_scr", [S, D], F32, kind="Internal").ap()
    B_scr2 = Buf()
    q_scr = nc.dram_tensor("q_scr", [8, 128, S], BF16, kind="Internal").ap()
    kvT_scr = nc.dram_tensor("kvT_scr", [8, 128, S], BF16, kind="Internal").ap()
    gate_scr = nc.dram_tensor("gate_scr", [S, 48], F32, kind="Internal").ap()
    mg_scr = nc.dram_tensor("mg_scr", [S, 2048], F32, kind="Internal").ap()
    mix_scr = nc.dram_tensor("mix_scr", [S, D], F32, kind="Internal").ap()
    odns_scr = nc.dram_tensor("odns_scr", [NS, D], F32, kind="Internal").ap()
    mixs_scr = nc.dram_tensor("mixs_scr", [NS, D], F32, kind="Internal").ap()
    B_scr3 = Buf()

    with contextlib.ExitStack() as st:
        cin = sb(st, "cin", [NS, D], F32)
        cpin = sb(st, "cpin", [1, D], F32)
        cT = sb(st, "cT", [128, 8, 17], F32)
        bada = sb(st, "bada", [1, 6 * D], F32)
        wa = [sb(st, "wa%d" % i, [128, 8, 512], F32) for i in range(2)]
        B_cin, B_cpin, B_cT, B_bada = Buf(), Buf(), Buf(), Buf()
        B_wa = [Buf(), Buf()]
        k.dma("sp", cin[:, :], cs[:, :], W=[B_cin])
        k.dma("sp", cpin[:, :], cp[:, :], W=[B_cpin])
        k.dma("sp", bada[:, :], b_ada[:, :], W=[B_bada])
        k.dma("sp", rows[0:1, 6 * D:7 * D], g1[:, :], W=[B_rows])
        k.dma("sp", rows[0:1, 7 * D:8 * D], g2[:, :], W=[B_rows])
        k.dma("sp", rows[0:1, 8 * D:9 * D], gf[:, :], W=[B_rows])
        pt = PS[0]
        for j in range(8):
            k.op("pe", lambda e, j=j: e.transpose(out=pt[:, j * 17:j * 17 + 1], in_=cpin[0:1, j * 128:(j + 1) * 128],
                                                  identity=identf[0:1, 0:1]), R=[B_cpin, B_identf], W=[PSB[0]])
            k.op("pe", lambda e, j=j: e.transpose(out=pt[:, j * 17 + 1:j * 17 + 17], in_=cin[:, j * 128:(j + 1) * 128],
                                                  identity=identf[0:NS, 0:NS]), R=[B_cin, B_identf], W=[PSB[0]])
        k.op("dve", lambda e: e.tensor_copy(out=cT[:, :, :], in_=pt[:, 0:136].rearrange("p (j s) -> p j s", s=17)),
             R=[PSB[0]], W=[B_cT])
        wview = w_ada.rearrange("(kc p) n -> p kc n", p=128)
        for n in range(12):
            wb = wa[n % 2]
            k.dma("sp", wb[:, :, :], wview[:, :, n * 512:(n + 1) * 512], W=[B_wa[n % 2]])
            pa, pb = PS[1 + (n % 2) * 2], PS[2 + (n % 2) * 2]
            Ba, Bb = PSB[1 + (n % 2) * 2], PSB[2 + (n % 2) * 2]
            for kc in range(8):
                k.op("pe", lambda e, kc=kc, wb=wb, pa=pa: e.matmul(pa[0:1, :], lhsT=cT[:, kc, 0:1], rhs=wb[:, kc, :],
                                                                  start=(kc == 0), stop=False),
                     R=[B_cT, B_wa[n % 2]], W=[Ba])
            k.op("pe", lambda e, n=n, pa=pa: e.matmul(pa[0:1, :], lhsT=ones[0:1, 0:1], rhs=bada[0:1, n * 512:(n + 1) * 512],
                                                      start=False, stop=True), R=[B_ones, B_bada], W=[Ba])
            for kc in range(8):
                k.op("pe", lambda e, kc=kc, wb=wb, pb=pb: e.matmul(pb[0:NS, :], lhsT=cT[:, kc, 1:17], rhs=wb[:, kc, :],
                                                                  start=(kc == 0), stop=False),
                     R=[B_cT, B_wa[n % 2]], W=[Bb])
            k.op("pe", lambda e, n=n, pb=pb: e.matmul(pb[0:NS, :], lhsT=ones[0:1, 0:NS], rhs=bada[0:1, n * 512:(n + 1) * 512],
                                                      start=False, stop=True), R=[B_ones, B_bada], W=[Bb])
            k.op("dve", lambda e, n=n, pa=pa: e.tensor_copy(out=rows[0:1, n * 512:(n + 1) * 512], in_=pa[0:1, :]),
                 R=[Ba], W=[B_rows])
            k.op("act", lambda e, n=n, pb=pb: e.activation(out=mods[:, n * 512:(n + 1) * 512], in_=pb[0:NS, :], func=AF.Identity),
                 R=[Bb], W=[B_mods])
        pt = PS[5]
        for j in range(72):
            k.op("pe", lambda e, j=j: e.matmul(pt[:, j:j + 1], lhsT=rows[0:1, j * 128:(j + 1) * 128], rhs=ones[0:1, 0:1],
                                               start=True, stop=True), R=[B_rows, B_ones], W=[PSB[5]])
        k.op("dve", lambda e: e.tensor_copy(out=vecT[:, :], in_=pt[:, 0:72]), R=[PSB[5]], W=[B_vecT])
        k.op("dve", lambda e: e.scalar_tensor_tensor(out=gm1[:, :], in0=vecT[:, 8:16], scalar=1.0, in1=vecT[:, 48:56],
                                                     op0=ALU.add, op1=ALU.mult), R=[B_vecT], W=[B_gm1])
        k.barrier()

    with contextlib.ExitStack() as st:
        xt = [sb(st, "xt%d" % i, [128, D], F32) for i in range(3)]
        B_xt = [Buf() for _ in range(3)]
        junk = sb(st, "junk", [128, D], F32)
        B_junk = Buf()
        ssq = [sb(st, "ssq%d" % i, [128, 1], F32) for i in range(2)]
        B_ssq = [Buf(), Buf()]
        xn = [sb(st, "xn%d" % i, [128, D], BF16) for i in range(2)]
        B_xn = [Buf(), Buf()]
        for tt in range(NT):
            x_, bx = xt[tt % 3], B_xt[tt % 3]
            sq, bs = ssq[tt % 2], B_ssq[tt % 2]
            xn_, bn = xn[tt % 2], B_xn[tt % 2]
            k.dma("sp", x_[:, :], xp[tt * 128:(tt + 1) * 128, :], W=[bx])
            k.op("act", lambda e, x_=x_, sq=sq: e.activation(out=junk[:, :], in_=x_[:, :], func=AF.Square, accum_out=sq[:, :]),
                 R=[bx], W=[B_junk, bs])
            k.op("act", lambda e, sq=sq: e.activation(out=sq[:, :], in_=sq[:, :], func=AF.Sqrt, scale=1.0 / D, bias=1e-6),
                 R=[bs], W=[bs])
            k.op("dve", lambda e, sq=sq: e.reciprocal(out=sq[:, :], in_=sq[:, :]), R=[bs], W=[bs])
            k.op("dve", lambda e, x_=x_, sq=sq, xn_=xn_: e.tensor_scalar(out=xn_[:, :], in0=x_[:, :], scalar1=sq[:, 0:1],
                                                                       scalar2=None, op0=ALU.mult), R=[bx, bs], W=[bn])
            pi = tt % 2
            ptv = PS[pi][:, :].bitcast(BF16)
            for j in range(8):
                k.op("pe", lambda e, j=j, xn_=xn_, ptv=ptv: e.transpose(out=ptv[:, j * 128:(j + 1) * 128],
                                                                       in_=xn_[:, j * 128:(j + 1) * 128], identity=ident[:, :]),
                     R=[bn, B_ident], W=[PSB[pi]])
            for j in range(8):
                if j % 2 == 0:
                    k.op("act", lambda e, j=j, ptv=ptv, tt=tt: e.activation(out=hT[:, j, tt * 128:(tt + 1) * 128],
                                                                           in_=ptv[:, j * 128:(j + 1) * 128], func=AF.Identity,
                                                                           scale=gm1[:, j:j + 1], bias=vecT[:, j:j + 1]),
                         R=[PSB[pi], B_gm1, B_vecT], W=[B_hT[tt]])
                else:
                    k.op("dve", lambda e, j=j, ptv=ptv, tt=tt: e.tensor_scalar(out=hT[:, j, tt * 128:(tt + 1) * 128],
                                                                              in0=ptv[:, j * 128:(j + 1) * 128],
                                                                              scalar1=gm1[:, j:j + 1], scalar2=vecT[:, j:j + 1],
                                                                              op0=ALU.mult, op1=ALU.add),
                         R=[PSB[pi], B_gm1, B_vecT], W=[B_hT[tt]])
        xsi = sb(st, "xsi", [NS, D], F32)
        gms = sb(st, "gms", [NS, D], F32)
        hs = sb(st, "hs", [NS, D], F32)
        hsb = sb(st, "hsb", [NS, D], BF16)
        sqs = sb(st, "sqs", [NS, 1], F32)
        B_xsi, B_gms, B_hs, B_hsb, B_sqs = Buf(), Buf(), Buf(), Buf(), Buf()
        k.dma("sp", xsi[:, :], xs[:, :], W=[B_xsi])
        k.op("act", lambda e: e.activation(out=junk[0:NS, :], in_=xsi[:, :], func=AF.Square, accum_out=sqs[:, :]),
             R=[B_xsi], W=[B_junk, B_sqs])
        k.op("act", lambda e: e.activation(out=sqs[:, :], in_=sqs[:, :], func=AF.Sqrt, scale=1.0 / D, bias=1e-6),
             R=[B_sqs], W=[B_sqs])
        k.op("dve", lambda e: e.reciprocal(out=sqs[:, :], in_=sqs[:, :]), R=[B_sqs], W=[B_sqs])
        for h in range(2):
            k.op("pe", lambda e, h=h: e.matmul(PS[2 + h][0:NS, :], lhsT=ones[0:1, 0:NS],
                                               rhs=rows[0:1, 6 * D + h * 512:6 * D + (h + 1) * 512], start=True, stop=True),
                 R=[B_ones, B_rows], W=[PSB[2 + h]])
            k.op("dve", lambda e, h=h: e.scalar_tensor_tensor(out=gms[:, h * 512:(h + 1) * 512],
                                                              in0=mods[:, D + h * 512:D + (h + 1) * 512], scalar=1.0,
                                                              in1=PS[2 + h][0:NS, :], op0=ALU.add, op1=ALU.mult),
                 R=[B_mods, PSB[2 + h]], W=[B_gms])
        k.op("dve", lambda e: e.scalar_tensor_tensor(out=hs[:, :], in0=xsi[:, :], scalar=sqs[:, 0:1], in1=gms[:, :],
                                                     op0=ALU.mult, op1=ALU.mult), R=[B_xsi, B_sqs, B_gms], W=[B_hs])
        k.op("dve", lambda e: e.tensor_tensor(out=hsb[:, :], in0=hs[:, :], in1=mods[:, 0:D], op=ALU.add),
             R=[B_hs, B_mods], W=[B_hsb])
        ptv = PS[4][:, :].bitcast(BF16)
        for j in range(8):
            k.op("pe", lambda e, j=j: e.transpose(out=ptv[:, j * NS:(j + 1) * NS], in_=hsb[:, j * 128:(j + 1) * 128],
                                                  identity=ident[0:NS, 0:NS]), R=[B_hsb, B_ident], W=[PSB[4]])
        k.op("dve", lambda e: e.tensor_copy(out=hTs[:, :, :], in_=ptv[:, 0:8 * NS].rearrange("p (j s) -> p j s", s=NS)),
             R=[PSB[4]], W=[B_hTs])
        k.dma("sp", mods_scr[:, :], mods[:, :], R=[B_mods], W=[B_scr2])
        k.barrier()
    stAB.close()

    with contextlib.ExitStack() as st:
        proj_s = sb(st, "proj_s", [NS, INW], F32)
        wbuf = [sb(st, "wbuf%d" % i, [128, 8, 512], BF16) for i in range(2)]
        B_wbuf = [Buf(), Buf()]
        stage = [sb(st, "stage%d" % i, [128, 4, 512], F32) for i in range(2)]
        B_stage = [Buf(), Buf()]
        pcv = sb(st, "pcv", [3, 3072], F32)
        fstage = [sb(st, "fstage%d" % i, [128, S], BF16) for i in range(2)]
        B_fstage = [Buf(), Buf()]
        fsi = [0]
        B_pcv = Buf()
        wv = w_in.rearrange("(kc p) n -> p kc n", p=128)
        jobs = []
        for c in range(3):
            jobs.append(("kv", C_KV + c * 512, 512, c))
        for c in range(6):
            jobs.append(("qkv", C_QKV + c * 512, 512, c))
        for c in range(2):
            jobs.append(("z", 5680 + c * 512, 512, c))
        jobs.append(("s", 6704, 16, 0))
        if STOP >= 9:
            for c in range(2):
                jobs.append(("feat", c * 512, 512, [(cc, q_scr[c * 4 + cc], 0.125) for cc in range(4)]))
            jobs.append(("feat", 1024, 512, [(cc, kvT_scr[cc], 1.0) for cc in range(4)]))
            jobs.append(("feat", 1536, 256, [(cc, kvT_scr[4 + cc], 1.0) for cc in range(2)]))
            jobs.append(("feat", 2048, 256, [(cc, kvT_scr[6 + cc], 1.0) for cc in range(2)]))
            jobs.append(("gate", 2560, 48, 0))
            for c in range(4):
                jobs.append(("mg", 6720 + c * 512, 512, c))
        psi = 0
        evi = 0
        sti = 0
        for ji, (kind, c0, ncol, ci) in enumerate(jobs):
            wb, bw = wbuf[ji % 2], B_wbuf[ji % 2]
            k.dma("pool", wb[:, :, 0:ncol], wv[:, :, c0:c0 + ncol], W=[bw])
            p_, bp = PS[psi % 8], PSB[psi % 8]
            psi += 1
            for kc in range(8):
                k.op("pe", lambda e, kc=kc, wb=wb, p_=p_, ncol=ncol: e.matmul(p_[0:NS, 0:ncol], lhsT=hTs[:, kc, :],
                                                                              rhs=wb[:, kc, 0:ncol], start=(kc == 0), stop=(kc == 7)),
                     R=[B_hTs, bw], W=[bp])
            k.op("act", lambda e, p_=p_, c0=c0, ncol=ncol: e.activation(out=proj_s[:, c0:c0 + ncol], in_=p_[0:NS, 0:ncol],
                                                                        func=AF.Identity), R=[bp], W=[B_projs])
            if kind == "feat":
                for (cc, dst, scl) in ci:
                    fs, bfs = fstage[fsi[0] % 2], B_fstage[fsi[0] % 2]
                    fsi[0] += 1
                    for tb in range(S // 512):
                        p_, bp = PS[psi % 8], PSB[psi % 8]
                        psi += 1
                        for kc in range(8):
                            k.op("pe", lambda e, kc=kc, wb=wb, p_=p_, tb=tb, cc=cc: e.matmul(p_[:, :], lhsT=wb[:, kc, cc * 128:(cc + 1) * 128],
                                                                                            rhs=hT[:, kc, tb * 512:(tb + 1) * 512], start=(kc == 0), stop=(kc == 7)),
                                 R=[B_hT[tb * 4 + i] for i in range(4)] + [bw], W=[bp])
                        if tb % 2 == 0:
                            k.op("act", lambda e, p_=p_, fs=fs, tb=tb, scl=scl: e.activation(out=fs[:, tb * 512:(tb + 1) * 512], in_=p_[:, :], func=AF.Identity, scale=scl),
                                 R=[bp], W=[bfs])
                        else:
                            k.op("dve", lambda e, p_=p_, fs=fs, tb=tb, scl=scl: e.tensor_scalar(out=fs[:, tb * 512:(tb + 1) * 512], in0=p_[:, :], scalar1=scl, scalar2=None, op0=ALU.mult),
                                 R=[bp], W=[bfs])
                    k.dma("sp", dst, fs[:, :], R=[bfs], W=[B_scr2])
            if kind in ("gate", "mg"):
                for tt in range(NT):
                    p_, bp = PS[psi % 8], PSB[psi % 8]
                    psi += 1
                    for kc in range(8):
                        k.op("pe", lambda e, kc=kc, wb=wb, p_=p_, tt=tt, ncol=ncol: e.matmul(p_[:, 0:ncol], lhsT=hT[:, kc, tt * 128:(tt + 1) * 128],
                                                                                            rhs=wb[:, kc, 0:ncol], start=(kc == 0), stop=(kc == 7)),
                             R=[B_hT[tt], bw], W=[bp])
                    sg, bsg = stage[sti % 2], B_stage[sti % 2]
                    t4 = tt % 4
                    k.op("act", lambda e, p_=p_, sg=sg, t4=t4, ncol=ncol: e.activation(out=sg[:, t4, 0:ncol], in_=p_[:, 0:ncol], func=AF.Sigmoid), R=[bp], W=[bsg])
                    if t4 == 3:
                        g = tt // 4
                        if kind == "gate":
                            k.dma("sp", gate_scr[g * 512:(g + 1) * 512, :].rearrange("(t p) d -> p t d", p=128), sg[:, :, 0:48], R=[bsg], W=[B_scr2])
                        else:
                            k.dma("sp", mg_scr[g * 512:(g + 1) * 512, ci * 512:(ci + 1) * 512].rearrange("(t p) d -> p t d", p=128), sg[:, :, :], R=[bsg], W=[B_scr2])
                        sti += 1
            if kind in ("kv", "z"):
                for tt in range(NT):
                    p_, bp = PS[psi % 8], PSB[psi % 8]
                    psi += 1
                    for kc in range(8):
                        k.op("pe", lambda e, kc=kc, wb=wb, p_=p_, tt=tt: e.matmul(p_[:, :], lhsT=hT[:, kc, tt * 128:(tt + 1) * 128],
                                                                                rhs=wb[:, kc, :], start=(kc == 0), stop=(kc == 7)),
                             R=[B_hT[tt], bw], W=[bp])
                    sg, bsg = stage[sti % 2], B_stage[sti % 2]
                    t4 = tt % 4
                    if evi % 2 == 0 or kind == "z":
                        k.op("act", lambda e, p_=p_, sg=sg, t4=t4, kind=kind: e.activation(out=sg[:, t4, :], in_=p_[:, :],
                                                                                        func=(AF.Silu if kind == "z" else AF.Identity)),
                             R=[bp], W=[bsg])
                    else:
                        k.op("dve", lambda e, p_=p_, sg=sg, t4=t4: e.tensor_copy(out=sg[:, t4, :], in_=p_[:, :]),
                             R=[bp], W=[bsg])
                    evi += 1
                    if t4 == 3 and kind == "z":
                        g = tt // 4
                        k.dma("sp", szt_scr[g * 512:(g + 1) * 512, ci * 512:(ci + 1) * 512].rearrange("(t p) d -> p t d", p=128),
                              sg[:, :, :], R=[bsg], W=[B_scr2])
                        sti += 1
                    elif t4 == 3:
                        g = tt // 4
                        for half in range(2):
                            k.dma("sp", o_pkv[2 * ci + half, g * 512:(g + 1) * 512, :].rearrange("(t p) d -> p t d", p=128),
                                  sg[:, :, half * 256:(half + 1) * 256], R=[bsg], W=[DOUT])
                        sti += 1
            elif kind == "qkv":
                p_, bp = PS[psi % 8], PSB[psi % 8]
                psi += 1
                for kc in range(8):
                    k.op("pe", lambda e, kc=kc, wb=wb, p_=p_: e.matmul(p_[0:3, :], lhsT=hT[:, kc, S - 3:S], rhs=wb[:, kc, :],
                                                                      start=(kc == 0), stop=(kc == 7)),
                         R=[B_hT[NT - 1], bw], W=[bp])
                k.op("dve", lambda e, p_=p_, ci=ci: e.tensor_copy(out=pcv[:, ci * 512:(ci + 1) * 512], in_=p_[0:3, :]),
                     R=[bp], W=[B_pcv])
        k.dma("sp", o_pconv[:, :], pcv[:, :], R=[B_pcv], W=[DOUT])
        k.dma("sp", o_skv[:, :], proj_s[:, C_KV:C_KV + 1536], R=[B_projs], W=[DOUT])
        k.dma("sp", o_sconv[:, 2, :], proj_s[:, C_QKV:C_QKV + 3072], R=[B_projs], W=[DOUT])
        k.dma("sp", o_sconv[:, 0:2, :], sconv[:, 1:3, :], W=[DOUT])
        k.dma("sp", o_skwin[:, 511, :], proj_s[:, C_KV + 1024:C_KV + 1280], R=[B_projs], W=[DOUT])
        k.dma("sp", o_svwin[:, 511, :], proj_s[:, C_KV + 1280:C_KV + 1536], R=[B_projs], W=[DOUT])
        k.dma("sp", projs_scr[:, :], proj_s[:, :], R=[B_projs], W=[B_scr2])
        for q4 in range(4):
            k.dma("sp", o_skwin[q4 * 4:(q4 + 1) * 4, 0:511, :], ckwin[q4 * 4:(q4 + 1) * 4, 1:512, :], W=[DOUT])
            k.dma("pool", o_svwin[q4 * 4:(q4 + 1) * 4, 0:511, :], cvwin[q4 * 4:(q4 + 1) * 4, 1:512, :], W=[DOUT])
        k.barrier()
    with contextlib.ExitStack() as st:
        NEG = -1.0e9
        wqkv = sb(st, "wqkv", [128, 8, 3072], BF16)
        wab = sb(st, "wab", [128, 8, 16], BF16)
        B_wqkv, B_wab = Buf(), Buf()
        for c in range(6):
            k.dma("pool", wqkv[:, :, c * 512:(c + 1) * 512], wv[:, :, C_QKV + c * 512:C_QKV + (c + 1) * 512], W=[B_wqkv])
        k.dma("pool", wab[:, :, :], wv[:, :, 6704:6720], W=[B_wab])
        cwT = sb(st, "cwT", [128, 24, 4], F32)
        dtb = sb(st, "dtb", [128, 8], F32)
        nAe = sb(st, "nAe", [128, 8], F32)
        ngb = sb(st, "ngb", [128, 128], F32)
        onesb = sb(st, "onesb", [128, 128], BF16)
        onesf = sb(st, "onesf", [128, 128], F32)
        maskA = sb(st, "maskA", [128, 128], F32)
        maskM = sb(st, "maskM", [128, 128], F32)
        Uc = sb(st, "Uc", [128, 128], F32)
        Uf = sb(st, "Uf", [128, 128], F32)
        Ue0 = sb(st, "Ue0", [128, 128], F32)
        Ue1 = sb(st, "Ue1", [128, 128], F32)
        B_c = Buf()
        k.dma("sp", dtb[:, :], dt_bias[0, :].partition_broadcast(128), W=[B_c])
        k.dma("sp", nAe[:, :], a_log[0, :].partition_broadcast(128), W=[B_c])
        k.dma("sp", ngb[:, :], dn_ng[0, :].partition_broadcast(128), W=[B_c])
        k.op("act", lambda e: e.activation(out=nAe[:, :], in_=nAe[:, :], func=AF.Exp), R=[B_c], W=[B_c])
        k.op("dve", lambda e: e.tensor_scalar(out=nAe[:, :], in0=nAe[:, :], scalar1=-1.0, scalar2=None, op0=ALU.mult), R=[B_c], W=[B_c])
        k.op("pool", lambda e: e.memset(onesb[:, :], 1.0), W=[B_c])
        k.op("pool", lambda e: e.memset(onesf[:, :], 1.0), W=[B_c])
        for (m_, base_) in ((maskA, 0), (maskM, -1)):
            k.op("pool", lambda e, m_=m_: e.memset(m_[:, :], 0.0), R=[B_c], W=[B_c])
            k.op("pool", lambda e, m_=m_, base_=base_: e.affine_select(out=m_[:, :], in_=m_[:, :], pattern=[[1, 128]], compare_op=ALU.is_ge,
                                                                     fill=NEG, base=base_, channel_multiplier=-1), R=[B_c], W=[B_c])
            k.op("pool", lambda e, m_=m_: e.memset(m_[0:64, 64:128], NEG), R=[B_c], W=[B_c])
        k.op("pool", lambda e: e.memset(Uc[:, :], 1.0), R=[B_c], W=[B_c])
        k.op("pool", lambda e: e.affine_select(out=Uc[:, :], in_=Uc[:, :], pattern=[[1, 128]], compare_op=ALU.is_ge,
                                               fill=0.0, base=0, channel_multiplier=-1), R=[B_c], W=[B_c])
        k.op("pool", lambda e: e.memset(Uc[0:64, 64:128], 0.0), R=[B_c], W=[B_c])
        k.op("pool", lambda e: e.memset(Uf[:, :], 0.0), R=[B_c], W=[B_c])
        k.op("pool", lambda e: e.memset(Uf[0:64, 0:64], 1.0), R=[B_c], W=[B_c])
        k.op("pool", lambda e: e.memset(Uf[64:128, 64:128], 1.0), R=[B_c], W=[B_c])
        k.op("pool", lambda e: e.memset(Ue0[:, :], 0.0), R=[B_c], W=[B_c])
        k.op("pool", lambda e: e.memset(Ue0[0:64, :], 1.0), R=[B_c], W=[B_c])
        k.op("pool", lambda e: e.memset(Ue1[:, :], 0.0), R=[B_c], W=[B_c])
        k.op("pool", lambda e: e.memset(Ue1[64:128, :], 1.0), R=[B_c], W=[B_c])
        k.barrier()

        with contextlib.ExitStack() as stc:
            cw4 = sb(stc, "cw4", [4, 3072], F32)
            k.dma("sp", cw4[:, :], conv_w[:, :], W=[B_c])
            ptc = PS[0]
            for j in range(24):
                k.op("pe", lambda e, j=j: e.transpose(out=ptc[:, j * 4:(j + 1) * 4], in_=cw4[0:4, j * 128:(j + 1) * 128],
                                                      identity=identf[0:4, 0:4]), R=[B_c, B_identf], W=[PSB[0]])
            k.op("dve", lambda e: e.tensor_copy(out=cwT[:, :, :], in_=ptc[:, 0:96].rearrange("p (j w) -> p j w", w=4)), R=[PSB[0]], W=[B_c])
            k.barrier()
        pslot = [0]

        def ps_half():
            i = pslot[0] % 8
            pslot[0] += 1
            return PS[i][:, 0:256], PSB[i]

        def ps_full():
            i = pslot[0] % 8
            pslot[0] += 1
            return PS[i][:, :], [PSB[i]]

        halo = sb(st, "halo", [128, 24, 3], F32)
        B_halo = [Buf() for _ in range(24)]
        k.op("pool", lambda e: e.memset(halo[:, :, :], 0.0), W=B_halo)
        xc = [sb(st, "xc0", [128, 515], F32)] * 2
        B_xc = [Buf()] * 2
        acc = [sb(st, "acc0", [128, 512], F32)] * 2
        B_acc = [Buf()] * 2
        ysl = [sb(st, "ysl0", [128, 512], F32)] * 2
        B_ysl = [Buf()] * 2
        sqb = [sb(st, "sqb0", [128, 512], BF16)] * 2
        B_sqb = [Buf()] * 2
        lnt = [sb(st, "lnt0", [128, 512], F32)] * 2
        B_lnt = [Buf()] * 2
        qkvT = [sb(st, "qkvT%d" % i, [128, 24, 512], BF16) for i in range(1)]
        B_qkvT = [[Buf() for _ in range(24)] for _ in range(1)]
        Sst = sb(st, "Sst", [128, 8, 128], F32)
        Sbf = sb(st, "Sbf", [128, 8, 128], BF16)
        B_S = [Buf() for _ in range(8)]
        B_Sbf = [Buf() for _ in range(8)]
        k.op("pool", lambda e: e.memset(Sst[:, :, :], 0.0), W=B_S)
        k.op("pool", lambda e: e.memset(Sbf[:, :, :], 0.0), W=B_Sbf)
        sm = [sb(st, "sm%d" % i, [128, 96], F32) for i in range(2)]
        B_sm = [Buf(), Buf()]
        NW = 2
        dg = [sb(st, "dg%d" % i, [128, 2, 128], F32) for i in range(NW)]
        GG = dg
        DD = [sb(st, "DD%d" % i, [128, 2, 128], F32) for i in range(NW)]
        EBt = [sb(st, "EB%d" % i, [128, 128], F32) for i in range(NW)]
        MR = [[sb(st, "MR%d_%d" % (i, j), [128, 2, 128], F32) for j in range(2)] for i in range(NW)]
        LL = [[sb(st, "LL%d_%d" % (i, j), [128, 128], F32) for j in range(2)] for i in range(NW)]
        kbg = [sb(st, "kbg%d" % i, [128, 128], F32) for i in range(NW)]
        vb = [sb(st, "vb%d" % i, [128, 128], F32) for i in range(NW)]
        B_dg = [Buf() for _ in range(NW)]
        B_GG = B_dg
        B_DD = [Buf() for _ in range(NW)]
        B_EB = [Buf() for _ in range(NW)]
        B_MR = [[Buf(), Buf()] for _ in range(NW)]
        B_LL = [[Buf(), Buf()] for _ in range(NW)]
        B_kbg = [Buf() for _ in range(NW)]
        B_vb = [Buf() for _ in range(NW)]
        AT = [[sb(st, "AT_%d" % h, [128, 128], BF16) for h in range(8)]] * 2
        usb = [[sb(st, "usb_%d" % h, [128, 128], F32) for h in range(8)]] * 2
        wTs = [[sb(st, "wTs_%d" % h, [128, 128], F32) for h in range(8)]] * 2
        qdT = [[sb(st, "qdT_%d" % h, [128, 128], BF16) for h in range(8)]] * 2
        kdc = [[sb(st, "kdc_%d" % h, [128, 128], F32) for h in range(8)]] * 2
        B_AT = [[Buf() for _ in range(8)]] * 2
        B_usb = [[Buf() for _ in range(8)]] * 2
        B_wTs = [[Buf() for _ in range(8)]] * 2
        B_qdT = [[Buf() for _ in range(8)]] * 2
        B_kdc = [[Buf() for _ in range(8)]] * 2
        vnw = [sb(st, "vnw%d" % h, [128, 128], F32) for h in range(8)]
        vnb = [sb(st, "vnb%d" % h, [128, 128], BF16) for h in range(8)]
        B_vnw = [Buf() for _ in range(8)]
        B_vnb = [Buf() for _ in range(8)]
        otile = [sb(st, "otile%d" % i, [128, 8, 128], F32) for i in range(1)] * 2
        B_ot = [[Buf() for _ in range(8)]] * 2
        orn = [sb(st, "orn%d" % i, [128, 8], F32) for i in range(2)]
        B_orn = [Buf(), Buf()]
        szt = [sb(st, "szt0", [128, 1024], F32)] * 2
        B_sz = [Buf()] * 2
        odn = [sb(st, "odn0", [128, 1024], F32)] * 2
        B_odn = [Buf()] * 2
        B_scr = Buf()
        wi = [0]

        for gq in range(int(os.environ.get('DN_NG', '8'))):
            qb = qkvT[0]
            Bq = B_qkvT[0]
            hR = [B_hT[gq * 4 + i] for i in range(4)]
            for j in range(24):
                pf, bpf = ps_full()
                for kc in range(8):
                    k.op("pe", lambda e, kc=kc, j=j, pf=pf: e.matmul(pf, lhsT=wqkv[:, kc, j * 128:(j + 1) * 128],
                                                                    rhs=hT[:, kc, gq * 512:(gq + 1) * 512], start=(kc == 0), stop=(kc == 7)),
                         R=hR + [B_wqkv], W=bpf)
                x_, bx = xc[j % 2], B_xc[j % 2]
                a_, ba = acc[j % 2], B_acc[j % 2]
                y_, by = ysl[j % 2], B_ysl[j % 2]
                k.op("act", lambda e, x_=x_, pf=pf: e.activation(out=x_[:, 3:515], in_=pf, func=AF.Identity), R=bpf, W=[bx])
                k.op("pool", lambda e, x_=x_, j=j: e.tensor_copy(out=x_[:, 0:3], in_=halo[:, j, :]), R=[B_halo[j]], W=[bx])
                k.op("pool", lambda e, x_=x_, j=j: e.tensor_copy(out=halo[:, j, :], in_=x_[:, 512:515]), R=[bx], W=[B_halo[j]])
                k.op("dve", lambda e, x_=x_, a_=a_, j=j: e.tensor_scalar(out=a_[:, :], in0=x_[:, 0:512], scalar1=cwT[:, j, 0:1], scalar2=None,
                                                                       op0=ALU.mult), R=[bx, B_c], W=[ba])
                for w_ in range(1, 4):
                    k.op("dve", lambda e, x_=x_, a_=a_, j=j, w_=w_: e.scalar_tensor_tensor(out=a_[:, :], in0=x_[:, w_:w_ + 512],
                                                                                         scalar=cwT[:, j, w_:w_ + 1], in1=a_[:, :],
                                                                                         op0=ALU.mult, op1=ALU.add), R=[bx, B_c, ba], W=[ba])
                if j >= 16:
                    k.op("act", lambda e, a_=a_, j=j: e.activation(out=qb[:, j, :], in_=a_[:, :], func=AF.Silu), R=[ba], W=[Bq[j]])
                else:
                    s_, bs_ = sqb[j % 2], B_sqb[j % 2]
                    l_, bl_ = lnt[j % 2], B_lnt[j % 2]
                    k.op("act", lambda e, a_=a_, y_=y_: e.activation(out=y_[:, :], in_=a_[:, :], func=AF.Silu), R=[ba], W=[by])
                    k.op("act", lambda e, y_=y_, s_=s_: e.activation(out=s_[:, :], in_=y_[:, :], func=AF.Square), R=[by], W=[bs_])
                    pf2, bpf2 = ps_full()
                    k.op("pe", lambda e, s_=s_, pf2=pf2: e.matmul(pf2, lhsT=onesb[:, :], rhs=s_[:, :], start=True, stop=True),
                         R=[bs_, B_c], W=bpf2)
                    k.op("act", lambda e, l_=l_, pf2=pf2: e.activation(out=l_[:, :], in_=pf2, func=AF.Ln, bias=1e-6), R=bpf2, W=[bl_])
                    k.op("act", lambda e, l_=l_: e.activation(out=l_[:, :], in_=l_[:, :], func=AF.Exp, scale=-0.5), R=[bl_], W=[bl_])
                    sc_ = (128.0 ** -0.5) if j < 8 else 1.0
                    k.op("dve", lambda e, y_=y_, l_=l_, j=j, sc_=sc_: e.scalar_tensor_tensor(out=qb[:, j, :], in0=y_[:, :], scalar=sc_,
                                                                                           in1=l_[:, :], op0=ALU.mult, op1=ALU.mult),
                         R=[by, bl_], W=[Bq[j]])
            if STOP < 2:
                continue
            for pl in range(4):
                p = gq * 4 + pl
                par = p % 2
                tsl = slice(pl * 128, (pl + 1) * 128)
                tok = slice(p * 128, (p + 1) * 128)
                s_, bsm = sm[par], B_sm[par]
                ph, bph = ps_half()
                for kc in range(8):
                    k.op("pe", lambda e, kc=kc, ph=ph: e.matmul(ph[:, 0:16], lhsT=hT[:, kc, tok], rhs=wab[:, kc, :], start=(kc == 0), stop=(kc == 7)),
                         R=[B_hT[p], B_wab], W=[bph])
                k.op("dve", lambda e, ph=ph, s_=s_: e.tensor_tensor(out=s_[:, 0:8], in0=ph[:, 0:8], in1=dtb[:, :], op=ALU.add), R=[bph, B_c], W=[bsm])
                k.op("act", lambda e, s_=s_: e.activation(out=s_[:, 0:8], in_=s_[:, 0:8], func=AF.Exp), R=[bsm], W=[bsm])
                k.op("act", lambda e, s_=s_: e.activation(out=s_[:, 0:8], in_=s_[:, 0:8], func=AF.Ln, bias=1.0), R=[bsm], W=[bsm])
                k.op("dve", lambda e, s_=s_: e.tensor_tensor(out=s_[:, 0:8], in0=s_[:, 0:8], in1=nAe[:, :], op=ALU.mult), R=[bsm, B_c], W=[bsm])
                k.op("act", lambda e, ph=ph, s_=s_: e.activation(out=s_[:, 8:16], in_=ph[:, 8:16], func=AF.Sigmoid), R=[bph], W=[bsm])
                k.op("act", lambda e, s_=s_: e.activation(out=s_[:, 16:24], in_=s_[:, 8:16], func=AF.Ln), R=[bsm], W=[bsm])
                ph2, bph2 = ps_half()
                for ui, U_ in enumerate((Uc, Uf, Ue0, Ue1)):
                    k.op("pe", lambda e, ui=ui, U_=U_, ph2=ph2, s_=s_: e.matmul(ph2[:, ui * 8:(ui + 1) * 8], lhsT=U_[:, :], rhs=s_[:, 0:8],
                                                                             start=True, stop=True), R=[bsm, B_c], W=[bph2])
                k.op("dve", lambda e, ph2=ph2, s_=s_: e.tensor_copy(out=s_[:, 24:32], in_=ph2[:, 0:8]), R=[bph2], W=[bsm])
                k.op("dve", lambda e, s_=s_: e.tensor_scalar(out=s_[:, 32:40], in0=s_[:, 24:32], scalar1=-1.0, scalar2=None, op0=ALU.mult), R=[bsm], W=[bsm])
                k.op("dve", lambda e, s_=s_: e.tensor_tensor(out=s_[:, 40:48], in0=s_[:, 24:32], in1=s_[:, 16:24], op=ALU.add), R=[bsm], W=[bsm])
                k.op("act", lambda e, s_=s_: e.activation(out=s_[:, 48:56], in_=s_[:, 24:32], func=AF.Exp), R=[bsm], W=[bsm])
                k.op("dve", lambda e, s_=s_: e.tensor_tensor(out=s_[:, 48:56], in0=s_[:, 48:56], in1=s_[:, 8:16], op=ALU.mult), R=[bsm], W=[bsm])
                k.op("dve", lambda e, ph2=ph2, s_=s_: e.tensor_tensor(out=s_[:, 56:64], in0=ph2[:, 8:16], in1=s_[:, 24:32], op=ALU.subtract), R=[bph2, bsm], W=[bsm])
                k.op("act", lambda e, s_=s_: e.activation(out=s_[:, 56:64], in_=s_[:, 56:64], func=AF.Exp), R=[bsm], W=[bsm])
                k.op("act", lambda e, ph2=ph2, s_=s_: e.activation(out=s_[:, 64:80], in_=ph2[:, 16:32], func=AF.Exp), R=[bph2], W=[bsm])
                sz_, bsz = szt[par], B_sz[par]
                k.dma("sp", sz_[:, :], szt_scr[tok, :], R=[B_scr2], W=[bsz])
                for h in range(8 if STOP >= 3 else 0):
                    w = wi[0] % NW
                    wi[0] += 1
                    qT_ = qb[:, h, tsl]
                    kT_ = qb[:, 8 + h, tsl]
                    vT_ = qb[:, 16 + h, tsl]
                    Rq = [Bq[h]]
                    Rk = [Bq[8 + h]]
                    Rv = [Bq[16 + h]]
                    k.op("dve", lambda e, w=w, s_=s_, h=h: e.tensor_scalar(out=dg[w][:, 0, :], in0=identf[:, :], scalar1=s_[:, 24 + h:25 + h],
                                                                         scalar2=None, op0=ALU.mult), R=[bsm, B_identf], W=[B_dg[w]])
                    k.op("dve", lambda e, w=w, s_=s_, h=h: e.tensor_scalar(out=dg[w][:, 1, :], in0=identf[:, :], scalar1=s_[:, 40 + h:41 + h],
                                                                         scalar2=None, op0=ALU.mult), R=[bsm, B_identf], W=[B_dg[w]])
                    pbc, bpbc = ps_half()
                    k.op("pe", lambda e, w=w, pbc=pbc: e.matmul(pbc, lhsT=onesf[:, :], rhs=dg[w][:, :, :].rearrange("p a b -> p (a b)"),
                                                               start=True, stop=True), R=[B_dg[w], B_c], W=[bpbc])
                    k.op("dve", lambda e, w=w, pbc=pbc, s_=s_, h=h: e.scalar_tensor_tensor(out=GG[w][:, 0, :], in0=pbc[:, 0:128], scalar=s_[:, 32 + h:33 + h],
                                                                                         in1=maskA[:, :], op0=ALU.add, op1=ALU.add),
                         R=[bpbc, bsm, B_c], W=[B_GG[w]])
                    k.op("dve", lambda e, w=w, pbc=pbc, s_=s_, h=h: e.scalar_tensor_tensor(out=GG[w][:, 1, :], in0=pbc[:, 128:256], scalar=s_[:, 32 + h:33 + h],
                                                                                         in1=maskM[:, :], op0=ALU.add, op1=ALU.add),
                         R=[bpbc, bsm, B_c], W=[B_GG[w]])
                    k.op("act", lambda e, w=w: e.activation(out=DD[w][:, :, :], in_=GG[w][:, :, :], func=AF.Exp), R=[B_GG[w]], W=[B_DD[w]])
                    k.op("act", lambda e, w=w, pbc=pbc: e.activation(out=EBt[w][:, :], in_=pbc[:, 0:128], func=AF.Exp), R=[bpbc], W=[B_EB[w]])
                    k.op("dve", lambda e, w=w, qT_=qT_, par=par, h=h: e.tensor_tensor(out=qdT[par][h][:, :], in0=qT_, in1=EBt[w][:, :], op=ALU.mult),
                         R=Rq + [B_EB[w]], W=[B_qdT[par][h]])
                    pkq, bpkq = ps_half()
                    k.op("pe", lambda e, pkq=pkq, kT_=kT_, qT_=qT_: e.matmul(pkq[:, 0:128], lhsT=kT_, rhs=qT_, start=True, stop=True), R=Rk + Rq, W=[bpkq])
                    k.op("pe", lambda e, pkq=pkq, kT_=kT_: e.matmul(pkq[:, 128:256], lhsT=kT_, rhs=kT_, start=True, stop=True), R=Rk, W=[bpkq])
                    k.op("dve", lambda e, w=w, pkq=pkq, par=par, h=h: e.tensor_tensor(out=AT[par][h][:, :], in0=pkq[:, 0:128], in1=DD[w][:, 0, :], op=ALU.mult),
                         R=[bpkq, B_DD[w]], W=[B_AT[par][h]])
                    k.op("dve", lambda e, w=w, pkq=pkq: e.tensor_tensor(out=MR[w][0][:, 0, :], in0=pkq[:, 128:256], in1=DD[w][:, 1, :], op=ALU.mult),
                         R=[bpkq, B_DD[w]], W=[B_MR[w][0]])
                    k.op("dve", lambda e, w=w: e.tensor_tensor(out=MR[w][0][:, 1, :], in0=identf[:, :], in1=MR[w][0][:, 0, :], op=ALU.subtract),
                         R=[B_identf, B_MR[w][0]], W=[B_MR[w][0]])
                    ptr, bptr = ps_half()
                    ptrb = ptr
                    k.op("pe", lambda e, w=w, ptrb=ptrb: e.transpose(out=ptrb[:, 0:128], in_=MR[w][0][:, 0, :], identity=identf[:, :]),
                         R=[B_MR[w][0], B_identf], W=[bptr])
                    k.op("act", lambda e, w=w, ptrb=ptrb: e.activation(out=LL[w][0][:, :], in_=ptrb[:, 0:128], func=AF.Identity), R=[bptr], W=[B_LL[w][0]])
                    cur = 0
                    for lev in range(6):
                        nxt = 1 - cur
                        if lev == 0:
                            pm, bpm = ps_half()
                            k.op("pe", lambda e, w=w, cur=cur, pm=pm: e.matmul(pm[:, 0:128], lhsT=LL[w][cur][:, :], rhs=MR[w][cur][:, 0, :], start=True, stop=True),
                                 R=[B_LL[w][cur], B_MR[w][cur]], W=[bpm])
                            k.op("pe", lambda e, w=w, cur=cur, pm=pm: e.matmul(pm[:, 128:256], lhsT=MR[w][cur][:, 0, :], rhs=LL[w][cur][:, :], start=True, stop=True),
                                 R=[B_LL[w][cur], B_MR[w][cur]], W=[bpm])
                            k.op("act", lambda e, w=w, nxt=nxt, pm=pm: e.activation(out=MR[w][nxt][:, 0, :], in_=pm[:, 0:128], func=AF.Identity), R=[bpm], W=[B_MR[w][nxt]])
                            k.op("dve", lambda e, w=w, cur=cur, nxt=nxt: e.tensor_copy(out=MR[w][nxt][:, 1, :], in_=MR[w][cur][:, 1, :]), R=[B_MR[w][cur]], W=[B_MR[w][nxt]])
                            k.op("act", lambda e, w=w, nxt=nxt, pm=pm: e.activation(out=LL[w][nxt][:, :], in_=pm[:, 128:256], func=AF.Identity), R=[bpm], W=[B_LL[w][nxt]])
                        elif lev < 5:
                            pm, bpm = ps_half()
                            pl2, bpl2 = ps_half()
                            k.op("pe", lambda e, w=w, cur=cur, pm=pm: e.matmul(pm, lhsT=LL[w][cur][:, :], rhs=MR[w][cur][:, :, :].rearrange("p a b -> p (a b)"),
                                                                              start=True, stop=True), R=[B_LL[w][cur], B_MR[w][cur]], W=[bpm])
                            k.op("pe", lambda e, w=w, cur=cur, pl2=pl2: e.matmul(pl2[:, 0:128], lhsT=MR[w][cur][:, 0, :], rhs=LL[w][cur][:, :], start=True, stop=True),
                                 R=[B_LL[w][cur], B_MR[w][cur]], W=[bpl2])
                            k.op("act", lambda e, w=w, nxt=nxt, pm=pm: e.activation(out=MR[w][nxt][:, 0, :], in_=pm[:, 0:128], func=AF.Identity), R=[bpm], W=[B_MR[w][nxt]])
                            k.op("dve", lambda e, w=w, cur=cur, nxt=nxt, pm=pm: e.tensor_tensor(out=MR[w][nxt][:, 1, :], in0=pm[:, 128:256], in1=MR[w][cur][:, 1, :], op=ALU.add),
                                 R=[bpm, B_MR[w][cur]], W=[B_MR[w][nxt]])
                            k.op("act", lambda e, w=w, nxt=nxt, pl2=pl2: e.activation(out=LL[w][nxt][:, :], in_=pl2[:, 0:128], func=AF.Identity), R=[bpl2], W=[B_LL[w][nxt]])
                        else:
                            pm, bpm = ps_half()
                            k.op("pe", lambda e, w=w, cur=cur, pm=pm: e.matmul(pm[:, 0:128], lhsT=LL[w][cur][:, :], rhs=MR[w][cur][:, 1, :], start=True, stop=True),
                                 R=[B_LL[w][cur], B_MR[w][cur]], W=[bpm])
                            k.op("dve", lambda e, w=w, cur=cur, nxt=nxt, pm=pm: e.tensor_tensor(out=MR[w][nxt][:, 1, :], in0=pm[:, 0:128], in1=MR[w][cur][:, 1, :], op=ALU.add),
                                 R=[bpm, B_MR[w][cur]], W=[B_MR[w][nxt]])
                        cur = nxt
                    Rfin = MR[w][cur][:, 1, :]
                    B_Rfin = B_MR[w][cur]
                    pkt, bpkt = ps_half()
                    pktb = pkt.bitcast(BF16)
                    k.op("pe", lambda e, pktb=pktb, kT_=kT_: e.transpose(out=pktb[:, 0:128], in_=kT_, identity=ident[:, :]), R=Rk + [B_ident], W=[bpkt])
                    k.op("pe", lambda e, pktb=pktb, vT_=vT_: e.transpose(out=pktb[:, 128:256], in_=vT_, identity=ident[:, :]), R=Rv + [B_ident], W=[bpkt])
                    k.op("act", lambda e, w=w, pktb=pktb, s_=s_, h=h: e.activation(out=kbg[w][:, :], in_=pktb[:, 0:128], func=AF.Identity, scale=s_[:, 48 + h:49 + h]),
                         R=[bpkt, bsm], W=[B_kbg[w]])
                    k.op("act", lambda e, pktb=pktb, s_=s_, h=h, par=par: e.activation(out=kdc[par][h][:, :], in_=pktb[:, 0:128], func=AF.Identity, scale=s_[:, 56 + h:57 + h]),
                         R=[bpkt, bsm], W=[B_kdc[par][h]])
                    k.op("act", lambda e, w=w, pktb=pktb, s_=s_, h=h: e.activation(out=vb[w][:, :], in_=pktb[:, 128:256], func=AF.Identity, scale=s_[:, 8 + h:9 + h]),
                         R=[bpkt, bsm], W=[B_vb[w]])
                    puw, bpuw = ps_half()
                    k.op("pe", lambda e, w=w, puw=puw, Rfin=Rfin: e.matmul(puw[:, 0:128], lhsT=Rfin, rhs=vb[w][:, :], start=True, stop=True),
                         R=[B_Rfin, B_vb[w]], W=[bpuw])
                    k.op("pe", lambda e, w=w, puw=puw, Rfin=Rfin: e.matmul(puw[:, 128:256], lhsT=kbg[w][:, :], rhs=Rfin, start=True, stop=True),
                         R=[B_Rfin, B_kbg[w]], W=[bpuw])
                    k.op("dve", lambda e, puw=puw, par=par, h=h: e.tensor_copy(out=usb[par][h][:, :], in_=puw[:, 0:128]), R=[bpuw], W=[B_usb[par][h]])
                    k.op("dve", lambda e, puw=puw, par=par, h=h: e.tensor_copy(out=wTs[par][h][:, :], in_=puw[:, 128:256]), R=[bpuw], W=[B_wTs[par][h]])
                ot_, bot = otile[par], B_ot[par]
                if STOP < 4:
                    continue
                for e_ in range(2):
                    rs = slice(e_ * 64, (e_ + 1) * 64)
                    for h in range(8):
                        p1, bp1 = ps_half()
                        k.op("pe", lambda e, p1=p1, par=par, h=h: e.matmul(p1[:, 0:128], lhsT=wTs[par][h][:, :], rhs=Sst[:, h, :], start=True, stop=True),
                             R=[B_wTs[par][h], B_S[h]], W=[bp1])
                        k.op("dve", lambda e, p1=p1, par=par, h=h, rs=rs: e.tensor_tensor(out=vnw[h][rs, :], in0=usb[par][h][rs, :], in1=p1[rs, 0:128], op=ALU.subtract),
                             R=[bp1, B_usb[par][h]], W=[B_vnw[h]])
                        k.op("act", lambda e, h=h, rs=rs: e.activation(out=vnb[h][rs, :], in_=vnw[h][rs, :], func=AF.Identity), R=[B_vnw[h]], W=[B_vnb[h]])
                        k.op("pe", lambda e, p1=p1, par=par, h=h: e.matmul(p1[:, 128:256], lhsT=qdT[par][h][:, :], rhs=Sbf[:, h, :], start=True, stop=False),
                             R=[B_qdT[par][h], B_Sbf[h]], W=[bp1])
                        k.op("pe", lambda e, p1=p1, par=par, h=h, rs=rs: e.matmul(p1[:, 128:256], lhsT=AT[par][h][rs, :], rhs=vnb[h][rs, :], start=False, stop=True),
                             R=[B_AT[par][h], B_vnb[h]], W=[bp1])
                        k.op("act", lambda e, p1=p1, ot_=ot_, h=h, rs=rs: e.activation(out=ot_[rs, h, :], in_=p1[rs, 128:256], func=AF.Identity), R=[bp1], W=[bot[h]])
                        p2, bp2 = ps_half()
                        k.op("pe", lambda e, p2=p2, par=par, h=h, rs=rs: e.matmul(p2[:, 0:128], lhsT=kdc[par][h][rs, :], rhs=vnw[h][rs, :], start=True, stop=True),
                             R=[B_kdc[par][h], B_vnw[h]], W=[bp2])
                        k.op("dve", lambda e, p2=p2, h=h, s_=s_, e_=e_: e.scalar_tensor_tensor(out=Sst[:, h, :], in0=Sst[:, h, :], scalar=s_[:, 64 + e_ * 8 + h:65 + e_ * 8 + h],
                                                                                            in1=p2[:, 0:128], op0=ALU.mult, op1=ALU.add),
                             R=[bp2, bsm, B_S[h]], W=[B_S[h]])
                        k.op("act", lambda e, h=h: e.activation(out=Sbf[:, h, :], in_=Sst[:, h, :], func=AF.Identity), R=[B_S[h]], W=[B_Sbf[h]])
                on_, bon = odn[par], B_odn[par]
                osq = on_[:, :].rearrange("p (h d) -> p h d", d=128)
                k.op("act", lambda e, ot_=ot_, osq=osq: e.activation(out=osq, in_=ot_[:, :, :], func=AF.Square), R=bot, W=[bon])
                k.op("dve", lambda e, par=par, osq=osq: e.tensor_reduce(out=orn[par][:, :], in_=osq, axis=mybir.AxisListType.X, op=ALU.add), R=[bon], W=[B_orn[par]])
                k.op("act", lambda e, par=par: e.activation(out=orn[par][:, :], in_=orn[par][:, :], func=AF.Sqrt, scale=1.0 / 128, bias=1e-6), R=[B_orn[par]], W=[B_orn[par]])
                k.op("dve", lambda e, par=par: e.reciprocal(out=orn[par][:, :], in_=orn[par][:, :]), R=[B_orn[par]], W=[B_orn[par]])
                for h in range(8):
                    k.op("dve", lambda e, ot_=ot_, on_=on_, h=h, par=par: e.scalar_tensor_tensor(out=on_[:, h * 128:(h + 1) * 128], in0=ot_[:, h, :], scalar=orn[par][:, h:h + 1],
                                                                                               in1=ngb[:, :], op0=ALU.mult, op1=ALU.mult),
                         R=[bot[h], B_orn[par], B_c], W=[bon])
                k.op("dve", lambda e, on_=on_, sz_=sz_: e.tensor_tensor(out=on_[:, :], in0=on_[:, :], in1=sz_[:, :], op=ALU.mult), R=[bon, bsz], W=[bon])
                k.dma("sp", odn_scr[tok, :], on_[:, :], R=[bon], W=[B_scr])
        for h in range(8):
            k.dma("sp", o_pdn[h, :, :], Sst[:, h, :], R=[B_S[h]], W=[DOUT])
        k.barrier()
    with contextlib.ExitStack() as st:
        onesf2 = sb(st, "onesf2", [128, 128], F32)
        B_o2 = Buf()
        k.op("pool", lambda e: e.memset(onesf2[:, :], 1.0), W=[B_o2])
        sq = sb(st, "s_qkv", [NS, 3072], F32)
        scv = sb(st, "s_scv", [NS, 3, 3072], F32)
        cwb = sb(st, "s_cwb", [NS, 4, 3072], F32)
        yv = sb(st, "s_y", [NS, 3072], F32)
        sab = sb(st, "s_ab", [NS, 16], F32)
        sdt = sb(st, "s_dt", [NS, 8], F32)
        sAe = sb(st, "s_Ae", [NS, 8], F32)
        val = sb(st, "s_val", [NS, 32], F32)
        B_sq, B_scv, B_cwb, B_yv, B_sab, B_val = Buf(), Buf(), Buf(), Buf(), Buf(), Buf()
        k.dma("sp", sq[:, :], projs_scr[:, C_QKV:C_QKV + 3072], R=[B_scr2], W=[B_sq])
        k.dma("sp", scv[:, :, :], sconv[:, :, :], W=[B_scv])
        for w_ in range(4):
            k.dma("sp", cwb[:, w_, :], conv_w[w_, :].partition_broadcast(NS), W=[B_cwb])
        k.dma("sp", sab[:, :], projs_scr[:, 6704:6720], R=[B_scr2], W=[B_sab])
        k.dma("sp", sdt[:, :], dt_bias[0, :].partition_broadcast(NS), W=[B_sab])
        k.dma("sp", sAe[:, :], a_log[0, :].partition_broadcast(NS), W=[B_sab])
        k.op("dve", lambda e: e.tensor_tensor(out=yv[:, :], in0=sq[:, :], in1=cwb[:, 3, :], op=ALU.mult), R=[B_sq, B_cwb], W=[B_yv])
        for w_ in range(3):
            k.op("dve", lambda e, w_=w_: e.tensor_tensor(out=scv[:, w_, :], in0=scv[:, w_, :], in1=cwb[:, w_, :], op=ALU.mult), R=[B_scv, B_cwb], W=[B_scv])
            k.op("dve", lambda e, w_=w_: e.tensor_tensor(out=yv[:, :], in0=yv[:, :], in1=scv[:, w_, :], op=ALU.add), R=[B_scv, B_yv], W=[B_yv])
        k.op("act", lambda e: e.activation(out=yv[:, :], in_=yv[:, :], func=AF.Silu), R=[B_yv], W=[B_yv])
        ssq = sb(st, "s_ssq", [NS, 16], F32)
        B_ssq = Buf()
        sqr = scv[:, 0, 0:2048]
        k.op("act", lambda e: e.activation(out=sqr, in_=yv[:, 0:2048], func=AF.Square), R=[B_yv, B_scv], W=[B_scv])
        k.op("dve", lambda e: e.tensor_reduce(out=ssq[:, :], in_=sqr.rearrange("p (h d) -> p h d", d=128), axis=mybir.AxisListType.X, op=ALU.add),
             R=[B_scv], W=[B_ssq])
        k.op("act", lambda e: e.activation(out=ssq[:, :], in_=ssq[:, :], func=AF.Sqrt, bias=1e-6), R=[B_ssq], W=[B_ssq])
        k.op("dve", lambda e: e.reciprocal(out=ssq[:, :], in_=ssq[:, :]), R=[B_ssq], W=[B_ssq])
        k.op("dve", lambda e: e.tensor_scalar(out=ssq[:, 0:8], in0=ssq[:, 0:8], scalar1=128.0 ** -0.5, scalar2=None, op0=ALU.mult), R=[B_ssq], W=[B_ssq])
        for hh in range(16):
            k.op("dve", lambda e, hh=hh: e.tensor_scalar(out=yv[:, hh * 128:(hh + 1) * 128], in0=yv[:, hh * 128:(hh + 1) * 128],
                                                       scalar1=ssq[:, hh:hh + 1], scalar2=None, op0=ALU.mult), R=[B_ssq, B_yv], W=[B_yv])
        k.op("dve", lambda e: e.tensor_tensor(out=scv[:, 1, 0:1024], in0=yv[:, 0:1024], in1=yv[:, 1024:2048], op=ALU.mult), R=[B_yv, B_scv], W=[B_scv])
        k.op("dve", lambda e: e.tensor_reduce(out=val[:, 24:32], in_=scv[:, 1, 0:1024].rearrange("p (h d) -> p h d", d=128), axis=mybir.AxisListType.X, op=ALU.add),
             R=[B_scv], W=[B_val])
        k.op("dve", lambda e: e.tensor_tensor(out=sab[:, 0:8], in0=sab[:, 0:8], in1=sdt[:, :], op=ALU.add), R=[B_sab], W=[B_sab])
        k.op("act", lambda e: e.activation(out=sab[:, 0:8], in_=sab[:, 0:8], func=AF.Exp), R=[B_sab], W=[B_sab])
        k.op("act", lambda e: e.activation(out=sab[:, 0:8], in_=sab[:, 0:8], func=AF.Ln, bias=1.0), R=[B_sab], W=[B_sab])
        k.op("act", lambda e: e.activation(out=sAe[:, :], in_=sAe[:, :], func=AF.Exp), R=[B_sab], W=[B_sab])
        k.op("dve", lambda e: e.tensor_tensor(out=sab[:, 0:8], in0=sab[:, 0:8], in1=sAe[:, :], op=ALU.mult), R=[B_sab], W=[B_sab])
        k.op("act", lambda e: e.activation(out=val[:, 0:8], in_=sab[:, 0:8], func=AF.Exp, scale=-1.0), R=[B_sab], W=[B_val])
        k.op("act", lambda e: e.activation(out=val[:, 8:16], in_=sab[:, 8:16], func=AF.Sigmoid), R=[B_sab], W=[B_val])
        k.op("dve", lambda e: e.tensor_tensor(out=val[:, 16:24], in0=val[:, 0:8], in1=val[:, 8:16], op=ALU.mult), R=[B_val], W=[B_val])
        vex = sb(st, "s_vex", [NS, NS, 32], F32)
        bcs = sb(st, "s_bc", [128, NS, 32], F32)
        B_vex, B_bcs = Buf(), Buf()
        for s_i in range(NS):
            k.op("dve", lambda e, s_i=s_i: e.tensor_scalar(out=vex[:, s_i, :], in0=val[:, :], scalar1=identf[0:NS, s_i:s_i + 1], scalar2=None, op0=ALU.mult),
                 R=[B_val, B_identf], W=[B_vex])
        pb_ = PS[0]
        k.op("pe", lambda e: e.matmul(pb_[:, 0:NS * 32], lhsT=onesf2[0:NS, :], rhs=vex[:, :, :].rearrange("p a b -> p (a b)"), start=True, stop=True),
             R=[B_vex, B_o2], W=[PSB[0]])
        k.op("dve", lambda e: e.tensor_copy(out=bcs[:, :, :], in_=pb_[:, 0:NS * 32].rearrange("p (a b) -> p a b", b=32)), R=[PSB[0]], W=[B_bcs])
        qkvTs = sb(st, "s_qkvT", [128, 24, NS], F32)
        B_qTs = Buf()
        pt_ = PS[1]
        for j in range(24):
            k.op("pe", lambda e, j=j: e.transpose(out=pt_[:, j * NS:(j + 1) * NS], in_=yv[:, j * 128:(j + 1) * 128], identity=identf[0:NS, 0:NS]),
                 R=[B_yv, B_identf], W=[PSB[1]])
        k.op("dve", lambda e: e.tensor_copy(out=qkvTs[:, :, :], in_=pt_[:, 0:24 * NS].rearrange("p (j s) -> p j s", s=NS)), R=[PSB[1]], W=[B_qTs])
        kq = sb(st, "s_kq", [128, NS, 8, 2], F32)
        B_kq = Buf()
        for h in range(8):
            k.op("dve", lambda e, h=h: e.tensor_copy(out=kq[:, :, h, 0], in_=qkvTs[:, 8 + h, :]), R=[B_qTs], W=[B_kq])
            k.op("dve", lambda e, h=h: e.tensor_copy(out=kq[:, :, h, 1], in_=qkvTs[:, h, :]), R=[B_qTs], W=[B_kq])
        Sin = [sb(st, "s_Sin%d" % i, [128, 8, 128], F32) for i in range(2)]
        Sout = [sb(st, "s_Sout%d" % i, [128, 8, 128], F32) for i in range(2)]
        B_Sin = [Buf(), Buf()]
        B_Sout = [Buf(), Buf()]
        ksq = sb(st, "s_ksq", [128, NS, 8, 2], F32)
        oTs = sb(st, "s_oTs", [128, 8, NS], F32)
        B_oTs = Buf()
        B_ksq = Buf()
        vnc = [sb(st, "s_vnc%d" % i, [128, 8], F32) for i in range(2)]
        B_vnc = [Buf(), Buf()]
        dgv = [sb(st, "s_dgv%d" % i, [128, 128], F32) for i in range(2)]
        B_dgv = [Buf(), Buf()]
        sgt = [sb(st, "s_sgt%d" % i, [128, 128], F32) for i in range(2)]
        B_sgt = [Buf(), Buf()]
        pi_ = [2]

        def nextps():
            i = 2 + (pi_[0] % 6)
            pi_[0] += 1
            return PS[i], PSB[i]
        ui = 0
        for s_i in range(NS):
            si_, bsi = Sin[s_i % 2], B_Sin[s_i % 2]
            so_, bso = Sout[s_i % 2], B_Sout[s_i % 2]
            vn_, bvn = vnc[s_i % 2], B_vnc[s_i % 2]
            k.dma("sp", si_[:, :, :], state_dn[s_i].rearrange("h k v -> k h v"), W=[bsi])
            pk, bpk = nextps()
            for h in range(8):
                k.op("pe", lambda e, h=h, si_=si_, pk=pk, s_i=s_i: e.matmul(pk[:, h * 2:h * 2 + 2], lhsT=si_[:, h, :], rhs=kq[:, s_i, h, :], start=True, stop=True),
                     R=[bsi, B_kq], W=[bpk])
            k.op("dve", lambda e, pk=pk, s_i=s_i: e.tensor_copy(out=ksq[:, s_i, :, :], in_=pk[:, 0:16].rearrange("p (h t) -> p h t", t=2)), R=[bpk], W=[B_ksq])
            k.op("dve", lambda e, vn_=vn_, s_i=s_i: e.tensor_tensor(out=vn_[:, :], in0=ksq[:, s_i, :, 0], in1=bcs[:, s_i, 16:24], op=ALU.mult), R=[B_ksq, B_bcs], W=[bvn])
            k.op("dve", lambda e, s_i=s_i: e.tensor_tensor(out=ksq[:, s_i, :, 0], in0=qkvTs[:, 16:24, s_i], in1=bcs[:, s_i, 8:16], op=ALU.mult), R=[B_qTs, B_bcs, B_ksq], W=[B_ksq])
            k.op("dve", lambda e, vn_=vn_, s_i=s_i: e.tensor_tensor(out=vn_[:, :], in0=ksq[:, s_i, :, 0], in1=vn_[:, :], op=ALU.subtract), R=[B_ksq, bvn], W=[bvn])
            k.op("dve", lambda e, s_i=s_i: e.tensor_tensor(out=oTs[:, :, s_i], in0=ksq[:, s_i, :, 1], in1=bcs[:, s_i, 0:8], op=ALU.mult), R=[B_ksq, B_bcs], W=[B_oTs])
            k.op("dve", lambda e, s_i=s_i, vn_=vn_: e.tensor_tensor(out=ksq[:, s_i, :, 1], in0=vn_[:, :], in1=bcs[:, s_i, 24:32], op=ALU.mult), R=[bvn, B_bcs, B_ksq], W=[B_ksq])
            k.op("dve", lambda e, s_i=s_i: e.tensor_tensor(out=oTs[:, :, s_i], in0=oTs[:, :, s_i], in1=ksq[:, s_i, :, 1], op=ALU.add), R=[B_ksq, B_oTs], W=[B_oTs])
            for h in range(8):
                dg_, bdg = dgv[ui % 2], B_dgv[ui % 2]
                sg_, bsg = sgt[ui % 2], B_sgt[ui % 2]
                ui += 1
                k.op("dve", lambda e, dg_=dg_, vn_=vn_, h=h: e.tensor_scalar(out=dg_[:, :], in0=identf[:, :], scalar1=vn_[:, h:h + 1], scalar2=None, op0=ALU.mult),
                     R=[bvn, B_identf], W=[bdg])
                pv, bpv = nextps()
                k.op("pe", lambda e, pv=pv, dg_=dg_: e.matmul(pv[:, 0:128], lhsT=onesf2[:, :], rhs=dg_[:, :], start=True, stop=True), R=[bdg, B_o2], W=[bpv])
                k.op("act", lambda e, sg_=sg_, si_=si_, h=h, s_i=s_i: e.activation(out=sg_[:, :], in_=si_[:, h, :], func=AF.Identity, scale=bcs[:, s_i, h:h + 1],
                                                                               bias=zcol[:, 0:1]), R=[bsi, B_bcs], W=[bsg])
                k.op("dve", lambda e, so_=so_, pv=pv, sg_=sg_, h=h, s_i=s_i: e.scalar_tensor_tensor(out=so_[:, h, :], in0=pv[:, 0:128], scalar=qkvTs[:, 8 + h, s_i:s_i + 1],
                                                                                                in1=sg_[:, :], op0=ALU.mult, op1=ALU.add),
                     R=[bpv, B_qTs, bsg], W=[bso])
            k.dma("sp", o_sdn[s_i].rearrange("h k v -> k h v"), so_[:, :, :], R=[bso], W=[DOUT])
        ot_s = yv
        pt2 = PS[1]
        for h in range(8):
            k.op("pe", lambda e, h=h: e.transpose(out=pt2[0:NS, h * 128:(h + 1) * 128] if h < 4 else PS[0][0:NS, (h - 4) * 128:(h - 3) * 128], in_=oTs[:, h, :], identity=identf[:, :]),
                 R=[B_oTs, B_identf], W=[PSB[1] if h < 4 else PSB[0]])
        k.op("dve", lambda e: e.tensor_copy(out=ot_s[:, 0:512], in_=pt2[0:NS, :]), R=[PSB[1], B_yv], W=[B_yv])
        k.op("dve", lambda e: e.tensor_copy(out=ot_s[:, 512:1024], in_=PS[0][0:NS, :]), R=[PSB[0], B_yv], W=[B_yv])
        k.dma("sp", ot_s[:, 1024:2048], projs_scr[:, 5680:6704], R=[B_scr2, B_yv], W=[B_yv])
        ngs = sb(st, "s_ngs", [NS, 128], F32)
        B_ngs = Buf()
        k.dma("sp", ngs[:, :], dn_ng[0, :].partition_broadcast(NS), W=[B_ngs])
        k.op("act", lambda e: e.activation(out=ot_s[:, 2048:3072], in_=ot_s[:, 0:1024], func=AF.Square), R=[B_yv], W=[B_yv])
        k.op("dve", lambda e: e.tensor_reduce(out=ssq[:, 0:8], in_=ot_s[:, 2048:3072].rearrange("p (h d) -> p h d", d=128), axis=mybir.AxisListType.X, op=ALU.add), R=[B_yv, B_ssq], W=[B_ssq])
        k.op("act", lambda e: e.activation(out=ssq[:, 0:8], in_=ssq[:, 0:8], func=AF.Sqrt, scale=1.0 / 128, bias=1e-6), R=[B_ssq], W=[B_ssq])
        k.op("dve", lambda e: e.reciprocal(out=ssq[:, 0:8], in_=ssq[:, 0:8]), R=[B_ssq], W=[B_ssq])
        k.op("act", lambda e: e.activation(out=ot_s[:, 1024:2048], in_=ot_s[:, 1024:2048], func=AF.Silu), R=[B_yv], W=[B_yv])
        for h in range(8):
            k.op("dve", lambda e, h=h: e.scalar_tensor_tensor(out=ot_s[:, h * 128:(h + 1) * 128], in0=ot_s[:, h * 128:(h + 1) * 128], scalar=ssq[:, h:h + 1], in1=ngs[:, :],
                                                              op0=ALU.mult, op1=ALU.mult), R=[B_yv, B_ssq, B_ngs], W=[B_yv])
        k.op("dve", lambda e: e.tensor_tensor(out=ot_s[:, 0:1024], in0=ot_s[:, 0:1024], in1=ot_s[:, 1024:2048], op=ALU.mult), R=[B_yv], W=[B_yv])
        k.dma("sp", odns_scr[:, :], ot_s[:, 0:1024], R=[B_yv], W=[B_scr3])
        k.barrier()
    st1.close()
    with contextlib.ExitStack() as st:
        BIG = 1.0e30
        KE = sb(st, "KE", [128, 4, S], BF16)
        KW = sb(st, "KW", [128, 4, S], BF16)
        VAs = sb(st, "VAs", [128, 4, 32, 128], BF16)
        VAw = sb(st, "VAw", [128, 4, 32, 128], BF16)
        kcT = sb(st, "kcT", [128, 4, 256], BF16)
        VC = sb(st, "VC", [128, 4, 2, 128], BF16)
        OV = sb(st, "OV", [128, 2, 128], BF16)
        B_KE, B_KW, B_VAs, B_VAw, B_kcT, B_VC, B_OV = Buf(), Buf(), Buf(), Buf(), Buf(), Buf(), Buf()
        for g in range(4):
            rs_ = slice((g % 2) * 64, (g % 2) * 64 + 64)
            k.dma("sp", KE[0:64, g, :], kvT_scr[4 + g // 2][rs_, :], R=[B_scr2], W=[B_KE])
            k.dma("sp", KW[0:64, g, :], kvT_scr[6 + g // 2][rs_, :], R=[B_scr2], W=[B_KW])
            k.dma("pool", VAs[:, g, :, 0:64], o_pkv[3].rearrange("(c p) (g d) -> p g c d", p=128, d=64)[:, g], R=[DOUT], W=[B_VAs])
            k.dma("pool", VAw[:, g, :, 0:64], o_pkv[5].rearrange("(c p) (g d) -> p g c d", p=128, d=64)[:, g], R=[DOUT], W=[B_VAw])
            k.op("pool", lambda e, g=g: e.memset(VAs[:, g, :, 64:128], 1.0), W=[B_VAs])
            k.op("pool", lambda e, g=g: e.memset(VAw[:, g, :, 64:128], 1.0), W=[B_VAw])
        k.op("pool", lambda e: e.memset(VC[:, :, :, 64:128], 1.0), W=[B_VC])
        k.op("pool", lambda e: e.memset(kcT[:, :, :], 0.0), W=[B_kcT])
        with contextlib.ExitStack() as stc:
            Et = sb(stc, "Et", [64, S], BF16)
            B_Et = Buf()
            k.op("pool", lambda e: e.memset(Et[:, :], 30000.0), W=[B_Et])
            k.op("pool", lambda e: e.affine_select(out=Et[:, :], in_=Et[:, :], pattern=[[1, S]], compare_op=ALU.is_ge, fill=0.0, base=0, channel_multiplier=-64), R=[B_Et], W=[B_Et])
            k.op("pool", lambda e: e.affine_select(out=Et[:, :], in_=Et[:, :], pattern=[[-1, S]], compare_op=ALU.is_ge, fill=0.0, base=63, channel_multiplier=64), R=[B_Et], W=[B_Et])
            for g in range(4):
                k.op("dve" if g % 2 else "pool", lambda e, g=g: e.tensor_copy(out=KE[64:128, g, :], in_=Et[0:64, :]), R=[B_Et], W=[B_KE])
            ovf = sb(stc, "ovf", [128, 2, 2, 64], F32)
            B_ovf = Buf()
            k.op("pool", lambda e: e.memset(ovf[:, :, :, :], 0.5), W=[B_ovf])
            for cc in range(2):
                for wi_, (lo, hi) in enumerate(((-1, 3), (0, 2))):
                    k.op("pool", lambda e, cc=cc, wi_=wi_, lo=lo: e.affine_select(out=ovf[:, cc, wi_, :], in_=ovf[:, cc, wi_, :], pattern=[[-4, 64]], compare_op=ALU.is_ge,
                                                                                fill=0.0, base=128 * cc - lo, channel_multiplier=1), R=[B_ovf], W=[B_ovf])
                    k.op("pool", lambda e, cc=cc, wi_=wi_, hi=hi: e.affine_select(out=ovf[:, cc, wi_, :], in_=ovf[:, cc, wi_, :], pattern=[[4, 64]], compare_op=ALU.is_ge,
                                                                                fill=0.0, base=hi - 128 * cc, channel_multiplier=-1), R=[B_ovf], W=[B_ovf])
            k.op("pool", lambda e: e.memset(OV[:, :, 0:64], 0.0), W=[B_OV])
            k.op("dve", lambda e: e.tensor_tensor(out=OV[:, :, 64:128], in0=ovf[:, :, 0, :], in1=ovf[:, :, 1, :], op=ALU.add), R=[B_ovf], W=[B_OV])
            XT = sb(stc, "XT", [128, 2, S], BF16)
            W1 = sb(stc, "W1", [128, 32, 128], BF16)
            w2 = sb(stc, "w2", [128, 64], BF16)
            pe32 = sb(stc, "pe32", [32, 64], F32)
            peT = sb(stc, "peT", [128, 32], BF16)
            pec = sb(stc, "pec", [128, 1], F32)
            Sds = sb(stc, "Sds", [128, 256], F32)
            xg = sb(stc, "xg", [128, 256], F32)
            x2 = sb(stc, "x2", [128, 256], F32)
            hid = sb(stc, "hid", [128, 256], BF16)
            B_XT, B_W1, B_w2, B_pe32, B_peT, B_pec, B_Sds, B_xg, B_x2, B_hid = (Buf() for _ in range(10))
            k.op("pool", lambda e: e.memset(hid[:, :], 0.0), W=[B_hid])
            for kind_ in range(2):
                w1d, w2d, ped = (w1k, w2k, pek) if kind_ == 0 else (w1v, w2v, pev)
                for c2 in range(2):
                    k.dma("sp", XT[:, c2, :], kvT_scr[2 * kind_ + c2], R=[B_scr2], W=[B_XT])
                for half in range(2):
                    k.dma("pool", W1[half * 64:(half + 1) * 64, :, :], w1d.rearrange("p d e -> d p e"), W=[B_W1])
                k.dma("pool", w2[:, :], w2d[:, :], W=[B_w2])
                k.dma("sp", pe32[:, :], ped[:, :], W=[B_pe32])
                k.op("pe", lambda e: e.transpose(out=PS[0][0:64, 0:32], in_=pe32[:, :], identity=identf[0:32, 0:32]), R=[B_pe32, B_identf], W=[PSB[0]])
                k.op("dve", lambda e: e.tensor_copy(out=peT[0:64, :], in_=PS[0][0:64, 0:32]), R=[PSB[0]], W=[B_peT])
                for p_ in range(32):
                    k.op("pe", lambda e, p_=p_: e.matmul(PS[1][:, 0:1], lhsT=W1[0:64, p_, :], rhs=peT[0:64, p_:p_ + 1], start=(p_ == 0), stop=(p_ == 31)),
                         R=[B_W1, B_peT], W=[PSB[1]])
                k.op("dve", lambda e: e.tensor_copy(out=pec[:, :], in_=PS[1][:, 0:1]), R=[PSB[1]], W=[B_pec])
                for g in range(4):
                    rs_ = slice((g % 2) * 64, (g % 2) * 64 + 64)
                    c2 = g // 2
                    pF, bF = PS[2 + (g % 2) * 2], PSB[2 + (g % 2) * 2]
                    pS, bS = PS[3 + (g % 2) * 2], PSB[3 + (g % 2) * 2]
                    for p_ in range(16):
                        k.op("pe", lambda e, p_=p_, rs_=rs_, c2=c2, pF=pF: e.matmul(pF[:, 0:256], lhsT=W1[rs_, p_, :], rhs=XT[rs_, c2, p_:S:16], start=(p_ == 0), stop=(p_ == 15)),
                             R=[B_W1, B_XT], W=[bF])
                    for p_ in range(16):
                        k.op("pe", lambda e, p_=p_, rs_=rs_, c2=c2, pS=pS: e.matmul(pS[:, 0:256], lhsT=W1[rs_, 16 + p_, :], rhs=XT[rs_, c2, p_:S:16], start=(p_ == 0), stop=(p_ == 15)),
                             R=[B_W1, B_XT], W=[bS])
                    k.op("act", lambda e, pS=pS: e.activation(out=Sds[:, :], in_=pS[:, 0:256], func=AF.Identity), R=[bS], W=[B_Sds])
                    k.op("dve", lambda e, pF=pF: e.scalar_tensor_tensor(out=xg[:, 0:255], in0=pF[:, 0:255], scalar=pec[:, 0:1], in1=Sds[:, 1:256], op0=ALU.add, op1=ALU.add),
                         R=[bF, B_pec, B_Sds], W=[B_xg])
                    k.op("act", lambda e: e.activation(out=x2[:, 0:255], in_=xg[:, 0:255], func=AF.Square), R=[B_xg], W=[B_x2])
                    k.op("dve", lambda e: e.tensor_scalar(out=x2[:, 0:255], in0=x2[:, 0:255], scalar1=0.044715, scalar2=1.0, op0=ALU.mult, op1=ALU.add), R=[B_x2], W=[B_x2])
                    k.op("dve", lambda e: e.tensor_tensor(out=x2[:, 0:255], in0=x2[:, 0:255], in1=xg[:, 0:255], op=ALU.mult), R=[B_x2, B_xg], W=[B_x2])
                    k.op("act", lambda e: e.activation(out=x2[:, 0:255], in_=x2[:, 0:255], func=AF.Sigmoid, scale=1.5957691216), R=[B_x2], W=[B_x2])
                    k.op("dve", lambda e: e.tensor_tensor(out=hid[:, 0:255], in0=x2[:, 0:255], in1=xg[:, 0:255], op=ALU.mult), R=[B_x2, B_xg, B_hid], W=[B_hid])
                    if kind_ == 0:
                        k.op("pe", lambda e: e.matmul(PS[6][0:64, 0:256], lhsT=w2[:, :], rhs=hid[:, :], start=True, stop=True), R=[B_w2, B_hid], W=[PSB[6]])
                        k.op("dve", lambda e, g=g: e.tensor_copy(out=kcT[0:64, g, :], in_=PS[6][0:64, 0:256]), R=[PSB[6]], W=[B_kcT])
                    else:
                        for cc in range(2):
                            k.op("pe", lambda e, cc=cc: e.matmul(PS[6 + cc][:, 0:64], lhsT=hid[:, cc * 128:(cc + 1) * 128], rhs=w2[:, :], start=True, stop=True),
                                 R=[B_w2, B_hid], W=[PSB[6 + cc]])
                            k.op("dve", lambda e, cc=cc, g=g: e.tensor_copy(out=VC[:, g, cc, 0:64], in_=PS[6 + cc][:, 0:64]), R=[PSB[6 + cc]], W=[B_VC])
            k.barrier()
        RH = sb(st, "RH", [128, 16, 512], BF16)
        B_RHq = [Buf() for _ in range(16)]
        B_RHs = [Buf() for _ in range(4)]
        onsa = sb(st, "onsa", [128, 4, D], F32)
        B_onsa = [Buf() for _ in range(16)]
        PT = [sb(st, "PT%d" % i, [128, 512], BF16) for i in range(4)]
        B_PT = [Buf() for _ in range(4)]
        Oev = [sb(st, "Oev%d" % i, [128, 512], F32) for i in range(2)]
        B_Oev = [Buf(), Buf()]
        Gt = sb(st, "Gt", [128, 4, 48], F32)
        B_Gt = Buf()
        impacc = sb(st, "impacc", [128, 512], F32)
        B_imp = Buf()
        imtmp = sb(st, "imtmp", [128, 512], F32)
        B_imtmp = Buf()
        impm = sb(st, "impm", [128, 64], F32)
        imp2 = sb(st, "imp2", [128, 64], F32)
        m8 = sb(st, "m8", [128, 16], F32)
        bsel = sb(st, "bsel", [128, 128], BF16)
        B_impm, B_imp2, B_m8, B_bsel = Buf(), Buf(), Buf(), Buf()
        k.op("pool", lambda e: e.memset(bsel[:, :], 0.0), W=[B_bsel])
        rr = [sb(st, "rr%d" % i, [128, 4], F32) for i in range(2)]
        B_rr = [Buf(), Buf()]
        mgt = sb(st, "mgt", [128, 2048], F32)
        odt = sb(st, "odt", [128, D], F32)
        mxt = sb(st, "mxt", [128, D], F32)
        B_mgt, B_odt, B_mxt = Buf(), Buf(), Buf()
        B_mix = Buf()
        cnt = {"s": 0, "o": 0, "pt": 0, "ev": 0}

        def ps_s():
            i = cnt["s"] % 3
            cnt["s"] += 1
            return PS[i], PSB[i]

        def ps_o():
            i = 3 + cnt["o"] % 2
            cnt["o"] += 1
            return PS[i], PSB[i]

        def finish(h, br, pO, bO, want_imp=None):
            ev, bev = Oev[cnt["ev"] % 2], B_Oev[cnt["ev"] % 2]
            r_, br_ = rr[cnt["ev"] % 2], B_rr[cnt["ev"] % 2]
            cnt["ev"] += 1
            k.op("act", lambda e: e.activation(out=ev[:, :], in_=pO[:, :], func=AF.Identity), R=[bO], W=[bev])
            k.op("dve", lambda e: e.tensor_scalar(out=ev[64:128, :], in0=ev[64:128, :], scalar1=1e-30, scalar2=None, op0=ALU.max), R=[bev], W=[bev])
            if want_imp is not None:
                pI, bI, first = want_imp
                k.op("dve", lambda e: e.reciprocal(out=imtmp[64:128, :], in_=ev[64:128, :]), R=[bev], W=[B_imtmp])
                if first:
                    k.op("dve", lambda e: e.tensor_tensor(out=impacc[64:128, :], in0=pI[64:128, :], in1=imtmp[64:128, :], op=ALU.mult), R=[bI, B_imtmp], W=[B_imp])
                else:
                    k.op("dve", lambda e: e.tensor_tensor(out=imtmp[64:128, :], in0=pI[64:128, :], in1=imtmp[64:128, :], op=ALU.mult), R=[bI, B_imtmp], W=[B_imtmp])
                    k.op("dve", lambda e: e.tensor_tensor(out=impacc[64:128, :], in0=impacc[64:128, :], in1=imtmp[64:128, :], op=ALU.add), R=[B_imtmp, B_imp], W=[B_imp])
            pT, bT = PS[6], PSB[6]
            for qt in range(4):
                k.op("pe", lambda e, qt=qt: e.transpose(out=pT[:, qt * 128:(qt + 1) * 128], in_=ev[:, qt * 128:(qt + 1) * 128], identity=identf[:, :]),
                     R=[bev, B_identf], W=[bT])
            pT3 = pT[:, :].rearrange("p (t c) -> p t c", c=128)
            k.op("dve", lambda e: e.reciprocal(out=r_[:, :], in_=pT3[:, :, 64]), R=[bT], W=[br_])
            k.op("dve", lambda e: e.tensor_tensor(out=r_[:, :], in0=r_[:, :], in1=Gt[:, :, h * 3 + br], op=ALU.mult), R=[br_, B_Gt], W=[br_])
            for qt in range(4):
                if br == 0:
                    k.op("act", lambda e, qt=qt: e.activation(out=onsa[:, qt, h * 64:(h + 1) * 64], in_=pT[:, qt * 128:qt * 128 + 64], func=AF.Identity,
                                                             scale=r_[:, qt:qt + 1], bias=zcol[:, 0:1]), R=[bT, br_], W=[B_onsa[h]])
                else:
                    k.op("dve", lambda e, qt=qt: e.scalar_tensor_tensor(out=onsa[:, qt, h * 64:(h + 1) * 64], in0=pT[:, qt * 128:qt * 128 + 64], scalar=r_[:, qt:qt + 1],
                                                                       in1=onsa[:, qt, h * 64:(h + 1) * 64], op0=ALU.mult, op1=ALU.add),
                         R=[bT, br_, B_onsa[h]], W=[B_onsa[h]])

        def unit(lhs_fn, rh_ap, Rl, Rr, va_fn, Rv, chunks, maskfn, pO, bO, extra=None):
            n = len(chunks)
            for i, c in enumerate(chunks):
                pS_, bS_ = ps_s()
                k.op("pe", lambda e, c=c: e.matmul(pS_[:, :], lhsT=lhs_fn(c), rhs=rh_ap, start=True, stop=True), R=Rl + Rr, W=[bS_])
                pt, bpt = PT[cnt["pt"] % 4], B_PT[cnt["pt"] % 4]
                cnt["pt"] += 1
                k.op("act", lambda e: e.activation(out=pt[:, :], in_=pS_[:, :], func=AF.Exp), R=[bS_], W=[bpt])
                for (pat, base, cm) in maskfn(c):
                    k.op("pool", lambda e, pat=pat, base=base, cm=cm: e.affine_select(out=pt[:, :], in_=pt[:, :], pattern=[[pat, 512]], compare_op=ALU.is_ge,
                                                                                    fill=0.0, base=base, channel_multiplier=cm), R=[bpt], W=[bpt])
                k.op("pe", lambda e, c=c: e.matmul(pO[:, :], lhsT=va_fn(c), rhs=pt[:, :], start=(i == 0), stop=(i == n - 1)), R=Rv + [bpt], W=[bO])
                if extra is not None:
                    pI, bI = extra
                    k.op("pe", lambda e, c=c: e.matmul(pI[:, :], lhsT=OV[:, c, :], rhs=pt[:, :], start=(i == 0), stop=(i == n - 1)), R=[B_OV, bpt], W=[bI])

        for qb in range(int(os.environ.get("NSA_NQB", "8"))):
            q0 = qb * 512
            for c8 in range(8):
                for hf in range(2):
                    k.dma("sp", RH[0:64, 2 * c8 + hf, :], q_scr[c8][hf * 64:(hf + 1) * 64, q0:q0 + 512], R=[B_scr2], W=[B_RHq[2 * c8 + hf]])
            k.dma("sp", Gt[:, :, :], gate_scr[q0:q0 + 512, :].rearrange("(t p) d -> p t d", p=128), R=[B_scr2], W=[B_Gt])
            for g in range(4):
                ccs = [0] if qb < 4 else [0, 1]
                for hh in range(4):
                    h = 4 * g + hh
                    pO, bO = ps_o()
                    pI, bI = PS[5], PSB[5]

                    def mk(c, qb=qb):
                        base = -(2048 * c + 31 - 512 * qb)
                        if base - 16 * 127 >= 0:
                            return []
                        return [(1, base, -16)]
                    unit(lambda c, g=g: kcT[0:64, g, c * 128:(c + 1) * 128], RH[0:64, h, :], [B_kcT], [B_RHq[h]],
                         lambda c, g=g: VC[:, g, c, :], [B_VC], ccs, mk, pO, bO, extra=(pI, bI))
                    finish(h, 0, pO, bO, want_imp=(pI, bI, hh == 0))
                for qt in range(4):
                    t = qb * 4 + qt
                    pT, bT = PS[7], PSB[7]
                    k.op("pe", lambda e, qt=qt: e.transpose(out=pT[:, 0:64], in_=impacc[64:128, qt * 128:(qt + 1) * 128], identity=identf[64:128, 64:128]),
                         R=[B_imp, B_identf], W=[bT])
                    k.op("dve", lambda e: e.tensor_copy(out=impm[:, :], in_=pT[:, 0:64]), R=[bT], W=[B_impm])
                    k.op("pool", lambda e: e.memset(impm[:, 0:1], BIG), R=[B_impm], W=[B_impm])
                    if 2 * t + 2 < 64:
                        k.op("pool", lambda e, t=t: e.memset(impm[:, 2 * t + 2:64], -BIG), R=[B_impm], W=[B_impm])
                    k.op("pool", lambda e, t=t: e.memset(impm[0:64, 2 * t:2 * t + 1], BIG), R=[B_impm], W=[B_impm])
                    k.op("pool", lambda e, t=t: e.memset(impm[64:128, 2 * t + 1:2 * t + 2], BIG), R=[B_impm], W=[B_impm])
                    k.op("pool", lambda e, t=t: e.memset(impm[0:64, 2 * t + 1:2 * t + 2], -BIG), R=[B_impm], W=[B_impm])
                    k.op("dve", lambda e: e.max(out=m8[:, 0:8], in_=impm[:, :]), R=[B_impm], W=[B_m8])
                    k.op("dve", lambda e: e.match_replace(out=imp2[:, :], in_to_replace=m8[:, 0:8], in_values=impm[:, :], imm_value=-3.0e38), R=[B_impm, B_m8], W=[B_imp2])
                    k.op("dve", lambda e: e.max(out=m8[:, 8:16], in_=imp2[:, :]), R=[B_imp2, B_m8], W=[B_m8])
                    k.op("dve", lambda e: e.tensor_scalar(out=bsel[:, 64:128], in0=impm[:, :], scalar1=m8[:, 15:16], scalar2=1.0, op0=ALU.is_ge, op1=ALU.subtract),
                         R=[B_impm, B_m8], W=[B_bsel])
                    pTb = pT[:, :].bitcast(BF16)
                    k.op("pe", lambda e: e.transpose(out=pTb[:, 512:640], in_=bsel[:, :], identity=ident[:, :]), R=[B_bsel, B_ident], W=[bT])
                    for hh in range(4):
                        k.op("dve" if hh % 2 else "act", (lambda e, hh=hh, qt=qt, g=g: e.tensor_copy(out=RH[64:128, 4 * g + hh, qt * 128:(qt + 1) * 128], in_=pTb[64:128, 512:640])) if hh % 2 else
                             (lambda e, hh=hh, qt=qt, g=g: e.activation(out=RH[64:128, 4 * g + hh, qt * 128:(qt + 1) * 128], in_=pTb[64:128, 512:640], func=AF.Identity)),
                             R=[bT], W=[B_RHs[g]])
            for h in range(16):
                g = h // 4
                pO, bO = ps_o()

                def mk_s(c, qb=qb):
                    if c < 4 * qb:
                        return []
                    return [(1, 512 * qb - 128 * c, -1)]
                unit(lambda c, g=g: KE[:, g, c * 128:(c + 1) * 128], RH[:, h, :], [B_KE], [B_RHq[h], B_RHs[g]],
                     lambda c, g=g: VAs[:, g, c, :], [B_VAs], list(range(0, 4 * qb + 4)), mk_s, pO, bO)
                finish(h, 1, pO, bO)
                pO, bO = ps_o()

                def mk_w(c, qb=qb):
                    if c >= 4 * qb:
                        return [(1, 512 * qb - 128 * c, -1)]
                    return [(-1, 128 * c - 512 * qb + 511, 1)]
                unit(lambda c, g=g: KW[0:64, g, c * 128:(c + 1) * 128], RH[0:64, h, :], [B_KW], [B_RHq[h]],
                     lambda c, g=g: VAw[:, g, c, :], [B_VAw], list(range(max(0, 4 * qb - 4), 4 * qb + 4)), mk_w, pO, bO)
                finish(h, 2, pO, bO)
            for qt in range(4):
                r0 = q0 + qt * 128
                k.dma("sp", mgt[:, :], mg_scr[r0:r0 + 128, :], R=[B_scr2], W=[B_mgt])
                k.dma("sp", odt[:, :], odn_scr[r0:r0 + 128, :], R=[B_scr], W=[B_odt])
                k.op("dve", lambda e, qt=qt: e.tensor_tensor(out=mxt[:, :], in0=onsa[:, qt, :], in1=mgt[:, 0:D], op=ALU.mult), R=B_onsa + [B_mgt], W=[B_mxt])
                k.op("dve", lambda e: e.tensor_tensor(out=odt[:, :], in0=odt[:, :], in1=mgt[:, D:2 * D], op=ALU.mult), R=[B_odt, B_mgt], W=[B_odt])
                k.op("dve", lambda e: e.tensor_tensor(out=mxt[:, :], in0=mxt[:, :], in1=odt[:, :], op=ALU.add), R=[B_odt, B_mxt], W=[B_mxt])
                k.dma("sp", mix_scr[r0:r0 + 128, :], mxt[:, :], R=[B_mxt], W=[B_mix])
        if os.environ.get('DBG_MIX'):
            for i8 in range(8):
                k.dma("sp", o_yp[i8 * 512:(i8 + 1) * 512, :], mix_scr[i8 * 512:(i8 + 1) * 512, :], R=[B_mix], W=[DOUT])
        k.barrier()
    with contextlib.ExitStack() as st:
        BIG = 1.0e30
        PAST = 2048
        LP = 2112
        onesE = sb(st, "onesE", [128, 128], F32)
        B_onesE = Buf()
        k.op("pool", lambda e: e.memset(onesE[:, :], 1.0), W=[B_onesE])
        W1s = [sb(st, "W1s%d" % i, [128, 32, 128], BF16) for i in range(2)]
        w2s = [sb(st, "w2s%d" % i, [128, 64], BF16) for i in range(2)]
        pecs = sb(st, "pecs", [128, 2], F32)
        pe32s = sb(st, "pe32s", [32, 64], F32)
        peTs = sb(st, "peTs", [128, 32], BF16)
        B_W1s, B_w2s, B_pecs, B_pe32s, B_peTs = Buf(), Buf(), Buf(), Buf(), Buf()
        for kind_ in range(2):
            w1d, w2d, ped = (w1k, w2k, pek) if kind_ == 0 else (w1v, w2v, pev)
            for half in range(2):
                k.dma("pool", W1s[kind_][half * 64:(half + 1) * 64, :, :], w1d.rearrange("p d e -> d p e"), W=[B_W1s])
            k.dma("pool", w2s[kind_][:, :], w2d[:, :], W=[B_w2s])
            k.dma("sp", pe32s[:, :], ped[:, :], R=[B_pe32s], W=[B_pe32s])
            k.op("pe", lambda e: e.transpose(out=PS[0][0:64, 0:32], in_=pe32s[:, :], identity=identf[0:32, 0:32]), R=[B_pe32s, B_identf], W=[PSB[0]])
            k.op("dve", lambda e: e.tensor_copy(out=peTs[0:64, :], in_=PS[0][0:64, 0:32]), R=[PSB[0], B_peTs], W=[B_peTs])
            for p_ in range(32):
                k.op("pe", lambda e, p_=p_, kind_=kind_: e.matmul(PS[1][:, 0:1], lhsT=W1s[kind_][0:64, p_, :], rhs=peTs[0:64, p_:p_ + 1], start=(p_ == 0), stop=(p_ == 31)),
                     R=[B_W1s, B_peTs], W=[PSB[1]])
            k.op("dve", lambda e, kind_=kind_: e.tensor_copy(out=pecs[:, kind_:kind_ + 1], in_=PS[1][:, 0:1]), R=[PSB[1]], W=[B_pecs])
        OVs = sb(st, "OVs", [128, 128], BF16)
        B_OVs = Buf()
        with contextlib.ExitStack() as stc:
            ovf = sb(stc, "ovfs", [128, 2, 64], F32)
            B_ovf = Buf()
            k.op("pool", lambda e: e.memset(ovf[:, :, :], 0.5), W=[B_ovf])
            for wi_, (lo, hi) in enumerate(((-1, 3), (0, 2))):
                k.op("pool", lambda e, wi_=wi_, lo=lo: e.affine_select(out=ovf[:, wi_, :], in_=ovf[:, wi_, :], pattern=[[-4, 64]], compare_op=ALU.is_ge, fill=0.0, base=-lo, channel_multiplier=1), R=[B_ovf], W=[B_ovf])
                k.op("pool", lambda e, wi_=wi_, hi=hi: e.affine_select(out=ovf[:, wi_, :], in_=ovf[:, wi_, :], pattern=[[4, 64]], compare_op=ALU.is_ge, fill=0.0, base=hi, channel_multiplier=-1), R=[B_ovf], W=[B_ovf])
            k.op("pool", lambda e: e.memset(OVs[:, 0:64], 0.0), W=[B_OVs])
            k.op("dve", lambda e: e.tensor_tensor(out=OVs[:, 64:128], in0=ovf[:, 0, :], in1=ovf[:, 1, :], op=ALU.add), R=[B_ovf], W=[B_OVs])
            k.barrier()
        KEs = sb(st, "KEs", [128, 4, PAST], BF16)
        B_KEs = Buf()
        B_KEe = Buf()
        with contextlib.ExitStack() as stc:
            Et = sb(stc, "Ets", [64, PAST], BF16)
            B_Et = Buf()
            k.op("pool", lambda e: e.memset(Et[:, :], 30000.0), W=[B_Et])
            k.op("pool", lambda e: e.affine_select(out=Et[:, :], in_=Et[:, :], pattern=[[1, PAST]], compare_op=ALU.is_ge, fill=0.0, base=0, channel_multiplier=-64), R=[B_Et], W=[B_Et])
            k.op("pool", lambda e: e.affine_select(out=Et[:, :], in_=Et[:, :], pattern=[[-1, PAST]], compare_op=ALU.is_ge, fill=0.0, base=63, channel_multiplier=64), R=[B_Et], W=[B_Et])
            for g in range(4):
                k.op("dve", lambda e, g=g: e.tensor_copy(out=KEs[64:128, g, :], in_=Et[0:64, :]), R=[B_Et], W=[B_KEe])
            k.barrier()
        ps_tok = sb(st, "ps_tok", [NS, 2608], F32)
        B_pstok = Buf()
        k.dma("sp", ps_tok[:, :], projs_scr[:, 0:2608], R=[B_scr2], W=[B_pstok])
        RHS = sb(st, "RHS", [128, NS, 16], BF16)
        B_RHSq, B_RHSs = Buf(), Buf()
        KN = sb(st, "KN", [64, 3, 4, NS], BF16)
        B_KN = Buf()
        for c8 in range(8):
            pq, bq = PS[2 + c8 % 4], PSB[2 + c8 % 4]
            k.op("pe", lambda e, c8=c8, pq=pq: e.transpose(out=pq[:, 0:NS], in_=ps_tok[:, c8 * 128:(c8 + 1) * 128], identity=identf[0:NS, 0:NS]), R=[B_pstok, B_identf], W=[bq])
            k.op("dve", lambda e, c8=c8, pq=pq: e.tensor_scalar(out=RHS[0:64, :, 2 * c8], in0=pq[0:64, 0:NS], scalar1=0.125, scalar2=None, op0=ALU.mult), R=[bq], W=[B_RHSq])
            k.op("dve", lambda e, c8=c8, pq=pq: e.tensor_scalar(out=RHS[0:64, :, 2 * c8 + 1], in0=pq[64:128, 0:NS], scalar1=0.125, scalar2=None, op0=ALU.mult), R=[bq], W=[B_RHSq])
        for ki, c0 in enumerate((1024, 1536, 2048)):
            for c2 in range(2):
                pq, bq = PS[2 + (ki * 2 + c2) % 4], PSB[2 + (ki * 2 + c2) % 4]
                k.op("pe", lambda e, c0=c0, c2=c2, pq=pq: e.transpose(out=pq[:, 0:NS], in_=ps_tok[:, c0 + c2 * 128:c0 + (c2 + 1) * 128], identity=identf[0:NS, 0:NS]), R=[B_pstok, B_identf], W=[bq])
                k.op("dve", lambda e, ki=ki, c2=c2, pq=pq: e.tensor_copy(out=KN[0:64, ki, 2 * c2, :], in_=pq[0:64, 0:NS]), R=[bq], W=[B_KN])
                k.op("dve", lambda e, ki=ki, c2=c2, pq=pq: e.tensor_copy(out=KN[0:64, ki, 2 * c2 + 1, :], in_=pq[64:128, 0:NS]), R=[bq], W=[B_KN])
        VNc = sb(st, "VNc", [128, 2, NS], BF16)
        KNc = sb(st, "KNc", [128, 2, NS], BF16)
        B_VNc = Buf()
        for ki, (c0, dst) in enumerate(((1024, KNc), (1280, VNc))):
            for c2 in range(2):
                pq, bq = PS[6 + c2], PSB[6 + c2]
                k.op("pe", lambda e, c0=c0, c2=c2, pq=pq: e.transpose(out=pq[:, 0:NS], in_=ps_tok[:, c0 + c2 * 128:c0 + (c2 + 1) * 128], identity=identf[0:NS, 0:NS]), R=[B_pstok, B_identf], W=[bq])
                k.op("dve", lambda e, dst=dst, c2=c2, pq=pq: e.tensor_copy(out=dst[:, c2, :], in_=pq[:, 0:NS]), R=[bq], W=[B_VNc])
        gsg = sb(st, "gsg", [NS, 48], F32)
        gex = sb(st, "gex", [NS, NS, 48], F32)
        Gb = sb(st, "Gb", [128, NS, 48], F32)
        B_gsg, B_gex, B_Gb = Buf(), Buf(), Buf()
        k.op("act", lambda e: e.activation(out=gsg[:, :], in_=ps_tok[:, 2560:2608], func=AF.Sigmoid), R=[B_pstok], W=[B_gsg])
        for s_i in range(NS):
            k.op("dve", lambda e, s_i=s_i: e.tensor_scalar(out=gex[:, s_i, :], in0=gsg[:, :], scalar1=identf[0:NS, s_i:s_i + 1], scalar2=None, op0=ALU.mult), R=[B_gsg, B_identf], W=[B_gex])
        for hf in range(2):
            k.op("pe", lambda e, hf=hf: e.matmul(PS[hf][:, 0:384], lhsT=onesE[0:NS, :], rhs=gex[:, hf * 8:(hf + 1) * 8, :].rearrange("p a b -> p (a b)"), start=True, stop=True),
                 R=[B_gex, B_onesE], W=[PSB[hf]])
            k.op("dve", lambda e, hf=hf: e.tensor_copy(out=Gb[:, hf * 8:(hf + 1) * 8, :], in_=PS[hf][:, 0:384].rearrange("p (a b) -> p a b", b=48)), R=[PSB[hf]], W=[B_Gb])
        VN = sb(st, "VN", [1, NS, 2, 4, 128], BF16)
        B_VN = Buf()
        k.op("pool", lambda e: e.memset(VN[:, :, :, :, 64:128], 1.0), W=[B_VN])
        for s_i in range(NS):
            for ki, c0 in enumerate((1792, 2304)):
                k.dma("pool", VN[0:1, s_i, ki, :, 0:64], projs_scr[s_i:s_i + 1, c0:c0 + 256].rearrange("o (g d) -> o g d", d=64), R=[B_scr2], W=[B_VN])
        ptab = sb(st, "ptab", [128, NS * 16], I32)
        ioi = sb(st, "ioi", [128, 1], I32)
        iof = sb(st, "iof", [128, 1], F32)
        idxa = sb(st, "idxa", [128, NS * 16], I32)
        B_ptab, B_io = Buf(), Buf()
        k.dma("sp", ptab[:, :], ptbl[0, :].partition_broadcast(128), W=[B_ptab])
        k.op("pool", lambda e: e.iota(ioi[:, :], pattern=[[0, 1]], base=0, channel_multiplier=1), W=[B_io])
        k.op("dve", lambda e: e.tensor_copy(out=iof[:, :], in_=ioi[:, :]), R=[B_io], W=[B_io])
        k.op("dve", lambda e: e.tensor_scalar(out=idxa[:, :], in0=ptab[:, :], scalar1=128.0, scalar2=iof[:, 0:1], op0=ALU.mult, op1=ALU.add), R=[B_ptab, B_io], W=[B_ptab])
        pools2 = [p_.rearrange("n p d -> (n p) d") for p_ in (pk_cmp, pv_cmp, pk_slc, pv_slc)]
        pg = [[sb(st, "pg%d_%d" % (i, j), [128, 256], F32) for j in range(4)] for i in range(2)]
        B_pg = [[Buf() for _ in range(4)] for _ in range(2)]
        KWs = sb(st, "KWs", [128, 4, 512], BF16)
        XT2 = [sb(st, "XT2_%d" % i, [128, 2, LP], BF16) for i in range(2)]
        VAss = sb(st, "VAss", [128, 16, 4, 128], BF16)
        VAws = sb(st, "VAws", [128, 4, 4, 128], BF16)
        B_KWs, B_VAss, B_VAws = Buf(), Buf(), Buf()
        B_XT2 = [Buf(), Buf()]
        k.op("pool", lambda e: e.memset(VAss[:, :, :, 64:128], 1.0), W=[B_VAss])
        k.op("pool", lambda e: e.memset(VAws[:, :, :, 64:128], 1.0), W=[B_VAws])
        for i in range(2):
            k.op("pool", lambda e, i=i: e.memset(XT2[i][:, :, PAST:LP], 0.0), W=[B_XT2[i]])
        kcTs = sb(st, "kcTs", [128, 4, 128], BF16)
        VCs = sb(st, "VCs", [128, 4, 128], BF16)
        B_kcTs, B_VCs = Buf(), Buf()
        k.op("pool", lambda e: e.memset(VCs[:, :, 64:128], 1.0), W=[B_VCs])
        Sdz = sb(st, "Sdz", [128, 132], F32)
        xgz = sb(st, "xgz", [128, 132], F32)
        x2z = sb(st, "x2z", [128, 132], F32)
        hidz = sb(st, "hidz", [128, 128], BF16)
        B_Sdz, B_xgz, B_x2z, B_hidz = Buf(), Buf(), Buf(), Buf()
        PTs = [sb(st, "PTs%d" % i, [128, 4], BF16) for i in range(4)]
        B_PTs = [Buf() for _ in range(4)]
        pti = [0]
        onT = sb(st, "onT", [64, NS, 16], F32)
        B_onT = Buf()
        rcp = sb(st, "rcp", [64, 4], F32)
        tq = sb(st, "tq", [64, 4], F32)
        B_rcp, B_tq = Buf(), Buf()
        Oes = sb(st, "Oes", [128, 4], F32)
        B_Oes = Buf()
        impc = sb(st, "impc", [128, 4], F32)
        imt = sb(st, "imt", [128, 4], F32)
        B_impc, B_imt = Buf(), Buf()
        impr = sb(st, "impr", [4, 64], F32)
        imp2r = sb(st, "imp2r", [4, 64], F32)
        m8r = sb(st, "m8r", [4, 16], F32)
        bselr = sb(st, "bselr", [4, 128], BF16)
        B_impr, B_imp2r, B_m8r, B_bselr = Buf(), Buf(), Buf(), Buf()
        k.op("pool", lambda e: e.memset(bselr[:, :], 0.0), W=[B_bselr])
        pools = (pk_cmp, pv_cmp, pk_slc, pv_slc)
        esp = [0]

        def eps():
            i = esp[0] % 5
            esp[0] += 1
            return PS[i], PSB[i]

        def fin_branch(s_i, g, br, pO, bO, first):
            k.op("act", lambda e: e.activation(out=Oes[:, :], in_=pO[:, 0:4], func=AF.Identity), R=[bO], W=[B_Oes])
            k.op("dve", lambda e: e.tensor_scalar(out=Oes[64:128, :], in0=Oes[64:128, :], scalar1=1e-30, scalar2=None, op0=ALU.max), R=[B_Oes], W=[B_Oes])
            k.op("dve", lambda e: e.reciprocal(out=rcp[0:64, :], in_=Oes[64:128, :]), R=[B_Oes], W=[B_rcp])
            k.op("dve", lambda e: e.tensor_tensor(out=tq[0:64, :], in0=Oes[0:64, :], in1=rcp[0:64, :], op=ALU.mult), R=[B_Oes, B_rcp], W=[B_tq])
            gcol = Gb[0:64, s_i, :].rearrange("p (h b) -> p h b", b=3)[:, 4 * g:4 * g + 4, br]
            if first:
                k.op("dve", lambda e: e.tensor_tensor(out=onT[0:64, s_i, 4 * g:4 * g + 4], in0=tq[0:64, :], in1=gcol, op=ALU.mult), R=[B_tq, B_Gb], W=[B_onT])
            else:
                k.op("dve", lambda e: e.tensor_tensor(out=tq[0:64, :], in0=tq[0:64, :], in1=gcol, op=ALU.mult), R=[B_tq, B_Gb], W=[B_tq])
                k.op("dve", lambda e: e.tensor_tensor(out=onT[0:64, s_i, 4 * g:4 * g + 4], in0=onT[0:64, s_i, 4 * g:4 * g + 4], in1=tq[0:64, :], op=ALU.add), R=[B_tq, B_onT], W=[B_onT])

        def score_pv(lhsT, rhs, Rl, va, Rv, pO, bO, start, stop, np_=128, mask=None, imp=None):
            pS_, bS_ = eps()
            k.op("pe", lambda e: e.matmul(pS_[0:np_, 0:4], lhsT=lhsT, rhs=rhs, start=True, stop=True), R=Rl, W=[bS_])
            pt, bpt = PTs[pti[0] % 4], B_PTs[pti[0] % 4]
            pti[0] += 1
            k.op("act", lambda e: e.activation(out=pt[0:np_, :], in_=pS_[0:np_, 0:4], func=AF.Exp), R=[bS_], W=[bpt])
            if mask is not None:
                k.op("pool", lambda e: e.affine_select(out=pt[:, :], in_=pt[:, :], pattern=[[0, 4]], compare_op=ALU.is_ge, fill=0.0, base=mask[0], channel_multiplier=mask[1]), R=[bpt], W=[bpt])
            k.op("pe", lambda e: e.matmul(pO[:, 0:4], lhsT=va, rhs=pt[0:np_, :], start=start, stop=stop), R=Rv + [bpt], W=[bO])
            if imp is not None:
                k.op("pe", lambda e: e.matmul(imp[0][:, 0:4], lhsT=OVs[:, :], rhs=pt[:, :], start=True, stop=True), R=[B_OVs, bpt], W=[imp[1]])

        for s_i in range(int(os.environ.get("NSA_NS", str(NS)))):
            xk, bxk = XT2[0], B_XT2[0]
            xv, bxv = XT2[1], B_XT2[1]
            for j in range(16):
                pgs, bpgs = pg[j % 2], B_pg[j % 2]
                for pi_ in range(4):
                    k.op("pool", lambda e, pi_=pi_, j=j, s_i=s_i, pgs=pgs: e.indirect_dma_start(out=pgs[pi_][:, :], out_offset=None, in_=pools2[pi_],
                                                                                           in_offset=bass.IndirectOffsetOnAxis(ap=idxa[:, s_i * 16 + j:s_i * 16 + j + 1], axis=0)),
                         R=[B_ptab], W=[bpgs[pi_]], dma=True)
                for pi_, (dst, bd) in ((0, (xk, bxk)), (1, (xv, bxv))):
                    pq, bq = eps()
                    pqb = pq[:, :].bitcast(BF16)
                    for c2 in range(2):
                        k.op("pe", lambda e, c2=c2, pi_=pi_, pq=pq, pgs=pgs: e.transpose(out=pq[:, c2 * 128:(c2 + 1) * 128], in_=pgs[pi_][:, c2 * 128:(c2 + 1) * 128], identity=identf[:, :]),
                             R=[bpgs[pi_], B_identf], W=[bq])
                    k.op("act" if pi_ == 0 else "dve", (lambda e, dst=dst, j=j, pq=pq: e.activation(out=dst[:, :, j * 128:(j + 1) * 128], in_=pq[:, 0:256].rearrange("p (c t) -> p c t", t=128), func=AF.Identity)) if pi_ == 0 else
                         (lambda e, dst=dst, j=j, pq=pq: e.tensor_copy(out=dst[:, :, j * 128:(j + 1) * 128], in_=pq[:, 0:256].rearrange("p (c t) -> p c t", t=128))),
                         R=[bq], W=[bd])
                pq, bq = eps()
                for c2 in range(2):
                    k.op("pe", lambda e, c2=c2, pq=pq, pgs=pgs: e.transpose(out=pq[:, c2 * 128:(c2 + 1) * 128], in_=pgs[2][:, c2 * 128:(c2 + 1) * 128], identity=identf[:, :]),
                         R=[bpgs[2], B_identf], W=[bq])
                for g in range(4):
                    k.op("act" if g % 2 else "dve", (lambda e, g=g, j=j, pq=pq: e.activation(out=KEs[0:64, g, j * 128:(j + 1) * 128], in_=pq[(g % 2) * 64:(g % 2) * 64 + 64, (g // 2) * 128:(g // 2 + 1) * 128], func=AF.Identity)) if g % 2 else
                         (lambda e, g=g, j=j, pq=pq: e.tensor_copy(out=KEs[0:64, g, j * 128:(j + 1) * 128], in_=pq[(g % 2) * 64:(g % 2) * 64 + 64, (g // 2) * 128:(g // 2 + 1) * 128])),
                         R=[bq], W=[B_KEs])
                k.op("pool", lambda e, j=j, pgs=pgs: e.tensor_copy(out=VAss[:, j, :, 0:64], in_=pgs[3][:, :].rearrange("p (g d) -> p g d", d=64)), R=[bpgs[3]], W=[B_VAss])
            for c4 in range(4):
                pgs, bpgs = pg[c4 % 2], B_pg[c4 % 2]
                k.dma("sp", pgs[0][:, :], ckwin[s_i, c4 * 128:(c4 + 1) * 128, :], W=[bpgs[0]])
                k.dma("sp", pgs[1][:, :], cvwin[s_i, c4 * 128:(c4 + 1) * 128, :], W=[bpgs[1]])
                pq, bq = eps()
                for c2 in range(2):
                    k.op("pe", lambda e, c2=c2, pq=pq, pgs=pgs: e.transpose(out=pq[:, c2 * 128:(c2 + 1) * 128], in_=pgs[0][:, c2 * 128:(c2 + 1) * 128], identity=identf[:, :]),
                         R=[bpgs[0], B_identf], W=[bq])
                for g in range(4):
                    k.op("dve", lambda e, g=g, c4=c4, pq=pq: e.tensor_copy(out=KWs[0:64, g, c4 * 128:(c4 + 1) * 128], in_=pq[(g % 2) * 64:(g % 2) * 64 + 64, (g // 2) * 128:(g // 2 + 1) * 128]),
                         R=[bq], W=[B_KWs])
                k.op("pool", lambda e, c4=c4, pgs=pgs: e.tensor_copy(out=VAws[:, c4, :, 0:64], in_=pgs[1][:, :].rearrange("p (g d) -> p g d", d=64)), R=[bpgs[1]], W=[B_VAws])
            k.op("dve", lambda e, s_i=s_i: e.tensor_copy(out=xk[:, :, PAST], in_=KNc[:, :, s_i]), R=[B_VNc], W=[bxk])
            k.op("dve", lambda e, s_i=s_i: e.tensor_copy(out=xv[:, :, PAST], in_=VNc[:, :, s_i]), R=[B_VNc], W=[bxv])
            for kind_ in range(2):
                xt_, bxt = (xk, bxk) if kind_ == 0 else (xv, bxv)
                for g in range(4):
                    rs_ = slice((g % 2) * 64, (g % 2) * 64 + 64)
                    c2 = g // 2
                    pF, bF = eps()
                    pS, bS = eps()
                    for p_ in range(16):
                        k.op("pe", lambda e, p_=p_, pF=pF: e.matmul(pF[:, 0:132], lhsT=W1s[kind_][rs_, p_, :], rhs=xt_[rs_, c2, p_:LP:16], start=(p_ == 0), stop=(p_ == 15)), R=[B_W1s, bxt], W=[bF])
                    for p_ in range(16):
                        k.op("pe", lambda e, p_=p_, pS=pS: e.matmul(pS[:, 0:132], lhsT=W1s[kind_][rs_, 16 + p_, :], rhs=xt_[rs_, c2, p_:LP:16], start=(p_ == 0), stop=(p_ == 15)), R=[B_W1s, bxt], W=[bS])
                    k.op("act", lambda e, pS=pS: e.activation(out=Sdz[:, :], in_=pS[:, 0:132], func=AF.Identity), R=[bS], W=[B_Sdz])
                    k.op("dve", lambda e, pF=pF: e.scalar_tensor_tensor(out=xgz[:, 0:128], in0=pF[:, 0:128], scalar=pecs[:, kind_:kind_ + 1], in1=Sdz[:, 1:129], op0=ALU.add, op1=ALU.add), R=[bF, B_pecs, B_Sdz], W=[B_xgz])
                    k.op("act", lambda e: e.activation(out=x2z[:, 0:128], in_=xgz[:, 0:128], func=AF.Square), R=[B_xgz], W=[B_x2z])
                    k.op("dve", lambda e: e.tensor_scalar(out=x2z[:, 0:128], in0=x2z[:, 0:128], scalar1=0.044715, scalar2=1.0, op0=ALU.mult, op1=ALU.add), R=[B_x2z], W=[B_x2z])
                    k.op("dve", lambda e: e.tensor_tensor(out=x2z[:, 0:128], in0=x2z[:, 0:128], in1=xgz[:, 0:128], op=ALU.mult), R=[B_x2z, B_xgz], W=[B_x2z])
                    k.op("act", lambda e: e.activation(out=x2z[:, 0:128], in_=x2z[:, 0:128], func=AF.Sigmoid, scale=1.5957691216), R=[B_x2z], W=[B_x2z])
                    k.op("dve", lambda e: e.tensor_tensor(out=hidz[:, :], in0=x2z[:, 0:128], in1=xgz[:, 0:128], op=ALU.mult), R=[B_x2z, B_xgz], W=[B_hidz])
                    pR, bR = eps()
                    if kind_ == 0:
                        k.op("pe", lambda e, pR=pR: e.matmul(pR[0:64, 0:128], lhsT=w2s[0][:, :], rhs=hidz[:, :], start=True, stop=True), R=[B_w2s, B_hidz], W=[bR])
                        k.op("dve", lambda e, g=g, pR=pR: e.tensor_copy(out=kcTs[0:64, g, :], in_=pR[0:64, 0:128]), R=[bR], W=[B_kcTs])
                    else:
                        k.op("pe", lambda e, pR=pR: e.matmul(pR[:, 0:64], lhsT=hidz[:, :], rhs=w2s[1][:, :], start=True, stop=True), R=[B_w2s, B_hidz], W=[bR])
                        k.op("dve", lambda e, g=g, pR=pR: e.tensor_copy(out=VCs[:, g, 0:64], in_=pR[:, 0:64]), R=[bR], W=[B_VCs])
            for g in range(4):
                pO, bO = PS[5], PSB[5]
                pI, bI = PS[6], PSB[6]
                score_pv(kcTs[0:64, g, :], RHS[0:64, s_i, 4 * g:4 * g + 4], [B_kcTs, B_RHSq], VCs[:, g, :], [B_VCs], pO, bO, True, True, mask=(126, -1), imp=(pI, bI))
                fin_branch(s_i, g, 0, pO, bO, True)
                k.op("dve", lambda e: e.reciprocal(out=imt[64:128, :], in_=Oes[64:128, :]), R=[B_Oes], W=[B_imt])
                k.op("dve", lambda e, pI=pI: e.tensor_tensor(out=imt[64:128, :], in0=pI[64:128, 0:4], in1=imt[64:128, :], op=ALU.mult), R=[bI, B_imt], W=[B_imt])
                k.op("dve", lambda e, g=g: e.tensor_reduce(out=impc[64:128, g:g + 1], in_=imt[64:128, :], axis=mybir.AxisListType.X, op=ALU.add), R=[B_imt], W=[B_impc])
            pT, bT = PS[7], PSB[7]
            k.op("pe", lambda e: e.transpose(out=pT[0:4, 0:64], in_=impc[64:128, :], identity=identf[64:128, 64:128]), R=[B_impc, B_identf], W=[bT])
            k.op("dve", lambda e: e.tensor_copy(out=impr[:, :], in_=pT[0:4, 0:64]), R=[bT], W=[B_impr])
            k.op("pool", lambda e: e.memset(impr[:, 0:1], BIG), R=[B_impr], W=[B_impr])
            k.op("pool", lambda e: e.memset(impr[:, 32:33], BIG), R=[B_impr], W=[B_impr])
            k.op("pool", lambda e: e.memset(impr[:, 33:64], -BIG), R=[B_impr], W=[B_impr])
            k.op("dve", lambda e: e.max(out=m8r[:, 0:8], in_=impr[:, :]), R=[B_impr], W=[B_m8r])
            k.op("dve", lambda e: e.match_replace(out=imp2r[:, :], in_to_replace=m8r[:, 0:8], in_values=impr[:, :], imm_value=-3.0e38), R=[B_impr, B_m8r], W=[B_imp2r])
            k.op("dve", lambda e: e.max(out=m8r[:, 8:16], in_=imp2r[:, :]), R=[B_imp2r, B_m8r], W=[B_m8r])
            k.op("dve", lambda e: e.tensor_scalar(out=bselr[:, 64:128], in0=impr[:, :], scalar1=m8r[:, 15:16], scalar2=1.0, op0=ALU.is_ge, op1=ALU.subtract), R=[B_impr, B_m8r], W=[B_bselr])
            pTb = pT[:, :].bitcast(BF16)
            k.op("pe", lambda e: e.transpose(out=pTb[:, 512:516], in_=bselr[:, :], identity=ident[0:4, 0:4]), R=[B_bselr, B_ident], W=[bT])
            for hh in range(4):
                k.op("dve", lambda e, hh=hh, s_i=s_i: e.tensor_copy(out=RHS[64:128, s_i, :].rearrange("p (g h) -> p g h", h=4)[:, :, hh], in_=pTb[64:128, 512:516]), R=[bT], W=[B_RHSs])
            for g in range(4):
                pO, bO = PS[5], PSB[5]
                for c in range(16):
                    score_pv(KEs[:, g, c * 128:(c + 1) * 128], RHS[:, s_i, 4 * g:4 * g + 4], [B_KEs, B_KEe, B_RHSq, B_RHSs], VAss[:, c, g, :], [B_VAss], pO, bO, c == 0, False)
                score_pv(KN[0:64, 1, g, s_i:s_i + 1], RHS[0:64, s_i, 4 * g:4 * g + 4], [B_KN, B_RHSq], VN[0:1, s_i, 0, g, :], [B_VN], pO, bO, False, True, np_=1)
                fin_branch(s_i, g, 1, pO, bO, False)
            for g in range(4):
                pO, bO = PS[6], PSB[6]
                for c in range(4):
                    score_pv(KWs[0:64, g, c * 128:(c + 1) * 128], RHS[0:64, s_i, 4 * g:4 * g + 4], [B_KWs, B_RHSq], VAws[:, c, g, :], [B_VAws], pO, bO, c == 0, False,
                             mask=((-1, 1) if c == 0 else None))
                score_pv(KN[0:64, 2, g, s_i:s_i + 1], RHS[0:64, s_i, 4 * g:4 * g + 4], [B_KN, B_RHSq], VN[0:1, s_i, 1, g, :], [B_VN], pO, bO, False, True, np_=1)
                fin_branch(s_i, g, 2, pO, bO, False)
        ons = sb(st, "ons", [NS, D], F32)
        ods = sb(st, "ods", [NS, D], F32)
        mgs = sb(st, "mgs", [NS, 2048], F32)
        B_ons, B_ods, B_mgs = Buf(), Buf(), Buf()
        for h in range(16):
            pq, bq = PS[h // 8], PSB[h // 8]
            k.op("pe", lambda e, h=h, pq=pq: e.transpose(out=pq[0:NS, (h % 8) * 64:(h % 8 + 1) * 64], in_=onT[0:64, :, h], identity=identf[0:64, 0:64]), R=[B_onT, B_identf], W=[bq])
        for hf in range(2):
            k.op("dve", lambda e, hf=hf: e.tensor_copy(out=ons[:, hf * 512:(hf + 1) * 512], in_=PS[hf][0:NS, :]), R=[PSB[hf]], W=[B_ons])
        k.dma("sp", mgs[:, :], projs_scr[:, 6720:8768], R=[B_scr2], W=[B_mgs])
        k.dma("sp", ods[:, :], odns_scr[:, :], R=[B_scr3], W=[B_ods])
        k.op("act", lambda e: e.activation(out=mgs[:, :], in_=mgs[:, :], func=AF.Sigmoid), R=[B_mgs], W=[B_mgs])
        k.op("dve", lambda e: e.tensor_tensor(out=ons[:, :], in0=ons[:, :], in1=mgs[:, 0:D], op=ALU.mult), R=[B_ons, B_mgs], W=[B_ons])
        k.op("dve", lambda e: e.tensor_tensor(out=ods[:, :], in0=ods[:, :], in1=mgs[:, D:2 * D], op=ALU.mult), R=[B_ods, B_mgs], W=[B_ods])
        k.op("dve", lambda e: e.tensor_tensor(out=ons[:, :], in0=ons[:, :], in1=ods[:, :], op=ALU.add), R=[B_ons, B_ods], W=[B_ons])
        k.dma("sp", mixs_scr[:, :], ons[:, :], R=[B_ons], W=[B_scr3])
        k.barrier()
    with contextlib.ExitStack() as st:
        wo = sb(st, "wo", [128, 8, D], BF16)
        wu = sb(st, "wu", [128, 8, 4 * D], BF16)
        wd = sb(st, "wd", [128, 32, D], BF16)
        B_wo, B_wu, B_wd = Buf(), Buf(), Buf()
        wov = w_out.rearrange("(kc p) n -> p kc n", p=128)
        wuv = w_up.rearrange("(kc p) n -> p kc n", p=128)
        wdv = w_down.rearrange("(kc p) n -> p kc n", p=128)
        for c in range(2):
            k.dma("pool", wo[:, :, c * 512:(c + 1) * 512], wov[:, :, c * 512:(c + 1) * 512], W=[B_wo])
        for c in range(8):
            k.dma("pool", wu[:, :, c * 512:(c + 1) * 512], wuv[:, :, c * 512:(c + 1) * 512], W=[B_wu])
        for c in range(8):
            k.dma("pool", wd[:, c * 4:(c + 1) * 4, :], wdv[:, c * 4:(c + 1) * 4, :], W=[B_wd])
        onesF = sb(st, "onesF", [128, 128], F32)
        dgF = sb(st, "dgF", [128, 128], F32)
        bct = sb(st, "bct", [128, 3, D], F32)
        gm2 = sb(st, "gm2", [128, 8], F32)
        B_onesF, B_dgF, B_bct, B_gm2 = Buf(), Buf(), Buf(), Buf()
        k.op("pool", lambda e: e.memset(onesF[:, :], 1.0), W=[B_onesF])
        k.op("dve", lambda e: e.scalar_tensor_tensor(out=gm2[:, :], in0=vecT[:, 32:40], scalar=1.0, in1=vecT[:, 56:64], op0=ALU.add, op1=ALU.mult), R=[B_vecT], W=[B_gm2])
        fps = [0]

        def fp():
            i = fps[0] % 8
            fps[0] += 1
            return PS[i], PSB[i]
        for vi, c0 in enumerate((16, 40, 64)):
            for j in range(8):
                k.op("dve", lambda e, c0=c0, j=j: e.tensor_scalar(out=dgF[:, :], in0=identf[:, :], scalar1=vecT[:, c0 + j:c0 + j + 1], scalar2=None, op0=ALU.mult),
                     R=[B_vecT, B_identf, B_dgF], W=[B_dgF])
                pp, bpp = fp()
                k.op("pe", lambda e, pp=pp: e.matmul(pp[:, 0:128], lhsT=onesF[:, :], rhs=dgF[:, :], start=True, stop=True), R=[B_onesF, B_dgF], W=[bpp])
                k.op("act", lambda e, pp=pp, vi=vi, j=j: e.activation(out=bct[:, vi, j * 128:(j + 1) * 128], in_=pp[:, 0:128], func=AF.Identity), R=[bpp], W=[B_bct])
        TB = 2
        mixb = sb(st, "mixb", [128, TB, D], BF16)
        mixT = sb(st, "mixT", [128, 8, TB * 128], BF16)
        x1 = sb(st, "x1", [128, TB, D], F32)
        tmpF = sb(st, "tmpF", [128, D], F32)
        xnb = sb(st, "xnb", [128, D], BF16)
        h2T = sb(st, "h2T", [128, 8, TB * 128], BF16)
        uT = sb(st, "uT", [128, 32, TB * 128], BF16)
        rl = [sb(st, "rl%d" % i, [128, TB * 128], F32) for i in range(2)]
        sq2 = sb(st, "sq2", [128, 2], F32)
        B_mixb, B_mixT, B_x1, B_tmpF, B_xnb, B_h2T, B_sq2 = (Buf() for _ in range(7))
        yout, B_yout = tmpF, B_tmpF
        B_uT = [Buf() for _ in range(32)]
        B_rl = [Buf(), Buf()]
        for blk in range(S // (TB * 128)):
            t0 = blk * TB * 128
            k.dma("pool", mixb[:, :, :], mix_scr[t0:t0 + TB * 128, :].rearrange("(t p) d -> p t d", p=128), R=[B_mix], W=[B_mixb])
            k.dma("sp", x1[:, :, :], xp[t0:t0 + TB * 128, :].rearrange("(t p) d -> p t d", p=128), W=[B_x1])
            for tt in range(TB):
                pp, bpp = fp()
                ppb = pp[:, :].bitcast(BF16)
                for j in range(8):
                    k.op("pe", lambda e, tt=tt, j=j, ppb=ppb: e.transpose(out=ppb[:, j * 128:(j + 1) * 128], in_=mixb[:, tt, j * 128:(j + 1) * 128], identity=ident[:, :]),
                         R=[B_mixb, B_ident], W=[bpp])
                k.op("act", lambda e, tt=tt, ppb=ppb: e.activation(out=mixT[:, :, tt * 128:(tt + 1) * 128], in_=ppb[:, :].rearrange("p (j t) -> p j t", t=128), func=AF.Identity),
                     R=[bpp], W=[B_mixT])
            for tt in range(TB):
                for hf in range(2):
                    pp, bpp = fp()
                    for kc in range(8):
                        k.op("pe", lambda e, tt=tt, hf=hf, kc=kc, pp=pp: e.matmul(pp[:, :], lhsT=mixT[:, kc, tt * 128:(tt + 1) * 128], rhs=wo[:, kc, hf * 512:(hf + 1) * 512],
                                                                                 start=(kc == 0), stop=(kc == 7)), R=[B_mixT, B_wo], W=[bpp])
                    k.op("dve", lambda e, tt=tt, hf=hf, pp=pp: e.tensor_tensor(out=tmpF[:, hf * 512:(hf + 1) * 512], in0=pp[:, :], in1=bct[:, 0, hf * 512:(hf + 1) * 512], op=ALU.mult),
                         R=[bpp, B_bct], W=[B_tmpF])
                k.op("dve", lambda e, tt=tt: e.tensor_tensor(out=x1[:, tt, :], in0=x1[:, tt, :], in1=tmpF[:, :], op=ALU.add), R=[B_x1, B_tmpF], W=[B_x1])
                k.op("act", lambda e, tt=tt: e.activation(out=tmpF[:, :], in_=x1[:, tt, :], func=AF.Square, accum_out=sq2[:, 0:1]), R=[B_x1], W=[B_tmpF, B_sq2])
                k.op("act", lambda e: e.activation(out=sq2[:, 0:1], in_=sq2[:, 0:1], func=AF.Sqrt, scale=1.0 / D, bias=1e-6), R=[B_sq2], W=[B_sq2])
                k.op("dve", lambda e: e.reciprocal(out=sq2[:, 0:1], in_=sq2[:, 0:1]), R=[B_sq2], W=[B_sq2])
                k.op("dve", lambda e, tt=tt: e.tensor_scalar(out=xnb[:, :], in0=x1[:, tt, :], scalar1=sq2[:, 0:1], scalar2=None, op0=ALU.mult), R=[B_x1, B_sq2], W=[B_xnb])
                pp, bpp = fp()
                ppb = pp[:, :].bitcast(BF16)
                for j in range(8):
                    k.op("pe", lambda e, j=j, ppb=ppb: e.transpose(out=ppb[:, j * 128:(j + 1) * 128], in_=xnb[:, j * 128:(j + 1) * 128], identity=ident[:, :]),
                         R=[B_xnb, B_ident], W=[bpp])
                for j in range(8):
                    if j % 2 == 0:
                        k.op("act", lambda e, j=j, tt=tt, ppb=ppb: e.activation(out=h2T[:, j, tt * 128:(tt + 1) * 128], in_=ppb[:, j * 128:(j + 1) * 128], func=AF.Identity,
                                                                               scale=gm2[:, j:j + 1], bias=vecT[:, 24 + j:25 + j]), R=[bpp, B_gm2, B_vecT], W=[B_h2T])
                    else:
                        k.op("dve", lambda e, j=j, tt=tt, ppb=ppb: e.tensor_scalar(out=h2T[:, j, tt * 128:(tt + 1) * 128], in0=ppb[:, j * 128:(j + 1) * 128],
                                                                                  scalar1=gm2[:, j:j + 1], scalar2=vecT[:, 24 + j:25 + j], op0=ALU.mult, op1=ALU.add),
                             R=[bpp, B_gm2, B_vecT], W=[B_h2T])
            for hc in range(32):
                pp, bpp = fp()
                for kc in range(8):
                    k.op("pe", lambda e, hc=hc, kc=kc, pp=pp: e.matmul(pp[:, 0:TB * 128], lhsT=wu[:, kc, hc * 128:(hc + 1) * 128], rhs=h2T[:, kc, :], start=(kc == 0), stop=(kc == 7)),
                         R=[B_wu, B_h2T], W=[bpp])
                r_, br_ = rl[hc % 2], B_rl[hc % 2]
                k.op("act", lambda e, pp=pp, r_=r_: e.activation(out=r_[:, :], in_=pp[:, 0:TB * 128], func=AF.Relu), R=[bpp], W=[br_])
                k.op("dve" if hc % 2 else "pool", lambda e, hc=hc, r_=r_: e.tensor_tensor(out=uT[:, hc, :], in0=r_[:, :], in1=r_[:, :], op=ALU.mult), R=[br_], W=[B_uT[hc]])
            for tt in range(TB):
                for hf in range(2):
                    pp, bpp = fp()
                    for hc in range(32):
                        k.op("pe", lambda e, tt=tt, hf=hf, hc=hc, pp=pp: e.matmul(pp[:, :], lhsT=uT[:, hc, tt * 128:(tt + 1) * 128], rhs=wd[:, hc, hf * 512:(hf + 1) * 512],
                                                                                 start=(hc == 0), stop=(hc == 31)), R=[B_uT[hc], B_wd], W=[bpp])
                    k.op("dve", lambda e, tt=tt, hf=hf, pp=pp: e.tensor_tensor(out=tmpF[:, hf * 512:(hf + 1) * 512], in0=pp[:, :], in1=bct[:, 1, hf * 512:(hf + 1) * 512], op=ALU.mult),
                         R=[bpp, B_bct], W=[B_tmpF])
                k.op("dve", lambda e, tt=tt: e.tensor_tensor(out=x1[:, tt, :], in0=x1[:, tt, :], in1=tmpF[:, :], op=ALU.add), R=[B_x1, B_tmpF], W=[B_x1])
                k.op("act", lambda e, tt=tt: e.activation(out=tmpF[:, :], in_=x1[:, tt, :], func=AF.Square, accum_out=sq2[:, 1:2]), R=[B_x1], W=[B_tmpF, B_sq2])
                k.op("act", lambda e: e.activation(out=sq2[:, 1:2], in_=sq2[:, 1:2], func=AF.Sqrt, scale=1.0 / D, bias=1e-6), R=[B_sq2], W=[B_sq2])
                k.op("dve", lambda e: e.reciprocal(out=sq2[:, 1:2], in_=sq2[:, 1:2]), R=[B_sq2], W=[B_sq2])
                k.op("dve", lambda e, tt=tt: e.scalar_tensor_tensor(out=yout[:, :], in0=x1[:, tt, :], scalar=sq2[:, 1:2], in1=bct[:, 2, :], op0=ALU.mult, op1=ALU.mult),
                     R=[B_x1, B_sq2, B_bct], W=[B_yout])
                k.dma("sp", o_yp[t0 + tt * 128:t0 + (tt + 1) * 128, :], yout[:, :], R=[B_yout], W=[DOUT])
        gmT = sb(st, "gmT", [128, 8, NS], F32)
        shT = sb(st, "shT", [128, 8, NS], F32)
        B_gmT, B_shT = Buf(), Buf()
        B_m1 = B_x1
        k.dma("pool", mixb[0:NS, 0, :], mixs_scr[:, :], R=[B_scr3], W=[B_mixb])
        k.dma("sp", x1[0:NS, 0, :], xs[:, :], W=[B_x1])
        k.dma("sp", x1[0:NS, 1, :], mods_scr[:, 2 * D:3 * D], R=[B_scr2], W=[B_x1])
        pp, bpp = fp()
        ppb = pp[:, :].bitcast(BF16)
        for j in range(8):
            k.op("pe", lambda e, j=j, ppb=ppb: e.transpose(out=ppb[:, j * NS:(j + 1) * NS], in_=mixb[0:NS, 0, j * 128:(j + 1) * 128], identity=ident[0:NS, 0:NS]), R=[B_mixb, B_ident], W=[bpp])
        k.op("act", lambda e, ppb=ppb: e.activation(out=mixT[:, :, 0:NS], in_=ppb[:, 0:8 * NS].rearrange("p (j t) -> p j t", t=NS), func=AF.Identity), R=[bpp], W=[B_mixT])
        for hf in range(2):
            pp, bpp = fp()
            for kc in range(8):
                k.op("pe", lambda e, hf=hf, kc=kc, pp=pp: e.matmul(pp[0:NS, :], lhsT=mixT[:, kc, 0:NS], rhs=wo[:, kc, hf * 512:(hf + 1) * 512], start=(kc == 0), stop=(kc == 7)), R=[B_mixT, B_wo], W=[bpp])
            k.op("dve", lambda e, hf=hf, pp=pp: e.tensor_tensor(out=tmpF[0:NS, hf * 512:(hf + 1) * 512], in0=pp[0:NS, :], in1=x1[0:NS, 1, hf * 512:(hf + 1) * 512], op=ALU.mult), R=[bpp, B_x1], W=[B_tmpF])
        k.op("dve", lambda e: e.tensor_tensor(out=x1[0:NS, 0, :], in0=x1[0:NS, 0, :], in1=tmpF[0:NS, :], op=ALU.add), R=[B_x1, B_tmpF], W=[B_x1])
        for which, (c0, dstT) in enumerate(((4 * D, gmT), (3 * D, shT))):
            k.dma("sp", x1[0:NS, 1, :], mods_scr[:, c0:c0 + D], R=[B_scr2, B_x1], W=[B_x1])
            pp, bpp = fp()
            for j in range(8):
                k.op("pe", lambda e, j=j, pp=pp: e.transpose(out=pp[:, j * NS:(j + 1) * NS], in_=x1[0:NS, 1, j * 128:(j + 1) * 128], identity=identf[0:NS, 0:NS]), R=[B_x1, B_identf], W=[bpp])
            if which == 0:
                for j in range(8):
                    k.op("dve", lambda e, j=j, pp=pp: e.tensor_scalar(out=gmT[:, j, :], in0=pp[:, j * NS:(j + 1) * NS], scalar1=1.0, scalar2=vecT[:, 56 + j:57 + j], op0=ALU.add, op1=ALU.mult),
                         R=[bpp, B_vecT], W=[B_gmT])
            else:
                k.op("dve", lambda e, pp=pp: e.tensor_copy(out=shT[:, :, :], in_=pp[:, 0:8 * NS].rearrange("p (j t) -> p j t", t=NS)), R=[bpp], W=[B_shT])
        k.op("act", lambda e: e.activation(out=tmpF[0:NS, :], in_=x1[0:NS, 0, :], func=AF.Square, accum_out=sq2[0:NS, 0:1]), R=[B_x1], W=[B_tmpF, B_sq2])
        k.op("act", lambda e: e.activation(out=sq2[0:NS, 0:1], in_=sq2[0:NS, 0:1], func=AF.Sqrt, scale=1.0 / D, bias=1e-6), R=[B_sq2], W=[B_sq2])
        k.op("dve", lambda e: e.reciprocal(out=sq2[0:NS, 0:1], in_=sq2[0:NS, 0:1]), R=[B_sq2], W=[B_sq2])
        k.op("dve", lambda e: e.tensor_scalar(out=xnb[0:NS, :], in0=x1[0:NS, 0, :], scalar1=sq2[0:NS, 0:1], scalar2=None, op0=ALU.mult), R=[B_x1, B_sq2], W=[B_xnb])
        pp, bpp = fp()
        ppb = pp[:, :].bitcast(BF16)
        for j in range(8):
            k.op("pe", lambda e, j=j, ppb=ppb: e.transpose(out=ppb[:, j * NS:(j + 1) * NS], in_=xnb[0:NS, j * 128:(j + 1) * 128], identity=ident[0:NS, 0:NS]), R=[B_xnb, B_ident], W=[bpp])
        k.op("dve", lambda e, ppb=ppb: e.tensor_tensor(out=gmT[:, :, :], in0=ppb[:, 0:8 * NS].rearrange("p (j t) -> p j t", t=NS), in1=gmT[:, :, :], op=ALU.mult), R=[bpp, B_gmT], W=[B_gmT])
        k.op("dve", lambda e: e.tensor_tensor(out=h2T[:, :, 0:NS], in0=gmT[:, :, :], in1=shT[:, :, :], op=ALU.add), R=[B_gmT, B_shT], W=[B_h2T])
        for hc in range(32):
            pp, bpp = fp()
            for kc in range(8):
                k.op("pe", lambda e, hc=hc, kc=kc, pp=pp: e.matmul(pp[:, 0:NS], lhsT=wu[:, kc, hc * 128:(hc + 1) * 128], rhs=h2T[:, kc, 0:NS], start=(kc == 0), stop=(kc == 7)), R=[B_wu, B_h2T], W=[bpp])
            r_, br_ = rl[hc % 2], B_rl[hc % 2]
            k.op("act", lambda e, pp=pp, r_=r_: e.activation(out=r_[:, 0:NS], in_=pp[:, 0:NS], func=AF.Relu), R=[bpp], W=[br_])
            k.op("dve", lambda e, hc=hc, r_=r_: e.tensor_tensor(out=uT[:, hc, 0:NS], in0=r_[:, 0:NS], in1=r_[:, 0:NS], op=ALU.mult), R=[br_], W=[B_uT[hc]])
        k.dma("sp", x1[0:NS, 1, :], mods_scr[:, 5 * D:6 * D], R=[B_scr2, B_x1], W=[B_x1])
        for hf in range(2):
            pp, bpp = fp()
            for hc in range(32):
                k.op("pe", lambda e, hf=hf, hc=hc, pp=pp: e.matmul(pp[0:NS, :], lhsT=uT[:, hc, 0:NS], rhs=wd[:, hc, hf * 512:(hf + 1) * 512], start=(hc == 0), stop=(hc == 31)), R=[B_uT[hc], B_wd], W=[bpp])
            k.op("dve", lambda e, hf=hf, pp=pp: e.tensor_tensor(out=tmpF[0:NS, hf * 512:(hf + 1) * 512], in0=pp[0:NS, :], in1=x1[0:NS, 1, hf * 512:(hf + 1) * 512], op=ALU.mult), R=[bpp, B_x1], W=[B_tmpF])
        k.op("dve", lambda e: e.tensor_tensor(out=x1[0:NS, 0, :], in0=x1[0:NS, 0, :], in1=tmpF[0:NS, :], op=ALU.add), R=[B_x1, B_tmpF], W=[B_x1])
        k.op("act", lambda e: e.activation(out=tmpF[0:NS, :], in_=x1[0:NS, 0, :], func=AF.Square, accum_out=sq2[0:NS, 1:2]), R=[B_x1], W=[B_tmpF, B_sq2])
        k.op("act", lambda e: e.activation(out=sq2[0:NS, 1:2], in_=sq2[0:NS, 1:2], func=AF.Sqrt, scale=1.0 / D, bias=1e-6), R=[B_sq2], W=[B_sq2])
        k.op("dve", lambda e: e.reciprocal(out=sq2[0:NS, 1:2], in_=sq2[0:NS, 1:2]), R=[B_sq2], W=[B_sq2])
        k.op("dve", lambda e: e.scalar_tensor_tensor(out=tmpF[0:NS, :], in0=x1[0:NS, 0, :], scalar=sq2[0:NS, 1:2], in1=bct[0:NS, 2, :], op0=ALU.mult, op1=ALU.mult), R=[B_x1, B_sq2, B_bct], W=[B_tmpF])
        k.dma("sp", o_ys[:, :], tmpF[0:NS, :], R=[B_tmpF], W=[DOUT])
        k.barrier()
    st0.close()
    k.emit()
    return nc


_NC = None


def kernel(**inp):
    global _NC
    f = lambda a: np.ascontiguousarray(np.asarray(a, dtype=np.float32))
    if _NC is None:
        _NC = build_nc()
    nc = _NC
    pools_ = [f(inp[n_][0]).reshape(2560, 128, 256) for n_ in ("cache_k_cmp", "cache_v_cmp", "cache_k_slc", "cache_v_slc")]
    in_maps = []
    for i in range(8):
        b = i % 4
        s0 = i * NS
        in_maps.append({
            "xp": f(inp["x_prompt"][b]),
            "xs": f(inp["x_sample"][s0:s0 + NS, 0]),
            "cp": f(inp["c_prompt"][b:b + 1]),
            "cs": f(inp["c_sample"][s0:s0 + NS]),
            "w_ada": f(inp["w_ada"][0]),
            "b_ada": f(inp["b_ada"][0:1]),
            "g1": f(inp["norm1_g"][0:1]),
            "g2": f(inp["norm2_g"][0:1]),
            "gf": f(inp["final_g"][None, :]),
            "w_in": f(inp["w_in"][0]),
            "ckwin": f(inp["cache_k_win"][0, s0:s0 + NS]).reshape(NS, 512, 256),
            "cvwin": f(inp["cache_v_win"][0, s0:s0 + NS]).reshape(NS, 512, 256),
            "sconv": f(inp["state_conv"][0, s0:s0 + NS]),
            "conv_w": f(inp["dn_conv_w"][0]),
            "a_log": f(inp["dn_a_log"]),
            "dt_bias": f(inp["dn_dt_bias"]),
            "dn_ng": f(inp["dn_norm_g"]),
            "state_dn": f(inp["state_dn"][0, s0:s0 + NS]),
            "pk_cmp": pools_[0], "pv_cmp": pools_[1], "pk_slc": pools_[2], "pv_slc": pools_[3],
            "ptbl": np.ascontiguousarray(np.asarray(inp["page_table"][s0:s0 + NS], dtype=np.int32).reshape(1, NS * 16)),
            "w_out": f(inp["w_out"][0]), "w_up": f(inp["w_up"][0]), "w_down": f(inp["w_down"][0]),
            "w1k": f(inp["cmp_w1_k"][0]), "w2k": f(inp["cmp_w2_k"][0]), "pek": f(inp["cmp_pe_k"][0]),
            "w1v": f(inp["cmp_w1_v"][0]), "w2v": f(inp["cmp_w2_v"][0]), "pev": f(inp["cmp_pe_v"][0]),
        })
    res = run_bass_kernel_spmd(nc, in_maps, core_ids=list(range(8)))
    R = res.results
    B = 4
    y_prompt = np.stack([R[b]["o_yp"] for b in range(B)], 0)
    y_sample = np.concatenate([R[i]["o_ys"] for i in range(8)], 0)[:, None, :]
    pkv = np.stack([R[b]["o_pkv"] for b in range(B)], 0)
    p_kv = [pkv[:, j].reshape(1, B, S, 4, 64) for j in range(6)]
    p_kv[4] = p_kv[4][:, :, S - 512:]
    p_kv[5] = p_kv[5][:, :, S - 512:]
    p_conv = np.stack([R[b]["o_pconv"] for b in range(B)], 0)[None]
    p_dn = np.stack([R[b]["o_pdn"] for b in range(B)], 0)[None]
    skv = np.concatenate([R[i]["o_skv"] for i in range(8)], 0)
    s_kv = [skv[:, j * 256:(j + 1) * 256].reshape(1, 128, 1, 4, 64) for j in range(4)]
    s_kwin = np.concatenate([R[i]["o_skwin"] for i in range(8)], 0).reshape(1, 128, 512, 4, 64)
    s_vwin = np.concatenate([R[i]["o_svwin"] for i in range(8)], 0).reshape(1, 128, 512, 4, 64)
    s_conv = np.concatenate([R[i]["o_sconv"] for i in range(8)], 0)[None]
    s_dn = np.concatenate([R[i]["o_sdn"] for i in range(8)], 0)[None]
    return (y_prompt, y_sample, p_kv[0], p_kv[1], p_kv[2], p_kv[3], p_kv[4], p_kv[5], p_conv, p_dn,
            s_kv[0], s_kv[1], s_kv[2], s_kv[3], s_kwin, s_vwin, s_conv, s_dn)
```

```python
import contextlib
import os
STOP = int(os.environ.get('DN_STOP', '9'))
import numpy as np
import concourse.bass as bass
import concourse.mybir as mybir
from concourse.bass_utils import run_bass_kernel_spmd

F32 = mybir.dt.float32
BF16 = mybir.dt.bfloat16
I32 = mybir.dt.int32
AF = mybir.ActivationFunctionType
ALU = mybir.AluOpType

S = 4096
D = 1024
NS = 16
INW = 8768
C_KV = 1024
C_QKV = 2608
NT = S // 128


class Buf:
    __slots__ = ("w", "r", "name")

    def __init__(self, name=""):
        self.w = None
        self.r = {}
        self.name = name


class _Rec:
    def __getattr__(self, name):
        def f(*a, **kw):
            self.call = (name, a, kw)
            return self
        return f


class KB:
    NDMASEM = 24

    def __init__(self, nc):
        self.nc = nc
        self.engs = {"pe": nc.tensor, "act": nc.scalar, "dve": nc.vector,
                     "pool": nc.gpsimd, "sp": nc.sync}
        self.prog = {e: [] for e in self.engs}
        self.cnt = {}
        self.waited = {e: {} for e in self.engs}
        self.ndma = 0
        self.sems = {}
        for e in self.engs:
            self.sems[e] = nc.alloc_semaphore("s_" + e)
        for i in range(self.NDMASEM):
            self.sems["dma%d" % i] = nc.alloc_semaphore("s_dma%d" % i)

    def op(self, eng, fn, R=(), W=(), dma=False):
        if fn is None:
            fn = self._raw
        else:
            rec = _Rec()
            fn(rec)
            name_, a_, kw_ = rec.call
            fn = (lambda e, name_=name_, a_=a_, kw_=kw_: getattr(e, name_)(*a_, **kw_))
        deps = {}

        def add(s, v):
            if deps.get(s, 0) < v:
                deps[s] = v
        for b in R:
            if b.w:
                add(*b.w)
        for b in W:
            if b.w:
                add(*b.w)
            for s, v in b.r.items():
                if s == eng and not dma:
                    continue
                add(s, v)
        if dma:
            sname = "dma%d" % (self.ndma % self.NDMASEM)
            prev = self.cnt.get(sname, 0)
            if prev:
                add(sname, prev)
            val = prev + 16
            self.ndma += 1
        else:
            sname = eng
            val = self.cnt.get(eng, 0) + 1
        self.cnt[sname] = val
        waits = []
        for s, v in deps.items():
            if s == "pe" and eng == "pe" and not dma:
                continue
            if self.waited[eng].get(s, 0) >= v:
                continue
            self.waited[eng][s] = v
            waits.append((s, v))
        self.prog[eng].append((waits, fn, sname, 16 if dma else 1))
        for b in R:
            if b.r.get(sname, 0) < val:
                b.r[sname] = val
        for b in W:
            b.w = (sname, val)
            b.r = {}

    def raw(self, eng, fn, R=(), W=(), dma=False):
        self._raw = fn
        self.op(eng, None, R=R, W=W, dma=dma)

    def dma(self, eng, out, in_, R=(), W=(), **kw):
        self.op(eng, lambda e: e.dma_start(out=out, in_=in_, **kw), R=R, W=W, dma=True)

    def barrier(self):
        for e in self.engs:
            waits = []
            for s, v in self.cnt.items():
                if self.waited[e].get(s, 0) >= v:
                    continue
                self.waited[e][s] = v
                waits.append((s, v))
            if waits:
                self.prog[e].append((waits, None, None, 0))

    def emit(self):
        nc = self.nc
        self.barrier()
        with nc.Block() as block:
            def mk(ename):
                def body(e):
                    for waits, fn, sname, inc in self.prog[ename]:
                        for s, v in waits:
                            e.wait_ge(self.sems[s], v)
                        if fn is not None:
                            fn(e).then_inc(self.sems[sname], inc)
                return body
            block.tensor(mk("pe"))
            block.scalar(mk("act"))
            block.vector(mk("dve"))
            block.gpsimd(mk("pool"))
            block.sync(mk("sp"))


def build_nc():
    nc = bass.Bass("TRN2", target_bir_lowering=False)

    def din(name, shape, dt=F32):
        return nc.dram_tensor(name, list(shape), dt, kind="ExternalInput").ap()

    def dout(name, shape, dt=F32):
        return nc.dram_tensor(name, list(shape), dt, kind="ExternalOutput").ap()

    xp = din("xp", [S, D])
    xs = din("xs", [NS, D])
    cp = din("cp", [1, D])
    cs = din("cs", [NS, D])
    w_ada = din("w_ada", [D, 6 * D])
    b_ada = din("b_ada", [1, 6 * D])
    g1 = din("g1", [1, D])
    g2 = din("g2", [1, D])
    gf = din("gf", [1, D])
    w_in = din("w_in", [D, INW])
    ckwin = din("ckwin", [NS, 512, 256])
    cvwin = din("cvwin", [NS, 512, 256])
    sconv = din("sconv", [NS, 3, 3072])
    conv_w = din("conv_w", [4, 3072])
    a_log = din("a_log", [1, 8])
    dt_bias = din("dt_bias", [1, 8])
    dn_ng = din("dn_ng", [1, 128])
    state_dn = din("state_dn", [NS, 8, 128, 128])
    pk_cmp = din("pk_cmp", [2560, 128, 256])
    pv_cmp = din("pv_cmp", [2560, 128, 256])
    pk_slc = din("pk_slc", [2560, 128, 256])
    pv_slc = din("pv_slc", [2560, 128, 256])
    ptbl = din("ptbl", [1, NS * 16], I32)
    w_out = din("w_out", [D, D])
    w_up = din("w_up", [D, 4 * D])
    w_down = din("w_down", [4 * D, D])
    w1k = din("w1k", [32, 64, 128])
    w2k = din("w2k", [128, 64])
    pek = din("pek", [32, 64])
    w1v = din("w1v", [32, 64, 128])
    w2v = din("w2v", [128, 64])
    pev = din("pev", [32, 64])

    o_pkv = dout("o_pkv", [6, S, 256])
    o_pconv = dout("o_pconv", [3, 3072])
    o_skv = dout("o_skv", [NS, 1536])
    o_skwin = dout("o_skwin", [NS, 512, 256])
    o_svwin = dout("o_svwin", [NS, 512, 256])
    o_sconv = dout("o_sconv", [NS, 3, 3072])
    o_pdn = dout("o_pdn", [8, 128, 128])
    o_sdn = dout("o_sdn", [NS, 8, 128, 128])
    o_yp = dout("o_yp", [S, D])
    o_ys = dout("o_ys", [NS, D])
    odn_scr = nc.dram_tensor("odn_scr", [S, D], F32, kind="Internal").ap()

    k = KB(nc)
    wv = w_in.rearrange("(kc p) n -> p kc n", p=128)
    st0 = contextlib.ExitStack()

    def sb(stack, name, shape, dt):
        return stack.enter_context(nc.sbuf_tensor(name, list(shape), dt))

    PS = [nc.alloc_psum_tensor("ps%d" % i, [128, 512], F32) for i in range(8)]
    PSB = [Buf("ps%d" % i) for i in range(8)]
    DOUT = Buf("dram_out")

    identf = sb(st0, "identf", [128, 128], F32)
    ident = sb(st0, "ident", [128, 128], BF16)
    ones = sb(st0, "ones", [1, 128], F32)
    B_identf, B_ident, B_ones = Buf(), Buf(), Buf()
    k.op("pool", lambda e: e.memset(identf[:, :], 0.0), W=[B_identf])
    k.op("pool", lambda e: e.affine_select(out=identf[:, :], in_=identf[:, :], pattern=[[-1, 128]],
                                           compare_op=ALU.not_equal, fill=1.0, base=0, channel_multiplier=1),
         R=[B_identf], W=[B_identf])
    k.op("dve", lambda e: e.tensor_copy(out=ident[:, :], in_=identf[:, :]), R=[B_identf], W=[B_ident])
    k.op("pool", lambda e: e.memset(ones[:, :], 1.0), W=[B_ones])
    zcol = sb(st0, "zcol", [128, 1], F32)
    k.op("pool", lambda e: e.memset(zcol[:, :], 0.0), W=[Buf()])

    vecT = sb(st0, "vecT", [128, 72], F32)
    gm1 = sb(st0, "gm1", [128, 8], F32)
    st1 = contextlib.ExitStack()
    hT = sb(st1, "hT", [128, 8, S], BF16)
    B_hT = [Buf("hT%d" % i) for i in range(NT)]
    hTs = sb(st1, "hTs", [128, 8, NS], BF16)
    B_hTs = Buf()
    stAB = contextlib.ExitStack()
    mods = sb(stAB, "mods", [NS, 6 * D], F32)
    B_vecT, B_gm1, B_mods = Buf(), Buf(), Buf()
    rows = sb(stAB, "rows", [1, 9 * D], F32)
    B_rows = Buf()
    B_projs = Buf()
    mods_scr = nc.dram_tensor("mods_scr", [NS, 6 * D], F32, kind="Internal").ap()
    projs_scr = nc.dram_tensor("projs_scr", [NS, INW], F32, kind="Internal").ap()
    szt_scr = nc.dram_tensor("szt_scr", [S, D], F32, kind="Internal").ap()
    B_scr2 = Buf()
    q_scr = nc.dram_tensor("q_scr", [8, 128, S], BF16, kind="Internal").ap()
    kvT_scr = nc.dram_tensor("kvT_scr", [8, 128, S], BF16, kind="Internal").ap()
    gate_scr = nc.dram_tensor("gate_scr", [S, 48], F32, kind="Internal").ap()
    mg_scr = nc.dram_tensor("mg_scr", [S, 2048], F32, kind="Internal").ap()
    mix_scr = nc.dram_tensor("mix_scr", [S, D], F32, kind="Internal").ap()
    odns_scr = nc.dram_tensor("odns_scr", [NS, D], F32, kind="Internal").ap()
    mixs_scr = nc.dram_tensor("mixs_scr", [NS, D], F32, kind="Internal").ap()
    B_scr3 = Buf()

    with contextlib.ExitStack() as st:
        cin = sb(st, "cin", [NS, D], F32)
        cpin = sb(st, "cpin", [1, D], F32)
        cT = sb(st, "cT", [128, 8, 17], F32)
        bada = sb(st, "bada", [1, 6 * D], F32)
        wa = [sb(st, "wa%d" % i, [128, 8, 512], F32) for i in range(2)]
        B_cin, B_cpin, B_cT, B_bada = Buf(), Buf(), Buf(), Buf()
        B_wa = [Buf(), Buf()]
        k.dma("sp", cin[:, :], cs[:, :], W=[B_cin])
        k.dma("sp", cpin[:, :], cp[:, :], W=[B_cpin])
        k.dma("sp", bada[:, :], b_ada[:, :], W=[B_bada])
        k.dma("sp", rows[0:1, 6 * D:7 * D], g1[:, :], W=[B_rows])
        k.dma("sp", rows[0:1, 7 * D:8 * D], g2[:, :], W=[B_rows])
        k.dma("sp", rows[0:1, 8 * D:9 * D], gf[:, :], W=[B_rows])
        pt = PS[0]
        for j in range(8):
            k.op("pe", lambda e, j=j: e.transpose(out=pt[:, j * 17:j * 17 + 1], in_=cpin[0:1, j * 128:(j + 1) * 128],
                                                  identity=identf[0:1, 0:1]), R=[B_cpin, B_identf], W=[PSB[0]])
            k.op("pe", lambda e, j=j: e.transpose(out=pt[:, j * 17 + 1:j * 17 + 17], in_=cin[:, j * 128:(j + 1) * 128],
                                                  identity=identf[0:NS, 0:NS]), R=[B_cin, B_identf], W=[PSB[0]])
        k.op("dve", lambda e: e.tensor_copy(out=cT[:, :, :], in_=pt[:, 0:136].rearrange("p (j s) -> p j s", s=17)),
             R=[PSB[0]], W=[B_cT])
        wview = w_ada.rearrange("(kc p) n -> p kc n", p=128)
        for n in range(12):
            wb = wa[n % 2]
            k.dma("sp", wb[:, :, :], wview[:, :, n * 512:(n + 1) * 512], W=[B_wa[n % 2]])
            pa, pb = PS[1 + (n % 2) * 2], PS[2 + (n % 2) * 2]
            Ba, Bb = PSB[1 + (n % 2) * 2], PSB[2 + (n % 2) * 2]
            for kc in range(8):
                k.op("pe", lambda e, kc=kc, wb=wb, pa=pa: e.matmul(pa[0:1, :], lhsT=cT[:, kc, 0:1], rhs=wb[:, kc, :],
                                                                  start=(kc == 0), stop=False),
                     R=[B_cT, B_wa[n % 2]], W=[Ba])
            k.op("pe", lambda e, n=n, pa=pa: e.matmul(pa[0:1, :], lhsT=ones[0:1, 0:1], rhs=bada[0:1, n * 512:(n + 1) * 512],
                                                      start=False, stop=True), R=[B_ones, B_bada], W=[Ba])
            for kc in range(8):
                k.op("pe", lambda e, kc=kc, wb=wb, pb=pb: e.matmul(pb[0:NS, :], lhsT=cT[:, kc, 1:17], rhs=wb[:, kc, :],
                                                                  start=(kc == 0), stop=False),
                     R=[B_cT, B_wa[n % 2]], W=[Bb])
            k.op("pe", lambda e, n=n, pb=pb: e.matmul(pb[0:NS, :], lhsT=ones[0:1, 0:NS], rhs=bada[0:1, n * 512:(n + 1) * 512],
                                                      start=False, stop=True), R=[B_ones, B_bada], W=[Bb])
            k.op("dve", lambda e, n=n, pa=pa: e.tensor_copy(out=rows[0:1, n * 512:(n + 1) * 512], in_=pa[0:1, :]),
                 R=[Ba], W=[B_rows])
            k.op("act", lambda e, n=n, pb=pb: e.activation(out=mods[:, n * 512:(n + 1) * 512], in_=pb[0:NS, :], func=AF.Identity),
                 R=[Bb], W=[B_mods])
        pt = PS[5]
        for j in range(72):
            k.op("pe", lambda e, j=j: e.matmul(pt[:, j:j + 1], lhsT=rows[0:1, j * 128:(j + 1) * 128], rhs=ones[0:1, 0:1],
                                               start=True, stop=True), R=[B_rows, B_ones], W=[PSB[5]])
        k.op("dve", lambda e: e.tensor_copy(out=vecT[:, :], in_=pt[:, 0:72]), R=[PSB[5]], W=[B_vecT])
        k.op("dve", lambda e: e.scalar_tensor_tensor(out=gm1[:, :], in0=vecT[:, 8:16], scalar=1.0, in1=vecT[:, 48:56],
                                                     op0=ALU.add, op1=ALU.mult), R=[B_vecT], W=[B_gm1])
        k.barrier()

    with contextlib.ExitStack() as st:
        xt = [sb(st, "xt%d" % i, [128, D], F32) for i in range(3)]
        B_xt = [Buf() for _ in range(3)]
        junk = sb(st, "junk", [128, D], F32)
        B_junk = Buf()
        ssq = [sb(st, "ssq%d" % i, [128, 1], F32) for i in range(2)]
        B_ssq = [Buf(), Buf()]
        xn = [sb(st, "xn%d" % i, [128, D], BF16) for i in range(2)]
        B_xn = [Buf(), Buf()]
        for tt in range(NT):
            x_, bx = xt[tt % 3], B_xt[tt % 3]
            sq, bs = ssq[tt % 2], B_ssq[tt % 2]
            xn_, bn = xn[tt % 2], B_xn[tt % 2]
            k.dma("sp", x_[:, :], xp[tt * 128:(tt + 1) * 128, :], W=[bx])
            k.op("act", lambda e, x_=x_, sq=sq: e.activation(out=junk[:, :], in_=x_[:, :], func=AF.Square, accum_out=sq[:, :]),
                 R=[bx], W=[B_junk, bs])
            k.op("act", lambda e, sq=sq: e.activation(out=sq[:, :], in_=sq[:, :], func=AF.Sqrt, scale=1.0 / D, bias=1e-6),
                 R=[bs], W=[bs])
            k.op("dve", lambda e, sq=sq: e.reciprocal(out=sq[:, :], in_=sq[:, :]), R=[bs], W=[bs])
            k.op("dve", lambda e, x_=x_, sq=sq, xn_=xn_: e.tensor_scalar(out=xn_[:, :], in0=x_[:, :], scalar1=sq[:, 0:1],
                                                                       scalar2=None, op0=ALU.mult), R=[bx, bs], W=[bn])
            pi = tt % 2
            ptv = PS[pi][:, :].bitcast(BF16)
            for j in range(8):
                k.op("pe", lambda e, j=j, xn_=xn_, ptv=ptv: e.transpose(out=ptv[:, j * 128:(j + 1) * 128],
                                                                       in_=xn_[:, j * 128:(j + 1) * 128], identity=ident[:, :]),
                     R=[bn, B_ident], W=[PSB[pi]])
            for j in range(8):
                if j % 2 == 0:
                    k.op("act", lambda e, j=j, ptv=ptv, tt=tt: e.activation(out=hT[:, j, tt * 128:(tt + 1) * 128],
                                                                           in_=ptv[:, j * 128:(j + 1) * 128], func=AF.Identity,
                                                                           scale=gm1[:, j:j + 1], bias=vecT[:, j:j + 1]),
                         R=[PSB[pi], B_gm1, B_vecT], W=[B_hT[tt]])
                else:
                    k.op("dve", lambda e, j=j, ptv=ptv, tt=tt: e.tensor_scalar(out=hT[:, j, tt * 128:(tt + 1) * 128],
                                                                              in0=ptv[:, j * 128:(j + 1) * 128],
                                                                              scalar1=gm1[:, j:j + 1], scalar2=vecT[:, j:j + 1],
                                                                              op0=ALU.mult, op1=ALU.add),
                         R=[PSB[pi], B_gm1, B_vecT], W=[B_hT[tt]])
        xsi = sb(st, "xsi", [NS, D], F32)
        gms = sb(st, "gms", [NS, D], F32)
        hs = sb(st, "hs", [NS, D], F32)
        hsb = sb(st, "hsb", [NS, D], BF16)
        sqs = sb(st, "sqs", [NS, 1], F32)
        B_xsi, B_gms, B_hs, B_hsb, B_sqs = Buf(), Buf(), Buf(), Buf(), Buf()
        k.dma("sp", xsi[:, :], xs[:, :], W=[B_xsi])
        k.op("act", lambda e: e.activation(out=junk[0:NS, :], in_=xsi[:, :], func=AF.Square, accum_out=sqs[:, :]),
             R=[B_xsi], W=[B_junk, B_sqs])
        k.op("act", lambda e: e.activation(out=sqs[:, :], in_=sqs[:, :], func=AF.Sqrt, scale=1.0 / D, bias=1e-6),
             R=[B_sqs], W=[B_sqs])
        k.op("dve", lambda e: e.reciprocal(out=sqs[:, :], in_=sqs[:, :]), R=[B_sqs], W=[B_sqs])
        for h in range(2):
            k.op("pe", lambda e, h=h: e.matmul(PS[2 + h][0:NS, :], lhsT=ones[0:1, 0:NS],
                                               rhs=rows[0:1, 6 * D + h * 512:6 * D + (h + 1) * 512], start=True, stop=True),
                 R=[B_ones, B_rows], W=[PSB[2 + h]])
            k.op("dve", lambda e, h=h: e.scalar_tensor_tensor(out=gms[:, h * 512:(h + 1) * 512],
                                                              in0=mods[:, D + h * 512:D + (h + 1) * 512], scalar=1.0,
                                                              in1=PS[2 + h][0:NS, :], op0=ALU.add, op1=ALU.mult),
                 R=[B_mods, PSB[2 + h]], W=[B_gms])
        k.op("dve", lambda e: e.scalar_tensor_tensor(out=hs[:, :], in0=xsi[:, :], scalar=sqs[:, 0:1], in1=gms[:, :],
                                                     op0=ALU.mult, op1=ALU.mult), R=[B_xsi, B_sqs, B_gms], W=[B_hs])
        k.op("dve", lambda e: e.tensor_tensor(out=hsb[:, :], in0=hs[:, :], in1=mods[:, 0:D], op=ALU.add),
             R=[B_hs, B_mods], W=[B_hsb])
        ptv = PS[4][:, :].bitcast(BF16)
        for j in range(8):
            k.op("pe", lambda e, j=j: e.transpose(out=ptv[:, j * NS:(j + 1) * NS], in_=hsb[:, j * 128:(j + 1) * 128],
                                                  identity=ident[0:NS, 0:NS]), R=[B_hsb, B_ident], W=[PSB[4]])
        k.op("dve", lambda e: e.tensor_copy(out=hTs[:, :, :], in_=ptv[:, 0:8 * NS].rearrange("p (j s) -> p j s", s=NS)),
             R=[PSB[4]], W=[B_hTs])
        k.dma("sp", mods_scr[:, :], mods[:, :], R=[B_mods], W=[B_scr2])
        k.barrier()
    stAB.close()

    with contextlib.ExitStack() as st:
        proj_s = sb(st, "proj_s", [NS, INW], F32)
        wbuf = [sb(st, "wbuf%d" % i, [128, 8, 512], BF16) for i in range(2)]
        B_wbuf = [Buf(), Buf()]
        stage = [sb(st, "stage%d" % i, [128, 4, 512], F32) for i in range(2)]
        B_stage = [Buf(), Buf()]
        pcv = sb(st, "pcv", [3, 3072], F32)
        fstage = [sb(st, "fstage%d" % i, [128, S], BF16) for i in range(2)]
        B_fstage = [Buf(), Buf()]
        fsi = [0]
        B_pcv = Buf()
        wv = w_in.rearrange("(kc p) n -> p kc n", p=128)
        jobs = []
        for c in range(3):
            jobs.append(("kv", C_KV + c * 512, 512, c))
        for c in range(6):
            jobs.append(("qkv", C_QKV + c * 512, 512, c))
        for c in range(2):
            jobs.append(("z", 5680 + c * 512, 512, c))
        jobs.append(("s", 6704, 16, 0))
        if STOP >= 9:
            for c in range(2):
                jobs.append(("feat", c * 512, 512, [(cc, q_scr[c * 4 + cc], 0.125) for cc in range(4)]))
            jobs.append(("feat", 1024, 512, [(cc, kvT_scr[cc], 1.0) for cc in range(4)]))
            jobs.append(("feat", 1536, 256, [(cc, kvT_scr[4 + cc], 1.0) for cc in range(2)]))
            jobs.append(("feat", 2048, 256, [(cc, kvT_scr[6 + cc], 1.0) for cc in range(2)]))
            jobs.append(("gate", 2560, 48, 0))
            for c in range(4):
                jobs.append(("mg", 6720 + c * 512, 512, c))
        psi = 0
        evi = 0
        sti = 0
        for ji, (kind, c0, ncol, ci) in enumerate(jobs):
            wb, bw = wbuf[ji % 2], B_wbuf[ji % 2]
            k.dma("pool", wb[:, :, 0:ncol], wv[:, :, c0:c0 + ncol], W=[bw])
            p_, bp = PS[psi % 8], PSB[psi % 8]
            psi += 1
            for kc in range(8):
                k.op("pe", lambda e, kc=kc, wb=wb, p_=p_, ncol=ncol: e.matmul(p_[0:NS, 0:ncol], lhsT=hTs[:, kc, :],
                                                                              rhs=wb[:, kc, 0:ncol], start=(kc == 0), stop=(kc == 7)),
                     R=[B_hTs, bw], W=[bp])
            k.op("act", lambda e, p_=p_, c0=c0, ncol=ncol: e.activation(out=proj_s[:, c0:c0 + ncol], in_=p_[0:NS, 0:ncol],
                                                                        func=AF.Identity), R=[bp], W=[B_projs])
            if kind == "feat":
                for (cc, dst, scl) in ci:
                    fs, bfs = fstage[fsi[0] % 2], B_fstage[fsi[0] % 2]
                    fsi[0] += 1
                    for tb in range(S // 512):
                        p_, bp = PS[psi % 8], PSB[psi % 8]
                        psi += 1
                        for kc in range(8):
                            k.op("pe", lambda e, kc=kc, wb=wb, p_=p_, tb=tb, cc=cc: e.matmul(p_[:, :], lhsT=wb[:, kc, cc * 128:(cc + 1) * 128],
                                                                                            rhs=hT[:, kc, tb * 512:(tb + 1) * 512], start=(kc == 0), stop=(kc == 7)),
                                 R=[B_hT[tb * 4 + i] for i in range(4)] + [bw], W=[bp])
                        if tb % 2 == 0:
                            k.op("act", lambda e, p_=p_, fs=fs, tb=tb, scl=scl: e.activation(out=fs[:, tb * 512:(tb + 1) * 512], in_=p_[:, :], func=AF.Identity, scale=scl),
                                 R=[bp], W=[bfs])
                        else:
                            k.op("dve", lambda e, p_=p_, fs=fs, tb=tb, scl=scl: e.tensor_scalar(out=fs[:, tb * 512:(tb + 1) * 512], in0=p_[:, :], scalar1=scl, scalar2=None, op0=ALU.mult),
                                 R=[bp], W=[bfs])
                    k.dma("sp", dst, fs[:, :], R=[bfs], W=[B_scr2])
            if kind in ("gate", "mg"):
                for tt in range(NT):
                    p_, bp = PS[psi % 8], PSB[psi % 8]
                    psi += 1
                    for kc in range(8):
                        k.op("pe", lambda e, kc=kc, wb=wb, p_=p_, tt=tt, ncol=ncol: e.matmul(p_[:, 0:ncol], lhsT=hT[:, kc, tt * 128:(tt + 1) * 128],
                                                                                            rhs=wb[:, kc, 0:ncol], start=(kc == 0), stop=(kc == 7)),
                             R=[B_hT[tt], bw], W=[bp])
                    sg, bsg = stage[sti % 2], B_stage[sti % 2]
                    t4 = tt % 4
                    k.op("act", lambda e, p_=p_, sg=sg, t4=t4, ncol=ncol: e.activation(out=sg[:, t4, 0:ncol], in_=p_[:, 0:ncol], func=AF.Sigmoid), R=[bp], W=[bsg])
                    if t4 == 3:
                        g = tt // 4
                        if kind == "gate":
                            k.dma("sp", gate_scr[g * 512:(g + 1) * 512, :].rearrange("(t p) d -> p t d", p=128), sg[:, :, 0:48], R=[bsg], W=[B_scr2])
                        else:
                            k.dma("sp", mg_scr[g * 512:(g + 1) * 512, ci * 512:(ci + 1) * 512].rearrange("(t p) d -> p t d", p=128), sg[:, :, :], R=[bsg], W=[B_scr2])
                        sti += 1
            if kind in ("kv", "z"):
                for tt in range(NT):
                    p_, bp = PS[psi % 8], PSB[psi % 8]
                    psi += 1
                    for kc in range(8):
                        k.op("pe", lambda e, kc=kc, wb=wb, p_=p_, tt=tt: e.matmul(p_[:, :], lhsT=hT[:, kc, tt * 128:(tt + 1) * 128],
                                                                                rhs=wb[:, kc, :], start=(kc == 0), stop=(kc == 7)),
                             R=[B_hT[tt], bw], W=[bp])
                    sg, bsg = stage[sti % 2], B_stage[sti % 2]
                    t4 = tt % 4
                    if evi % 2 == 0 or kind == "z":
                        k.op("act", lambda e, p_=p_, sg=sg, t4=t4, kind=kind: e.activation(out=sg[:, t4, :], in_=p_[:, :],
                                                                                        func=(AF.Silu if kind == "z" else AF.Identity)),
                             R=[bp], W=[bsg])
                    else:
                        k.op("dve", lambda e, p_=p_, sg=sg, t4=t4: e.tensor_copy(out=sg[:, t4, :], in_=p_[:, :]),
                             R=[bp], W=[bsg])
                    evi += 1
                    if t4 == 3 and kind == "z":
                        g = tt // 4
                        k.dma("sp", szt_scr[g * 512:(g + 1) * 512, ci * 512:(ci + 1) * 512].rearrange("(t p) d -> p t d", p=128),
                              sg[:, :, :], R=[bsg], W=[B_scr2])
                        sti += 1
                    elif t4 == 3:
                        g = tt // 4
                        for half in range(2):
                            k.dma("sp", o_pkv[2 * ci + half, g * 512:(g + 1) * 512, :].rearrange("(t p) d -> p t d", p=128),
                                  sg[:, :, half * 256:(half + 1) * 256], R=[bsg], W=[DOUT])
                        sti += 1
            elif kind == "qkv":
                p_, bp = PS[psi % 8], PSB[psi % 8]
                psi += 1
                for kc in range(8):
                    k.op("pe", lambda e, kc=kc, wb=wb, p_=p_: e.matmul(p_[0:3, :], lhsT=hT[:, kc, S - 3:S], rhs=wb[:, kc, :],
                                                                      start=(kc == 0), stop=(kc == 7)),
                         R=[B_hT[NT - 1], bw], W=[bp])
                k.op("dve", lambda e, p_=p_, ci=ci: e.tensor_copy(out=pcv[:, ci * 512:(ci + 1) * 512], in_=p_[0:3, :]),
                     R=[bp], W=[B_pcv])
        k.dma("sp", o_pconv[:, :], pcv[:, :], R=[B_pcv], W=[DOUT])
        k.dma("sp", o_skv[:, :], proj_s[:, C_KV:C_KV + 1536], R=[B_projs], W=[DOUT])
        k.dma("sp", o_sconv[:, 2, :], proj_s[:, C_QKV:C_QKV + 3072], R=[B_projs], W=[DOUT])
        k.dma("sp", o_sconv[:, 0:2, :], sconv[:, 1:3, :], W=[DOUT])
        k.dma("sp", o_skwin[:, 511, :], proj_s[:, C_KV + 1024:C_KV + 1280], R=[B_projs], W=[DOUT])
        k.dma("sp", o_svwin[:, 511, :], proj_s[:, C_KV + 1280:C_KV + 1536], R=[B_projs], W=[DOUT])
        k.dma("sp", projs_scr[:, :], proj_s[:, :], R=[B_projs], W=[B_scr2])
        for q4 in range(4):
            k.dma("sp", o_skwin[q4 * 4:(q4 + 1) * 4, 0:511, :], ckwin[q4 * 4:(q4 + 1) * 4, 1:512, :], W=[DOUT])
            k.dma("pool", o_svwin[q4 * 4:(q4 + 1) * 4, 0:511, :], cvwin[q4 * 4:(q4 + 1) * 4, 1:512, :], W=[DOUT])
        k.barrier()
    with contextlib.ExitStack() as st:
        NEG = -1.0e9
        wqkv = sb(st, "wqkv", [128, 8, 3072], BF16)
        wab = sb(st, "wab", [128, 8, 16], BF16)
        B_wqkv, B_wab = Buf(), Buf()
        for c in range(6):
            k.dma("pool", wqkv[:, :, c * 512:(c + 1) * 512], wv[:, :, C_QKV + c * 512:C_QKV + (c + 1) * 512], W=[B_wqkv])
        k.dma("pool", wab[:, :, :], wv[:, :, 6704:6720], W=[B_wab])
        cwT = sb(st, "cwT", [128, 24, 4], F32)
        dtb = sb(st, "dtb", [128, 8], F32)
        nAe = sb(st, "nAe", [128, 8], F32)
        ngb = sb(st, "ngb", [128, 128], F32)
        onesb = sb(st, "onesb", [128, 128], BF16)
        onesf = sb(st, "onesf", [128, 128], F32)
        maskA = sb(st, "maskA", [128, 128], F32)
        maskM = sb(st, "maskM", [128, 128], F32)
        Uc = sb(st, "Uc", [128, 128], F32)
        Uf = sb(st, "Uf", [128, 128], F32)
        Ue0 = sb(st, "Ue0", [128, 128], F32)
        Ue1 = sb(st, "Ue1", [128, 128], F32)
        B_c = Buf()
        k.dma("sp", dtb[:, :], dt_bias[0, :].partition_broadcast(128), W=[B_c])
        k.dma("sp", nAe[:, :], a_log[0, :].partition_broadcast(128), W=[B_c])
        k.dma("sp", ngb[:, :], dn_ng[0, :].partition_broadcast(128), W=[B_c])
        k.op("act", lambda e: e.activation(out=nAe[:, :], in_=nAe[:, :], func=AF.Exp), R=[B_c], W=[B_c])
        k.op("dve", lambda e: e.tensor_scalar(out=nAe[:, :], in0=nAe[:, :], scalar1=-1.0, scalar2=None, op0=ALU.mult), R=[B_c], W=[B_c])
        k.op("pool", lambda e: e.memset(onesb[:, :], 1.0), W=[B_c])
        k.op("pool", lambda e: e.memset(onesf[:, :], 1.0), W=[B_c])
        for (m_, base_) in ((maskA, 0), (maskM, -1)):
            k.op("pool", lambda e, m_=m_: e.memset(m_[:, :], 0.0), R=[B_c], W=[B_c])
            k.op("pool", lambda e, m_=m_, base_=base_: e.affine_select(out=m_[:, :], in_=m_[:, :], pattern=[[1, 128]], compare_op=ALU.is_ge,
                                                                     fill=NEG, base=base_, channel_multiplier=-1), R=[B_c], W=[B_c])
            k.op("pool", lambda e, m_=m_: e.memset(m_[0:64, 64:128], NEG), R=[B_c], W=[B_c])
        k.op("pool", lambda e: e.memset(Uc[:, :], 1.0), R=[B_c], W=[B_c])
        k.op("pool", lambda e: e.affine_select(out=Uc[:, :], in_=Uc[:, :], pattern=[[1, 128]], compare_op=ALU.is_ge,
                                               fill=0.0, base=0, channel_multiplier=-1), R=[B_c], W=[B_c])
        k.op("pool", lambda e: e.memset(Uc[0:64, 64:128], 0.0), R=[B_c], W=[B_c])
        k.op("pool", lambda e: e.memset(Uf[:, :], 0.0), R=[B_c], W=[B_c])
        k.op("pool", lambda e: e.memset(Uf[0:64, 0:64], 1.0), R=[B_c], W=[B_c])
        k.op("pool", lambda e: e.memset(Uf[64:128, 64:128], 1.0), R=[B_c], W=[B_c])
        k.op("pool", lambda e: e.memset(Ue0[:, :], 0.0), R=[B_c], W=[B_c])
        k.op("pool", lambda e: e.memset(Ue0[0:64, :], 1.0), R=[B_c], W=[B_c])
        k.op("pool", lambda e: e.memset(Ue1[:, :], 0.0), R=[B_c], W=[B_c])
        k.op("pool", lambda e: e.memset(Ue1[64:128, :], 1.0), R=[B_c], W=[B_c])
        k.barrier()

        with contextlib.ExitStack() as stc:
            cw4 = sb(stc, "cw4", [4, 3072], F32)
            k.dma("sp", cw4[:, :], conv_w[:, :], W=[B_c])
            ptc = PS[0]
            for j in range(24):
                k.op("pe", lambda e, j=j: e.transpose(out=ptc[:, j * 4:(j + 1) * 4], in_=cw4[0:4, j * 128:(j + 1) * 128],
                                                      identity=identf[0:4, 0:4]), R=[B_c, B_identf], W=[PSB[0]])
            k.op("dve", lambda e: e.tensor_copy(out=cwT[:, :, :], in_=ptc[:, 0:96].rearrange("p (j w) -> p j w", w=4)), R=[PSB[0]], W=[B_c])
            k.barrier()
        pslot = [0]

        def ps_half():
            i = pslot[0] % 8
            pslot[0] += 1
            return PS[i][:, 0:256], PSB[i]

        def ps_full():
            i = pslot[0] % 8
            pslot[0] += 1
            return PS[i][:, :], [PSB[i]]

        halo = sb(st, "halo", [128, 24, 3], F32)
        B_halo = [Buf() for _ in range(24)]
        k.op("pool", lambda e: e.memset(halo[:, :, :], 0.0), W=B_halo)
        xc = [sb(st, "xc0", [128, 515], F32)] * 2
        B_xc = [Buf()] * 2
        acc = [sb(st, "acc0", [128, 512], F32)] * 2
        B_acc = [Buf()] * 2
        ysl = [sb(st, "ysl0", [128, 512], F32)] * 2
        B_ysl = [Buf()] * 2
        sqb = [sb(st, "sqb0", [128, 512], BF16)] * 2
        B_sqb = [Buf()] * 2
        lnt = [sb(st, "lnt0", [128, 512], F32)] * 2
        B_lnt = [Buf()] * 2
        qkvT = [sb(st, "qkvT%d" % i, [128, 24, 512], BF16) for i in range(1)]
        B_qkvT = [[Buf() for _ in range(24)] for _ in range(1)]
        Sst = sb(st, "Sst", [128, 8, 128], F32)
        Sbf = sb(st, "Sbf", [128, 8, 128], BF16)
        B_S = [Buf() for _ in range(8)]
        B_Sbf = [Buf() for _ in range(8)]
        k.op("pool", lambda e: e.memset(Sst[:, :, :], 0.0), W=B_S)
        k.op("pool", lambda e: e.memset(Sbf[:, :, :], 0.0), W=B_Sbf)
        sm = [sb(st, "sm%d" % i, [128, 96], F32) for i in range(2)]
        B_sm = [Buf(), Buf()]
        NW = 2
        dg = [sb(st, "dg%d" % i, [128, 2, 128], F32) for i in range(NW)]
        GG = dg
        DD = [sb(st, "DD%d" % i, [128, 2, 128], F32) for i in range(NW)]
        EBt = [sb(st, "EB%d" % i, [128, 128], F32) for i in range(NW)]
        MR = [[sb(st, "MR%d_%d" % (i, j), [128, 2, 128], F32) for j in range(2)] for i in range(NW)]
        LL = [[sb(st, "LL%d_%d" % (i, j), [128, 128], F32) for j in range(2)] for i in range(NW)]
        kbg = [sb(st, "kbg%d" % i, [128, 128], F32) for i in range(NW)]
        vb = [sb(st, "vb%d" % i, [128, 128], F32) for i in range(NW)]
        B_dg = [Buf() for _ in range(NW)]
        B_GG = B_dg
        B_DD = [Buf() for _ in range(NW)]
        B_EB = [Buf() for _ in range(NW)]
        B_MR = [[Buf(), Buf()] for _ in range(NW)]
        B_LL = [[Buf(), Buf()] for _ in range(NW)]
        B_kbg = [Buf() for _ in range(NW)]
        B_vb = [Buf() for _ in range(NW)]
        AT = [[sb(st, "AT_%d" % h, [128, 128], BF16) for h in range(8)]] * 2
        usb = [[sb(st, "usb_%d" % h, [128, 128], F32) for h in range(8)]] * 2
        wTs = [[sb(st, "wTs_%d" % h, [128, 128], F32) for h in range(8)]] * 2
        qdT = [[sb(st, "qdT_%d" % h, [128, 128], BF16) for h in range(8)]] * 2
        kdc = [[sb(st, "kdc_%d" % h, [128, 128], F32) for h in range(8)]] * 2
        B_AT = [[Buf() for _ in range(8)]] * 2
        B_usb = [[Buf() for _ in range(8)]] * 2
        B_wTs = [[Buf() for _ in range(8)]] * 2
        B_qdT = [[Buf() for _ in range(8)]] * 2
        B_kdc = [[Buf() for _ in range(8)]] * 2
        vnw = [sb(st, "vnw%d" % h, [128, 128], F32) for h in range(8)]
        vnb = [sb(st, "vnb%d" % h, [128, 128], BF16) for h in range(8)]
        B_vnw = [Buf() for _ in range(8)]
        B_vnb = [Buf() for _ in range(8)]
        otile = [sb(st, "otile%d" % i, [128, 8, 128], F32) for i in range(1)] * 2
        B_ot = [[Buf() for _ in range(8)]] * 2
        orn = [sb(st, "orn%d" % i, [128, 8], F32) for i in range(2)]
        B_orn = [Buf(), Buf()]
        szt = [sb(st, "szt0", [128, 1024], F32)] * 2
        B_sz = [Buf()] * 2
        odn = [sb(st, "odn0", [128, 1024], F32)] * 2
        B_odn = [Buf()] * 2
        B_scr = Buf()
        wi = [0]

        for gq in range(int(os.environ.get('DN_NG', '8'))):
            qb = qkvT[0]
            Bq = B_qkvT[0]
            hR = [B_hT[gq * 4 + i] for i in range(4)]
            for j in range(24):
                pf, bpf = ps_full()
                for kc in range(8):
                    k.op("pe", lambda e, kc=kc, j=j, pf=pf: e.matmul(pf, lhsT=wqkv[:, kc, j * 128:(j + 1) * 128],
                                                                    rhs=hT[:, kc, gq * 512:(gq + 1) * 512], start=(kc == 0), stop=(kc == 7)),
                         R=hR + [B_wqkv], W=bpf)
                x_, bx = xc[j % 2], B_xc[j % 2]
                a_, ba = acc[j % 2], B_acc[j % 2]
                y_, by = ysl[j % 2], B_ysl[j % 2]
                k.op("act", lambda e, x_=x_, pf=pf: e.activation(out=x_[:, 3:515], in_=pf, func=AF.Identity), R=bpf, W=[bx])
                k.op("pool", lambda e, x_=x_, j=j: e.tensor_copy(out=x_[:, 0:3], in_=halo[:, j, :]), R=[B_halo[j]], W=[bx])
                k.op("pool", lambda e, x_=x_, j=j: e.tensor_copy(out=halo[:, j, :], in_=x_[:, 512:515]), R=[bx], W=[B_halo[j]])
                k.op("dve", lambda e, x_=x_, a_=a_, j=j: e.tensor_scalar(out=a_[:, :], in0=x_[:, 0:512], scalar1=cwT[:, j, 0:1], scalar2=None,
                                                                       op0=ALU.mult), R=[bx, B_c], W=[ba])
                for w_ in range(1, 4):
                    k.op("dve", lambda e, x_=x_, a_=a_, j=j, w_=w_: e.scalar_tensor_tensor(out=a_[:, :], in0=x_[:, w_:w_ + 512],
                                                                                         scalar=cwT[:, j, w_:w_ + 1], in1=a_[:, :],
                                                                                         op0=ALU.mult, op1=ALU.add), R=[bx, B_c, ba], W=[ba])
                if j >= 16:
                    k.op("act", lambda e, a_=a_, j=j: e.activation(out=qb[:, j, :], in_=a_[:, :], func=AF.Silu), R=[ba], W=[Bq[j]])
                else:
                    s_, bs_ = sqb[j % 2], B_sqb[j % 2]
                    l_, bl_ = lnt[j % 2], B_lnt[j % 2]
                    k.op("act", lambda e, a_=a_, y_=y_: e.activation(out=y_[:, :], in_=a_[:, :], func=AF.Silu), R=[ba], W=[by])
                    k.op("act", lambda e, y_=y_, s_=s_: e.activation(out=s_[:, :], in_=y_[:, :], func=AF.Square), R=[by], W=[bs_])
                    pf2, bpf2 = ps_full()
                    k.op("pe", lambda e, s_=s_, pf2=pf2: e.matmul(pf2, lhsT=onesb[:, :], rhs=s_[:, :], start=True, stop=True),
                         R=[bs_, B_c], W=bpf2)
                    k.op("act", lambda e, l_=l_, pf2=pf2: e.activation(out=l_[:, :], in_=pf2, func=AF.Ln, bias=1e-6), R=bpf2, W=[bl_])
                    k.op("act", lambda e, l_=l_: e.activation(out=l_[:, :], in_=l_[:, :], func=AF.Exp, scale=-0.5), R=[bl_], W=[bl_])
                    sc_ = (128.0 ** -0.5) if j < 8 else 1.0
                    k.op("dve", lambda e, y_=y_, l_=l_, j=j, sc_=sc_: e.scalar_tensor_tensor(out=qb[:, j, :], in0=y_[:, :], scalar=sc_,
                                                                                           in1=l_[:, :], op0=ALU.mult, op1=ALU.mult),
                         R=[by, bl_], W=[Bq[j]])
            if STOP < 2:
                continue
            for pl in range(4):
                p = gq * 4 + pl
                par = p % 2
                tsl = slice(pl * 128, (pl + 1) * 128)
                tok = slice(p * 128, (p + 1) * 128)
                s_, bsm = sm[par], B_sm[par]
                ph, bph = ps_half()
                for kc in range(8):
                    k.op("pe", lambda e, kc=kc, ph=ph: e.matmul(ph[:, 0:16], lhsT=hT[:, kc, tok], rhs=wab[:, kc, :], start=(kc == 0), stop=(kc == 7)),
                         R=[B_hT[p], B_wab], W=[bph])
                k.op("dve", lambda e, ph=ph, s_=s_: e.tensor_tensor(out=s_[:, 0:8], in0=ph[:, 0:8], in1=dtb[:, :], op=ALU.add), R=[bph, B_c], W=[bsm])
                k.op("act", lambda e, s_=s_: e.activation(out=s_[:, 0:8], in_=s_[:, 0:8], func=AF.Exp), R=[bsm], W=[bsm])
                k.op("act", lambda e, s_=s_: e.activation(out=s_[:, 0:8], in_=s_[:, 0:8], func=AF.Ln, bias=1.0), R=[bsm], W=[bsm])
                k.op("dve", lambda e, s_=s_: e.tensor_tensor(out=s_[:, 0:8], in0=s_[:, 0:8], in1=nAe[:, :], op=ALU.mult), R=[bsm, B_c], W=[bsm])
                k.op("act", lambda e, ph=ph, s_=s_: e.activation(out=s_[:, 8:16], in_=ph[:, 8:16], func=AF.Sigmoid), R=[bph], W=[bsm])
                k.op("act", lambda e, s_=s_: e.activation(out=s_[:, 16:24], in_=s_[:, 8:16], func=AF.Ln), R=[bsm], W=[bsm])
                ph2, bph2 = ps_half()
                for ui, U_ in enumerate((Uc, Uf, Ue0, Ue1)):
                    k.op("pe", lambda e, ui=ui, U_=U_, ph2=ph2, s_=s_: e.matmul(ph2[:, ui * 8:(ui + 1) * 8], lhsT=U_[:, :], rhs=s_[:, 0:8],
                                                                             start=True, stop=True), R=[bsm, B_c], W=[bph2])
                k.op("dve", lambda e, ph2=ph2, s_=s_: e.tensor_copy(out=s_[:, 24:32], in_=ph2[:, 0:8]), R=[bph2], W=[bsm])
                k.op("dve", lambda e, s_=s_: e.tensor_scalar(out=s_[:, 32:40], in0=s_[:, 24:32], scalar1=-1.0, scalar2=None, op0=ALU.mult), R=[bsm], W=[bsm])
                k.op("dve", lambda e, s_=s_: e.tensor_tensor(out=s_[:, 40:48], in0=s_[:, 24:32], in1=s_[:, 16:24], op=ALU.add), R=[bsm], W=[bsm])
                k.op("act", lambda e, s_=s_: e.activation(out=s_[:, 48:56], in_=s_[:, 24:32], func=AF.Exp), R=[bsm], W=[bsm])
                k.op("dve", lambda e, s_=s_: e.tensor_tensor(out=s_[:, 48:56], in0=s_[:, 48:56], in1=s_[:, 8:16], op=ALU.mult), R=[bsm], W=[bsm])
                k.op("dve", lambda e, ph2=ph2, s_=s_: e.tensor_tensor(out=s_[:, 56:64], in0=ph2[:, 8:16], in1=s_[:, 24:32], op=ALU.subtract), R=[bph2, bsm], W=[bsm])
                k.op("act", lambda e, s_=s_: e.activation(out=s_[:, 56:64], in_=s_[:, 56:64], func=AF.Exp), R=[bsm], W=[bsm])
                k.op("act", lambda e, ph2=ph2, s_=s_: e.activation(out=s_[:, 64:80], in_=ph2[:, 16:32], func=AF.Exp), R=[bph2], W=[bsm])
                sz_, bsz = szt[par], B_sz[par]
                k.dma("sp", sz_[:, :], szt_scr[tok, :], R=[B_scr2], W=[bsz])
                for h in range(8 if STOP >= 3 else 0):
                    w = wi[0] % NW
                    wi[0] += 1
                    qT_ = qb[:, h, tsl]
                    kT_ = qb[:, 8 + h, tsl]
                    vT_ = qb[:, 16 + h, tsl]
                    Rq = [Bq[h]]
                    Rk = [Bq[8 + h]]
                    Rv = [Bq[16 + h]]
                    k.op("dve", lambda e, w=w, s_=s_, h=h: e.tensor_scalar(out=dg[w][:, 0, :], in0=identf[:, :], scalar1=s_[:, 24 + h:25 + h],
                                                                         scalar2=None, op0=ALU.mult), R=[bsm, B_identf], W=[B_dg[w]])
                    k.op("dve", lambda e, w=w, s_=s_, h=h: e.tensor_scalar(out=dg[w][:, 1, :], in0=identf[:, :], scalar1=s_[:, 40 + h:41 + h],
                                                                         scalar2=None, op0=ALU.mult), R=[bsm, B_identf], W=[B_dg[w]])
                    pbc, bpbc = ps_half()
                    k.op("pe", lambda e, w=w, pbc=pbc: e.matmul(pbc, lhsT=onesf[:, :], rhs=dg[w][:, :, :].rearrange("p a b -> p (a b)"),
                                                               start=True, stop=True), R=[B_dg[w], B_c], W=[bpbc])
                    k.op("dve", lambda e, w=w, pbc=pbc, s_=s_, h=h: e.scalar_tensor_tensor(out=GG[w][:, 0, :], in0=pbc[:, 0:128], scalar=s_[:, 32 + h:33 + h],
                                                                                         in1=maskA[:, :], op0=ALU.add, op1=ALU.add),
                         R=[bpbc, bsm, B_c], W=[B_GG[w]])
                    k.op("dve", lambda e, w=w, pbc=pbc, s_=s_, h=h: e.scalar_tensor_tensor(out=GG[w][:, 1, :], in0=pbc[:, 128:256], scalar=s_[:, 32 + h:33 + h],
                                                                                         in1=maskM[:, :], op0=ALU.add, op1=ALU.add),
                         R=[bpbc, bsm, B_c], W=[B_GG[w]])
                    k.op("act", lambda e, w=w: e.activation(out=DD[w][:, :, :], in_=GG[w][:, :, :], func=AF.Exp), R=[B_GG[w]], W=[B_DD[w]])
                    k.op("act", lambda e, w=w, pbc=pbc: e.activation(out=EBt[w][:, :], in_=pbc[:, 0:128], func=AF.Exp), R=[bpbc], W=[B_EB[w]])
                    k.op("dve", lambda e, w=w, qT_=qT_, par=par, h=h: e.tensor_tensor(out=qdT[par][h][:, :], in0=qT_, in1=EBt[w][:, :], op=ALU.mult),
                         R=Rq + [B_EB[w]], W=[B_qdT[par][h]])
                    pkq, bpkq = ps_half()
                    k.op("pe", lambda e, pkq=pkq, kT_=kT_, qT_=qT_: e.matmul(pkq[:, 0:128], lhsT=kT_, rhs=qT_, start=True, stop=True), R=Rk + Rq, W=[bpkq])
                    k.op("pe", lambda e, pkq=pkq, kT_=kT_: e.matmul(pkq[:, 128:256], lhsT=kT_, rhs=kT_, start=True, stop=True), R=Rk, W=[bpkq])
                    k.op("dve", lambda e, w=w, pkq=pkq, par=par, h=h: e.tensor_tensor(out=AT[par][h][:, :], in0=pkq[:, 0:128], in1=DD[w][:, 0, :], op=ALU.mult),
                         R=[bpkq, B_DD[w]], W=[B_AT[par][h]])
                    k.op("dve", lambda e, w=w, pkq=pkq: e.tensor_tensor(out=MR[w][0][:, 0, :], in0=pkq[:, 128:256], in1=DD[w][:, 1, :], op=ALU.mult),
                         R=[bpkq, B_DD[w]], W=[B_MR[w][0]])
                    k.op("dve", lambda e, w=w: e.tensor_tensor(out=MR[w][0][:, 1, :], in0=identf[:, :], in1=MR[w][0][:, 0, :], op=ALU.subtract),
                         R=[B_identf, B_MR[w][0]], W=[B_MR[w][0]])
                    ptr, bptr = ps_half()
                    ptrb = ptr
                    k.op("pe", lambda e, w=w, ptrb=ptrb: e.transpose(out=ptrb[:, 0:128], in_=MR[w][0][:, 0, :], identity=identf[:, :]),
                         R=[B_MR[w][0], B_identf], W=[bptr])
                    k.op("act", lambda e, w=w, ptrb=ptrb: e.activation(out=LL[w][0][:, :], in_=ptrb[:, 0:128], func=AF.Identity), R=[bptr], W=[B_LL[w][0]])
                    cur = 0
                    for lev in range(6):
                        nxt = 1 - cur
                        if lev == 0:
                            pm, bpm = ps_half()
                            k.op("pe", lambda e, w=w, cur=cur, pm=pm: e.matmul(pm[:, 0:128], lhsT=LL[w][cur][:, :], rhs=MR[w][cur][:, 0, :], start=True, stop=True),
                                 R=[B_LL[w][cur], B_MR[w][cur]], W=[bpm])
                            k.op("pe", lambda e, w=w, cur=cur, pm=pm: e.matmul(pm[:, 128:256], lhsT=MR[w][cur][:, 0, :], rhs=LL[w][cur][:, :], start=True, stop=True),
                                 R=[B_LL[w][cur], B_MR[w][cur]], W=[bpm])
                            k.op("act", lambda e, w=w, nxt=nxt, pm=pm: e.activation(out=MR[w][nxt][:, 0, :], in_=pm[:, 0:128], func=AF.Identity), R=[bpm], W=[B_MR[w][nxt]])
                            k.op("dve", lambda e, w=w, cur=cur, nxt=nxt: e.tensor_copy(out=MR[w][nxt][:, 1, :], in_=MR[w][cur][:, 1, :]), R=[B_MR[w][cur]], W=[B_MR[w][nxt]])
                            k.op("act", lambda e, w=w, nxt=nxt, pm=pm: e.activation(out=LL[w][nxt][:, :], in_=pm[:, 128:256], func=AF.Identity), R=[bpm], W=[B_LL[w][nxt]])
                        elif lev < 5:
                            pm, bpm = ps_half()
                            pl2, bpl2 = ps_half()
                            k.op("pe", lambda e, w=w, cur=cur, pm=pm: e.matmul(pm, lhsT=LL[w][cur][:, :], rhs=MR[w][cur][:, :, :].rearrange("p a b -> p (a b)"),
                                                                              start=True, stop=True), R=[B_LL[w][cur], B_MR[w][cur]], W=[bpm])
                            k.op("pe", lambda e, w=w, cur=cur, pl2=pl2: e.matmul(pl2[:, 0:128], lhsT=MR[w][cur][:, 0, :], rhs=LL[w][cur][:, :], start=True, stop=True),
                                 R=[B_LL[w][cur], B_MR[w][cur]], W=[bpl2])
                            k.op("act", lambda e, w=w, nxt=nxt, pm=pm: e.activation(out=MR[w][nxt][:, 0, :], in_=pm[:, 0:128], func=AF.Identity), R=[bpm], W=[B_MR[w][nxt]])
                            k.op("dve", lambda e, w=w, cur=cur, nxt=nxt, pm=pm: e.tensor_tensor(out=MR[w][nxt][:, 1, :], in0=pm[:, 128:256], in1=MR[w][cur][:, 1, :], op=ALU.add),
                                 R=[bpm, B_MR[w][cur]], W=[B_MR[w][nxt]])
                            k.op("act", lambda e, w=w, nxt=nxt, pl2=pl2: e.activation(out=LL[w][nxt][:, :], in_=pl2[:, 0:128], func=AF.Identity), R=[bpl2], W=[B_LL[w][nxt]])
                        else:
                            pm, bpm = ps_half()
                            k.op("pe", lambda e, w=w, cur=cur, pm=pm: e.matmul(pm[:, 0:128], lhsT=LL[w][cur][:, :], rhs=MR[w][cur][:, 1, :], start=True, stop=True),
                                 R=[B_LL[w][cur], B_MR[w][cur]], W=[bpm])
                            k.op("dve", lambda e, w=w, cur=cur, nxt=nxt, pm=pm: e.tensor_tensor(out=MR[w][nxt][:, 1, :], in0=pm[:, 0:128], in1=MR[w][cur][:, 1, :], op=ALU.add),
                                 R=[bpm, B_MR[w][cur]], W=[B_MR[w][nxt]])
                        cur = nxt
                    Rfin = MR[w][cur][:, 1, :]
                    B_Rfin = B_MR[w][cur]
                    pkt, bpkt = ps_half()
                    pktb = pkt.bitcast(BF16)
                    k.op("pe", lambda e, pktb=pktb, kT_=kT_: e.transpose(out=pktb[:, 0:128], in_=kT_, identity=ident[:, :]), R=Rk + [B_ident], W=[bpkt])
                    k.op("pe", lambda e, pktb=pktb, vT_=vT_: e.transpose(out=pktb[:, 128:256], in_=vT_, identity=ident[:, :]), R=Rv + [B_ident], W=[bpkt])
                    k.op("act", lambda e, w=w, pktb=pktb, s_=s_, h=h: e.activation(out=kbg[w][:, :], in_=pktb[:, 0:128], func=AF.Identity, scale=s_[:, 48 + h:49 + h]),
                         R=[bpkt, bsm], W=[B_kbg[w]])
                    k.op("act", lambda e, pktb=pktb, s_=s_, h=h, par=par: e.activation(out=kdc[par][h][:, :], in_=pktb[:, 0:128], func=AF.Identity, scale=s_[:, 56 + h:57 + h]),
                         R=[bpkt, bsm], W=[B_kdc[par][h]])
                    k.op("act", lambda e, w=w, pktb=pktb, s_=s_, h=h: e.activation(out=vb[w][:, :], in_=pktb[:, 128:256], func=AF.Identity, scale=s_[:, 8 + h:9 + h]),
                         R=[bpkt, bsm], W=[B_vb[w]])
                    puw, bpuw = ps_half()
                    k.op("pe", lambda e, w=w, puw=puw, Rfin=Rfin: e.matmul(puw[:, 0:128], lhsT=Rfin, rhs=vb[w][:, :], start=True, stop=True),
                         R=[B_Rfin, B_vb[w]], W=[bpuw])
                    k.op("pe", lambda e, w=w, puw=puw, Rfin=Rfin: e.matmul(puw[:, 128:256], lhsT=kbg[w][:, :], rhs=Rfin, start=True, stop=True),
                         R=[B_Rfin, B_kbg[w]], W=[bpuw])
                    k.op("dve", lambda e, puw=puw, par=par, h=h: e.tensor_copy(out=usb[par][h][:, :], in_=puw[:, 0:128]), R=[bpuw], W=[B_usb[par][h]])
                    k.op("dve", lambda e, puw=puw, par=par, h=h: e.tensor_copy(out=wTs[par][h][:, :], in_=puw[:, 128:256]), R=[bpuw], W=[B_wTs[par][h]])
                ot_, bot = otile[par], B_ot[par]
                if STOP < 4:
                    continue
                for e_ in range(2):
                    rs = slice(e_ * 64, (e_ + 1) * 64)
                    for h in range(8):
                        p1, bp1 = ps_half()
                        k.op("pe", lambda e, p1=p1, par=par, h=h: e.matmul(p1[:, 0:128], lhsT=wTs[par][h][:, :], rhs=Sst[:, h, :], start=True, stop=True),
                             R=[B_wTs[par][h], B_S[h]], W=[bp1])
                        k.op("dve", lambda e, p1=p1, par=par, h=h, rs=rs: e.tensor_tensor(out=vnw[h][rs, :], in0=usb[par][h][rs, :], in1=p1[rs, 0:128], op=ALU.subtract),
                             R=[bp1, B_usb[par][h]], W=[B_vnw[h]])
                        k.op("act", lambda e, h=h, rs=rs: e.activation(out=vnb[h][rs, :], in_=vnw[h][rs, :], func=AF.Identity), R=[B_vnw[h]], W=[B_vnb[h]])
                        k.op("pe", lambda e, p1=p1, par=par, h=h: e.matmul(p1[:, 128:256], lhsT=qdT[par][h][:, :], rhs=Sbf[:, h, :], start=True, stop=False),
                             R=[B_qdT[par][h], B_Sbf[h]], W=[bp1])
                        k.op("pe", lambda e, p1=p1, par=par, h=h, rs=rs: e.matmul(p1[:, 128:256], lhsT=AT[par][h][rs, :], rhs=vnb[h][rs, :], start=False, stop=True),
                             R=[B_AT[par][h], B_vnb[h]], W=[bp1])
                        k.op("act", lambda e, p1=p1, ot_=ot_, h=h, rs=rs: e.activation(out=ot_[rs, h, :], in_=p1[rs, 128:256], func=AF.Identity), R=[bp1], W=[bot[h]])
                        p2, bp2 = ps_half()
                        k.op("pe", lambda e, p2=p2, par=par, h=h, rs=rs: e.matmul(p2[:, 0:128], lhsT=kdc[par][h][rs, :], rhs=vnw[h][rs, :], start=True, stop=True),
                             R=[B_kdc[par][h], B_vnw[h]], W=[bp2])
                        k.op("dve", lambda e, p2=p2, h=h, s_=s_, e_=e_: e.scalar_tensor_tensor(out=Sst[:, h, :], in0=Sst[:, h, :], scalar=s_[:, 64 + e_ * 8 + h:65 + e_ * 8 + h],
                                                                                            in1=p2[:, 0:128], op0=ALU.mult, op1=ALU.add),
                             R=[bp2, bsm, B_S[h]], W=[B_S[h]])
                        k.op("act", lambda e, h=h: e.activation(out=Sbf[:, h, :], in_=Sst[:, h, :], func=AF.Identity), R=[B_S[h]], W=[B_Sbf[h]])
                on_, bon = odn[par], B_odn[par]
                osq = on_[:, :].rearrange("p (h d) -> p h d", d=128)
                k.op("act", lambda e, ot_=ot_, osq=osq: e.activation(out=osq, in_=ot_[:, :, :], func=AF.Square), R=bot, W=[bon])
                k.op("dve", lambda e, par=par, osq=osq: e.tensor_reduce(out=orn[par][:, :], in_=osq, axis=mybir.AxisListType.X, op=ALU.add), R=[bon], W=[B_orn[par]])
                k.op("act", lambda e, par=par: e.activation(out=orn[par][:, :], in_=orn[par][:, :], func=AF.Sqrt, scale=1.0 / 128, bias=1e-6), R=[B_orn[par]], W=[B_orn[par]])
                k.op("dve", lambda e, par=par: e.reciprocal(out=orn[par][:, :], in_=orn[par][:, :]), R=[B_orn[par]], W=[B_orn[par]])
                for h in range(8):
                    k.op("dve", lambda e, ot_=ot_, on_=on_, h=h, par=par: e.scalar_tensor_tensor(out=on_[:, h * 128:(h + 1) * 128], in0=ot_[:, h, :], scalar=orn[par][:, h:h + 1],
                                                                                               in1=ngb[:, :], op0=ALU.mult, op1=ALU.mult),
                         R=[bot[h], B_orn[par], B_c], W=[bon])
                k.op("dve", lambda e, on_=on_, sz_=sz_: e.tensor_tensor(out=on_[:, :], in0=on_[:, :], in1=sz_[:, :], op=ALU.mult), R=[bon, bsz], W=[bon])
                k.dma("sp", odn_scr[tok, :], on_[:, :], R=[bon], W=[B_scr])
        for h in range(8):
            k.dma("sp", o_pdn[h, :, :], Sst[:, h, :], R=[B_S[h]], W=[DOUT])
        k.barrier()
    with contextlib.ExitStack() as st:
        onesf2 = sb(st, "onesf2", [128, 128], F32)
        B_o2 = Buf()
        k.op("pool", lambda e: e.memset(onesf2[:, :], 1.0), W=[B_o2])
        sq = sb(st, "s_qkv", [NS, 3072], F32)
        scv = sb(st, "s_scv", [NS, 3, 3072], F32)
        cwb = sb(st, "s_cwb", [NS, 4, 3072], F32)
        yv = sb(st, "s_y", [NS, 3072], F32)
        sab = sb(st, "s_ab", [NS, 16], F32)
        sdt = sb(st, "s_dt", [NS, 8], F32)
        sAe = sb(st, "s_Ae", [NS, 8], F32)
        val = sb(st, "s_val", [NS, 32], F32)
        B_sq, B_scv, B_cwb, B_yv, B_sab, B_val = Buf(), Buf(), Buf(), Buf(), Buf(), Buf()
        k.dma("sp", sq[:, :], projs_scr[:, C_QKV:C_QKV + 3072], R=[B_scr2], W=[B_sq])
        k.dma("sp", scv[:, :, :], sconv[:, :, :], W=[B_scv])
        for w_ in range(4):
            k.dma("sp", cwb[:, w_, :], conv_w[w_, :].partition_broadcast(NS), W=[B_cwb])
        k.dma("sp", sab[:, :], projs_scr[:, 6704:6720], R=[B_scr2], W=[B_sab])
        k.dma("sp", sdt[:, :], dt_bias[0, :].partition_broadcast(NS), W=[B_sab])
        k.dma("sp", sAe[:, :], a_log[0, :].partition_broadcast(NS), W=[B_sab])
        k.op("dve", lambda e: e.tensor_tensor(out=yv[:, :], in0=sq[:, :], in1=cwb[:, 3, :], op=ALU.mult), R=[B_sq, B_cwb], W=[B_yv])
        for w_ in range(3):
            k.op("dve", lambda e, w_=w_: e.tensor_tensor(out=scv[:, w_, :], in0=scv[:, w_, :], in1=cwb[:, w_, :], op=ALU.mult), R=[B_scv, B_cwb], W=[B_scv])
            k.op("dve", lambda e, w_=w_: e.tensor_tensor(out=yv[:, :], in0=yv[:, :], in1=scv[:, w_, :], op=ALU.add), R=[B_scv, B_yv], W=[B_yv])
        k.op("act", lambda e: e.activation(out=yv[:, :], in_=yv[:, :], func=AF.Silu), R=[B_yv], W=[B_yv])
        ssq = sb(st, "s_ssq", [NS, 16], F32)
        B_ssq = Buf()
        sqr = scv[:, 0, 0:2048]
        k.op("act", lambda e: e.activation(out=sqr, in_=yv[:, 0:2048], func=AF.Square), R=[B_yv, B_scv], W=[B_scv])
        k.op("dve", lambda e: e.tensor_reduce(out=ssq[:, :], in_=sqr.rearrange("p (h d) -> p h d", d=128), axis=mybir.AxisListType.X, op=ALU.add),
             R=[B_scv], W=[B_ssq])
        k.op("act", lambda e: e.activation(out=ssq[:, :], in_=ssq[:, :], func=AF.Sqrt, bias=1e-6), R=[B_ssq], W=[B_ssq])
        k.op("dve", lambda e: e.reciprocal(out=ssq[:, :], in_=ssq[:, :]), R=[B_ssq], W=[B_ssq])
        k.op("dve", lambda e: e.tensor_scalar(out=ssq[:, 0:8], in0=ssq[:, 0:8], scalar1=128.0 ** -0.5, scalar2=None, op0=ALU.mult), R=[B_ssq], W=[B_ssq])
        for hh in range(16):
            k.op("dve", lambda e, hh=hh: e.tensor_scalar(out=yv[:, hh * 128:(hh + 1) * 128], in0=yv[:, hh * 128:(hh + 1) * 128],
                                                       scalar1=ssq[:, hh:hh + 1], scalar2=None, op0=ALU.mult), R=[B_ssq, B_yv], W=[B_yv])
        k.op("dve", lambda e: e.tensor_tensor(out=scv[:, 1, 0:1024], in0=yv[:, 0:1024], in1=yv[:, 1024:2048], op=ALU.mult), R=[B_yv, B_scv], W=[B_scv])
        k.op("dve", lambda e: e.tensor_reduce(out=val[:, 24:32], in_=scv[:, 1, 0:1024].rearrange("p (h d) -> p h d", d=128), axis=mybir.AxisListType.X, op=ALU.add),
             R=[B_scv], W=[B_val])
        k.op("dve", lambda e: e.tensor_tensor(out=sab[:, 0:8], in0=sab[:, 0:8], in1=sdt[:, :], op=ALU.add), R=[B_sab], W=[B_sab])
        k.op("act", lambda e: e.activation(out=sab[:, 0:8], in_=sab[:, 0:8], func=AF.Exp), R=[B_sab], W=[B_sab])
        k.op("act", lambda e: e.activation(out=sab[:, 0:8], in_=sab[:, 0:8], func=AF.Ln, bias=1.0), R=[B_sab], W=[B_sab])
        k.op("act", lambda e: e.activation(out=sAe[:, :], in_=sAe[:, :], func=AF.Exp), R=[B_sab], W=[B_sab])
        k.op("dve", lambda e: e.tensor_tensor(out=sab[:, 0:8], in0=sab[:, 0:8], in1=sAe[:, :], op=ALU.mult), R=[B_sab], W=[B_sab])
        k.op("act", lambda e: e.activation(out=val[:, 0:8], in_=sab[:, 0:8], func=AF.Exp, scale=-1.0), R=[B_sab], W=[B_val])
        k.op("act", lambda e: e.activation(out=val[:, 8:16], in_=sab[:, 8:16], func=AF.Sigmoid), R=[B_sab], W=[B_val])
        k.op("dve", lambda e: e.tensor_tensor(out=val[:, 16:24], in0=val[:, 0:8], in1=val[:, 8:16], op=ALU.mult), R=[B_val], W=[B_val])
        vex = sb(st, "s_vex", [NS, NS, 32], F32)
        bcs = sb(st, "s_bc", [128, NS, 32], F32)
        B_vex, B_bcs = Buf(), Buf()
        for s_i in range(NS):
            k.op("dve", lambda e, s_i=s_i: e.tensor_scalar(out=vex[:, s_i, :], in0=val[:, :], scalar1=identf[0:NS, s_i:s_i + 1], scalar2=None, op0=ALU.mult),
                 R=[B_val, B_identf], W=[B_vex])
        pb_ = PS[0]
        k.op("pe", lambda e: e.matmul(pb_[:, 0:NS * 32], lhsT=onesf2[0:NS, :], rhs=vex[:, :, :].rearrange("p a b -> p (a b)"), start=True, stop=True),
             R=[B_vex, B_o2], W=[PSB[0]])
        k.op("dve", lambda e: e.tensor_copy(out=bcs[:, :, :], in_=pb_[:, 0:NS * 32].rearrange("p (a b) -> p a b", b=32)), R=[PSB[0]], W=[B_bcs])
        qkvTs = sb(st, "s_qkvT", [128, 24, NS], F32)
        B_qTs = Buf()
        pt_ = PS[1]
        for j in range(24):
            k.op("pe", lambda e, j=j: e.transpose(out=pt_[:, j * NS:(j + 1) * NS], in_=yv[:, j * 128:(j + 1) * 128], identity=identf[0:NS, 0:NS]),
                 R=[B_yv, B_identf], W=[PSB[1]])
        k.op("dve", lambda e: e.tensor_copy(out=qkvTs[:, :, :], in_=pt_[:, 0:24 * NS].rearrange("p (j s) -> p j s", s=NS)), R=[PSB[1]], W=[B_qTs])
        kq = sb(st, "s_kq", [128, NS, 8, 2], F32)
        B_kq = Buf()
        for h in range(8):
            k.op("dve", lambda e, h=h: e.tensor_copy(out=kq[:, :, h, 0], in_=qkvTs[:, 8 + h, :]), R=[B_qTs], W=[B_kq])
            k.op("dve", lambda e, h=h: e.tensor_copy(out=kq[:, :, h, 1], in_=qkvTs[:, h, :]), R=[B_qTs], W=[B_kq])
        Sin = [sb(st, "s_Sin%d" % i, [128, 8, 128], F32) for i in range(2)]
        Sout = [sb(st, "s_Sout%d" % i, [128, 8, 128], F32) for i in range(2)]
        B_Sin = [Buf(), Buf()]
        B_Sout = [Buf(), Buf()]
        ksq = sb(st, "s_ksq", [128, NS, 8, 2], F32)
        oTs = sb(st, "s_oTs", [128, 8, NS], F32)
        B_oTs = Buf()
        B_ksq = Buf()
        vnc = [sb(st, "s_vnc%d" % i, [128, 8], F32) for i in range(2)]
        B_vnc = [Buf(), Buf()]
        dgv = [sb(st, "s_dgv%d" % i, [128, 128], F32) for i in range(2)]
        B_dgv = [Buf(), Buf()]
        sgt = [sb(st, "s_sgt%d" % i, [128, 128], F32) for i in range(2)]
        B_sgt = [Buf(), Buf()]
        pi_ = [2]

        def nextps():
            i = 2 + (pi_[0] % 6)
            pi_[0] += 1
            return PS[i], PSB[i]
        ui = 0
        for s_i in range(NS):
            si_, bsi = Sin[s_i % 2], B_Sin[s_i % 2]
            so_, bso = Sout[s_i % 2], B_Sout[s_i % 2]
            vn_, bvn = vnc[s_i % 2], B_vnc[s_i % 2]
            k.dma("sp", si_[:, :, :], state_dn[s_i].rearrange("h k v -> k h v"), W=[bsi])
            pk, bpk = nextps()
            for h in range(8):
                k.op("pe", lambda e, h=h, si_=si_, pk=pk, s_i=s_i: e.matmul(pk[:, h * 2:h * 2 + 2], lhsT=si_[:, h, :], rhs=kq[:, s_i, h, :], start=True, stop=True),
                     R=[bsi, B_kq], W=[bpk])
            k.op("dve", lambda e, pk=pk, s_i=s_i: e.tensor_copy(out=ksq[:, s_i, :, :], in_=pk[:, 0:16].rearrange("p (h t) -> p h t", t=2)), R=[bpk], W=[B_ksq])
            k.op("dve", lambda e, vn_=vn_, s_i=s_i: e.tensor_tensor(out=vn_[:, :], in0=ksq[:, s_i, :, 0], in1=bcs[:, s_i, 16:24], op=ALU.mult), R=[B_ksq, B_bcs], W=[bvn])
            k.op("dve", lambda e, s_i=s_i: e.tensor_tensor(out=ksq[:, s_i, :, 0], in0=qkvTs[:, 16:24, s_i], in1=bcs[:, s_i, 8:16], op=ALU.mult), R=[B_qTs, B_bcs, B_ksq], W=[B_ksq])
            k.op("dve", lambda e, vn_=vn_, s_i=s_i: e.tensor_tensor(out=vn_[:, :], in0=ksq[:, s_i, :, 0], in1=vn_[:, :], op=ALU.subtract), R=[B_ksq, bvn], W=[bvn])
            k.op("dve", lambda e, s_i=s_i: e.tensor_tensor(out=oTs[:, :, s_i], in0=ksq[:, s_i, :, 1], in1=bcs[:, s_i, 0:8], op=ALU.mult), R=[B_ksq, B_bcs], W=[B_oTs])
            k.op("dve", lambda e, s_i=s_i, vn_=vn_: e.tensor_tensor(out=ksq[:, s_i, :, 1], in0=vn_[:, :], in1=bcs[:, s_i, 24:32], op=ALU.mult), R=[bvn, B_bcs, B_ksq], W=[B_ksq])
            k.op("dve", lambda e, s_i=s_i: e.tensor_tensor(out=oTs[:, :, s_i], in0=oTs[:, :, s_i], in1=ksq[:, s_i, :, 1], op=ALU.add), R=[B_ksq, B_oTs], W=[B_oTs])
            for h in range(8):
                dg_, bdg = dgv[ui % 2], B_dgv[ui % 2]
                sg_, bsg = sgt[ui % 2], B_sgt[ui % 2]
                ui += 1
                k.op("dve", lambda e, dg_=dg_, vn_=vn_, h=h: e.tensor_scalar(out=dg_[:, :], in0=identf[:, :], scalar1=vn_[:, h:h + 1], scalar2=None, op0=ALU.mult),
                     R=[bvn, B_identf], W=[bdg])
                pv, bpv = nextps()
                k.op("pe", lambda e, pv=pv, dg_=dg_: e.matmul(pv[:, 0:128], lhsT=onesf2[:, :], rhs=dg_[:, :], start=True, stop=True), R=[bdg, B_o2], W=[bpv])
                k.op("act", lambda e, sg_=sg_, si_=si_, h=h, s_i=s_i: e.activation(out=sg_[:, :], in_=si_[:, h, :], func=AF.Identity, scale=bcs[:, s_i, h:h + 1],
                                                                               bias=zcol[:, 0:1]), R=[bsi, B_bcs], W=[bsg])
                k.op("dve", lambda e, so_=so_, pv=pv, sg_=sg_, h=h, s_i=s_i: e.scalar_tensor_tensor(out=so_[:, h, :], in0=pv[:, 0:128], scalar=qkvTs[:, 8 + h, s_i:s_i + 1],
                                                                                                in1=sg_[:, :], op0=ALU.mult, op1=ALU.add),
                     R=[bpv, B_qTs, bsg], W=[bso])
            k.dma("sp", o_sdn[s_i].rearrange("h k v -> k h v"), so_[:, :, :], R=[bso], W=[DOUT])
        ot_s = yv
        pt2 = PS[1]
        for h in range(8):
            k.op("pe", lambda e, h=h: e.transpose(out=pt2[0:NS, h * 128:(h + 1) * 128] if h < 4 else PS[0][0:NS, (h - 4) * 128:(h - 3) * 128], in_=oTs[:, h, :], identity=identf[:, :]),
                 R=[B_oTs, B_identf], W=[PSB[1] if h < 4 else PSB[0]])
        k.op("dve", lambda e: e.tensor_copy(out=ot_s[:, 0:512], in_=pt2[0:NS, :]), R=[PSB[1], B_yv], W=[B_yv])
        k.op("dve", lambda e: e.tensor_copy(out=ot_s[:, 512:1024], in_=PS[0][0:NS, :]), R=[PSB[0], B_yv], W=[B_yv])
        k.dma("sp", ot_s[:, 1024:2048], projs_scr[:, 5680:6704], R=[B_scr2, B_yv], W=[B_yv])
        ngs = sb(st, "s_ngs", [NS, 128], F32)
        B_ngs = Buf()
        k.dma("sp", ngs[:, :], dn_ng[0, :].partition_broadcast(NS), W=[B_ngs])
        k.op("act", lambda e: e.activation(out=ot_s[:, 2048:3072], in_=ot_s[:, 0:1024], func=AF.Square), R=[B_yv], W=[B_yv])
        k.op("dve", lambda e: e.tensor_reduce(out=ssq[:, 0:8], in_=ot_s[:, 2048:3072].rearrange("p (h d) -> p h d", d=128), axis=mybir.AxisListType.X, op=ALU.add), R=[B_yv, B_ssq], W=[B_ssq])
        k.op("act", lambda e: e.activation(out=ssq[:, 0:8], in_=ssq[:, 0:8], func=AF.Sqrt, scale=1.0 / 128, bias=1e-6), R=[B_ssq], W=[B_ssq])
        k.op("dve", lambda e: e.reciprocal(out=ssq[:, 0:8], in_=ssq[:, 0:8]), R=[B_ssq], W=[B_ssq])
        k.op("act", lambda e: e.activation(out=ot_s[:, 1024:2048], in_=ot_s[:, 1024:2048], func=AF.Silu), R=[B_yv], W=[B_yv])
        for h in range(8):
            k.op("dve", lambda e, h=h: e.scalar_tensor_tensor(out=ot_s[:, h * 128:(h + 1) * 128], in0=ot_s[:, h * 128:(h + 1) * 128], scalar=ssq[:, h:h + 1], in1=ngs[:, :],
                                                              op0=ALU.mult, op1=ALU.mult), R=[B_yv, B_ssq, B_ngs], W=[B_yv])
        k.op("dve", lambda e: e.tensor_tensor(out=ot_s[:, 0:1024], in0=ot_s[:, 0:1024], in1=ot_s[:, 1024:2048], op=ALU.mult), R=[B_yv], W=[B_yv])
        k.dma("sp", odns_scr[:, :], ot_s[:, 0:1024], R=[B_yv], W=[B_scr3])
        k.barrier()
    st1.close()
    with contextlib.ExitStack() as st:
        BIG = 1.0e30
        KE = sb(st, "KE", [128, 4, S], BF16)
        KW = sb(st, "KW", [128, 4, S], BF16)
        VAs = sb(st, "VAs", [128, 4, 32, 128], BF16)
        VAw = sb(st, "VAw", [128, 4, 32, 128], BF16)
        kcT = sb(st, "kcT", [128, 4, 256], BF16)
        VC = sb(st, "VC", [128, 4, 2, 128], BF16)
        OV = sb(st, "OV", [128, 2, 128], BF16)
        B_KE, B_KW, B_VAs, B_VAw, B_kcT, B_VC, B_OV = Buf(), Buf(), Buf(), Buf(), Buf(), Buf(), Buf()
        for g in range(4):
            rs_ = slice((g % 2) * 64, (g % 2) * 64 + 64)
            k.dma("sp", KE[0:64, g, :], kvT_scr[4 + g // 2][rs_, :], R=[B_scr2], W=[B_KE])
            k.dma("sp", KW[0:64, g, :], kvT_scr[6 + g // 2][rs_, :], R=[B_scr2], W=[B_KW])
            k.dma("pool", VAs[:, g, :, 0:64], o_pkv[3].rearrange("(c p) (g d) -> p g c d", p=128, d=64)[:, g], R=[DOUT], W=[B_VAs])
            k.dma("pool", VAw[:, g, :, 0:64], o_pkv[5].rearrange("(c p) (g d) -> p g c d", p=128, d=64)[:, g], R=[DOUT], W=[B_VAw])
            k.op("pool", lambda e, g=g: e.memset(VAs[:, g, :, 64:128], 1.0), W=[B_VAs])
            k.op("pool", lambda e, g=g: e.memset(VAw[:, g, :, 64:128], 1.0), W=[B_VAw])
        k.op("pool", lambda e: e.memset(VC[:, :, :, 64:128], 1.0), W=[B_VC])
        k.op("pool", lambda e: e.memset(kcT[:, :, :], 0.0), W=[B_kcT])
        with contextlib.ExitStack() as stc:
            Et = sb(stc, "Et", [64, S], BF16)
            B_Et = Buf()
            k.op("pool", lambda e: e.memset(Et[:, :], 30000.0), W=[B_Et])
            k.op("pool", lambda e: e.affine_select(out=Et[:, :], in_=Et[:, :], pattern=[[1, S]], compare_op=ALU.is_ge, fill=0.0, base=0, channel_multiplier=-64), R=[B_Et], W=[B_Et])
            k.op("pool", lambda e: e.affine_select(out=Et[:, :], in_=Et[:, :], pattern=[[-1, S]], compare_op=ALU.is_ge, fill=0.0, base=63, channel_multiplier=64), R=[B_Et], W=[B_Et])
            for g in range(4):
                k.op("dve" if g % 2 else "pool", lambda e, g=g: e.tensor_copy(out=KE[64:128, g, :], in_=Et[0:64, :]), R=[B_Et], W=[B_KE])
            ovf = sb(stc, "ovf", [128, 2, 2, 64], F32)
            B_ovf = Buf()
            k.op("pool", lambda e: e.memset(ovf[:, :, :, :], 0.5), W=[B_ovf])
            for cc in range(2):
                for wi_, (lo, hi) in enumerate(((-1, 3), (0, 2))):
                    k.op("pool", lambda e, cc=cc, wi_=wi_, lo=lo: e.affine_select(out=ovf[:, cc, wi_, :], in_=ovf[:, cc, wi_, :], pattern=[[-4, 64]], compare_op=ALU.is_ge,
                                                                                fill=0.0, base=128 * cc - lo, channel_multiplier=1), R=[B_ovf], W=[B_ovf])
                    k.op("pool", lambda e, cc=cc, wi_=wi_, hi=hi: e.affine_select(out=ovf[:, cc, wi_, :], in_=ovf[:, cc, wi_, :], pattern=[[4, 64]], compare_op=ALU.is_ge,
                                                                                fill=0.0, base=hi - 128 * cc, channel_multiplier=-1), R=[B_ovf], W=[B_ovf])
            k.op("pool", lambda e: e.memset(OV[:, :, 0:64], 0.0), W=[B_OV])
            k.op("dve", lambda e: e.tensor_tensor(out=OV[:, :, 64:128], in0=ovf[:, :, 0, :], in1=ovf[:, :, 1, :], op=ALU.add), R=[B_ovf], W=[B_OV])
            XT = sb(stc, "XT", [128, 2, S], BF16)
            W1 = sb(stc, "W1", [128, 32, 128], BF16)
            w2 = sb(stc, "w2", [128, 64], BF16)
            pe32 = sb(stc, "pe32", [32, 64], F32)
            peT = sb(stc, "peT", [128, 32], BF16)
            pec = sb(stc, "pec", [128, 1], F32)
            Sds = sb(stc, "Sds", [128, 256], F32)
            xg = sb(stc, "xg", [128, 256], F32)
            x2 = sb(stc, "x2", [128, 256], F32)
            hid = sb(stc, "hid", [128, 256], BF16)
            B_XT, B_W1, B_w2, B_pe32, B_peT, B_pec, B_Sds, B_xg, B_x2, B_hid = (Buf() for _ in range(10))
            k.op("pool", lambda e: e.memset(hid[:, :], 0.0), W=[B_hid])
            for kind_ in range(2):
                w1d, w2d, ped = (w1k, w2k, pek) if kind_ == 0 else (w1v, w2v, pev)
                for c2 in range(2):
                    k.dma("sp", XT[:, c2, :], kvT_scr[2 * kind_ + c2], R=[B_scr2], W=[B_XT])
                for half in range(2):
                    k.dma("pool", W1[half * 64:(half + 1) * 64, :, :], w1d.rearrange("p d e -> d p e"), W=[B_W1])
                k.dma("pool", w2[:, :], w2d[:, :], W=[B_w2])
                k.dma("sp", pe32[:, :], ped[:, :], W=[B_pe32])
                k.op("pe", lambda e: e.transpose(out=PS[0][0:64, 0:32], in_=pe32[:, :], identity=identf[0:32, 0:32]), R=[B_pe32, B_identf], W=[PSB[0]])
                k.op("dve", lambda e: e.tensor_copy(out=peT[0:64, :], in_=PS[0][0:64, 0:32]), R=[PSB[0]], W=[B_peT])
                for p_ in range(32):
                    k.op("pe", lambda e, p_=p_: e.matmul(PS[1][:, 0:1], lhsT=W1[0:64, p_, :], rhs=peT[0:64, p_:p_ + 1], start=(p_ == 0), stop=(p_ == 31)),
                         R=[B_W1, B_peT], W=[PSB[1]])
                k.op("dve", lambda e: e.tensor_copy(out=pec[:, :], in_=PS[1][:, 0:1]), R=[PSB[1]], W=[B_pec])
                for g in range(4):
                    rs_ = slice((g % 2) * 64, (g % 2) * 64 + 64)
                    c2 = g // 2
                    pF, bF = PS[2 + (g % 2) * 2], PSB[2 + (g % 2) * 2]
                    pS, bS = PS[3 + (g % 2) * 2], PSB[3 + (g % 2) * 2]
                    for p_ in range(16):
                        k.op("pe", lambda e, p_=p_, rs_=rs_, c2=c2, pF=pF: e.matmul(pF[:, 0:256], lhsT=W1[rs_, p_, :], rhs=XT[rs_, c2, p_:S:16], start=(p_ == 0), stop=(p_ == 15)),
                             R=[B_W1, B_XT], W=[bF])
                    for p_ in range(16):
                        k.op("pe", lambda e, p_=p_, rs_=rs_, c2=c2, pS=pS: e.matmul(pS[:, 0:256], lhsT=W1[rs_, 16 + p_, :], rhs=XT[rs_, c2, p_:S:16], start=(p_ == 0), stop=(p_ == 15)),
                             R=[B_W1, B_XT], W=[bS])
                    k.op("act", lambda e, pS=pS: e.activation(out=Sds[:, :], in_=pS[:, 0:256], func=AF.Identity), R=[bS], W=[B_Sds])
                    k.op("dve", lambda e, pF=pF: e.scalar_tensor_tensor(out=xg[:, 0:255], in0=pF[:, 0:255], scalar=pec[:, 0:1], in1=Sds[:, 1:256], op0=ALU.add, op1=ALU.add),
                         R=[bF, B_pec, B_Sds], W=[B_xg])
                    k.op("act", lambda e: e.activation(out=x2[:, 0:255], in_=xg[:, 0:255], func=AF.Square), R=[B_xg], W=[B_x2])
                    k.op("dve", lambda e: e.tensor_scalar(out=x2[:, 0:255], in0=x2[:, 0:255], scalar1=0.044715, scalar2=1.0, op0=ALU.mult, op1=ALU.add), R=[B_x2], W=[B_x2])
                    k.op("dve", lambda e: e.tensor_tensor(out=x2[:, 0:255], in0=x2[:, 0:255], in1=xg[:, 0:255], op=ALU.mult), R=[B_x2, B_xg], W=[B_x2])
                    k.op("act", lambda e: e.activation(out=x2[:, 0:255], in_=x2[:, 0:255], func=AF.Sigmoid, scale=1.5957691216), R=[B_x2], W=[B_x2])
                    k.op("dve", lambda e: e.tensor_tensor(out=hid[:, 0:255], in0=x2[:, 0:255], in1=xg[:, 0:255], op=ALU.mult), R=[B_x2, B_xg, B_hid], W=[B_hid])
                    if kind_ == 0:
                        k.op("pe", lambda e: e.matmul(PS[6][0:64, 0:256], lhsT=w2[:, :], rhs=hid[:, :], start=True, stop=True), R=[B_w2, B_hid], W=[PSB[6]])
                        k.op("dve", lambda e, g=g: e.tensor_copy(out=kcT[0:64, g, :], in_=PS[6][0:64, 0:256]), R=[PSB[6]], W=[B_kcT])
                    else:
                        for cc in range(2):
                            k.op("pe", lambda e, cc=cc: e.matmul(PS[6 + cc][:, 0:64], lhsT=hid[:, cc * 128:(cc + 1) * 128], rhs=w2[:, :], start=True, stop=True),
                                 R=[B_w2, B_hid], W=[PSB[6 + cc]])
                            k.op("dve", lambda e, cc=cc, g=g: e.tensor_copy(out=VC[:, g, cc, 0:64], in_=PS[6 + cc][:, 0:64]), R=[PSB[6 + cc]], W=[B_VC])
            k.barrier()
        RH = sb(st, "RH", [128, 16, 512], BF16)
        B_RHq = [Buf() for _ in range(16)]
        B_RHs = [Buf() for _ in range(4)]
        onsa = sb(st, "onsa", [128, 4, D], F32)
        B_onsa = [Buf() for _ in range(16)]
        PT = [sb(st, "PT%d" % i, [128, 512], BF16) for i in range(4)]
        B_PT = [Buf() for _ in range(4)]
        Oev = [sb(st, "Oev%d" % i, [128, 512], F32) for i in range(2)]
        B_Oev = [Buf(), Buf()]
        Gt = sb(st, "Gt", [128, 4, 48], F32)
        B_Gt = Buf()
        impacc = sb(st, "impacc", [128, 512], F32)
        B_imp = Buf()
        imtmp = sb(st, "imtmp", [128, 512], F32)
        B_imtmp = Buf()
        impm = sb(st, "impm", [128, 64], F32)
        imp2 = sb(st, "imp2", [128, 64], F32)
        m8 = sb(st, "m8", [128, 16], F32)
        bsel = sb(st, "bsel", [128, 128], BF16)
        B_impm, B_imp2, B_m8, B_bsel = Buf(), Buf(), Buf(), Buf()
        k.op("pool", lambda e: e.memset(bsel[:, :], 0.0), W=[B_bsel])
        rr = [sb(st, "rr%d" % i, [128, 4], F32) for i in range(2)]
        B_rr = [Buf(), Buf()]
        mgt = sb(st, "mgt", [128, 2048], F32)
        odt = sb(st, "odt", [128, D], F32)
        mxt = sb(st, "mxt", [128, D], F32)
        B_mgt, B_odt, B_mxt = Buf(), Buf(), Buf()
        B_mix = Buf()
        cnt = {"s": 0, "o": 0, "pt": 0, "ev": 0}

        def ps_s():
            i = cnt["s"] % 3
            cnt["s"] += 1
            return PS[i], PSB[i]

        def ps_o():
            i = 3 + cnt["o"] % 2
            cnt["o"] += 1
            return PS[i], PSB[i]

        def finish(h, br, pO, bO, want_imp=None):
            ev, bev = Oev[cnt["ev"] % 2], B_Oev[cnt["ev"] % 2]
            r_, br_ = rr[cnt["ev"] % 2], B_rr[cnt["ev"] % 2]
            cnt["ev"] += 1
            k.op("act", lambda e: e.activation(out=ev[:, :], in_=pO[:, :], func=AF.Identity), R=[bO], W=[bev])
            k.op("dve", lambda e: e.tensor_scalar(out=ev[64:128, :], in0=ev[64:128, :], scalar1=1e-30, scalar2=None, op0=ALU.max), R=[bev], W=[bev])
            if want_imp is not None:
                pI, bI, first = want_imp
                k.op("dve", lambda e: e.reciprocal(out=imtmp[64:128, :], in_=ev[64:128, :]), R=[bev], W=[B_imtmp])
                if first:
                    k.op("dve", lambda e: e.tensor_tensor(out=impacc[64:128, :], in0=pI[64:128, :], in1=imtmp[64:128, :], op=ALU.mult), R=[bI, B_imtmp], W=[B_imp])
                else:
                    k.op("dve", lambda e: e.tensor_tensor(out=imtmp[64:128, :], in0=pI[64:128, :], in1=imtmp[64:128, :], op=ALU.mult), R=[bI, B_imtmp], W=[B_imtmp])
                    k.op("dve", lambda e: e.tensor_tensor(out=impacc[64:128, :], in0=impacc[64:128, :], in1=imtmp[64:128, :], op=ALU.add), R=[B_imtmp, B_imp], W=[B_imp])
            pT, bT = PS[6], PSB[6]
            for qt in range(4):
                k.op("pe", lambda e, qt=qt: e.transpose(out=pT[:, qt * 128:(qt + 1) * 128], in_=ev[:, qt * 128:(qt + 1) * 128], identity=identf[:, :]),
                     R=[bev, B_identf], W=[bT])
            pT3 = pT[:, :].rearrange("p (t c) -> p t c", c=128)
            k.op("dve", lambda e: e.reciprocal(out=r_[:, :], in_=pT3[:, :, 64]), R=[bT], W=[br_])
            k.op("dve", lambda e: e.tensor_tensor(out=r_[:, :], in0=r_[:, :], in1=Gt[:, :, h * 3 + br], op=ALU.mult), R=[br_, B_Gt], W=[br_])
            for qt in range(4):
                if br == 0:
                    k.op("act", lambda e, qt=qt: e.activation(out=onsa[:, qt, h * 64:(h + 1) * 64], in_=pT[:, qt * 128:qt * 128 + 64], func=AF.Identity,
                                                             scale=r_[:, qt:qt + 1], bias=zcol[:, 0:1]), R=[bT, br_], W=[B_onsa[h]])
                else:
                    k.op("dve", lambda e, qt=qt: e.scalar_tensor_tensor(out=onsa[:, qt, h * 64:(h + 1) * 64], in0=pT[:, qt * 128:qt * 128 + 64], scalar=r_[:, qt:qt + 1],
                                                                       in1=onsa[:, qt, h * 64:(h + 1) * 64], op0=ALU.mult, op1=ALU.add),
                         R=[bT, br_, B_onsa[h]], W=[B_onsa[h]])

        def unit(lhs_fn, rh_ap, Rl, Rr, va_fn, Rv, chunks, maskfn, pO, bO, extra=None):
            n = len(chunks)
            for i, c in enumerate(chunks):
                pS_, bS_ = ps_s()
                k.op("pe", lambda e, c=c: e.matmul(pS_[:, :], lhsT=lhs_fn(c), rhs=rh_ap, start=True, stop=True), R=Rl + Rr, W=[bS_])
                pt, bpt = PT[cnt["pt"] % 4], B_PT[cnt["pt"] % 4]
                cnt["pt"] += 1
                k.op("act", lambda e: e.activation(out=pt[:, :], in_=pS_[:, :], func=AF.Exp), R=[bS_], W=[bpt])
                for (pat, base, cm) in maskfn(c):
                    k.op("pool", lambda e, pat=pat, base=base, cm=cm: e.affine_select(out=pt[:, :], in_=pt[:, :], pattern=[[pat, 512]], compare_op=ALU.is_ge,
                                                                                    fill=0.0, base=base, channel_multiplier=cm), R=[bpt], W=[bpt])
                k.op("pe", lambda e, c=c: e.matmul(pO[:, :], lhsT=va_fn(c), rhs=pt[:, :], start=(i == 0), stop=(i == n - 1)), R=Rv + [bpt], W=[bO])
                if extra is not None:
                    pI, bI = extra
                    k.op("pe", lambda e, c=c: e.matmul(pI[:, :], lhsT=OV[:, c, :], rhs=pt[:, :], start=(i == 0), stop=(i == n - 1)), R=[B_OV, bpt], W=[bI])

        for qb in range(int(os.environ.get("NSA_NQB", "8"))):
            q0 = qb * 512
            for c8 in range(8):
                for hf in range(2):
                    k.dma("sp", RH[0:64, 2 * c8 + hf, :], q_scr[c8][hf * 64:(hf + 1) * 64, q0:q0 + 512], R=[B_scr2], W=[B_RHq[2 * c8 + hf]])
            k.dma("sp", Gt[:, :, :], gate_scr[q0:q0 + 512, :].rearrange("(t p) d -> p t d", p=128), R=[B_scr2], W=[B_Gt])
            for g in range(4):
                ccs = [0] if qb < 4 else [0, 1]
                for hh in range(4):
                    h = 4 * g + hh
                    pO, bO = ps_o()
                    pI, bI = PS[5], PSB[5]

                    def mk(c, qb=qb):
                        base = -(2048 * c + 31 - 512 * qb)
                        if base - 16 * 127 >= 0:
                            return []
                        return [(1, base, -16)]
                    unit(lambda c, g=g: kcT[0:64, g, c * 128:(c + 1) * 128], RH[0:64, h, :], [B_kcT], [B_RHq[h]],
                         lambda c, g=g: VC[:, g, c, :], [B_VC], ccs, mk, pO, bO, extra=(pI, bI))
                    finish(h, 0, pO, bO, want_imp=(pI, bI, hh == 0))
                for qt in range(4):
                    t = qb * 4 + qt
                    pT, bT = PS[7], PSB[7]
                    k.op("pe", lambda e, qt=qt: e.transpose(out=pT[:, 0:64], in_=impacc[64:128, qt * 128:(qt + 1) * 128], identity=identf[64:128, 64:128]),
                         R=[B_imp, B_identf], W=[bT])
                    k.op("dve", lambda e: e.tensor_copy(out=impm[:, :], in_=pT[:, 0:64]), R=[bT], W=[B_impm])
                    k.op("pool", lambda e: e.memset(impm[:, 0:1], BIG), R=[B_impm], W=[B_impm])
                    if 2 * t + 2 < 64:
                        k.op("pool", lambda e, t=t: e.memset(impm[:, 2 * t + 2:64], -BIG), R=[B_impm], W=[B_impm])
                    k.op("pool", lambda e, t=t: e.memset(impm[0:64, 2 * t:2 * t + 1], BIG), R=[B_impm], W=[B_impm])
                    k.op("pool", lambda e, t=t: e.memset(impm[64:128, 2 * t + 1:2 * t + 2], BIG), R=[B_impm], W=[B_impm])
                    k.op("pool", lambda e, t=t: e.memset(impm[0:64, 2 * t + 1:2 * t + 2], -BIG), R=[B_impm], W=[B_impm])
                    k.op("dve", lambda e: e.max(out=m8[:, 0:8], in_=impm[:, :]), R=[B_impm], W=[B_m8])
                    k.op("dve", lambda e: e.match_replace(out=imp2[:, :], in_to_replace=m8[:, 0:8], in_values=impm[:, :], imm_value=-3.0e38), R=[B_impm, B_m8], W=[B_imp2])
                    k.op("dve", lambda e: e.max(out=m8[:, 8:16], in_=imp2[:, :]), R=[B_imp2, B_m8], W=[B_m8])
                    k.op("dve", lambda e: e.tensor_scalar(out=bsel[:, 64:128], in0=impm[:, :], scalar1=m8[:, 15:16], scalar2=1.0, op0=ALU.is_ge, op1=ALU.subtract),
                         R=[B_impm, B_m8], W=[B_bsel])
                    pTb = pT[:, :].bitcast(BF16)
                    k.op("pe", lambda e: e.transpose(out=pTb[:, 512:640], in_=bsel[:, :], identity=ident[:, :]), R=[B_bsel, B_ident], W=[bT])
                    for hh in range(4):
                        k.op("dve" if hh % 2 else "act", (lambda e, hh=hh, qt=qt, g=g: e.tensor_copy(out=RH[64:128, 4 * g + hh, qt * 128:(qt + 1) * 128], in_=pTb[64:128, 512:640])) if hh % 2 else
                             (lambda e, hh=hh, qt=qt, g=g: e.activation(out=RH[64:128, 4 * g + hh, qt * 128:(qt + 1) * 128], in_=pTb[64:128, 512:640], func=AF.Identity)),
                             R=[bT], W=[B_RHs[g]])
            for h in range(16):
                g = h // 4
                pO, bO = ps_o()

                def mk_s(c, qb=qb):
                    if c < 4 * qb:
                        return []
                    return [(1, 512 * qb - 128 * c, -1)]
                unit(lambda c, g=g: KE[:, g, c * 128:(c + 1) * 128], RH[:, h, :], [B_KE], [B_RHq[h], B_RHs[g]],
                     lambda c, g=g: VAs[:, g, c, :], [B_VAs], list(range(0, 4 * qb + 4)), mk_s, pO, bO)
                finish(h, 1, pO, bO)
                pO, bO = ps_o()

                def mk_w(c, qb=qb):
                    if c >= 4 * qb:
                        return [(1, 512 * qb - 128 * c, -1)]
                    return [(-1, 128 * c - 512 * qb + 511, 1)]
                unit(lambda c, g=g: KW[0:64, g, c * 128:(c + 1) * 128], RH[0:64, h, :], [B_KW], [B_RHq[h]],
                     lambda c, g=g: VAw[:, g, c, :], [B_VAw], list(range(max(0, 4 * qb - 4), 4 * qb + 4)), mk_w, pO, bO)
                finish(h, 2, pO, bO)
            for qt in range(4):
                r0 = q0 + qt * 128
                k.dma("sp", mgt[:, :], mg_scr[r0:r0 + 128, :], R=[B_scr2], W=[B_mgt])
                k.dma("sp", odt[:, :], odn_scr[r0:r0 + 128, :], R=[B_scr], W=[B_odt])
                k.op("dve", lambda e, qt=qt: e.tensor_tensor(out=mxt[:, :], in0=onsa[:, qt, :], in1=mgt[:, 0:D], op=ALU.mult), R=B_onsa + [B_mgt], W=[B_mxt])
                k.op("dve", lambda e: e.tensor_tensor(out=odt[:, :], in0=odt[:, :], in1=mgt[:, D:2 * D], op=ALU.mult), R=[B_odt, B_mgt], W=[B_odt])
                k.op("dve", lambda e: e.tensor_tensor(out=mxt[:, :], in0=mxt[:, :], in1=odt[:, :], op=ALU.add), R=[B_odt, B_mxt], W=[B_mxt])
                k.dma("sp", mix_scr[r0:r0 + 128, :], mxt[:, :], R=[B_mxt], W=[B_mix])
        if os.environ.get('DBG_MIX'):
            for i8 in range(8):
                k.dma("sp", o_yp[i8 * 512:(i8 + 1) * 512, :], mix_scr[i8 * 512:(i8 + 1) * 512, :], R=[B_mix], W=[DOUT])
        k.barrier()
    with contextlib.ExitStack() as st:
        BIG = 1.0e30
        PAST = 2048
        LP = 2112
        onesE = sb(st, "onesE", [128, 128], F32)
        B_onesE = Buf()
        k.op("pool", lambda e: e.memset(onesE[:, :], 1.0), W=[B_onesE])
        W1s = [sb(st, "W1s%d" % i, [128, 32, 128], BF16) for i in range(2)]
        w2s = [sb(st, "w2s%d" % i, [128, 64], BF16) for i in range(2)]
        pecs = sb(st, "pecs", [128, 2], F32)
        pe32s = sb(st, "pe32s", [32, 64], F32)
        peTs = sb(st, "peTs", [128, 32], BF16)
        B_W1s, B_w2s, B_pecs, B_pe32s, B_peTs = Buf(), Buf(), Buf(), Buf(), Buf()
        for kind_ in range(2):
            w1d, w2d, ped = (w1k, w2k, pek) if kind_ == 0 else (w1v, w2v, pev)
            for half in range(2):
                k.dma("pool", W1s[kind_][half * 64:(half + 1) * 64, :, :], w1d.rearrange("p d e -> d p e"), W=[B_W1s])
            k.dma("pool", w2s[kind_][:, :], w2d[:, :], W=[B_w2s])
            k.dma("sp", pe32s[:, :], ped[:, :], R=[B_pe32s], W=[B_pe32s])
            k.op("pe", lambda e: e.transpose(out=PS[0][0:64, 0:32], in_=pe32s[:, :], identity=identf[0:32, 0:32]), R=[B_pe32s, B_identf], W=[PSB[0]])
            k.op("dve", lambda e: e.tensor_copy(out=peTs[0:64, :], in_=PS[0][0:64, 0:32]), R=[PSB[0], B_peTs], W=[B_peTs])
            for p_ in range(32):
                k.op("pe", lambda e, p_=p_, kind_=kind_: e.matmul(PS[1][:, 0:1], lhsT=W1s[kind_][0:64, p_, :], rhs=peTs[0:64, p_:p_ + 1], start=(p_ == 0), stop=(p_ == 31)),
                     R=[B_W1s, B_peTs], W=[PSB[1]])
            k.op("dve", lambda e, kind_=kind_: e.tensor_copy(out=pecs[:, kind_:kind_ + 1], in_=PS[1][:, 0:1]), R=[PSB[1]], W=[B_pecs])
        OVs = sb(st, "OVs", [128, 128], BF16)
        B_OVs = Buf()
        with contextlib.ExitStack() as stc:
            ovf = sb(stc, "ovfs", [128, 2, 64], F32)
            B_ovf = Buf()
            k.op("pool", lambda e: e.memset(ovf[:, :, :], 0.5), W=[B_ovf])
            for wi_, (lo, hi) in enumerate(((-1, 3), (0, 2))):
                k.op("pool", lambda e, wi_=wi_, lo=lo: e.affine_select(out=ovf[:, wi_, :], in_=ovf[:, wi_, :], pattern=[[-4, 64]], compare_op=ALU.is_ge, fill=0.0, base=-lo, channel_multiplier=1), R=[B_ovf], W=[B_ovf])
                k.op("pool", lambda e, wi_=wi_, hi=hi: e.affine_select(out=ovf[:, wi_, :], in_=ovf[:, wi_, :], pattern=[[4, 64]], compare_op=ALU.is_ge, fill=0.0, base=hi, channel_multiplier=-1), R=[B_ovf], W=[B_ovf])
            k.op("pool", lambda e: e.memset(OVs[:, 0:64], 0.0), W=[B_OVs])
            k.op("dve", lambda e: e.tensor_tensor(out=OVs[:, 64:128], in0=ovf[:, 0, :], in1=ovf[:, 1, :], op=ALU.add), R=[B_ovf], W=[B_OVs])
            k.barrier()
        KEs = sb(st, "KEs", [128, 4, PAST], BF16)
        B_KEs = Buf()
        B_KEe = Buf()
        with contextlib.ExitStack() as stc:
            Et = sb(stc, "Ets", [64, PAST], BF16)
            B_Et = Buf()
            k.op("pool", lambda e: e.memset(Et[:, :], 30000.0), W=[B_Et])
            k.op("pool", lambda e: e.affine_select(out=Et[:, :], in_=Et[:, :], pattern=[[1, PAST]], compare_op=ALU.is_ge, fill=0.0, base=0, channel_multiplier=-64), R=[B_Et], W=[B_Et])
            k.op("pool", lambda e: e.affine_select(out=Et[:, :], in_=Et[:, :], pattern=[[-1, PAST]], compare_op=ALU.is_ge, fill=0.0, base=63, channel_multiplier=64), R=[B_Et], W=[B_Et])
            for g in range(4):
                k.op("dve", lambda e, g=g: e.tensor_copy(out=KEs[64:128, g, :], in_=Et[0:64, :]), R=[B_Et], W=[B_KEe])
            k.barrier()
        ps_tok = sb(st, "ps_tok", [NS, 2608], F32)
        B_pstok = Buf()
        k.dma("sp", ps_tok[:, :], projs_scr[:, 0:2608], R=[B_scr2], W=[B_pstok])
        RHS = sb(st, "RHS", [128, NS, 16], BF16)
        B_RHSq, B_RHSs = Buf(), Buf()
        KN = sb(st, "KN", [64, 3, 4, NS], BF16)
        B_KN = Buf()
        for c8 in range(8):
            pq, bq = PS[2 + c8 % 4], PSB[2 + c8 % 4]
            k.op("pe", lambda e, c8=c8, pq=pq: e.transpose(out=pq[:, 0:NS], in_=ps_tok[:, c8 * 128:(c8 + 1) * 128], identity=identf[0:NS, 0:NS]), R=[B_pstok, B_identf], W=[bq])
            k.op("dve", lambda e, c8=c8, pq=pq: e.tensor_scalar(out=RHS[0:64, :, 2 * c8], in0=pq[0:64, 0:NS], scalar1=0.125, scalar2=None, op0=ALU.mult), R=[bq], W=[B_RHSq])
            k.op("dve", lambda e, c8=c8, pq=pq: e.tensor_scalar(out=RHS[0:64, :, 2 * c8 + 1], in0=pq[64:128, 0:NS], scalar1=0.125, scalar2=None, op0=ALU.mult), R=[bq], W=[B_RHSq])
        for ki, c0 in enumerate((1024, 1536, 2048)):
            for c2 in range(2):
                pq, bq = PS[2 + (ki * 2 + c2) % 4], PSB[2 + (ki * 2 + c2) % 4]
                k.op("pe", lambda e, c0=c0, c2=c2, pq=pq: e.transpose(out=pq[:, 0:NS], in_=ps_tok[:, c0 + c2 * 128:c0 + (c2 + 1) * 128], identity=identf[0:NS, 0:NS]), R=[B_pstok, B_identf], W=[bq])
                k.op("dve", lambda e, ki=ki, c2=c2, pq=pq: e.tensor_copy(out=KN[0:64, ki, 2 * c2, :], in_=pq[0:64, 0:NS]), R=[bq], W=[B_KN])
                k.op("dve", lambda e, ki=ki, c2=c2, pq=pq: e.tensor_copy(out=KN[0:64, ki, 2 * c2 + 1, :], in_=pq[64:128, 0:NS]), R=[bq], W=[B_KN])
        VNc = sb(st, "VNc", [128, 2, NS], BF16)
        KNc = sb(st, "KNc", [128, 2, NS], BF16)
        B_VNc = Buf()
        for ki, (c0, dst) in enumerate(((1024, KNc), (1280, VNc))):
            for c2 in range(2):
                pq, bq = PS[6 + c2], PSB[6 + c2]
                k.op("pe", lambda e, c0=c0, c2=c2, pq=pq: e.transpose(out=pq[:, 0:NS], in_=ps_tok[:, c0 + c2 * 128:c0 + (c2 + 1) * 128], identity=identf[0:NS, 0:NS]), R=[B_pstok, B_identf], W=[bq])
                k.op("dve", lambda e, dst=dst, c2=c2, pq=pq: e.tensor_copy(out=dst[:, c2, :], in_=pq[:, 0:NS]), R=[bq], W=[B_VNc])
        gsg = sb(st, "gsg", [NS, 48], F32)
        gex = sb(st, "gex", [NS, NS, 48], F32)
        Gb = sb(st, "Gb", [128, NS, 48], F32)
        B_gsg, B_gex, B_Gb = Buf(), Buf(), Buf()
        k.op("act", lambda e: e.activation(out=gsg[:, :], in_=ps_tok[:, 2560:2608], func=AF.Sigmoid), R=[B_pstok], W=[B_gsg])
        for s_i in range(NS):
            k.op("dve", lambda e, s_i=s_i: e.tensor_scalar(out=gex[:, s_i, :], in0=gsg[:, :], scalar1=identf[0:NS, s_i:s_i + 1], scalar2=None, op0=ALU.mult), R=[B_gsg, B_identf], W=[B_gex])
        for hf in range(2):
            k.op("pe", lambda e, hf=hf: e.matmul(PS[hf][:, 0:384], lhsT=onesE[0:NS, :], rhs=gex[:, hf * 8:(hf + 1) * 8, :].rearrange("p a b -> p (a b)"), start=True, stop=True),
                 R=[B_gex, B_onesE], W=[PSB[hf]])
            k.op("dve", lambda e, hf=hf: e.tensor_copy(out=Gb[:, hf * 8:(hf + 1) * 8, :], in_=PS[hf][:, 0:384].rearrange("p (a b) -> p a b", b=48)), R=[PSB[hf]], W=[B_Gb])
        VN = sb(st, "VN", [1, NS, 2, 4, 128], BF16)
        B_VN = Buf()
        k.op("pool", lambda e: e.memset(VN[:, :, :, :, 64:128], 1.0), W=[B_VN])
        for s_i in range(NS):
            for ki, c0 in enumerate((1792, 2304)):
                k.dma("pool", VN[0:1, s_i, ki, :, 0:64], projs_scr[s_i:s_i + 1, c0:c0 + 256].rearrange("o (g d) -> o g d", d=64), R=[B_scr2], W=[B_VN])
        ptab = sb(st, "ptab", [128, NS * 16], I32)
        ioi = sb(st, "ioi", [128, 1], I32)
        iof = sb(st, "iof", [128, 1], F32)
        idxa = sb(st, "idxa", [128, NS * 16], I32)
        B_ptab, B_io = Buf(), Buf()
        k.dma("sp", ptab[:, :], ptbl[0, :].partition_broadcast(128), W=[B_ptab])
        k.op("pool", lambda e: e.iota(ioi[:, :], pattern=[[0, 1]], base=0, channel_multiplier=1), W=[B_io])
        k.op("dve", lambda e: e.tensor_copy(out=iof[:, :], in_=ioi[:, :]), R=[B_io], W=[B_io])
        k.op("dve", lambda e: e.tensor_scalar(out=idxa[:, :], in0=ptab[:, :], scalar1=128.0, scalar2=iof[:, 0:1], op0=ALU.mult, op1=ALU.add), R=[B_ptab, B_io], W=[B_ptab])
        pools2 = [p_.rearrange("n p d -> (n p) d") for p_ in (pk_cmp, pv_cmp, pk_slc, pv_slc)]
        pg = [[sb(st, "pg%d_%d" % (i, j), [128, 256], F32) for j in range(4)] for i in range(2)]
        B_pg = [[Buf() for _ in range(4)] for _ in range(2)]
        KWs = sb(st, "KWs", [128, 4, 512], BF16)
        XT2 = [sb(st, "XT2_%d" % i, [128, 2, LP], BF16) for i in range(2)]
        VAss = sb(st, "VAss", [128, 16, 4, 128], BF16)
        VAws = sb(st, "VAws", [128, 4, 4, 128], BF16)
        B_KWs, B_VAss, B_VAws = Buf(), Buf(), Buf()
        B_XT2 = [Buf(), Buf()]
        k.op("pool", lambda e: e.memset(VAss[:, :, :, 64:128], 1.0), W=[B_VAss])
        k.op("pool", lambda e: e.memset(VAws[:, :, :, 64:128], 1.0), W=[B_VAws])
        for i in range(2):
            k.op("pool", lambda e, i=i: e.memset(XT2[i][:, :, PAST:LP], 0.0), W=[B_XT2[i]])
        kcTs = sb(st, "kcTs", [128, 4, 128], BF16)
        VCs = sb(st, "VCs", [128, 4, 128], BF16)
        B_kcTs, B_VCs = Buf(), Buf()
        k.op("pool", lambda e: e.memset(VCs[:, :, 64:128], 1.0), W=[B_VCs])
        Sdz = sb(st, "Sdz", [128, 132], F32)
        xgz = sb(st, "xgz", [128, 132], F32)
        x2z = sb(st, "x2z", [128, 132], F32)
        hidz = sb(st, "hidz", [128, 128], BF16)
        B_Sdz, B_xgz, B_x2z, B_hidz = Buf(), Buf(), Buf(), Buf()
        PTs = [sb(st, "PTs%d" % i, [128, 4], BF16) for i in range(4)]
        B_PTs = [Buf() for _ in range(4)]
        pti = [0]
        onT = sb(st, "onT", [64, NS, 16], F32)
        B_onT = Buf()
        rcp = sb(st, "rcp", [64, 4], F32)
        tq = sb(st, "tq", [64, 4], F32)
        B_rcp, B_tq = Buf(), Buf()
        Oes = sb(st, "Oes", [128, 4], F32)
        B_Oes = Buf()
        impc = sb(st, "impc", [128, 4], F32)
        imt = sb(st, "imt", [128, 4], F32)
        B_impc, B_imt = Buf(), Buf()
        impr = sb(st, "impr", [4, 64], F32)
        imp2r = sb(st, "imp2r", [4, 64], F32)
        m8r = sb(st, "m8r", [4, 16], F32)
        bselr = sb(st, "bselr", [4, 128], BF16)
        B_impr, B_imp2r, B_m8r, B_bselr = Buf(), Buf(), Buf(), Buf()
        k.op("pool", lambda e: e.memset(bselr[:, :], 0.0), W=[B_bselr])
        pools = (pk_cmp, pv_cmp, pk_slc, pv_slc)
        esp = [0]

        def eps():
            i = esp[0] % 5
            esp[0] += 1
            return PS[i], PSB[i]

        def fin_branch(s_i, g, br, pO, bO, first):
            k.op("act", lambda e: e.activation(out=Oes[:, :], in_=pO[:, 0:4], func=AF.Identity), R=[bO], W=[B_Oes])
            k.op("dve", lambda e: e.tensor_scalar(out=Oes[64:128, :], in0=Oes[64:128, :], scalar1=1e-30, scalar2=None, op0=ALU.max), R=[B_Oes], W=[B_Oes])
            k.op("dve", lambda e: e.reciprocal(out=rcp[0:64, :], in_=Oes[64:128, :]), R=[B_Oes], W=[B_rcp])
            k.op("dve", lambda e: e.tensor_tensor(out=tq[0:64, :], in0=Oes[0:64, :], in1=rcp[0:64, :], op=ALU.mult), R=[B_Oes, B_rcp], W=[B_tq])
            gcol = Gb[0:64, s_i, :].rearrange("p (h b) -> p h b", b=3)[:, 4 * g:4 * g + 4, br]
            if first:
                k.op("dve", lambda e: e.tensor_tensor(out=onT[0:64, s_i, 4 * g:4 * g + 4], in0=tq[0:64, :], in1=gcol, op=ALU.mult), R=[B_tq, B_Gb], W=[B_onT])
            else:
                k.op("dve", lambda e: e.tensor_tensor(out=tq[0:64, :], in0=tq[0:64, :], in1=gcol, op=ALU.mult), R=[B_tq, B_Gb], W=[B_tq])
                k.op("dve", lambda e: e.tensor_tensor(out=onT[0:64, s_i, 4 * g:4 * g + 4], in0=onT[0:64, s_i, 4 * g:4 * g + 4], in1=tq[0:64, :], op=ALU.add), R=[B_tq, B_onT], W=[B_onT])

        def score_pv(lhsT, rhs, Rl, va, Rv, pO, bO, start, stop, np_=128, mask=None, imp=None):
            pS_, bS_ = eps()
            k.op("pe", lambda e: e.matmul(pS_[0:np_, 0:4], lhsT=lhsT, rhs=rhs, start=True, stop=True), R=Rl, W=[bS_])
            pt, bpt = PTs[pti[0] % 4], B_PTs[pti[0] % 4]
            pti[0] += 1
            k.op("act", lambda e: e.activation(out=pt[0:np_, :], in_=pS_[0:np_, 0:4], func=AF.Exp), R=[bS_], W=[bpt])
            if mask is not None:
                k.op("pool", lambda e: e.affine_select(out=pt[:, :], in_=pt[:, :], pattern=[[0, 4]], compare_op=ALU.is_ge, fill=0.0, base=mask[0], channel_multiplier=mask[1]), R=[bpt], W=[bpt])
            k.op("pe", lambda e: e.matmul(pO[:, 0:4], lhsT=va, rhs=pt[0:np_, :], start=start, stop=stop), R=Rv + [bpt], W=[bO])
            if imp is not None:
                k.op("pe", lambda e: e.matmul(imp[0][:, 0:4], lhsT=OVs[:, :], rhs=pt[:, :], start=True, stop=True), R=[B_OVs, bpt], W=[imp[1]])

        for s_i in range(int(os.environ.get("NSA_NS", str(NS)))):
            xk, bxk = XT2[0], B_XT2[0]
            xv, bxv = XT2[1], B_XT2[1]
            for j in range(16):
                pgs, bpgs = pg[j % 2], B_pg[j % 2]
                for pi_ in range(4):
                    k.op("pool", lambda e, pi_=pi_, j=j, s_i=s_i, pgs=pgs: e.indirect_dma_start(out=pgs[pi_][:, :], out_offset=None, in_=pools2[pi_],
                                                                                           in_offset=bass.IndirectOffsetOnAxis(ap=idxa[:, s_i * 16 + j:s_i * 16 + j + 1], axis=0)),
                         R=[B_ptab], W=[bpgs[pi_]], dma=True)
                for pi_, (dst, bd) in ((0, (xk, bxk)), (1, (xv, bxv))):
                    pq, bq = eps()
                    pqb = pq[:, :].bitcast(BF16)
                    for c2 in range(2):
                        k.op("pe", lambda e, c2=c2, pi_=pi_, pq=pq, pgs=pgs: e.transpose(out=pq[:, c2 * 128:(c2 + 1) * 128], in_=pgs[pi_][:, c2 * 128:(c2 + 1) * 128], identity=identf[:, :]),
                             R=[bpgs[pi_], B_identf], W=[bq])
                    k.op("act" if pi_ == 0 else "dve", (lambda e, dst=dst, j=j, pq=pq: e.activation(out=dst[:, :, j * 128:(j + 1) * 128], in_=pq[:, 0:256].rearrange("p (c t) -> p c t", t=128), func=AF.Identity)) if pi_ == 0 else
                         (lambda e, dst=dst, j=j, pq=pq: e.tensor_copy(out=dst[:, :, j * 128:(j + 1) * 128], in_=pq[:, 0:256].rearrange("p (c t) -> p c t", t=128))),
                         R=[bq], W=[bd])
                pq, bq = eps()
                for c2 in range(2):
                    k.op("pe", lambda e, c2=c2, pq=pq, pgs=pgs: e.transpose(out=pq[:, c2 * 128:(c2 + 1) * 128], in_=pgs[2][:, c2 * 128:(c2 + 1) * 128], identity=identf[:, :]),
                         R=[bpgs[2], B_identf], W=[bq])
                for g in range(4):
                    k.op("act" if g % 2 else "dve", (lambda e, g=g, j=j, pq=pq: e.activation(out=KEs[0:64, g, j * 128:(j + 1) * 128], in_=pq[(g % 2) * 64:(g % 2) * 64 + 64, (g // 2) * 128:(g // 2 + 1) * 128], func=AF.Identity)) if g % 2 else
                         (lambda e, g=g, j=j, pq=pq: e.tensor_copy(out=KEs[0:64, g, j * 128:(j + 1) * 128], in_=pq[(g % 2) * 64:(g % 2) * 64 + 64, (g // 2) * 128:(g // 2 + 1) * 128])),
                         R=[bq], W=[B_KEs])
                k.op("pool", lambda e, j=j, pgs=pgs: e.tensor_copy(out=VAss[:, j, :, 0:64], in_=pgs[3][:, :].rearrange("p (g d) -> p g d", d=64)), R=[bpgs[3]], W=[B_VAss])
            for c4 in range(4):
                pgs, bpgs = pg[c4 % 2], B_pg[c4 % 2]
                k.dma("sp", pgs[0][:, :], ckwin[s_i, c4 * 128:(c4 + 1) * 128, :], W=[bpgs[0]])
                k.dma("sp", pgs[1][:, :], cvwin[s_i, c4 * 128:(c4 + 1) * 128, :], W=[bpgs[1]])
                pq, bq = eps()
                for c2 in range(2):
                    k.op("pe", lambda e, c2=c2, pq=pq, pgs=pgs: e.transpose(out=pq[:, c2 * 128:(c2 + 1) * 128], in_=pgs[0][:, c2 * 128:(c2 + 1) * 128], identity=identf[:, :]),
                         R=[bpgs[0], B_identf], W=[bq])
                for g in range(4):
                    k.op("dve", lambda e, g=g, c4=c4, pq=pq: e.tensor_copy(out=KWs[0:64, g, c4 * 128:(c4 + 1) * 128], in_=pq[(g % 2) * 64:(g % 2) * 64 + 64, (g // 2) * 128:(g // 2 + 1) * 128]),
                         R=[bq], W=[B_KWs])
                k.op("pool", lambda e, c4=c4, pgs=pgs: e.tensor_copy(out=VAws[:, c4, :, 0:64], in_=pgs[1][:, :].rearrange("p (g d) -> p g d", d=64)), R=[bpgs[1]], W=[B_VAws])
            k.op("dve", lambda e, s_i=s_i: e.tensor_copy(out=xk[:, :, PAST], in_=KNc[:, :, s_i]), R=[B_VNc], W=[bxk])
            k.op("dve", lambda e, s_i=s_i: e.tensor_copy(out=xv[:, :, PAST], in_=VNc[:, :, s_i]), R=[B_VNc], W=[bxv])
            for kind_ in range(2):
                xt_, bxt = (xk, bxk) if kind_ == 0 else (xv, bxv)
                for g in range(4):
                    rs_ = slice((g % 2) * 64, (g % 2) * 64 + 64)
                    c2 = g // 2
                    pF, bF = eps()
                    pS, bS = eps()
                    for p_ in range(16):
                        k.op("pe", lambda e, p_=p_, pF=pF: e.matmul(pF[:, 0:132], lhsT=W1s[kind_][rs_, p_, :], rhs=xt_[rs_, c2, p_:LP:16], start=(p_ == 0), stop=(p_ == 15)), R=[B_W1s, bxt], W=[bF])
                    for p_ in range(16):
                        k.op("pe", lambda e, p_=p_, pS=pS: e.matmul(pS[:, 0:132], lhsT=W1s[kind_][rs_, 16 + p_, :], rhs=xt_[rs_, c2, p_:LP:16], start=(p_ == 0), stop=(p_ == 15)), R=[B_W1s, bxt], W=[bS])
                    k.op("act", lambda e, pS=pS: e.activation(out=Sdz[:, :], in_=pS[:, 0:132], func=AF.Identity), R=[bS], W=[B_Sdz])
                    k.op("dve", lambda e, pF=pF: e.scalar_tensor_tensor(out=xgz[:, 0:128], in0=pF[:, 0:128], scalar=pecs[:, kind_:kind_ + 1], in1=Sdz[:, 1:129], op0=ALU.add, op1=ALU.add), R=[bF, B_pecs, B_Sdz], W=[B_xgz])
                    k.op("act", lambda e: e.activation(out=x2z[:, 0:128], in_=xgz[:, 0:128], func=AF.Square), R=[B_xgz], W=[B_x2z])
                    k.op("dve", lambda e: e.tensor_scalar(out=x2z[:, 0:128], in0=x2z[:, 0:128], scalar1=0.044715, scalar2=1.0, op0=ALU.mult, op1=ALU.add), R=[B_x2z], W=[B_x2z])
                    k.op("dve", lambda e: e.tensor_tensor(out=x2z[:, 0:128], in0=x2z[:, 0:128], in1=xgz[:, 0:128], op=ALU.mult), R=[B_x2z, B_xgz], W=[B_x2z])
                    k.op("act", lambda e: e.activation(out=x2z[:, 0:128], in_=x2z[:, 0:128], func=AF.Sigmoid, scale=1.5957691216), R=[B_x2z], W=[B_x2z])
                    k.op("dve", lambda e: e.tensor_tensor(out=hidz[:, :], in0=x2z[:, 0:128], in1=xgz[:, 0:128], op=ALU.mult), R=[B_x2z, B_xgz], W=[B_hidz])
                    pR, bR = eps()
                    if kind_ == 0:
                        k.op("pe", lambda e, pR=pR: e.matmul(pR[0:64, 0:128], lhsT=w2s[0][:, :], rhs=hidz[:, :], start=True, stop=True), R=[B_w2s, B_hidz], W=[bR])
                        k.op("dve", lambda e, g=g, pR=pR: e.tensor_copy(out=kcTs[0:64, g, :], in_=pR[0:64, 0:128]), R=[bR], W=[B_kcTs])
                    else:
                        k.op("pe", lambda e, pR=pR: e.matmul(pR[:, 0:64], lhsT=hidz[:, :], rhs=w2s[1][:, :], start=True, stop=True), R=[B_w2s, B_hidz], W=[bR])
                        k.op("dve", lambda e, g=g, pR=pR: e.tensor_copy(out=VCs[:, g, 0:64], in_=pR[:, 0:64]), R=[bR], W=[B_VCs])
            for g in range(4):
                pO, bO = PS[5], PSB[5]
                pI, bI = PS[6], PSB[6]
                score_pv(kcTs[0:64, g, :], RHS[0:64, s_i, 4 * g:4 * g + 4], [B_kcTs, B_RHSq], VCs[:, g, :], [B_VCs], pO, bO, True, True, mask=(126, -1), imp=(pI, bI))
                fin_branch(s_i, g, 0, pO, bO, True)
                k.op("dve", lambda e: e.reciprocal(out=imt[64:128, :], in_=Oes[64:128, :]), R=[B_Oes], W=[B_imt])
                k.op("dve", lambda e, pI=pI: e.tensor_tensor(out=imt[64:128, :], in0=pI[64:128, 0:4], in1=imt[64:128, :], op=ALU.mult), R=[bI, B_imt], W=[B_imt])
                k.op("dve", lambda e, g=g: e.tensor_reduce(out=impc[64:128, g:g + 1], in_=imt[64:128, :], axis=mybir.AxisListType.X, op=ALU.add), R=[B_imt], W=[B_impc])
            pT, bT = PS[7], PSB[7]
            k.op("pe", lambda e: e.transpose(out=pT[0:4, 0:64], in_=impc[64:128, :], identity=identf[64:128, 64:128]), R=[B_impc, B_identf], W=[bT])
            k.op("dve", lambda e: e.tensor_copy(out=impr[:, :], in_=pT[0:4, 0:64]), R=[bT], W=[B_impr])
            k.op("pool", lambda e: e.memset(impr[:, 0:1], BIG), R=[B_impr], W=[B_impr])
            k.op("pool", lambda e: e.memset(impr[:, 32:33], BIG), R=[B_impr], W=[B_impr])
            k.op("pool", lambda e: e.memset(impr[:, 33:64], -BIG), R=[B_impr], W=[B_impr])
            k.op("dve", lambda e: e.max(out=m8r[:, 0:8], in_=impr[:, :]), R=[B_impr], W=[B_m8r])
            k.op("dve", lambda e: e.match_replace(out=imp2r[:, :], in_to_replace=m8r[:, 0:8], in_values=impr[:, :], imm_value=-3.0e38), R=[B_impr, B_m8r], W=[B_imp2r])
            k.op("dve", lambda e: e.max(out=m8r[:, 8:16], in_=imp2r[:, :]), R=[B_imp2r, B_m8r], W=[B_m8r])
            k.op("dve", lambda e: e.tensor_scalar(out=bselr[:, 64:128], in0=impr[:, :], scalar1=m8r[:, 15:16], scalar2=1.0, op0=ALU.is_ge, op1=ALU.subtract), R=[B_impr, B_m8r], W=[B_bselr])
            pTb = pT[:, :].bitcast(BF16)
            k.op("pe", lambda e: e.transpose(out=pTb[:, 512:516], in_=bselr[:, :], identity=ident[0:4, 0:4]), R=[B_bselr, B_ident], W=[bT])
            for hh in range(4):
                k.op("dve", lambda e, hh=hh, s_i=s_i: e.tensor_copy(out=RHS[64:128, s_i, :].rearrange("p (g h) -> p g h", h=4)[:, :, hh], in_=pTb[64:128, 512:516]), R=[bT], W=[B_RHSs])
            for g in range(4):
                pO, bO = PS[5], PSB[5]
                for c in range(16):
                    score_pv(KEs[:, g, c * 128:(c + 1) * 128], RHS[:, s_i, 4 * g:4 * g + 4], [B_KEs, B_KEe, B_RHSq, B_RHSs], VAss[:, c, g, :], [B_VAss], pO, bO, c == 0, False)
                score_pv(KN[0:64, 1, g, s_i:s_i + 1], RHS[0:64, s_i, 4 * g:4 * g + 4], [B_KN, B_RHSq], VN[0:1, s_i, 0, g, :], [B_VN], pO, bO, False, True, np_=1)
                fin_branch(s_i, g, 1, pO, bO, False)
            for g in range(4):
                pO, bO = PS[6], PSB[6]
                for c in range(4):
                    score_pv(KWs[0:64, g, c * 128:(c + 1) * 128], RHS[0:64, s_i, 4 * g:4 * g + 4], [B_KWs, B_RHSq], VAws[:, c, g, :], [B_VAws], pO, bO, c == 0, False,
                             mask=((-1, 1) if c == 0 else None))
                score_pv(KN[0:64, 2, g, s_i:s_i + 1], RHS[0:64, s_i, 4 * g:4 * g + 4], [B_KN, B_RHSq], VN[0:1, s_i, 1, g, :], [B_VN], pO, bO, False, True, np_=1)
                fin_branch(s_i, g, 2, pO, bO, False)
        ons = sb(st, "ons", [NS, D], F32)
        ods = sb(st, "ods", [NS, D], F32)
        mgs = sb(st, "mgs", [NS, 2048], F32)
        B_ons, B_ods, B_mgs = Buf(), Buf(), Buf()
        for h in range(16):
            pq, bq = PS[h // 8], PSB[h // 8]
            k.op("pe", lambda e, h=h, pq=pq: e.transpose(out=pq[0:NS, (h % 8) * 64:(h % 8 + 1) * 64], in_=onT[0:64, :, h], identity=identf[0:64, 0:64]), R=[B_onT, B_identf], W=[bq])
        for hf in range(2):
            k.op("dve", lambda e, hf=hf: e.tensor_copy(out=ons[:, hf * 512:(hf + 1) * 512], in_=PS[hf][0:NS, :]), R=[PSB[hf]], W=[B_ons])
        k.dma("sp", mgs[:, :], projs_scr[:, 6720:8768], R=[B_scr2], W=[B_mgs])
        k.dma("sp", ods[:, :], odns_scr[:, :], R=[B_scr3], W=[B_ods])
        k.op("act", lambda e: e.activation(out=mgs[:, :], in_=mgs[:, :], func=AF.Sigmoid), R=[B_mgs], W=[B_mgs])
        k.op("dve", lambda e: e.tensor_tensor(out=ons[:, :], in0=ons[:, :], in1=mgs[:, 0:D], op=ALU.mult), R=[B_ons, B_mgs], W=[B_ons])
        k.op("dve", lambda e: e.tensor_tensor(out=ods[:, :], in0=ods[:, :], in1=mgs[:, D:2 * D], op=ALU.mult), R=[B_ods, B_mgs], W=[B_ods])
        k.op("dve", lambda e: e.tensor_tensor(out=ons[:, :], in0=ons[:, :], in1=ods[:, :], op=ALU.add), R=[B_ons, B_ods], W=[B_ons])
        k.dma("sp", mixs_scr[:, :], ons[:, :], R=[B_ons], W=[B_scr3])
        k.barrier()
    with contextlib.ExitStack() as st:
        wo = sb(st, "wo", [128, 8, D], BF16)
        wu = sb(st, "wu", [128, 8, 4 * D], BF16)
        wd = sb(st, "wd", [128, 32, D], BF16)
        B_wo, B_wu, B_wd = Buf(), Buf(), Buf()
        wov = w_out.rearrange("(kc p) n -> p kc n", p=128)
        wuv = w_up.rearrange("(kc p) n -> p kc n", p=128)
        wdv = w_down.rearrange("(kc p) n -> p kc n", p=128)
        for c in range(2):
            k.dma("pool", wo[:, :, c * 512:(c + 1) * 512], wov[:, :, c * 512:(c + 1) * 512], W=[B_wo])
        for c in range(8):
            k.dma("pool", wu[:, :, c * 512:(c + 1) * 512], wuv[:, :, c * 512:(c + 1) * 512], W=[B_wu])
        for c in range(8):
            k.dma("pool", wd[:, c * 4:(c + 1) * 4, :], wdv[:, c * 4:(c + 1) * 4, :], W=[B_wd])
        onesF = sb(st, "onesF", [128, 128], F32)
        dgF = sb(st, "dgF", [128, 128], F32)
        bct = sb(st, "bct", [128, 3, D], F32)
        gm2 = sb(st, "gm2", [128, 8], F32)
        B_onesF, B_dgF, B_bct, B_gm2 = Buf(), Buf(), Buf(), Buf()
        k.op("pool", lambda e: e.memset(onesF[:, :], 1.0), W=[B_onesF])
        k.op("dve", lambda e: e.scalar_tensor_tensor(out=gm2[:, :], in0=vecT[:, 32:40], scalar=1.0, in1=vecT[:, 56:64], op0=ALU.add, op1=ALU.mult), R=[B_vecT], W=[B_gm2])
        fps = [0]

        def fp():
            i = fps[0] % 8
            fps[0] += 1
            return PS[i], PSB[i]
        for vi, c0 in enumerate((16, 40, 64)):
            for j in range(8):
                k.op("dve", lambda e, c0=c0, j=j: e.tensor_scalar(out=dgF[:, :], in0=identf[:, :], scalar1=vecT[:, c0 + j:c0 + j + 1], scalar2=None, op0=ALU.mult),
                     R=[B_vecT, B_identf, B_dgF], W=[B_dgF])
                pp, bpp = fp()
                k.op("pe", lambda e, pp=pp: e.matmul(pp[:, 0:128], lhsT=onesF[:, :], rhs=dgF[:, :], start=True, stop=True), R=[B_onesF, B_dgF], W=[bpp])
                k.op("act", lambda e, pp=pp, vi=vi, j=j: e.activation(out=bct[:, vi, j * 128:(j + 1) * 128], in_=pp[:, 0:128], func=AF.Identity), R=[bpp], W=[B_bct])
        TB = 2
        mixb = sb(st, "mixb", [128, TB, D], BF16)
        mixT = sb(st, "mixT", [128, 8, TB * 128], BF16)
        x1 = sb(st, "x1", [128, TB, D], F32)
        tmpF = sb(st, "tmpF", [128, D], F32)
        xnb = sb(st, "xnb", [128, D], BF16)
        h2T = sb(st, "h2T", [128, 8, TB * 128], BF16)
        uT = sb(st, "uT", [128, 32, TB * 128], BF16)
        rl = [sb(st, "rl%d" % i, [128, TB * 128], F32) for i in range(2)]
        sq2 = sb(st, "sq2", [128, 2], F32)
        B_mixb, B_mixT, B_x1, B_tmpF, B_xnb, B_h2T, B_sq2 = (Buf() for _ in range(7))
        yout, B_yout = tmpF, B_tmpF
        B_uT = [Buf() for _ in range(32)]
        B_rl = [Buf(), Buf()]
        for blk in range(S // (TB * 128)):
            t0 = blk * TB * 128
            k.dma("pool", mixb[:, :, :], mix_scr[t0:t0 + TB * 128, :].rearrange("(t p) d -> p t d", p=128), R=[B_mix], W=[B_mixb])
            k.dma("sp", x1[:, :, :], xp[t0:t0 + TB * 128, :].rearrange("(t p) d -> p t d", p=128), W=[B_x1])
            for tt in range(TB):
                pp, bpp = fp()
                ppb = pp[:, :].bitcast(BF16)
                for j in range(8):
                    k.op("pe", lambda e, tt=tt, j=j, ppb=ppb: e.transpose(out=ppb[:, j * 128:(j + 1) * 128], in_=mixb[:, tt, j * 128:(j + 1) * 128], identity=ident[:, :]),
                         R=[B_mixb, B_ident], W=[bpp])
                k.op("act", lambda e, tt=tt, ppb=ppb: e.activation(out=mixT[:, :, tt * 128:(tt + 1) * 128], in_=ppb[:, :].rearrange("p (j t) -> p j t", t=128), func=AF.Identity),
                     R=[bpp], W=[B_mixT])
            for tt in range(TB):
                for hf in range(2):
                    pp, bpp = fp()
                    for kc in range(8):
                        k.op("pe", lambda e, tt=tt, hf=hf, kc=kc, pp=pp: e.matmul(pp[:, :], lhsT=mixT[:, kc, tt * 128:(tt + 1) * 128], rhs=wo[:, kc, hf * 512:(hf + 1) * 512],
                                                                                 start=(kc == 0), stop=(kc == 7)), R=[B_mixT, B_wo], W=[bpp])
                    k.op("dve", lambda e, tt=tt, hf=hf, pp=pp: e.tensor_tensor(out=tmpF[:, hf * 512:(hf + 1) * 512], in0=pp[:, :], in1=bct[:, 0, hf * 512:(hf + 1) * 512], op=ALU.mult),
                         R=[bpp, B_bct], W=[B_tmpF])
                k.op("dve", lambda e, tt=tt: e.tensor_tensor(out=x1[:, tt, :], in0=x1[:, tt, :], in1=tmpF[:, :], op=ALU.add), R=[B_x1, B_tmpF], W=[B_x1])
                k.op("act", lambda e, tt=tt: e.activation(out=tmpF[:, :], in_=x1[:, tt, :], func=AF.Square, accum_out=sq2[:, 0:1]), R=[B_x1], W=[B_tmpF, B_sq2])
                k.op("act", lambda e: e.activation(out=sq2[:, 0:1], in_=sq2[:, 0:1], func=AF.Sqrt, scale=1.0 / D, bias=1e-6), R=[B_sq2], W=[B_sq2])
                k.op("dve", lambda e: e.reciprocal(out=sq2[:, 0:1], in_=sq2[:, 0:1]), R=[B_sq2], W=[B_sq2])
                k.op("dve", lambda e, tt=tt: e.tensor_scalar(out=xnb[:, :], in0=x1[:, tt, :], scalar1=sq2[:, 0:1], scalar2=None, op0=ALU.mult), R=[B_x1, B_sq2], W=[B_xnb])
                pp, bpp = fp()
                ppb = pp[:, :].bitcast(BF16)
                for j in range(8):
                    k.op("pe", lambda e, j=j, ppb=ppb: e.transpose(out=ppb[:, j * 128:(j + 1) * 128], in_=xnb[:, j * 128:(j + 1) * 128], identity=ident[:, :]),
                         R=[B_xnb, B_ident], W=[bpp])
                for j in range(8):
                    if j % 2 == 0:
                        k.op("act", lambda e, j=j, tt=tt, ppb=ppb: e.activation(out=h2T[:, j, tt * 128:(tt + 1) * 128], in_=ppb[:, j * 128:(j + 1) * 128], func=AF.Identity,
                                                                               scale=gm2[:, j:j + 1], bias=vecT[:, 24 + j:25 + j]), R=[bpp, B_gm2, B_vecT], W=[B_h2T])
                    else:
                        k.op("dve", lambda e, j=j, tt=tt, ppb=ppb: e.tensor_scalar(out=h2T[:, j, tt * 128:(tt + 1) * 128], in0=ppb[:, j * 128:(j + 1) * 128],
                                                                                  scalar1=gm2[:, j:j + 1], scalar2=vecT[:, 24 + j:25 + j], op0=ALU.mult, op1=ALU.add),
                             R=[bpp, B_gm2, B_vecT], W=[B_h2T])
            for hc in range(32):
                pp, bpp = fp()
                for kc in range(8):
                    k.op("pe", lambda e, hc=hc, kc=kc, pp=pp: e.matmul(pp[:, 0:TB * 128], lhsT=wu[:, kc, hc * 128:(hc + 1) * 128], rhs=h2T[:, kc, :], start=(kc == 0), stop=(kc == 7)),
                         R=[B_wu, B_h2T], W=[bpp])
                r_, br_ = rl[hc % 2], B_rl[hc % 2]
                k.op("act", lambda e, pp=pp, r_=r_: e.activation(out=r_[:, :], in_=pp[:, 0:TB * 128], func=AF.Relu), R=[bpp], W=[br_])
                k.op("dve" if hc % 2 else "pool", lambda e, hc=hc, r_=r_: e.tensor_tensor(out=uT[:, hc, :], in0=r_[:, :], in1=r_[:, :], op=ALU.mult), R=[br_], W=[B_uT[hc]])
            for tt in range(TB):
                for hf in range(2):
                    pp, bpp = fp()
                    for hc in range(32):
                        k.op("pe", lambda e, tt=tt, hf=hf, hc=hc, pp=pp: e.matmul(pp[:, :], lhsT=uT[:, hc, tt * 128:(tt + 1) * 128], rhs=wd[:, hc, hf * 512:(hf + 1) * 512],
                                                                                 start=(hc == 0), stop=(hc == 31)), R=[B_uT[hc], B_wd], W=[bpp])
                    k.op("dve", lambda e, tt=tt, hf=hf, pp=pp: e.tensor_tensor(out=tmpF[:, hf * 512:(hf + 1) * 512], in0=pp[:, :], in1=bct[:, 1, hf * 512:(hf + 1) * 512], op=ALU.mult),
                         R=[bpp, B_bct], W=[B_tmpF])
                k.op("dve", lambda e, tt=tt: e.tensor_tensor(out=x1[:, tt, :], in0=x1[:, tt, :], in1=tmpF[:, :], op=ALU.add), R=[B_x1, B_tmpF], W=[B_x1])
                k.op("act", lambda e, tt=tt: e.activation(out=tmpF[:, :], in_=x1[:, tt, :], func=AF.Square, accum_out=sq2[:, 1:2]), R=[B_x1], W=[B_tmpF, B_sq2])
                k.op("act", lambda e: e.activation(out=sq2[:, 1:2], in_=sq2[:, 1:2], func=AF.Sqrt, scale=1.0 / D, bias=1e-6), R=[B_sq2], W=[B_sq2])
                k.op("dve", lambda e: e.reciprocal(out=sq2[:, 1:2], in_=sq2[:, 1:2]), R=[B_sq2], W=[B_sq2])
                k.op("dve", lambda e, tt=tt: e.scalar_tensor_tensor(out=yout[:, :], in0=x1[:, tt, :], scalar=sq2[:, 1:2], in1=bct[:, 2, :], op0=ALU.mult, op1=ALU.mult),
                     R=[B_x1, B_sq2, B_bct], W=[B_yout])
                k.dma("sp", o_yp[t0 + tt * 128:t0 + (tt + 1) * 128, :], yout[:, :], R=[B_yout], W=[DOUT])
        gmT = sb(st, "gmT", [128, 8, NS], F32)
        shT = sb(st, "shT", [128, 8, NS], F32)
        B_gmT, B_shT = Buf(), Buf()
        B_m1 = B_x1
        k.dma("pool", mixb[0:NS, 0, :], mixs_scr[:, :], R=[B_scr3], W=[B_mixb])
        k.dma("sp", x1[0:NS, 0, :], xs[:, :], W=[B_x1])
        k.dma("sp", x1[0:NS, 1, :], mods_scr[:, 2 * D:3 * D], R=[B_scr2], W=[B_x1])
        pp, bpp = fp()
        ppb = pp[:, :].bitcast(BF16)
        for j in range(8):
            k.op("pe", lambda e, j=j, ppb=ppb: e.transpose(out=ppb[:, j * NS:(j + 1) * NS], in_=mixb[0:NS, 0, j * 128:(j + 1) * 128], identity=ident[0:NS, 0:NS]), R=[B_mixb, B_ident], W=[bpp])
        k.op("act", lambda e, ppb=ppb: e.activation(out=mixT[:, :, 0:NS], in_=ppb[:, 0:8 * NS].rearrange("p (j t) -> p j t", t=NS), func=AF.Identity), R=[bpp], W=[B_mixT])
        for hf in range(2):
            pp, bpp = fp()
            for kc in range(8):
                k.op("pe", lambda e, hf=hf, kc=kc, pp=pp: e.matmul(pp[0:NS, :], lhsT=mixT[:, kc, 0:NS], rhs=wo[:, kc, hf * 512:(hf + 1) * 512], start=(kc == 0), stop=(kc == 7)), R=[B_mixT, B_wo], W=[bpp])
            k.op("dve", lambda e, hf=hf, pp=pp: e.tensor_tensor(out=tmpF[0:NS, hf * 512:(hf + 1) * 512], in0=pp[0:NS, :], in1=x1[0:NS, 1, hf * 512:(hf + 1) * 512], op=ALU.mult), R=[bpp, B_x1], W=[B_tmpF])
        k.op("dve", lambda e: e.tensor_tensor(out=x1[0:NS, 0, :], in0=x1[0:NS, 0, :], in1=tmpF[0:NS, :], op=ALU.add), R=[B_x1, B_tmpF], W=[B_x1])
        for which, (c0, dstT) in enumerate(((4 * D, gmT), (3 * D, shT))):
            k.dma("sp", x1[0:NS, 1, :], mods_scr[:, c0:c0 + D], R=[B_scr2, B_x1], W=[B_x1])
            pp, bpp = fp()
            for j in range(8):
                k.op("pe", lambda e, j=j, pp=pp: e.transpose(out=pp[:, j * NS:(j + 1) * NS], in_=x1[0:NS, 1, j * 128:(j + 1) * 128], identity=identf[0:NS, 0:NS]), R=[B_x1, B_identf], W=[bpp])
            if which == 0:
                for j in range(8):
                    k.op("dve", lambda e, j=j, pp=pp: e.tensor_scalar(out=gmT[:, j, :], in0=pp[:, j * NS:(j + 1) * NS], scalar1=1.0, scalar2=vecT[:, 56 + j:57 + j], op0=ALU.add, op1=ALU.mult),
                         R=[bpp, B_vecT], W=[B_gmT])
            else:
                k.op("dve", lambda e, pp=pp: e.tensor_copy(out=shT[:, :, :], in_=pp[:, 0:8 * NS].rearrange("p (j t) -> p j t", t=NS)), R=[bpp], W=[B_shT])
        k.op("act", lambda e: e.activation(out=tmpF[0:NS, :], in_=x1[0:NS, 0, :], func=AF.Square, accum_out=sq2[0:NS, 0:1]), R=[B_x1], W=[B_tmpF, B_sq2])
        k.op("act", lambda e: e.activation(out=sq2[0:NS, 0:1], in_=sq2[0:NS, 0:1], func=AF.Sqrt, scale=1.0 / D, bias=1e-6), R=[B_sq2], W=[B_sq2])
        k.op("dve", lambda e: e.reciprocal(out=sq2[0:NS, 0:1], in_=sq2[0:NS, 0:1]), R=[B_sq2], W=[B_sq2])
        k.op("dve", lambda e: e.tensor_scalar(out=xnb[0:NS, :], in0=x1[0:NS, 0, :], scalar1=sq2[0:NS, 0:1], scalar2=None, op0=ALU.mult), R=[B_x1, B_sq2], W=[B_xnb])
        pp, bpp = fp()
        ppb = pp[:, :].bitcast(BF16)
        for j in range(8):
            k.op("pe", lambda e, j=j, ppb=ppb: e.transpose(out=ppb[:, j * NS:(j + 1) * NS], in_=xnb[0:NS, j * 128:(j + 1) * 128], identity=ident[0:NS, 0:NS]), R=[B_xnb, B_ident], W=[bpp])
        k.op("dve", lambda e, ppb=ppb: e.tensor_tensor(out=gmT[:, :, :], in0=ppb[:, 0:8 * NS].rearrange("p (j t) -> p j t", t=NS), in1=gmT[:, :, :], op=ALU.mult), R=[bpp, B_gmT], W=[B_gmT])
        k.op("dve", lambda e: e.tensor_tensor(out=h2T[:, :, 0:NS], in0=gmT[:, :, :], in1=shT[:, :, :], op=ALU.add), R=[B_gmT, B_shT], W=[B_h2T])
        for hc in range(32):
            pp, bpp = fp()
            for kc in range(8):
                k.op("pe", lambda e, hc=hc, kc=kc, pp=pp: e.matmul(pp[:, 0:NS], lhsT=wu[:, kc, hc * 128:(hc + 1) * 128], rhs=h2T[:, kc, 0:NS], start=(kc == 0), stop=(kc == 7)), R=[B_wu, B_h2T], W=[bpp])
            r_, br_ = rl[hc % 2], B_rl[hc % 2]
            k.op("act", lambda e, pp=pp, r_=r_: e.activation(out=r_[:, 0:NS], in_=pp[:, 0:NS], func=AF.Relu), R=[bpp], W=[br_])
            k.op("dve", lambda e, hc=hc, r_=r_: e.tensor_tensor(out=uT[:, hc, 0:NS], in0=r_[:, 0:NS], in1=r_[:, 0:NS], op=ALU.mult), R=[br_], W=[B_uT[hc]])
        k.dma("sp", x1[0:NS, 1, :], mods_scr[:, 5 * D:6 * D], R=[B_scr2, B_x1], W=[B_x1])
        for hf in range(2):
            pp, bpp = fp()
            for hc in range(32):
                k.op("pe", lambda e, hf=hf, hc=hc, pp=pp: e.matmul(pp[0:NS, :], lhsT=uT[:, hc, 0:NS], rhs=wd[:, hc, hf * 512:(hf + 1) * 512], start=(hc == 0), stop=(hc == 31)), R=[B_uT[hc], B_wd], W=[bpp])
            k.op("dve", lambda e, hf=hf, pp=pp: e.tensor_tensor(out=tmpF[0:NS, hf * 512:(hf + 1) * 512], in0=pp[0:NS, :], in1=x1[0:NS, 1, hf * 512:(hf + 1) * 512], op=ALU.mult), R=[bpp, B_x1], W=[B_tmpF])
        k.op("dve", lambda e: e.tensor_tensor(out=x1[0:NS, 0, :], in0=x1[0:NS, 0, :], in1=tmpF[0:NS, :], op=ALU.add), R=[B_x1, B_tmpF], W=[B_x1])
        k.op("act", lambda e: e.activation(out=tmpF[0:NS, :], in_=x1[0:NS, 0, :], func=AF.Square, accum_out=sq2[0:NS, 1:2]), R=[B_x1], W=[B_tmpF, B_sq2])
        k.op("act", lambda e: e.activation(out=sq2[0:NS, 1:2], in_=sq2[0:NS, 1:2], func=AF.Sqrt, scale=1.0 / D, bias=1e-6), R=[B_sq2], W=[B_sq2])
        k.op("dve", lambda e: e.reciprocal(out=sq2[0:NS, 1:2], in_=sq2[0:NS, 1:2]), R=[B_sq2], W=[B_sq2])
        k.op("dve", lambda e: e.scalar_tensor_tensor(out=tmpF[0:NS, :], in0=x1[0:NS, 0, :], scalar=sq2[0:NS, 1:2], in1=bct[0:NS, 2, :], op0=ALU.mult, op1=ALU.mult), R=[B_x1, B_sq2, B_bct], W=[B_tmpF])
        k.dma("sp", o_ys[:, :], tmpF[0:NS, :], R=[B_tmpF], W=[DOUT])
        k.barrier()
    st0.close()
    k.emit()
    return nc


_NC = None


def kernel(**inp):
    global _NC
    f = lambda a: np.ascontiguousarray(np.asarray(a, dtype=np.float32))
    if _NC is None:
        _NC = build_nc()
    nc = _NC
    pools_ = [f(inp[n_][0]).reshape(2560, 128, 256) for n_ in ("cache_k_cmp", "cache_v_cmp", "cache_k_slc", "cache_v_slc")]
    in_maps = []
    for i in range(8):
        b = i % 4
        s0 = i * NS
        in_maps.append({
            "xp": f(inp["x_prompt"][b]),
            "xs": f(inp["x_sample"][s0:s0 + NS, 0]),
            "cp": f(inp["c_prompt"][b:b + 1]),
            "cs": f(inp["c_sample"][s0:s0 + NS]),
            "w_ada": f(inp["w_ada"][0]),
            "b_ada": f(inp["b_ada"][0:1]),
            "g1": f(inp["norm1_g"][0:1]),
            "g2": f(inp["norm2_g"][0:1]),
            "gf": f(inp["final_g"][None, :]),
            "w_in": f(inp["w_in"][0]),
            "ckwin": f(inp["cache_k_win"][0, s0:s0 + NS]).reshape(NS, 512, 256),
            "cvwin": f(inp["cache_v_win"][0, s0:s0 + NS]).reshape(NS, 512, 256),
            "sconv": f(inp["state_conv"][0, s0:s0 + NS]),
            "conv_w": f(inp["dn_conv_w"][0]),
            "a_log": f(inp["dn_a_log"]),
            "dt_bias": f(inp["dn_dt_bias"]),
            "dn_ng": f(inp["dn_norm_g"]),
            "state_dn": f(inp["state_dn"][0, s0:s0 + NS]),
            "pk_cmp": pools_[0], "pv_cmp": pools_[1], "pk_slc": pools_[2], "pv_slc": pools_[3],
            "ptbl": np.ascontiguousarray(np.asarray(inp["page_table"][s0:s0 + NS], dtype=np.int32).reshape(1, NS * 16)),
            "w_out": f(inp["w_out"][0]), "w_up": f(inp["w_up"][0]), "w_down": f(inp["w_down"][0]),
            "w1k": f(inp["cmp_w1_k"][0]), "w2k": f(inp["cmp_w2_k"][0]), "pek": f(inp["cmp_pe_k"][0]),
            "w1v": f(inp["cmp_w1_v"][0]), "w2v": f(inp["cmp_w2_v"][0]), "pev": f(inp["cmp_pe_v"][0]),
        })
    res = run_bass_kernel_spmd(nc, in_maps, core_ids=list(range(8)))
    R = res.results
    B = 4
    y_prompt = np.stack([R[b]["o_yp"] for b in range(B)], 0)
    y_sample = np.concatenate([R[i]["o_ys"] for i in range(8)], 0)[:, None, :]
    pkv = np.stack([R[b]["o_pkv"] for b in range(B)], 0)
    p_kv = [pkv[:, j].reshape(1, B, S, 4, 64) for j in range(6)]
    p_kv[4] = p_kv[4][:, :, S - 512:]
    p_kv[5] = p_kv[5][:, :, S - 512:]
    p_conv = np.stack([R[b]["o_pconv"] for b in range(B)], 0)[None]
    p_dn = np.stack([R[b]["o_pdn"] for b in range(B)], 0)[None]
    skv = np.concatenate([R[i]["o_skv"] for i in range(8)], 0)
    s_kv = [skv[:, j * 256:(j + 1) * 256].reshape(1, 128, 1, 4, 64) for j in range(4)]
    s_kwin = np.concatenate([R[i]["o_skwin"] for i in range(8)], 0).reshape(1, 128, 512, 4, 64)
    s_vwin = np.concatenate([R[i]["o_svwin"] for i in range(8)], 0).reshape(1, 128, 512, 4, 64)
    s_conv = np.concatenate([R[i]["o_sconv"] for i in range(8)], 0)[None]
    s_dn = np.concatenate([R[i]["o_sdn"] for i in range(8)], 0)[None]
    return (y_prompt, y_sample, p_kv[0], p_kv[1], p_kv[2], p_kv[3], p_kv[4], p_kv[5], p_conv, p_dn,
            s_kv[0], s_kv[1], s_kv[2], s_kv[3], s_kwin, s_vwin, s_conv, s_dn)
```

```python
import contextlib
import os
STOP = int(os.environ.get('DN_STOP', '9'))
import numpy as np
import concourse.bass as bass
import concourse.mybir as mybir
from concourse.bass_utils import run_bass_kernel_spmd

F32 = mybir.dt.float32
BF16 = mybir.dt.bfloat16
I32 = mybir.dt.int32
AF = mybir.ActivationFunctionType
ALU = mybir.AluOpType

S = 4096
D = 1024
NS = 16
INW = 8768
C_KV = 1024
C_QKV = 2608
NT = S // 128


class Buf:
    __slots__ = ("w", "r", "name")

    def __init__(self, name=""):
        self.w = None
        self.r = {}
        self.name = name


class _Rec:
    def __getattr__(self, name):
        def f(*a, **kw):
            self.call = (name, a, kw)
            return self
        return f


class KB:
    NDMASEM = 24

    def __init__(self, nc):
        self.nc = nc
        self.engs = {"pe": nc.tensor, "act": nc.scalar, "dve": nc.vector,
                     "pool": nc.gpsimd, "sp": nc.sync}
        self.prog = {e: [] for e in self.engs}
        self.cnt = {}
        self.waited = {e: {} for e in self.engs}
        self.ndma = 0
        self.sems = {}
        for e in self.engs:
            self.sems[e] = nc.alloc_semaphore("s_" + e)
        for i in range(self.NDMASEM):
            self.sems["dma%d" % i] = nc.alloc_semaphore("s_dma%d" % i)

    def op(self, eng, fn, R=(), W=(), dma=False):
        if fn is None:
            fn = self._raw
        else:
            rec = _Rec()
            fn(rec)
            name_, a_, kw_ = rec.call
            fn = (lambda e, name_=name_, a_=a_, kw_=kw_: getattr(e, name_)(*a_, **kw_))
        deps = {}

        def add(s, v):
            if deps.get(s, 0) < v:
                deps[s] = v
        for b in R:
            if b.w:
                add(*b.w)
        for b in W:
            if b.w:
                add(*b.w)
            for s, v in b.r.items():
                add(s, v)
        if dma:
            sname = "dma%d" % (self.ndma % self.NDMASEM)
            prev = self.cnt.get(sname, 0)
            if prev:
                add(sname, prev)
            val = prev + 16
            self.ndma += 1
        else:
            sname = eng
            val = self.cnt.get(eng, 0) + 1
        self.cnt[sname] = val
        waits = []
        for s, v in deps.items():
            if s == "pe" and eng == "pe" and not dma:
                continue
            if self.waited[eng].get(s, 0) >= v:
                continue
            self.waited[eng][s] = v
            waits.append((s, v))
        self.prog[eng].append((waits, fn, sname, 16 if dma else 1))
        for b in R:
            if b.r.get(sname, 0) < val:
                b.r[sname] = val
        for b in W:
            b.w = (sname, val)
            b.r = {}

    def raw(self, eng, fn, R=(), W=(), dma=False):
        self._raw = fn
        self.op(eng, None, R=R, W=W, dma=dma)

    def dma(self, eng, out, in_, R=(), W=(), **kw):
        self.op(eng, lambda e: e.dma_start(out=out, in_=in_, **kw), R=R, W=W, dma=True)

    def barrier(self):
        for e in self.engs:
            waits = []
            for s, v in self.cnt.items():
                if self.waited[e].get(s, 0) >= v:
                    continue
                self.waited[e][s] = v
                waits.append((s, v))
            if waits:
                self.prog[e].append((waits, None, None, 0))

    def emit(self):
        nc = self.nc
        self.barrier()
        with nc.Block() as block:
            def mk(ename):
                def body(e):
                    for waits, fn, sname, inc in self.prog[ename]:
                        for s, v in waits:
                            e.wait_ge(self.sems[s], v)
                        if fn is not None:
                            fn(e).then_inc(self.sems[sname], inc)
                return body
            block.tensor(mk("pe"))
            block.scalar(mk("act"))
            block.vector(mk("dve"))
            block.gpsimd(mk("pool"))
            block.sync(mk("sp"))


def build_nc():
    nc = bass.Bass("TRN2", target_bir_lowering=False)

    def din(name, shape, dt=F32):
        return nc.dram_tensor(name, list(shape), dt, kind="ExternalInput").ap()

    def dout(name, shape, dt=F32):
        return nc.dram_tensor(name, list(shape), dt, kind="ExternalOutput").ap()

    xp = din("xp", [S, D])
    xs = din("xs", [NS, D])
    cp = din("cp", [1, D])
    cs = din("cs", [NS, D])
    w_ada = din("w_ada", [D, 6 * D])
    b_ada = din("b_ada", [1, 6 * D])
    g1 = din("g1", [1, D])
    g2 = din("g2", [1, D])
    gf = din("gf", [1, D])
    w_in = din("w_in", [D, INW])
    ckwin = din("ckwin", [NS, 512, 256])
    cvwin = din("cvwin", [NS, 512, 256])
    sconv = din("sconv", [NS, 3, 3072])
    conv_w = din("conv_w", [4, 3072])
    a_log = din("a_log", [1, 8])
    dt_bias = din("dt_bias", [1, 8])
    dn_ng = din("dn_ng", [1, 128])
    state_dn = din("state_dn", [NS, 8, 128, 128])
    pk_cmp = din("pk_cmp", [2560, 128, 256])
    pv_cmp = din("pv_cmp", [2560, 128, 256])
    pk_slc = din("pk_slc", [2560, 128, 256])
    pv_slc = din("pv_slc", [2560, 128, 256])
    ptbl = din("ptbl", [1, NS * 16], I32)
    w_out = din("w_out", [D, D])
    w_up = din("w_up", [D, 4 * D])
    w_down = din("w_down", [4 * D, D])
    w1k = din("w1k", [32, 64, 128])
    w2k = din("w2k", [128, 64])
    pek = din("pek", [32, 64])
    w1v = din("w1v", [32, 64, 128])
    w2v = din("w2v", [128, 64])
    pev = din("pev", [32, 64])

    o_pkv = dout("o_pkv", [6, S, 256])
    o_pconv = dout("o_pconv", [3, 3072])
    o_skv = dout("o_skv", [NS, 1536])
    o_skwin = dout("o_skwin", [NS, 512, 256])
    o_svwin = dout("o_svwin", [NS, 512, 256])
    o_sconv = dout("o_sconv", [NS, 3, 3072])
    o_pdn = dout("o_pdn", [8, 128, 128])
    o_sdn = dout("o_sdn", [NS, 8, 128, 128])
    o_yp = dout("o_yp", [S, D])
    o_ys = dout("o_ys", [NS, D])
    odn_scr = nc.dram_tensor("odn_scr", [S, D], F32, kind="Internal").ap()

    k = KB(nc)
    wv = w_in.rearrange("(kc p) n -> p kc n", p=128)
    st0 = contextlib.ExitStack()

    def sb(stack, name, shape, dt):
        return stack.enter_context(nc.sbuf_tensor(name, list(shape), dt))

    PS = [nc.alloc_psum_tensor("ps%d" % i, [128, 512], F32) for i in range(8)]
    PSB = [Buf("ps%d" % i) for i in range(8)]
    DOUT = Buf("dram_out")

    identf = sb(st0, "identf", [128, 128], F32)
    ident = sb(st0, "ident", [128, 128], BF16)
    ones = sb(st0, "ones", [1, 128], F32)
    B_identf, B_ident, B_ones = Buf(), Buf(), Buf()
    k.op("pool", lambda e: e.memset(identf[:, :], 0.0), W=[B_identf])
    k.op("pool", lambda e: e.affine_select(out=identf[:, :], in_=identf[:, :], pattern=[[-1, 128]],
                                           compare_op=ALU.not_equal, fill=1.0, base=0, channel_multiplier=1),
         R=[B_identf], W=[B_identf])
    k.op("dve", lambda e: e.tensor_copy(out=ident[:, :], in_=identf[:, :]), R=[B_identf], W=[B_ident])
    k.op("pool", lambda e: e.memset(ones[:, :], 1.0), W=[B_ones])
    zcol = sb(st0, "zcol", [128, 1], F32)
    k.op("pool", lambda e: e.memset(zcol[:, :], 0.0), W=[Buf()])

    vecT = sb(st0, "vecT", [128, 72], F32)
    gm1 = sb(st0, "gm1", [128, 8], F32)
    st1 = contextlib.ExitStack()
    hT = sb(st1, "hT", [128, 8, S], BF16)
    B_hT = [Buf("hT%d" % i) for i in range(NT)]
    hTs = sb(st1, "hTs", [128, 8, NS], BF16)
    B_hTs = Buf()
    stAB = contextlib.ExitStack()
    mods = sb(stAB, "mods", [NS, 6 * D], F32)
    B_vecT, B_gm1, B_mods = Buf(), Buf(), Buf()
    rows = sb(stAB, "rows", [1, 9 * D], F32)
    B_rows = Buf()
    B_projs = Buf()
    mods_scr = nc.dram_tensor("mods_scr", [NS, 6 * D], F32, kind="Internal").ap()
    projs_scr = nc.dram_tensor("projs_scr", [NS, INW], F32, kind="Internal").ap()
    szt_scr = nc.dram_tensor("szt_scr", [S, D], F32, kind="Internal").ap()
    B_scr2 = Buf()
    q_scr = nc.dram_tensor("q_scr", [8, 128, S], BF16, kind="Internal").ap()
    kvT_scr = nc.dram_tensor("kvT_scr", [8, 128, S], BF16, kind="Internal").ap()
    gate_scr = nc.dram_tensor("gate_scr", [S, 48], F32, kind="Internal").ap()
    mg_scr = nc.dram_tensor("mg_scr", [S, 2048], F32, kind="Internal").ap()
    mix_scr = nc.dram_tensor("mix_scr", [S, D], F32, kind="Internal").ap()
    odns_scr = nc.dram_tensor("odns_scr", [NS, D], F32, kind="Internal").ap()
    mixs_scr = nc.dram_tensor("mixs_scr", [NS, D], F32, kind="Internal").ap()
    B_scr3 = Buf()

    with contextlib.ExitStack() as st:
        cin = sb(st, "cin", [NS, D], F32)
        cpin = sb(st, "cpin", [1, D], F32)
        cT = sb(st, "cT", [128, 8, 17], F32)
        bada = sb(st, "bada", [1, 6 * D], F32)
        wa = [sb(st, "wa%d" % i, [128, 8, 512], F32) for i in range(2)]
        B_cin, B_cpin, B_cT, B_bada = Buf(), Buf(), Buf(), Buf()
        B_wa = [Buf(), Buf()]
        k.dma("sp", cin[:, :], cs[:, :], W=[B_cin])
        k.dma("sp", cpin[:, :], cp[:, :], W=[B_cpin])
        k.dma("sp", bada[:, :], b_ada[:, :], W=[B_bada])
        k.dma("sp", rows[0:1, 6 * D:7 * D], g1[:, :], W=[B_rows])
        k.dma("sp", rows[0:1, 7 * D:8 * D], g2[:, :], W=[B_rows])
        k.dma("sp", rows[0:1, 8 * D:9 * D], gf[:, :], W=[B_rows])
        pt = PS[0]
        for j in range(8):
            k.op("pe", lambda e, j=j: e.transpose(out=pt[:, j * 17:j * 17 + 1], in_=cpin[0:1, j * 128:(j + 1) * 128],
                                                  identity=identf[0:1, 0:1]), R=[B_cpin, B_identf], W=[PSB[0]])
            k.op("pe", lambda e, j=j: e.transpose(out=pt[:, j * 17 + 1:j * 17 + 17], in_=cin[:, j * 128:(j + 1) * 128],
                                                  identity=identf[0:NS, 0:NS]), R=[B_cin, B_identf], W=[PSB[0]])
        k.op("dve", lambda e: e.tensor_copy(out=cT[:, :, :], in_=pt[:, 0:136].rearrange("p (j s) -> p j s", s=17)),
             R=[PSB[0]], W=[B_cT])
        wview = w_ada.rearrange("(kc p) n -> p kc n", p=128)
        for n in range(12):
            wb = wa[n % 2]
            k.dma("sp", wb[:, :, :], wview[:, :, n * 512:(n + 1) * 512], W=[B_wa[n % 2]])
            pa, pb = PS[1 + (n % 2) * 2], PS[2 + (n % 2) * 2]
            Ba, Bb = PSB[1 + (n % 2) * 2], PSB[2 + (n % 2) * 2]
            for kc in range(8):
                k.op("pe", lambda e, kc=kc, wb=wb, pa=pa: e.matmul(pa[0:1, :], lhsT=cT[:, kc, 0:1], rhs=wb[:, kc, :],
                                                                  start=(kc == 0), stop=False),
                     R=[B_cT, B_wa[n % 2]], W=[Ba])
            k.op("pe", lambda e, n=n, pa=pa: e.matmul(pa[0:1, :], lhsT=ones[0:1, 0:1], rhs=bada[0:1, n * 512:(n + 1) * 512],
                                                      start=False, stop=True), R=[B_ones, B_bada], W=[Ba])
            for kc in range(8):
                k.op("pe", lambda e, kc=kc, wb=wb, pb=pb: e.matmul(pb[0:NS, :], lhsT=cT[:, kc, 1:17], rhs=wb[:, kc, :],
                                                                  start=(kc == 0), stop=False),
                     R=[B_cT, B_wa[n % 2]], W=[Bb])
            k.op("pe", lambda e, n=n, pb=pb: e.matmul(pb[0:NS, :], lhsT=ones[0:1, 0:NS], rhs=bada[0:1, n * 512:(n + 1) * 512],
                                                      start=False, stop=True), R=[B_ones, B_bada], W=[Bb])
            k.op("dve", lambda e, n=n, pa=pa: e.tensor_copy(out=rows[0:1, n * 512:(n + 1) * 512], in_=pa[0:1, :]),
                 R=[Ba], W=[B_rows])
            k.op("act", lambda e, n=n, pb=pb: e.activation(out=mods[:, n * 512:(n + 1) * 512], in_=pb[0:NS, :], func=AF.Identity),
                 R=[Bb], W=[B_mods])
        pt = PS[5]
        for j in range(72):
            k.op("pe", lambda e, j=j: e.matmul(pt[:, j:j + 1], lhsT=rows[0:1, j * 128:(j + 1) * 128], rhs=ones[0:1, 0:1],
                                               start=True, stop=True), R=[B_rows, B_ones], W=[PSB[5]])
        k.op("dve", lambda e: e.tensor_copy(out=vecT[:, :], in_=pt[:, 0:72]), R=[PSB[5]], W=[B_vecT])
        k.op("dve", lambda e: e.scalar_tensor_tensor(out=gm1[:, :], in0=vecT[:, 8:16], scalar=1.0, in1=vecT[:, 48:56],
                                                     op0=ALU.add, op1=ALU.mult), R=[B_vecT], W=[B_gm1])
        k.barrier()

    with contextlib.ExitStack() as st:
        xt = [sb(st, "xt%d" % i, [128, D], F32) for i in range(3)]
        B_xt = [Buf() for _ in range(3)]
        junk = sb(st, "junk", [128, D], F32)
        B_junk = Buf()
        ssq = [sb(st, "ssq%d" % i, [128, 1], F32) for i in range(2)]
        B_ssq = [Buf(), Buf()]
        xn = [sb(st, "xn%d" % i, [128, D], BF16) for i in range(2)]
        B_xn = [Buf(), Buf()]
        for tt in range(NT):
            x_, bx = xt[tt % 3], B_xt[tt % 3]
            sq, bs = ssq[tt % 2], B_ssq[tt % 2]
            xn_, bn = xn[tt % 2], B_xn[tt % 2]
            k.dma("sp", x_[:, :], xp[tt * 128:(tt + 1) * 128, :], W=[bx])
            k.op("act", lambda e, x_=x_, sq=sq: e.activation(out=junk[:, :], in_=x_[:, :], func=AF.Square, accum_out=sq[:, :]),
                 R=[bx], W=[B_junk, bs])
            k.op("act", lambda e, sq=sq: e.activation(out=sq[:, :], in_=sq[:, :], func=AF.Sqrt, scale=1.0 / D, bias=1e-6),
                 R=[bs], W=[bs])
            k.op("dve", lambda e, sq=sq: e.reciprocal(out=sq[:, :], in_=sq[:, :]), R=[bs], W=[bs])
            k.op("dve", lambda e, x_=x_, sq=sq, xn_=xn_: e.tensor_scalar(out=xn_[:, :], in0=x_[:, :], scalar1=sq[:, 0:1],
                                                                       scalar2=None, op0=ALU.mult), R=[bx, bs], W=[bn])
            pi = tt % 2
            ptv = PS[pi][:, :].bitcast(BF16)
            for j in range(8):
                k.op("pe", lambda e, j=j, xn_=xn_, ptv=ptv: e.transpose(out=ptv[:, j * 128:(j + 1) * 128],
                                                                       in_=xn_[:, j * 128:(j + 1) * 128], identity=ident[:, :]),
                     R=[bn, B_ident], W=[PSB[pi]])
            for j in range(8):
                if j % 2 == 0:
                    k.op("act", lambda e, j=j, ptv=ptv, tt=tt: e.activation(out=hT[:, j, tt * 128:(tt + 1) * 128],
                                                                           in_=ptv[:, j * 128:(j + 1) * 128], func=AF.Identity,
                                                                           scale=gm1[:, j:j + 1], bias=vecT[:, j:j + 1]),
                         R=[PSB[pi], B_gm1, B_vecT], W=[B_hT[tt]])
                else:
                    k.op("dve", lambda e, j=j, ptv=ptv, tt=tt: e.tensor_scalar(out=hT[:, j, tt * 128:(tt + 1) * 128],
                                                                              in0=ptv[:, j * 128:(j + 1) * 128],
                                                                              scalar1=gm1[:, j:j + 1], scalar2=vecT[:, j:j + 1],
                                                                              op0=ALU.mult, op1=ALU.add),
                         R=[PSB[pi], B_gm1, B_vecT], W=[B_hT[tt]])
        xsi = sb(st, "xsi", [NS, D], F32)
        gms = sb(st, "gms", [NS, D], F32)
        hs = sb(st, "hs", [NS, D], F32)
        hsb = sb(st, "hsb", [NS, D], BF16)
        sqs = sb(st, "sqs", [NS, 1], F32)
        B_xsi, B_gms, B_hs, B_hsb, B_sqs = Buf(), Buf(), Buf(), Buf(), Buf()
        k.dma("sp", xsi[:, :], xs[:, :], W=[B_xsi])
        k.op("act", lambda e: e.activation(out=junk[0:NS, :], in_=xsi[:, :], func=AF.Square, accum_out=sqs[:, :]),
             R=[B_xsi], W=[B_junk, B_sqs])
        k.op("act", lambda e: e.activation(out=sqs[:, :], in_=sqs[:, :], func=AF.Sqrt, scale=1.0 / D, bias=1e-6),
             R=[B_sqs], W=[B_sqs])
        k.op("dve", lambda e: e.reciprocal(out=sqs[:, :], in_=sqs[:, :]), R=[B_sqs], W=[B_sqs])
        for h in range(2):
            k.op("pe", lambda e, h=h: e.matmul(PS[2 + h][0:NS, :], lhsT=ones[0:1, 0:NS],
                                               rhs=rows[0:1, 6 * D + h * 512:6 * D + (h + 1) * 512], start=True, stop=True),
                 R=[B_ones, B_rows], W=[PSB[2 + h]])
            k.op("dve", lambda e, h=h: e.scalar_tensor_tensor(out=gms[:, h * 512:(h + 1) * 512],
                                                              in0=mods[:, D + h * 512:D + (h + 1) * 512], scalar=1.0,
                                                              in1=PS[2 + h][0:NS, :], op0=ALU.add, op1=ALU.mult),
                 R=[B_mods, PSB[2 + h]], W=[B_gms])
        k.op("dve", lambda e: e.scalar_tensor_tensor(out=hs[:, :], in0=xsi[:, :], scalar=sqs[:, 0:1], in1=gms[:, :],
                                                     op0=ALU.mult, op1=ALU.mult), R=[B_xsi, B_sqs, B_gms], W=[B_hs])
        k.op("dve", lambda e: e.tensor_tensor(out=hsb[:, :], in0=hs[:, :], in1=mods[:, 0:D], op=ALU.add),
             R=[B_hs, B_mods], W=[B_hsb])
        ptv = PS[4][:, :].bitcast(BF16)
        for j in range(8):
            k.op("pe", lambda e, j=j: e.transpose(out=ptv[:, j * NS:(j + 1) * NS], in_=hsb[:, j * 128:(j + 1) * 128],
                                                  identity=ident[0:NS, 0:NS]), R=[B_hsb, B_ident], W=[PSB[4]])
        k.op("dve", lambda e: e.tensor_copy(out=hTs[:, :, :], in_=ptv[:, 0:8 * NS].rearrange("p (j s) -> p j s", s=NS)),
             R=[PSB[4]], W=[B_hTs])
        k.dma("sp", mods_scr[:, :], mods[:, :], R=[B_mods], W=[B_scr2])
        k.barrier()
    stAB.close()

    with contextlib.ExitStack() as st:
        proj_s = sb(st, "proj_s", [NS, INW], F32)
        wbuf = [sb(st, "wbuf%d" % i, [128, 8, 512], BF16) for i in range(2)]
        B_wbuf = [Buf(), Buf()]
        stage = [sb(st, "stage%d" % i, [128, 4, 512], F32) for i in range(2)]
        B_stage = [Buf(), Buf()]
        pcv = sb(st, "pcv", [3, 3072], F32)
        fstage = [sb(st, "fstage%d" % i, [128, S], BF16) for i in range(2)]
        B_fstage = [Buf(), Buf()]
        fsi = [0]
        B_pcv = Buf()
        wv = w_in.rearrange("(kc p) n -> p kc n", p=128)
        jobs = []
        for c in range(3):
            jobs.append(("kv", C_KV + c * 512, 512, c))
        for c in range(6):
            jobs.append(("qkv", C_QKV + c * 512, 512, c))
        for c in range(2):
            jobs.append(("z", 5680 + c * 512, 512, c))
        jobs.append(("s", 6704, 16, 0))
        if STOP >= 9:
            for c in range(2):
                jobs.append(("feat", c * 512, 512, [(cc, q_scr[c * 4 + cc], 0.125) for cc in range(4)]))
            jobs.append(("feat", 1024, 512, [(cc, kvT_scr[cc], 1.0) for cc in range(4)]))
            jobs.append(("feat", 1536, 256, [(cc, kvT_scr[4 + cc], 1.0) for cc in range(2)]))
            jobs.append(("feat", 2048, 256, [(cc, kvT_scr[6 + cc], 1.0) for cc in range(2)]))
            jobs.append(("gate", 2560, 48, 0))
            for c in range(4):
                jobs.append(("mg", 6720 + c * 512, 512, c))
        psi = 0
        evi = 0
        sti = 0
        for ji, (kind, c0, ncol, ci) in enumerate(jobs):
            wb, bw = wbuf[ji % 2], B_wbuf[ji % 2]
            k.dma("pool", wb[:, :, 0:ncol], wv[:, :, c0:c0 + ncol], W=[bw])
            p_, bp = PS[psi % 8], PSB[psi % 8]
            psi += 1
            for kc in range(8):
                k.op("pe", lambda e, kc=kc, wb=wb, p_=p_, ncol=ncol: e.matmul(p_[0:NS, 0:ncol], lhsT=hTs[:, kc, :],
                                                                              rhs=wb[:, kc, 0:ncol], start=(kc == 0), stop=(kc == 7)),
                     R=[B_hTs, bw], W=[bp])
            k.op("act", lambda e, p_=p_, c0=c0, ncol=ncol: e.activation(out=proj_s[:, c0:c0 + ncol], in_=p_[0:NS, 0:ncol],
                                                                        func=AF.Identity), R=[bp], W=[B_projs])
            if kind == "feat":
                for (cc, dst, scl) in ci:
                    fs, bfs = fstage[fsi[0] % 2], B_fstage[fsi[0] % 2]
                    fsi[0] += 1
                    for tb in range(S // 512):
                        p_, bp = PS[psi % 8], PSB[psi % 8]
                        psi += 1
                        for kc in range(8):
                            k.op("pe", lambda e, kc=kc, wb=wb, p_=p_, tb=tb, cc=cc: e.matmul(p_[:, :], lhsT=wb[:, kc, cc * 128:(cc + 1) * 128],
                                                                                            rhs=hT[:, kc, tb * 512:(tb + 1) * 512], start=(kc == 0), stop=(kc == 7)),
                                 R=[B_hT[tb * 4 + i] for i in range(4)] + [bw], W=[bp])
                        if tb % 2 == 0:
                            k.op("act", lambda e, p_=p_, fs=fs, tb=tb, scl=scl: e.activation(out=fs[:, tb * 512:(tb + 1) * 512], in_=p_[:, :], func=AF.Identity, scale=scl),
                                 R=[bp], W=[bfs])
                        else:
                            k.op("dve", lambda e, p_=p_, fs=fs, tb=tb, scl=scl: e.tensor_scalar(out=fs[:, tb * 512:(tb + 1) * 512], in0=p_[:, :], scalar1=scl, scalar2=None, op0=ALU.mult),
                                 R=[bp], W=[bfs])
                    k.dma("sp", dst, fs[:, :], R=[bfs], W=[B_scr2])
            if kind in ("gate", "mg"):
                for tt in range(NT):
                    p_, bp = PS[psi % 8], PSB[psi % 8]
                    psi += 1
                    for kc in range(8):
                        k.op("pe", lambda e, kc=kc, wb=wb, p_=p_, tt=tt, ncol=ncol: e.matmul(p_[:, 0:ncol], lhsT=hT[:, kc, tt * 128:(tt + 1) * 128],
                                                                                            rhs=wb[:, kc, 0:ncol], start=(kc == 0), stop=(kc == 7)),
                             R=[B_hT[tt], bw], W=[bp])
                    sg, bsg = stage[sti % 2], B_stage[sti % 2]
                    t4 = tt % 4
                    k.op("act", lambda e, p_=p_, sg=sg, t4=t4, ncol=ncol: e.activation(out=sg[:, t4, 0:ncol], in_=p_[:, 0:ncol], func=AF.Sigmoid), R=[bp], W=[bsg])
                    if t4 == 3:
                        g = tt // 4
                        if kind == "gate":
                            k.dma("sp", gate_scr[g * 512:(g + 1) * 512, :].rearrange("(t p) d -> p t d", p=128), sg[:, :, 0:48], R=[bsg], W=[B_scr2])
                        else:
                            k.dma("sp", mg_scr[g * 512:(g + 1) * 512, ci * 512:(ci + 1) * 512].rearrange("(t p) d -> p t d", p=128), sg[:, :, :], R=[bsg], W=[B_scr2])
                        sti += 1
            if kind in ("kv", "z"):
                for tt in range(NT):
                    p_, bp = PS[psi % 8], PSB[psi % 8]
                    psi += 1
                    for kc in range(8):
                        k.op("pe", lambda e, kc=kc, wb=wb, p_=p_, tt=tt: e.matmul(p_[:, :], lhsT=hT[:, kc, tt * 128:(tt + 1) * 128],
                                                                                rhs=wb[:, kc, :], start=(kc == 0), stop=(kc == 7)),
                             R=[B_hT[tt], bw], W=[bp])
                    sg, bsg = stage[sti % 2], B_stage[sti % 2]
                    t4 = tt % 4
                    if evi % 2 == 0 or kind == "z":
                        k.op("act", lambda e, p_=p_, sg=sg, t4=t4, kind=kind: e.activation(out=sg[:, t4, :], in_=p_[:, :],
                                                                                        func=(AF.Silu if kind == "z" else AF.Identity)),
                             R=[bp], W=[bsg])
                    else:
                        k.op("dve", lambda e, p_=p_, sg=sg, t4=t4: e.tensor_copy(out=sg[:, t4, :], in_=p_[:, :]),
                             R=[bp], W=[bsg])
                    evi += 1
                    if t4 == 3 and kind == "z":
                        g = tt // 4
                        k.dma("sp", szt_scr[g * 512:(g + 1) * 512, ci * 512:(ci + 1) * 512].rearrange("(t p) d -> p t d", p=128),
                              sg[:, :, :], R=[bsg], W=[B_scr2])
                        sti += 1
                    elif t4 == 3:
                        g = tt // 4
                        for half in range(2):
                            k.dma("sp", o_pkv[2 * ci + half, g * 512:(g + 1) * 512, :].rearrange("(t p) d -> p t d", p=128),
                                  sg[:, :, half * 256:(half + 1) * 256], R=[bsg], W=[DOUT])
                        sti += 1
            elif kind == "qkv":
                p_, bp = PS[psi % 8], PSB[psi % 8]
                psi += 1
                for kc in range(8):
                    k.op("pe", lambda e, kc=kc, wb=wb, p_=p_: e.matmul(p_[0:3, :], lhsT=hT[:, kc, S - 3:S], rhs=wb[:, kc, :],
                                                                      start=(kc == 0), stop=(kc == 7)),
                         R=[B_hT[NT - 1], bw], W=[bp])
                k.op("dve", lambda e, p_=p_, ci=ci: e.tensor_copy(out=pcv[:, ci * 512:(ci + 1) * 512], in_=p_[0:3, :]),
                     R=[bp], W=[B_pcv])
        k.dma("sp", o_pconv[:, :], pcv[:, :], R=[B_pcv], W=[DOUT])
        k.dma("sp", o_skv[:, :], proj_s[:, C_KV:C_KV + 1536], R=[B_projs], W=[DOUT])
        k.dma("sp", o_sconv[:, 2, :], proj_s[:, C_QKV:C_QKV + 3072], R=[B_projs], W=[DOUT])
        k.dma("sp", o_sconv[:, 0:2, :], sconv[:, 1:3, :], W=[DOUT])
        k.dma("sp", o_skwin[:, 511, :], proj_s[:, C_KV + 1024:C_KV + 1280], R=[B_projs], W=[DOUT])
        k.dma("sp", o_svwin[:, 511, :], proj_s[:, C_KV + 1280:C_KV + 1536], R=[B_projs], W=[DOUT])
        k.dma("sp", projs_scr[:, :], proj_s[:, :], R=[B_projs], W=[B_scr2])
        for q4 in range(4):
            k.dma("sp", o_skwin[q4 * 4:(q4 + 1) * 4, 0:511, :], ckwin[q4 * 4:(q4 + 1) * 4, 1:512, :], W=[DOUT])
            k.dma("pool", o_svwin[q4 * 4:(q4 + 1) * 4, 0:511, :], cvwin[q4 * 4:(q4 + 1) * 4, 1:512, :], W=[DOUT])
        k.barrier()
    with contextlib.ExitStack() as st:
        NEG = -1.0e9
        wqkv = sb(st, "wqkv", [128, 8, 3072], BF16)
        wab = sb(st, "wab", [128, 8, 16], BF16)
        B_wqkv, B_wab = Buf(), Buf()
        for c in range(6):
            k.dma("pool", wqkv[:, :, c * 512:(c + 1) * 512], wv[:, :, C_QKV + c * 512:C_QKV + (c + 1) * 512], W=[B_wqkv])
        k.dma("pool", wab[:, :, :], wv[:, :, 6704:6720], W=[B_wab])
        cwT = sb(st, "cwT", [128, 24, 4], F32)
        dtb = sb(st, "dtb", [128, 8], F32)
        nAe = sb(st, "nAe", [128, 8], F32)
        ngb = sb(st, "ngb", [128, 128], F32)
        onesb = sb(st, "onesb", [128, 128], BF16)
        onesf = sb(st, "onesf", [128, 128], F32)
        maskA = sb(st, "maskA", [128, 128], F32)
        maskM = sb(st, "maskM", [128, 128], F32)
        Uc = sb(st, "Uc", [128, 128], F32)
        Uf = sb(st, "Uf", [128, 128], F32)
        Ue0 = sb(st, "Ue0", [128, 128], F32)
        Ue1 = sb(st, "Ue1", [128, 128], F32)
        B_c = Buf()
        k.dma("sp", dtb[:, :], dt_bias[0, :].partition_broadcast(128), W=[B_c])
        k.dma("sp", nAe[:, :], a_log[0, :].partition_broadcast(128), W=[B_c])
        k.dma("sp", ngb[:, :], dn_ng[0, :].partition_broadcast(128), W=[B_c])
        k.op("act", lambda e: e.activation(out=nAe[:, :], in_=nAe[:, :], func=AF.Exp), R=[B_c], W=[B_c])
        k.op("dve", lambda e: e.tensor_scalar(out=nAe[:, :], in0=nAe[:, :], scalar1=-1.0, scalar2=None, op0=ALU.mult), R=[B_c], W=[B_c])
        k.op("pool", lambda e: e.memset(onesb[:, :], 1.0), W=[B_c])
        k.op("pool", lambda e: e.memset(onesf[:, :], 1.0), W=[B_c])
        for (m_, base_) in ((maskA, 0), (maskM, -1)):
            k.op("pool", lambda e, m_=m_: e.memset(m_[:, :], 0.0), R=[B_c], W=[B_c])
            k.op("pool", lambda e, m_=m_, base_=base_: e.affine_select(out=m_[:, :], in_=m_[:, :], pattern=[[1, 128]], compare_op=ALU.is_ge,
                                                                     fill=NEG, base=base_, channel_multiplier=-1), R=[B_c], W=[B_c])
            k.op("pool", lambda e, m_=m_: e.memset(m_[0:64, 64:128], NEG), R=[B_c], W=[B_c])
        k.op("pool", lambda e: e.memset(Uc[:, :], 1.0), R=[B_c], W=[B_c])
        k.op("pool", lambda e: e.affine_select(out=Uc[:, :], in_=Uc[:, :], pattern=[[1, 128]], compare_op=ALU.is_ge,
                                               fill=0.0, base=0, channel_multiplier=-1), R=[B_c], W=[B_c])
        k.op("pool", lambda e: e.memset(Uc[0:64, 64:128], 0.0), R=[B_c], W=[B_c])
        k.op("pool", lambda e: e.memset(Uf[:, :], 0.0), R=[B_c], W=[B_c])
        k.op("pool", lambda e: e.memset(Uf[0:64, 0:64], 1.0), R=[B_c], W=[B_c])
        k.op("pool", lambda e: e.memset(Uf[64:128, 64:128], 1.0), R=[B_c], W=[B_c])
        k.op("pool", lambda e: e.memset(Ue0[:, :], 0.0), R=[B_c], W=[B_c])
        k.op("pool", lambda e: e.memset(Ue0[0:64, :], 1.0), R=[B_c], W=[B_c])
        k.op("pool", lambda e: e.memset(Ue1[:, :], 0.0), R=[B_c], W=[B_c])
        k.op("pool", lambda e: e.memset(Ue1[64:128, :], 1.0), R=[B_c], W=[B_c])
        k.barrier()

        with contextlib.ExitStack() as stc:
            cw4 = sb(stc, "cw4", [4, 3072], F32)
            k.dma("sp", cw4[:, :], conv_w[:, :], W=[B_c])
            ptc = PS[0]
            for j in range(24):
                k.op("pe", lambda e, j=j: e.transpose(out=ptc[:, j * 4:(j + 1) * 4], in_=cw4[0:4, j * 128:(j + 1) * 128],
                                                      identity=identf[0:4, 0:4]), R=[B_c, B_identf], W=[PSB[0]])
            k.op("dve", lambda e: e.tensor_copy(out=cwT[:, :, :], in_=ptc[:, 0:96].rearrange("p (j w) -> p j w", w=4)), R=[PSB[0]], W=[B_c])
            k.barrier()
        pslot = [0]

        def ps_half():
            i = pslot[0] % 8
            pslot[0] += 1
            return PS[i][:, 0:256], PSB[i]

        def ps_full():
            i = pslot[0] % 8
            pslot[0] += 1
            return PS[i][:, :], [PSB[i]]

        halo = sb(st, "halo", [128, 24, 3], F32)
        B_halo = [Buf() for _ in range(24)]
        k.op("pool", lambda e: e.memset(halo[:, :, :], 0.0), W=B_halo)
        xc = [sb(st, "xc0", [128, 515], F32)] * 2
        B_xc = [Buf()] * 2
        acc = [sb(st, "acc0", [128, 512], F32)] * 2
        B_acc = [Buf()] * 2
        ysl = [sb(st, "ysl0", [128, 512], F32)] * 2
        B_ysl = [Buf()] * 2
        sqb = [sb(st, "sqb0", [128, 512], BF16)] * 2
        B_sqb = [Buf()] * 2
        lnt = [sb(st, "lnt0", [128, 512], F32)] * 2
        B_lnt = [Buf()] * 2
        qkvT = [sb(st, "qkvT%d" % i, [128, 24, 512], BF16) for i in range(1)]
        B_qkvT = [[Buf() for _ in range(24)] for _ in range(1)]
        Sst = sb(st, "Sst", [128, 8, 128], F32)
        Sbf = sb(st, "Sbf", [128, 8, 128], BF16)
        B_S = [Buf() for _ in range(8)]
        B_Sbf = [Buf() for _ in range(8)]
        k.op("pool", lambda e: e.memset(Sst[:, :, :], 0.0), W=B_S)
        k.op("pool", lambda e: e.memset(Sbf[:, :, :], 0.0), W=B_Sbf)
        sm = [sb(st, "sm%d" % i, [128, 96], F32) for i in range(2)]
        B_sm = [Buf(), Buf()]
        NW = 2
        dg = [sb(st, "dg%d" % i, [128, 2, 128], F32) for i in range(NW)]
        GG = dg
        DD = [sb(st, "DD%d" % i, [128, 2, 128], F32) for i in range(NW)]
        EBt = [sb(st, "EB%d" % i, [128, 128], F32) for i in range(NW)]
        MR = [[sb(st, "MR%d_%d" % (i, j), [128, 2, 128], F32) for j in range(2)] for i in range(NW)]
        LL = [[sb(st, "LL%d_%d" % (i, j), [128, 128], F32) for j in range(2)] for i in range(NW)]
        kbg = [sb(st, "kbg%d" % i, [128, 128], F32) for i in range(NW)]
        vb = [sb(st, "vb%d" % i, [128, 128], F32) for i in range(NW)]
        B_dg = [Buf() for _ in range(NW)]
        B_GG = B_dg
        B_DD = [Buf() for _ in range(NW)]
        B_EB = [Buf() for _ in range(NW)]
        B_MR = [[Buf(), Buf()] for _ in range(NW)]
        B_LL = [[Buf(), Buf()] for _ in range(NW)]
        B_kbg = [Buf() for _ in range(NW)]
        B_vb = [Buf() for _ in range(NW)]
        AT = [[sb(st, "AT_%d" % h, [128, 128], BF16) for h in range(8)]] * 2
        usb = [[sb(st, "usb_%d" % h, [128, 128], F32) for h in range(8)]] * 2
        wTs = [[sb(st, "wTs_%d" % h, [128, 128], F32) for h in range(8)]] * 2
        qdT = [[sb(st, "qdT_%d" % h, [128, 128], BF16) for h in range(8)]] * 2
        kdc = [[sb(st, "kdc_%d" % h, [128, 128], F32) for h in range(8)]] * 2
        B_AT = [[Buf() for _ in range(8)]] * 2
        B_usb = [[Buf() for _ in range(8)]] * 2
        B_wTs = [[Buf() for _ in range(8)]] * 2
        B_qdT = [[Buf() for _ in range(8)]] * 2
        B_kdc = [[Buf() for _ in range(8)]] * 2
        vnw = [sb(st, "vnw%d" % h, [128, 128], F32) for h in range(8)]
        vnb = [sb(st, "vnb%d" % h, [128, 128], BF16) for h in range(8)]
        B_vnw = [Buf() for _ in range(8)]
        B_vnb = [Buf() for _ in range(8)]
        otile = [sb(st, "otile%d" % i, [128, 8, 128], F32) for i in range(1)] * 2
        B_ot = [[Buf() for _ in range(8)]] * 2
        orn = [sb(st, "orn%d" % i, [128, 8], F32) for i in range(2)]
        B_orn = [Buf(), Buf()]
        szt = [sb(st, "szt0", [128, 1024], F32)] * 2
        B_sz = [Buf()] * 2
        odn = [sb(st, "odn0", [128, 1024], F32)] * 2
        B_odn = [Buf()] * 2
        B_scr = Buf()
        wi = [0]

        for gq in range(int(os.environ.get('DN_NG', '8'))):
            qb = qkvT[0]
            Bq = B_qkvT[0]
            hR = [B_hT[gq * 4 + i] for i in range(4)]
            for j in range(24):
                pf, bpf = ps_full()
                for kc in range(8):
                    k.op("pe", lambda e, kc=kc, j=j, pf=pf: e.matmul(pf, lhsT=wqkv[:, kc, j * 128:(j + 1) * 128],
                                                                    rhs=hT[:, kc, gq * 512:(gq + 1) * 512], start=(kc == 0), stop=(kc == 7)),
                         R=hR + [B_wqkv], W=bpf)
                x_, bx = xc[j % 2], B_xc[j % 2]
                a_, ba = acc[j % 2], B_acc[j % 2]
                y_, by = ysl[j % 2], B_ysl[j % 2]
                k.op("act", lambda e, x_=x_, pf=pf: e.activation(out=x_[:, 3:515], in_=pf, func=AF.Identity), R=bpf, W=[bx])
                k.op("pool", lambda e, x_=x_, j=j: e.tensor_copy(out=x_[:, 0:3], in_=halo[:, j, :]), R=[B_halo[j]], W=[bx])
                k.op("pool", lambda e, x_=x_, j=j: e.tensor_copy(out=halo[:, j, :], in_=x_[:, 512:515]), R=[bx], W=[B_halo[j]])
                k.op("dve", lambda e, x_=x_, a_=a_, j=j: e.tensor_scalar(out=a_[:, :], in0=x_[:, 0:512], scalar1=cwT[:, j, 0:1], scalar2=None,
                                                                       op0=ALU.mult), R=[bx, B_c], W=[ba])
                for w_ in range(1, 4):
                    k.op("dve", lambda e, x_=x_, a_=a_, j=j, w_=w_: e.scalar_tensor_tensor(out=a_[:, :], in0=x_[:, w_:w_ + 512],
                                                                                         scalar=cwT[:, j, w_:w_ + 1], in1=a_[:, :],
                                                                                         op0=ALU.mult, op1=ALU.add), R=[bx, B_c, ba], W=[ba])
                if j >= 16:
                    k.op("act", lambda e, a_=a_, j=j: e.activation(out=qb[:, j, :], in_=a_[:, :], func=AF.Silu), R=[ba], W=[Bq[j]])
                else:
                    s_, bs_ = sqb[j % 2], B_sqb[j % 2]
                    l_, bl_ = lnt[j % 2], B_lnt[j % 2]
                    k.op("act", lambda e, a_=a_, y_=y_: e.activation(out=y_[:, :], in_=a_[:, :], func=AF.Silu), R=[ba], W=[by])
                    k.op("act", lambda e, y_=y_, s_=s_: e.activation(out=s_[:, :], in_=y_[:, :], func=AF.Square), R=[by], W=[bs_])
                    pf2, bpf2 = ps_full()
                    k.op("pe", lambda e, s_=s_, pf2=pf2: e.matmul(pf2, lhsT=onesb[:, :], rhs=s_[:, :], start=True, stop=True),
                         R=[bs_, B_c], W=bpf2)
                    k.op("act", lambda e, l_=l_, pf2=pf2: e.activation(out=l_[:, :], in_=pf2, func=AF.Ln, bias=1e-6), R=bpf2, W=[bl_])
                    k.op("act", lambda e, l_=l_: e.activation(out=l_[:, :], in_=l_[:, :], func=AF.Exp, scale=-0.5), R=[bl_], W=[bl_])
                    sc_ = (128.0 ** -0.5) if j < 8 else 1.0
                    k.op("dve", lambda e, y_=y_, l_=l_, j=j, sc_=sc_: e.scalar_tensor_tensor(out=qb[:, j, :], in0=y_[:, :], scalar=sc_,
                                                                                           in1=l_[:, :], op0=ALU.mult, op1=ALU.mult),
                         R=[by, bl_], W=[Bq[j]])
            if STOP < 2:
                continue
            for pl in range(4):
                p = gq * 4 + pl
                par = p % 2
                tsl = slice(pl * 128, (pl + 1) * 128)
                tok = slice(p * 128, (p + 1) * 128)
                s_, bsm = sm[par], B_sm[par]
                ph, bph = ps_half()
                for kc in range(8):
                    k.op("pe", lambda e, kc=kc, ph=ph: e.matmul(ph[:, 0:16], lhsT=hT[:, kc, tok], rhs=wab[:, kc, :], start=(kc == 0), stop=(kc == 7)),
                         R=[B_hT[p], B_wab], W=[bph])
                k.op("dve", lambda e, ph=ph, s_=s_: e.tensor_tensor(out=s_[:, 0:8], in0=ph[:, 0:8], in1=dtb[:, :], op=ALU.add), R=[bph, B_c], W=[bsm])
                k.op("act", lambda e, s_=s_: e.activation(out=s_[:, 0:8], in_=s_[:, 0:8], func=AF.Exp), R=[bsm], W=[bsm])
                k.op("act", lambda e, s_=s_: e.activation(out=s_[:, 0:8], in_=s_[:, 0:8], func=AF.Ln, bias=1.0), R=[bsm], W=[bsm])
                k.op("dve", lambda e, s_=s_: e.tensor_tensor(out=s_[:, 0:8], in0=s_[:, 0:8], in1=nAe[:, :], op=ALU.mult), R=[bsm, B_c], W=[bsm])
                k.op("act", lambda e, ph=ph, s_=s_: e.activation(out=s_[:, 8:16], in_=ph[:, 8:16], func=AF.Sigmoid), R=[bph], W=[bsm])
                k.op("act", lambda e, s_=s_: e.activation(out=s_[:, 16:24], in_=s_[:, 8:16], func=AF.Ln), R=[bsm], W=[bsm])
                ph2, bph2 = ps_half()
                for ui, U_ in enumerate((Uc, Uf, Ue0, Ue1)):
                    k.op("pe", lambda e, ui=ui, U_=U_, ph2=ph2, s_=s_: e.matmul(ph2[:, ui * 8:(ui + 1) * 8], lhsT=U_[:, :], rhs=s_[:, 0:8],
                                                                             start=True, stop=True), R=[bsm, B_c], W=[bph2])
                k.op("dve", lambda e, ph2=ph2, s_=s_: e.tensor_copy(out=s_[:, 24:32], in_=ph2[:, 0:8]), R=[bph2], W=[bsm])
                k.op("dve", lambda e, s_=s_: e.tensor_scalar(out=s_[:, 32:40], in0=s_[:, 24:32], scalar1=-1.0, scalar2=None, op0=ALU.mult), R=[bsm], W=[bsm])
                k.op("dve", lambda e, s_=s_: e.tensor_tensor(out=s_[:, 40:48], in0=s_[:, 24:32], in1=s_[:, 16:24], op=ALU.add), R=[bsm], W=[bsm])
                k.op("act", lambda e, s_=s_: e.activation(out=s_[:, 48:56], in_=s_[:, 24:32], func=AF.Exp), R=[bsm], W=[bsm])
                k.op("dve", lambda e, s_=s_: e.tensor_tensor(out=s_[:, 48:56], in0=s_[:, 48:56], in1=s_[:, 8:16], op=ALU.mult), R=[bsm], W=[bsm])
                k.op("dve", lambda e, ph2=ph2, s_=s_: e.tensor_tensor(out=s_[:, 56:64], in0=ph2[:, 8:16], in1=s_[:, 24:32], op=ALU.subtract), R=[bph2, bsm], W=[bsm])
                k.op("act", lambda e, s_=s_: e.activation(out=s_[:, 56:64], in_=s_[:, 56:64], func=AF.Exp), R=[bsm], W=[bsm])
                k.op("act", lambda e, ph2=ph2, s_=s_: e.activation(out=s_[:, 64:80], in_=ph2[:, 16:32], func=AF.Exp), R=[bph2], W=[bsm])
                sz_, bsz = szt[par], B_sz[par]
                k.dma("sp", sz_[:, :], szt_scr[tok, :], R=[B_scr2], W=[bsz])
                for h in range(8 if STOP >= 3 else 0):
                    w = wi[0] % NW
                    wi[0] += 1
                    qT_ = qb[:, h, tsl]
                    kT_ = qb[:, 8 + h, tsl]
                    vT_ = qb[:, 16 + h, tsl]
                    Rq = [Bq[h]]
                    Rk = [Bq[8 + h]]
                    Rv = [Bq[16 + h]]
                    k.op("dve", lambda e, w=w, s_=s_, h=h: e.tensor_scalar(out=dg[w][:, 0, :], in0=identf[:, :], scalar1=s_[:, 24 + h:25 + h],
                                                                         scalar2=None, op0=ALU.mult), R=[bsm, B_identf], W=[B_dg[w]])
                    k.op("dve", lambda e, w=w, s_=s_, h=h: e.tensor_scalar(out=dg[w][:, 1, :], in0=identf[:, :], scalar1=s_[:, 40 + h:41 + h],
                                                                         scalar2=None, op0=ALU.mult), R=[bsm, B_identf], W=[B_dg[w]])
                    pbc, bpbc = ps_half()
                    k.op("pe", lambda e, w=w, pbc=pbc: e.matmul(pbc, lhsT=onesf[:, :], rhs=dg[w][:, :, :].rearrange("p a b -> p (a b)"),
                                                               start=True, stop=True), R=[B_dg[w], B_c], W=[bpbc])
                    k.op("dve", lambda e, w=w, pbc=pbc, s_=s_, h=h: e.scalar_tensor_tensor(out=GG[w][:, 0, :], in0=pbc[:, 0:128], scalar=s_[:, 32 + h:33 + h],
                                                                                         in1=maskA[:, :], op0=ALU.add, op1=ALU.add),
                         R=[bpbc, bsm, B_c], W=[B_GG[w]])
                    k.op("dve", lambda e, w=w, pbc=pbc, s_=s_, h=h: e.scalar_tensor_tensor(out=GG[w][:, 1, :], in0=pbc[:, 128:256], scalar=s_[:, 32 + h:33 + h],
                                                                                         in1=maskM[:, :], op0=ALU.add, op1=ALU.add),
                         R=[bpbc, bsm, B_c], W=[B_GG[w]])
                    k.op("act", lambda e, w=w: e.activation(out=DD[w][:, :, :], in_=GG[w][:, :, :], func=AF.Exp), R=[B_GG[w]], W=[B_DD[w]])
                    k.op("act", lambda e, w=w, pbc=pbc: e.activation(out=EBt[w][:, :], in_=pbc[:, 0:128], func=AF.Exp), R=[bpbc], W=[B_EB[w]])
                    k.op("dve", lambda e, w=w, qT_=qT_, par=par, h=h: e.tensor_tensor(out=qdT[par][h][:, :], in0=qT_, in1=EBt[w][:, :], op=ALU.mult),
                         R=Rq + [B_EB[w]], W=[B_qdT[par][h]])
                    pkq, bpkq = ps_half()
                    k.op("pe", lambda e, pkq=pkq, kT_=kT_, qT_=qT_: e.matmul(pkq[:, 0:128], lhsT=kT_, rhs=qT_, start=True, stop=True), R=Rk + Rq, W=[bpkq])
                    k.op("pe", lambda e, pkq=pkq, kT_=kT_: e.matmul(pkq[:, 128:256], lhsT=kT_, rhs=kT_, start=True, stop=True), R=Rk, W=[bpkq])
                    k.op("dve", lambda e, w=w, pkq=pkq, par=par, h=h: e.tensor_tensor(out=AT[par][h][:, :], in0=pkq[:, 0:128], in1=DD[w][:, 0, :], op=ALU.mult),
                         R=[bpkq, B_DD[w]], W=[B_AT[par][h]])
                    k.op("dve", lambda e, w=w, pkq=pkq: e.tensor_tensor(out=MR[w][0][:, 0, :], in0=pkq[:, 128:256], in1=DD[w][:, 1, :], op=ALU.mult),
                         R=[bpkq, B_DD[w]], W=[B_MR[w][0]])
                    k.op("dve", lambda e, w=w: e.tensor_tensor(out=MR[w][0][:, 1, :], in0=identf[:, :], in1=MR[w][0][:, 0, :], op=ALU.subtract),
                         R=[B_identf, B_MR[w][0]], W=[B_MR[w][0]])
                    ptr, bptr = ps_half()
                    ptrb = ptr
                    k.op("pe", lambda e, w=w, ptrb=ptrb: e.transpose(out=ptrb[:, 0:128], in_=MR[w][0][:, 0, :], identity=identf[:, :]),
                         R=[B_MR[w][0], B_identf], W=[bptr])
                    k.op("act", lambda e, w=w, ptrb=ptrb: e.activation(out=LL[w][0][:, :], in_=ptrb[:, 0:128], func=AF.Identity), R=[bptr], W=[B_LL[w][0]])
                    cur = 0
                    for lev in range(6):
                        nxt = 1 - cur
                        if lev == 0:
                            pm, bpm = ps_half()
                            k.op("pe", lambda e, w=w, cur=cur, pm=pm: e.matmul(pm[:, 0:128], lhsT=LL[w][cur][:, :], rhs=MR[w][cur][:, 0, :], start=True, stop=True),
                                 R=[B_LL[w][cur], B_MR[w][cur]], W=[bpm])
                            k.op("pe", lambda e, w=w, cur=cur, pm=pm: e.matmul(pm[:, 128:256], lhsT=MR[w][cur][:, 0, :], rhs=LL[w][cur][:, :], start=True, stop=True),
                                 R=[B_LL[w][cur], B_MR[w][cur]], W=[bpm])
                            k.op("act", lambda e, w=w, nxt=nxt, pm=pm: e.activation(out=MR[w][nxt][:, 0, :], in_=pm[:, 0:128], func=AF.Identity), R=[bpm], W=[B_MR[w][nxt]])
                            k.op("dve", lambda e, w=w, cur=cur, nxt=nxt: e.tensor_copy(out=MR[w][nxt][:, 1, :], in_=MR[w][cur][:, 1, :]), R=[B_MR[w][cur]], W=[B_MR[w][nxt]])
                            k.op("act", lambda e, w=w, nxt=nxt, pm=pm: e.activation(out=LL[w][nxt][:, :], in_=pm[:, 128:256], func=AF.Identity), R=[bpm], W=[B_LL[w][nxt]])
                        elif lev < 5:
                            pm, bpm = ps_half()
                            pl2, bpl2 = ps_half()
                            k.op("pe", lambda e, w=w, cur=cur, pm=pm: e.matmul(pm, lhsT=LL[w][cur][:, :], rhs=MR[w][cur][:, :, :].rearrange("p a b -> p (a b)"),
                                                                              start=True, stop=True), R=[B_LL[w][cur], B_MR[w][cur]], W=[bpm])
                            k.op("pe", lambda e, w=w, cur=cur, pl2=pl2: e.matmul(pl2[:, 0:128], lhsT=MR[w][cur][:, 0, :], rhs=LL[w][cur][:, :], start=True, stop=True),
                                 R=[B_LL[w][cur], B_MR[w][cur]], W=[bpl2])
                            k.op("act", lambda e, w=w, nxt=nxt, pm=pm: e.activation(out=MR[w][nxt][:, 0, :], in_=pm[:, 0:128], func=AF.Identity), R=[bpm], W=[B_MR[w][nxt]])
                            k.op("dve", lambda e, w=w, cur=cur, nxt=nxt, pm=pm: e.tensor_tensor(out=MR[w][nxt][:, 1, :], in0=pm[:, 128:256], in1=MR[w][cur][:, 1, :], op=ALU.add),
                                 R=[bpm, B_MR[w][cur]], W=[B_MR[w][nxt]])
                            k.op("act", lambda e, w=w, nxt=nxt, pl2=pl2: e.activation(out=LL[w][nxt][:, :], in_=pl2[:, 0:128], func=AF.Identity), R=[bpl2], W=[B_LL[w][nxt]])
                        else:
                            pm, bpm = ps_half()
                            k.op("pe", lambda e, w=w, cur=cur, pm=pm: e.matmul(pm[:, 0:128], lhsT=LL[w][cur][:, :], rhs=MR[w][cur][:, 1, :], start=True, stop=True),
                                 R=[B_LL[w][cur], B_MR[w][cur]], W=[bpm])
                            k.op("dve", lambda e, w=w, cur=cur, nxt=nxt, pm=pm: e.tensor_tensor(out=MR[w][nxt][:, 1, :], in0=pm[:, 0:128], in1=MR[w][cur][:, 1, :], op=ALU.add),
                                 R=[bpm, B_MR[w][cur]], W=[B_MR[w][nxt]])
                        cur = nxt
                    Rfin = MR[w][cur][:, 1, :]
                    B_Rfin = B_MR[w][cur]
                    pkt, bpkt = ps_half()
                    pktb = pkt.bitcast(BF16)
                    k.op("pe", lambda e, pktb=pktb, kT_=kT_: e.transpose(out=pktb[:, 0:128], in_=kT_, identity=ident[:, :]), R=Rk + [B_ident], W=[bpkt])
                    k.op("pe", lambda e, pktb=pktb, vT_=vT_: e.transpose(out=pktb[:, 128:256], in_=vT_, identity=ident[:, :]), R=Rv + [B_ident], W=[bpkt])
                    k.op("act", lambda e, w=w, pktb=pktb, s_=s_, h=h: e.activation(out=kbg[w][:, :], in_=pktb[:, 0:128], func=AF.Identity, scale=s_[:, 48 + h:49 + h]),
                         R=[bpkt, bsm], W=[B_kbg[w]])
                    k.op("act", lambda e, pktb=pktb, s_=s_, h=h, par=par: e.activation(out=kdc[par][h][:, :], in_=pktb[:, 0:128], func=AF.Identity, scale=s_[:, 56 + h:57 + h]),
                         R=[bpkt, bsm], W=[B_kdc[par][h]])
                    k.op("act", lambda e, w=w, pktb=pktb, s_=s_, h=h: e.activation(out=vb[w][:, :], in_=pktb[:, 128:256], func=AF.Identity, scale=s_[:, 8 + h:9 + h]),
                         R=[bpkt, bsm], W=[B_vb[w]])
                    puw, bpuw = ps_half()
                    k.op("pe", lambda e, w=w, puw=puw, Rfin=Rfin: e.matmul(puw[:, 0:128], lhsT=Rfin, rhs=vb[w][:, :], start=True, stop=True),
                         R=[B_Rfin, B_vb[w]], W=[bpuw])
                    k.op("pe", lambda e, w=w, puw=puw, Rfin=Rfin: e.matmul(puw[:, 128:256], lhsT=kbg[w][:, :], rhs=Rfin, start=True, stop=True),
                         R=[B_Rfin, B_kbg[w]], W=[bpuw])
                    k.op("dve", lambda e, puw=puw, par=par, h=h: e.tensor_copy(out=usb[par][h][:, :], in_=puw[:, 0:128]), R=[bpuw], W=[B_usb[par][h]])
                    k.op("dve", lambda e, puw=puw, par=par, h=h: e.tensor_copy(out=wTs[par][h][:, :], in_=puw[:, 128:256]), R=[bpuw], W=[B_wTs[par][h]])
                ot_, bot = otile[par], B_ot[par]
                if STOP < 4:
                    continue
                for e_ in range(2):
                    rs = slice(e_ * 64, (e_ + 1) * 64)
                    for h in range(8):
                        p1, bp1 = ps_half()
                        k.op("pe", lambda e, p1=p1, par=par, h=h: e.matmul(p1[:, 0:128], lhsT=wTs[par][h][:, :], rhs=Sst[:, h, :], start=True, stop=True),
                             R=[B_wTs[par][h], B_S[h]], W=[bp1])
                        k.op("dve", lambda e, p1=p1, par=par, h=h, rs=rs: e.tensor_tensor(out=vnw[h][rs, :], in0=usb[par][h][rs, :], in1=p1[rs, 0:128], op=ALU.subtract),
                             R=[bp1, B_usb[par][h]], W=[B_vnw[h]])
                        k.op("act", lambda e, h=h, rs=rs: e.activation(out=vnb[h][rs, :], in_=vnw[h][rs, :], func=AF.Identity), R=[B_vnw[h]], W=[B_vnb[h]])
                        k.op("pe", lambda e, p1=p1, par=par, h=h: e.matmul(p1[:, 128:256], lhsT=qdT[par][h][:, :], rhs=Sbf[:, h, :], start=True, stop=False),
                             R=[B_qdT[par][h], B_Sbf[h]], W=[bp1])
                        k.op("pe", lambda e, p1=p1, par=par, h=h, rs=rs: e.matmul(p1[:, 128:256], lhsT=AT[par][h][rs, :], rhs=vnb[h][rs, :], start=False, stop=True),
                             R=[B_AT[par][h], B_vnb[h]], W=[bp1])
                        k.op("act", lambda e, p1=p1, ot_=ot_, h=h, rs=rs: e.activation(out=ot_[rs, h, :], in_=p1[rs, 128:256], func=AF.Identity), R=[bp1], W=[bot[h]])
                        p2, bp2 = ps_half()
                        k.op("pe", lambda e, p2=p2, par=par, h=h, rs=rs: e.matmul(p2[:, 0:128], lhsT=kdc[par][h][rs, :], rhs=vnw[h][rs, :], start=True, stop=True),
                             R=[B_kdc[par][h], B_vnw[h]], W=[bp2])
                        k.op("dve", lambda e, p2=p2, h=h, s_=s_, e_=e_: e.scalar_tensor_tensor(out=Sst[:, h, :], in0=Sst[:, h, :], scalar=s_[:, 64 + e_ * 8 + h:65 + e_ * 8 + h],
                                                                                            in1=p2[:, 0:128], op0=ALU.mult, op1=ALU.add),
                             R=[bp2, bsm, B_S[h]], W=[B_S[h]])
                        k.op("act", lambda e, h=h: e.activation(out=Sbf[:, h, :], in_=Sst[:, h, :], func=AF.Identity), R=[B_S[h]], W=[B_Sbf[h]])
                on_, bon = odn[par], B_odn[par]
                osq = on_[:, :].rearrange("p (h d) -> p h d", d=128)
                k.op("act", lambda e, ot_=ot_, osq=osq: e.activation(out=osq, in_=ot_[:, :, :], func=AF.Square), R=bot, W=[bon])
                k.op("dve", lambda e, par=par, osq=osq: e.tensor_reduce(out=orn[par][:, :], in_=osq, axis=mybir.AxisListType.X, op=ALU.add), R=[bon], W=[B_orn[par]])
                k.op("act", lambda e, par=par: e.activation(out=orn[par][:, :], in_=orn[par][:, :], func=AF.Sqrt, scale=1.0 / 128, bias=1e-6), R=[B_orn[par]], W=[B_orn[par]])
                k.op("dve", lambda e, par=par: e.reciprocal(out=orn[par][:, :], in_=orn[par][:, :]), R=[B_orn[par]], W=[B_orn[par]])
                for h in range(8):
                    k.op("dve", lambda e, ot_=ot_, on_=on_, h=h, par=par: e.scalar_tensor_tensor(out=on_[:, h * 128:(h + 1) * 128], in0=ot_[:, h, :], scalar=orn[par][:, h:h + 1],
                                                                                               in1=ngb[:, :], op0=ALU.mult, op1=ALU.mult),
                         R=[bot[h], B_orn[par], B_c], W=[bon])
                k.op("dve", lambda e, on_=on_, sz_=sz_: e.tensor_tensor(out=on_[:, :], in0=on_[:, :], in1=sz_[:, :], op=ALU.mult), R=[bon, bsz], W=[bon])
                k.dma("sp", odn_scr[tok, :], on_[:, :], R=[bon], W=[B_scr])
        for h in range(8):
            k.dma("sp", o_pdn[h, :, :], Sst[:, h, :], R=[B_S[h]], W=[DOUT])
        k.barrier()
    with contextlib.ExitStack() as st:
        onesf2 = sb(st, "onesf2", [128, 128], F32)
        B_o2 = Buf()
        k.op("pool", lambda e: e.memset(onesf2[:, :], 1.0), W=[B_o2])
        sq = sb(st, "s_qkv", [NS, 3072], F32)
        scv = sb(st, "s_scv", [NS, 3, 3072], F32)
        cwb = sb(st, "s_cwb", [NS, 4, 3072], F32)
        yv = sb(st, "s_y", [NS, 3072], F32)
        sab = sb(st, "s_ab", [NS, 16], F32)
        sdt = sb(st, "s_dt", [NS, 8], F32)
        sAe = sb(st, "s_Ae", [NS, 8], F32)
        val = sb(st, "s_val", [NS, 32], F32)
        B_sq, B_scv, B_cwb, B_yv, B_sab, B_val = Buf(), Buf(), Buf(), Buf(), Buf(), Buf()
        k.dma("sp", sq[:, :], projs_scr[:, C_QKV:C_QKV + 3072], R=[B_scr2], W=[B_sq])
        k.dma("sp", scv[:, :, :], sconv[:, :, :], W=[B_scv])
        for w_ in range(4):
            k.dma("sp", cwb[:, w_, :], conv_w[w_, :].partition_broadcast(NS), W=[B_cwb])
        k.dma("sp", sab[:, :], projs_scr[:, 6704:6720], R=[B_scr2], W=[B_sab])
        k.dma("sp", sdt[:, :], dt_bias[0, :].partition_broadcast(NS), W=[B_sab])
        k.dma("sp", sAe[:, :], a_log[0, :].partition_broadcast(NS), W=[B_sab])
        k.op("dve", lambda e: e.tensor_tensor(out=yv[:, :], in0=sq[:, :], in1=cwb[:, 3, :], op=ALU.mult), R=[B_sq, B_cwb], W=[B_yv])
        for w_ in range(3):
            k.op("dve", lambda e, w_=w_: e.tensor_tensor(out=scv[:, w_, :], in0=scv[:, w_, :], in1=cwb[:, w_, :], op=ALU.mult), R=[B_scv, B_cwb], W=[B_scv])
            k.op("dve", lambda e, w_=w_: e.tensor_tensor(out=yv[:, :], in0=yv[:, :], in1=scv[:, w_, :], op=ALU.add), R=[B_scv, B_yv], W=[B_yv])
        k.op("act", lambda e: e.activation(out=yv[:, :], in_=yv[:, :], func=AF.Silu), R=[B_yv], W=[B_yv])
        ssq = sb(st, "s_ssq", [NS, 16], F32)
        B_ssq = Buf()
        sqr = scv[:, 0, 0:2048]
        k.op("act", lambda e: e.activation(out=sqr, in_=yv[:, 0:2048], func=AF.Square), R=[B_yv, B_scv], W=[B_scv])
        k.op("dve", lambda e: e.tensor_reduce(out=ssq[:, :], in_=sqr.rearrange("p (h d) -> p h d", d=128), axis=mybir.AxisListType.X, op=ALU.add),
             R=[B_scv], W=[B_ssq])
        k.op("act", lambda e: e.activation(out=ssq[:, :], in_=ssq[:, :], func=AF.Sqrt, bias=1e-6), R=[B_ssq], W=[B_ssq])
        k.op("dve", lambda e: e.reciprocal(out=ssq[:, :], in_=ssq[:, :]), R=[B_ssq], W=[B_ssq])
        k.op("dve", lambda e: e.tensor_scalar(out=ssq[:, 0:8], in0=ssq[:, 0:8], scalar1=128.0 ** -0.5, scalar2=None, op0=ALU.mult), R=[B_ssq], W=[B_ssq])
        for hh in range(16):
            k.op("dve", lambda e, hh=hh: e.tensor_scalar(out=yv[:, hh * 128:(hh + 1) * 128], in0=yv[:, hh * 128:(hh + 1) * 128],
                                                       scalar1=ssq[:, hh:hh + 1], scalar2=None, op0=ALU.mult), R=[B_ssq, B_yv], W=[B_yv])
        k.op("dve", lambda e: e.tensor_tensor(out=scv[:, 1, 0:1024], in0=yv[:, 0:1024], in1=yv[:, 1024:2048], op=ALU.mult), R=[B_yv, B_scv], W=[B_scv])
        k.op("dve", lambda e: e.tensor_reduce(out=val[:, 24:32], in_=scv[:, 1, 0:1024].rearrange("p (h d) -> p h d", d=128), axis=mybir.AxisListType.X, op=ALU.add),
             R=[B_scv], W=[B_val])
        k.op("dve", lambda e: e.tensor_tensor(out=sab[:, 0:8], in0=sab[:, 0:8], in1=sdt[:, :], op=ALU.add), R=[B_sab], W=[B_sab])
        k.op("act", lambda e: e.activation(out=sab[:, 0:8], in_=sab[:, 0:8], func=AF.Exp), R=[B_sab], W=[B_sab])
        k.op("act", lambda e: e.activation(out=sab[:, 0:8], in_=sab[:, 0:8], func=AF.Ln, bias=1.0), R=[B_sab], W=[B_sab])
        k.op("act", lambda e: e.activation(out=sAe[:, :], in_=sAe[:, :], func=AF.Exp), R=[B_sab], W=[B_sab])
        k.op("dve", lambda e: e.tensor_tensor(out=sab[:, 0:8], in0=sab[:, 0:8], in1=sAe[:, :], op=ALU.mult), R=[B_sab], W=[B_sab])
        k.op("act", lambda e: e.activation(out=val[:, 0:8], in_=sab[:, 0:8], func=AF.Exp, scale=-1.0), R=[B_sab], W=[B_val])
        k.op("act", lambda e: e.activation(out=val[:, 8:16], in_=sab[:, 8:16], func=AF.Sigmoid), R=[B_sab], W=[B_val])
        k.op("dve", lambda e: e.tensor_tensor(out=val[:, 16:24], in0=val[:, 0:8], in1=val[:, 8:16], op=ALU.mult), R=[B_val], W=[B_val])
        vex = sb(st, "s_vex", [NS, NS, 32], F32)
        bcs = sb(st, "s_bc", [128, NS, 32], F32)
        B_vex, B_bcs = Buf(), Buf()
        for s_i in range(NS):
            k.op("dve", lambda e, s_i=s_i: e.tensor_scalar(out=vex[:, s_i, :], in0=val[:, :], scalar1=identf[0:NS, s_i:s_i + 1], scalar2=None, op0=ALU.mult),
                 R=[B_val, B_identf], W=[B_vex])
        pb_ = PS[0]
        k.op("pe", lambda e: e.matmul(pb_[:, 0:NS * 32], lhsT=onesf2[0:NS, :], rhs=vex[:, :, :].rearrange("p a b -> p (a b)"), start=True, stop=True),
             R=[B_vex, B_o2], W=[PSB[0]])
        k.op("dve", lambda e: e.tensor_copy(out=bcs[:, :, :], in_=pb_[:, 0:NS * 32].rearrange("p (a b) -> p a b", b=32)), R=[PSB[0]], W=[B_bcs])
        qkvTs = sb(st, "s_qkvT", [128, 24, NS], F32)
        B_qTs = Buf()
        pt_ = PS[1]
        for j in range(24):
            k.op("pe", lambda e, j=j: e.transpose(out=pt_[:, j * NS:(j + 1) * NS], in_=yv[:, j * 128:(j + 1) * 128], identity=identf[0:NS, 0:NS]),
                 R=[B_yv, B_identf], W=[PSB[1]])
        k.op("dve", lambda e: e.tensor_copy(out=qkvTs[:, :, :], in_=pt_[:, 0:24 * NS].rearrange("p (j s) -> p j s", s=NS)), R=[PSB[1]], W=[B_qTs])
        kq = sb(st, "s_kq", [128, NS, 8, 2], F32)
        B_kq = Buf()
        for h in range(8):
            k.op("dve", lambda e, h=h: e.tensor_copy(out=kq[:, :, h, 0], in_=qkvTs[:, 8 + h, :]), R=[B_qTs], W=[B_kq])
            k.op("dve", lambda e, h=h: e.tensor_copy(out=kq[:, :, h, 1], in_=qkvTs[:, h, :]), R=[B_qTs], W=[B_kq])
        Sin = [sb(st, "s_Sin%d" % i, [128, 8, 128], F32) for i in range(2)]
        Sout = [sb(st, "s_Sout%d" % i, [128, 8, 128], F32) for i in range(2)]
        B_Sin = [Buf(), Buf()]
        B_Sout = [Buf(), Buf()]
        ksq = sb(st, "s_ksq", [128, NS, 8, 2], F32)
        oTs = sb(st, "s_oTs", [128, 8, NS], F32)
        B_oTs = Buf()
        B_ksq = Buf()
        vnc = [sb(st, "s_vnc%d" % i, [128, 8], F32) for i in range(2)]
        B_vnc = [Buf(), Buf()]
        dgv = [sb(st, "s_dgv%d" % i, [128, 128], F32) for i in range(2)]
        B_dgv = [Buf(), Buf()]
        sgt = [sb(st, "s_sgt%d" % i, [128, 128], F32) for i in range(2)]
        B_sgt = [Buf(), Buf()]
        pi_ = [2]

        def nextps():
            i = 2 + (pi_[0] % 6)
            pi_[0] += 1
            return PS[i], PSB[i]
        ui = 0
        for s_i in range(NS):
            si_, bsi = Sin[s_i % 2], B_Sin[s_i % 2]
            so_, bso = Sout[s_i % 2], B_Sout[s_i % 2]
            vn_, bvn = vnc[s_i % 2], B_vnc[s_i % 2]
            k.dma("sp", si_[:, :, :], state_dn[s_i].rearrange("h k v -> k h v"), W=[bsi])
            pk, bpk = nextps()
            for h in range(8):
                k.op("pe", lambda e, h=h, si_=si_, pk=pk, s_i=s_i: e.matmul(pk[:, h * 2:h * 2 + 2], lhsT=si_[:, h, :], rhs=kq[:, s_i, h, :], start=True, stop=True),
                     R=[bsi, B_kq], W=[bpk])
            k.op("dve", lambda e, pk=pk, s_i=s_i: e.tensor_copy(out=ksq[:, s_i, :, :], in_=pk[:, 0:16].rearrange("p (h t) -> p h t", t=2)), R=[bpk], W=[B_ksq])
            k.op("dve", lambda e, vn_=vn_, s_i=s_i: e.tensor_tensor(out=vn_[:, :], in0=ksq[:, s_i, :, 0], in1=bcs[:, s_i, 16:24], op=ALU.mult), R=[B_ksq, B_bcs], W=[bvn])
            k.op("dve", lambda e, s_i=s_i: e.tensor_tensor(out=ksq[:, s_i, :, 0], in0=qkvTs[:, 16:24, s_i], in1=bcs[:, s_i, 8:16], op=ALU.mult), R=[B_qTs, B_bcs, B_ksq], W=[B_ksq])
            k.op("dve", lambda e, vn_=vn_, s_i=s_i: e.tensor_tensor(out=vn_[:, :], in0=ksq[:, s_i, :, 0], in1=vn_[:, :], op=ALU.subtract), R=[B_ksq, bvn], W=[bvn])
            k.op("dve", lambda e, s_i=s_i: e.tensor_tensor(out=oTs[:, :, s_i], in0=ksq[:, s_i, :, 1], in1=bcs[:, s_i, 0:8], op=ALU.mult), R=[B_ksq, B_bcs], W=[B_oTs])
            k.op("dve", lambda e, s_i=s_i, vn_=vn_: e.tensor_tensor(out=ksq[:, s_i, :, 1], in0=vn_[:, :], in1=bcs[:, s_i, 24:32], op=ALU.mult), R=[bvn, B_bcs, B_ksq], W=[B_ksq])
            k.op("dve", lambda e, s_i=s_i: e.tensor_tensor(out=oTs[:, :, s_i], in0=oTs[:, :, s_i], in1=ksq[:, s_i, :, 1], op=ALU.add), R=[B_ksq, B_oTs], W=[B_oTs])
            for h in range(8):
                dg_, bdg = dgv[ui % 2], B_dgv[ui % 2]
                sg_, bsg = sgt[ui % 2], B_sgt[ui % 2]
                ui += 1
                k.op("dve", lambda e, dg_=dg_, vn_=vn_, h=h: e.tensor_scalar(out=dg_[:, :], in0=identf[:, :], scalar1=vn_[:, h:h + 1], scalar2=None, op0=ALU.mult),
                     R=[bvn, B_identf], W=[bdg])
                pv, bpv = nextps()
                k.op("pe", lambda e, pv=pv, dg_=dg_: e.matmul(pv[:, 0:128], lhsT=onesf2[:, :], rhs=dg_[:, :], start=True, stop=True), R=[bdg, B_o2], W=[bpv])
                k.op("act", lambda e, sg_=sg_, si_=si_, h=h, s_i=s_i: e.activation(out=sg_[:, :], in_=si_[:, h, :], func=AF.Identity, scale=bcs[:, s_i, h:h + 1],
                                                                               bias=zcol[:, 0:1]), R=[bsi, B_bcs], W=[bsg])
                k.op("dve", lambda e, so_=so_, pv=pv, sg_=sg_, h=h, s_i=s_i: e.scalar_tensor_tensor(out=so_[:, h, :], in0=pv[:, 0:128], scalar=qkvTs[:, 8 + h, s_i:s_i + 1],
                                                                                                in1=sg_[:, :], op0=ALU.mult, op1=ALU.add),
                     R=[bpv, B_qTs, bsg], W=[bso])
            k.dma("sp", o_sdn[s_i].rearrange("h k v -> k h v"), so_[:, :, :], R=[bso], W=[DOUT])
        ot_s = yv
        pt2 = PS[1]
        for h in range(8):
            k.op("pe", lambda e, h=h: e.transpose(out=pt2[0:NS, h * 128:(h + 1) * 128] if h < 4 else PS[0][0:NS, (h - 4) * 128:(h - 3) * 128], in_=oTs[:, h, :], identity=identf[:, :]),
                 R=[B_oTs, B_identf], W=[PSB[1] if h < 4 else PSB[0]])
        k.op("dve", lambda e: e.tensor_copy(out=ot_s[:, 0:512], in_=pt2[0:NS, :]), R=[PSB[1], B_yv], W=[B_yv])
        k.op("dve", lambda e: e.tensor_copy(out=ot_s[:, 512:1024], in_=PS[0][0:NS, :]), R=[PSB[0], B_yv], W=[B_yv])
        k.dma("sp", ot_s[:, 1024:2048], projs_scr[:, 5680:6704], R=[B_scr2, B_yv], W=[B_yv])
        ngs = sb(st, "s_ngs", [NS, 128], F32)
        B_ngs = Buf()
        k.dma("sp", ngs[:, :], dn_ng[0, :].partition_broadcast(NS), W=[B_ngs])
        k.op("act", lambda e: e.activation(out=ot_s[:, 2048:3072], in_=ot_s[:, 0:1024], func=AF.Square), R=[B_yv], W=[B_yv])
        k.op("dve", lambda e: e.tensor_reduce(out=ssq[:, 0:8], in_=ot_s[:, 2048:3072].rearrange("p (h d) -> p h d", d=128), axis=mybir.AxisListType.X, op=ALU.add), R=[B_yv, B_ssq], W=[B_ssq])
        k.op("act", lambda e: e.activation(out=ssq[:, 0:8], in_=ssq[:, 0:8], func=AF.Sqrt, scale=1.0 / 128, bias=1e-6), R=[B_ssq], W=[B_ssq])
        k.op("dve", lambda e: e.reciprocal(out=ssq[:, 0:8], in_=ssq[:, 0:8]), R=[B_ssq], W=[B_ssq])
        k.op("act", lambda e: e.activation(out=ot_s[:, 1024:2048], in_=ot_s[:, 1024:2048], func=AF.Silu), R=[B_yv], W=[B_yv])
        for h in range(8):
            k.op("dve", lambda e, h=h: e.scalar_tensor_tensor(out=ot_s[:, h * 128:(h + 1) * 128], in0=ot_s[:, h * 128:(h + 1) * 128], scalar=ssq[:, h:h + 1], in1=ngs[:, :],
                                                              op0=ALU.mult, op1=ALU.mult), R=[B_yv, B_ssq, B_ngs], W=[B_yv])
        k.op("dve", lambda e: e.tensor_tensor(out=ot_s[:, 0:1024], in0=ot_s[:, 0:1024], in1=ot_s[:, 1024:2048], op=ALU.mult), R=[B_yv], W=[B_yv])
        k.dma("sp", odns_scr[:, :], ot_s[:, 0:1024], R=[B_yv], W=[B_scr3])
        k.barrier()
    st1.close()
    with contextlib.ExitStack() as st:
        BIG = 1.0e30
        KE = sb(st, "KE", [128, 4, S], BF16)
        KW = sb(st, "KW", [128, 4, S], BF16)
        VAs = sb(st, "VAs", [128, 4, 32, 128], BF16)
        VAw = sb(st, "VAw", [128, 4, 32, 128], BF16)
        kcT = sb(st, "kcT", [128, 4, 256], BF16)
        VC = sb(st, "VC", [128, 4, 2, 128], BF16)
        OV = sb(st, "OV", [128, 2, 128], BF16)
        B_KE, B_KW, B_VAs, B_VAw, B_kcT, B_VC, B_OV = Buf(), Buf(), Buf(), Buf(), Buf(), Buf(), Buf()
        for g in range(4):
            rs_ = slice((g % 2) * 64, (g % 2) * 64 + 64)
            k.dma("sp", KE[0:64, g, :], kvT_scr[4 + g // 2][rs_, :], R=[B_scr2], W=[B_KE])
            k.dma("sp", KW[0:64, g, :], kvT_scr[6 + g // 2][rs_, :], R=[B_scr2], W=[B_KW])
            k.dma("pool", VAs[:, g, :, 0:64], o_pkv[3].rearrange("(c p) (g d) -> p g c d", p=128, d=64)[:, g], R=[DOUT], W=[B_VAs])
            k.dma("pool", VAw[:, g, :, 0:64], o_pkv[5].rearrange("(c p) (g d) -> p g c d", p=128, d=64)[:, g], R=[DOUT], W=[B_VAw])
            k.op("pool", lambda e, g=g: e.memset(VAs[:, g, :, 64:128], 1.0), W=[B_VAs])
            k.op("pool", lambda e, g=g: e.memset(VAw[:, g, :, 64:128], 1.0), W=[B_VAw])
        k.op("pool", lambda e: e.memset(VC[:, :, :, 64:128], 1.0), W=[B_VC])
        k.op("pool", lambda e: e.memset(kcT[:, :, :], 0.0), W=[B_kcT])
        with contextlib.ExitStack() as stc:
            Et = sb(stc, "Et", [64, S], BF16)
            B_Et = Buf()
            k.op("pool", lambda e: e.memset(Et[:, :], 30000.0), W=[B_Et])
            k.op("pool", lambda e: e.affine_select(out=Et[:, :], in_=Et[:, :], pattern=[[1, S]], compare_op=ALU.is_ge, fill=0.0, base=0, channel_multiplier=-64), R=[B_Et], W=[B_Et])
            k.op("pool", lambda e: e.affine_select(out=Et[:, :], in_=Et[:, :], pattern=[[-1, S]], compare_op=ALU.is_ge, fill=0.0, base=63, channel_multiplier=64), R=[B_Et], W=[B_Et])
            for g in range(4):
                k.op("dve" if g % 2 else "pool", lambda e, g=g: e.tensor_copy(out=KE[64:128, g, :], in_=Et[0:64, :]), R=[B_Et], W=[B_KE])
            ovf = sb(stc, "ovf", [128, 2, 2, 64], F32)
            B_ovf = Buf()
            k.op("pool", lambda e: e.memset(ovf[:, :, :, :], 0.5), W=[B_ovf])
            for cc in range(2):
                for wi_, (lo, hi) in enumerate(((-1, 3), (0, 2))):
                    k.op("pool", lambda e, cc=cc, wi_=wi_, lo=lo: e.affine_select(out=ovf[:, cc, wi_, :], in_=ovf[:, cc, wi_, :], pattern=[[-4, 64]], compare_op=ALU.is_ge,
                                                                                fill=0.0, base=128 * cc - lo, channel_multiplier=1), R=[B_ovf], W=[B_ovf])
                    k.op("pool", lambda e, cc=cc, wi_=wi_, hi=hi: e.affine_select(out=ovf[:, cc, wi_, :], in_=ovf[:, cc, wi_, :], pattern=[[4, 64]], compare_op=ALU.is_ge,
                                                                                fill=0.0, base=hi - 128 * cc, channel_multiplier=-1), R=[B_ovf], W=[B_ovf])
            k.op("pool", lambda e: e.memset(OV[:, :, 0:64], 0.0), W=[B_OV])
            k.op("dve", lambda e: e.tensor_tensor(out=OV[:, :, 64:128], in0=ovf[:, :, 0, :], in1=ovf[:, :, 1, :], op=ALU.add), R=[B_ovf], W=[B_OV])
            XT = sb(stc, "XT", [128, 2, S], BF16)
            W1 = sb(stc, "W1", [128, 32, 128], BF16)
            w2 = sb(stc, "w2", [128, 64], BF16)
            pe32 = sb(stc, "pe32", [32, 64], F32)
            peT = sb(stc, "peT", [128, 32], BF16)
            pec = sb(stc, "pec", [128, 1], F32)
            Sds = sb(stc, "Sds", [128, 256], F32)
            xg = sb(stc, "xg", [128, 256], F32)
            x2 = sb(stc, "x2", [128, 256], F32)
            hid = sb(stc, "hid", [128, 256], BF16)
            B_XT, B_W1, B_w2, B_pe32, B_peT, B_pec, B_Sds, B_xg, B_x2, B_hid = (Buf() for _ in range(10))
            k.op("pool", lambda e: e.memset(hid[:, :], 0.0), W=[B_hid])
            for kind_ in range(2):
                w1d, w2d, ped = (w1k, w2k, pek) if kind_ == 0 else (w1v, w2v, pev)
                for c2 in range(2):
                    k.dma("sp", XT[:, c2, :], kvT_scr[2 * kind_ + c2], R=[B_scr2], W=[B_XT])
                for half in range(2):
                    k.dma("pool", W1[half * 64:(half + 1) * 64, :, :], w1d.rearrange("p d e -> d p e"), W=[B_W1])
                k.dma("pool", w2[:, :], w2d[:, :], W=[B_w2])
                k.dma("sp", pe32[:, :], ped[:, :], W=[B_pe32])
                k.op("pe", lambda e: e.transpose(out=PS[0][0:64, 0:32], in_=pe32[:, :], identity=identf[0:32, 0:32]), R=[B_pe32, B_identf], W=[PSB[0]])
                k.op("dve", lambda e: e.tensor_copy(out=peT[0:64, :], in_=PS[0][0:64, 0:32]), R=[PSB[0]], W=[B_peT])
                for p_ in range(32):
                    k.op("pe", lambda e, p_=p_: e.matmul(PS[1][:, 0:1], lhsT=W1[0:64, p_, :], rhs=peT[0:64, p_:p_ + 1], start=(p_ == 0), stop=(p_ == 31)),
                         R=[B_W1, B_peT], W=[PSB[1]])
                k.op("dve", lambda e: e.tensor_copy(out=pec[:, :], in_=PS[1][:, 0:1]), R=[PSB[1]], W=[B_pec])
                for g in range(4):
                    rs_ = slice((g % 2) * 64, (g % 2) * 64 + 64)
                    c2 = g // 2
                    pF, bF = PS[2 + (g % 2) * 2], PSB[2 + (g % 2) * 2]
                    pS, bS = PS[3 + (g % 2) * 2], PSB[3 + (g % 2) * 2]
                    for p_ in range(16):
                        k.op("pe", lambda e, p_=p_, rs_=rs_, c2=c2, pF=pF: e.matmul(pF[:, 0:256], lhsT=W1[rs_, p_, :], rhs=XT[rs_, c2, p_:S:16], start=(p_ == 0), stop=(p_ == 15)),
                             R=[B_W1, B_XT], W=[bF])
                    for p_ in range(16):
                        k.op("pe", lambda e, p_=p_, rs_=rs_, c2=c2, pS=pS: e.matmul(pS[:, 0:256], lhsT=W1[rs_, 16 + p_, :], rhs=XT[rs_, c2, p_:S:16], start=(p_ == 0), stop=(p_ == 15)),
                             R=[B_W1, B_XT], W=[bS])
                    k.op("act", lambda e, pS=pS: e.activation(out=Sds[:, :], in_=pS[:, 0:256], func=AF.Identity), R=[bS], W=[B_Sds])
                    k.op("dve", lambda e, pF=pF: e.scalar_tensor_tensor(out=xg[:, 0:255], in0=pF[:, 0:255], scalar=pec[:, 0:1], in1=Sds[:, 1:256], op0=ALU.add, op1=ALU.add),
                         R=[bF, B_pec, B_Sds], W=[B_xg])
                    k.op("act", lambda e: e.activation(out=x2[:, 0:255], in_=xg[:, 0:255], func=AF.Square), R=[B_xg], W=[B_x2])
                    k.op("dve", lambda e: e.tensor_scalar(out=x2[:, 0:255], in0=x2[:, 0:255], scalar1=0.044715, scalar2=1.0, op0=ALU.mult, op1=ALU.add), R=[B_x2], W=[B_x2])
                    k.op("dve", lambda e: e.tensor_tensor(out=x2[:, 0:255], in0=x2[:, 0:255], in1=xg[:, 0:255], op=ALU.mult), R=[B_x2, B_xg], W=[B_x2])
                    k.op("act", lambda e: e.activation(out=x2[:, 0:255], in_=x2[:, 0:255], func=AF.Sigmoid, scale=1.5957691216), R=[B_x2], W=[B_x2])
                    k.op("dve", lambda e: e.tensor_tensor(out=hid[:, 0:255], in0=x2[:, 0:255], in1=xg[:, 0:255], op=ALU.mult), R=[B_x2, B_xg, B_hid], W=[B_hid])
                    if kind_ == 0:
                        k.op("pe", lambda e: e.matmul(PS[6][0:64, 0:256], lhsT=w2[:, :], rhs=hid[:, :], start=True, stop=True), R=[B_w2, B_hid], W=[PSB[6]])
                        k.op("dve", lambda e, g=g: e.tensor_copy(out=kcT[0:64, g, :], in_=PS[6][0:64, 0:256]), R=[PSB[6]], W=[B_kcT])
                    else:
                        for cc in range(2):
                            k.op("pe", lambda e, cc=cc: e.matmul(PS[6 + cc][:, 0:64], lhsT=hid[:, cc * 128:(cc + 1) * 128], rhs=w2[:, :], start=True, stop=True),
                                 R=[B_w2, B_hid], W=[PSB[6 + cc]])
                            k.op("dve", lambda e, cc=cc, g=g: e.tensor_copy(out=VC[:, g, cc, 0:64], in_=PS[6 + cc][:, 0:64]), R=[PSB[6 + cc]], W=[B_VC])
            k.barrier()
        RH = sb(st, "RH", [128, 16, 512], BF16)
        B_RHq = [Buf() for _ in range(16)]
        B_RHs = [Buf() for _ in range(4)]
        onsa = sb(st, "onsa", [128, 4, D], F32)
        B_onsa = [Buf() for _ in range(16)]
        PT = [sb(st, "PT%d" % i, [128, 512], BF16) for i in range(4)]
        B_PT = [Buf() for _ in range(4)]
        Oev = [sb(st, "Oev%d" % i, [128, 512], F32) for i in range(2)]
        B_Oev = [Buf(), Buf()]
        Gt = sb(st, "Gt", [128, 4, 48], F32)
        B_Gt = Buf()
        impacc = sb(st, "impacc", [128, 512], F32)
        B_imp = Buf()
        imtmp = sb(st, "imtmp", [128, 512], F32)
        B_imtmp = Buf()
        impm = sb(st, "impm", [128, 64], F32)
        imp2 = sb(st, "imp2", [128, 64], F32)
        m8 = sb(st, "m8", [128, 16], F32)
        bsel = sb(st, "bsel", [128, 128], BF16)
        B_impm, B_imp2, B_m8, B_bsel = Buf(), Buf(), Buf(), Buf()
        k.op("pool", lambda e: e.memset(bsel[:, :], 0.0), W=[B_bsel])
        rr = [sb(st, "rr%d" % i, [128, 4], F32) for i in range(2)]
        B_rr = [Buf(), Buf()]
        mgt = sb(st, "mgt", [128, 2048], F32)
        odt = sb(st, "odt", [128, D], F32)
        mxt = sb(st, "mxt", [128, D], F32)
        B_mgt, B_odt, B_mxt = Buf(), Buf(), Buf()
        B_mix = Buf()
        cnt = {"s": 0, "o": 0, "pt": 0, "ev": 0}

        def ps_s():
            i = cnt["s"] % 3
            cnt["s"] += 1
            return PS[i], PSB[i]

        def ps_o():
            i = 3 + cnt["o"] % 2
            cnt["o"] += 1
            return PS[i], PSB[i]

        def finish(h, br, pO, bO, want_imp=None):
            ev, bev = Oev[cnt["ev"] % 2], B_Oev[cnt["ev"] % 2]
            r_, br_ = rr[cnt["ev"] % 2], B_rr[cnt["ev"] % 2]
            cnt["ev"] += 1
            k.op("act", lambda e: e.activation(out=ev[:, :], in_=pO[:, :], func=AF.Identity), R=[bO], W=[bev])
            k.op("dve", lambda e: e.tensor_scalar(out=ev[64:128, :], in0=ev[64:128, :], scalar1=1e-30, scalar2=None, op0=ALU.max), R=[bev], W=[bev])
            if want_imp is not None:
                pI, bI, first = want_imp
                k.op("dve", lambda e: e.reciprocal(out=imtmp[64:128, :], in_=ev[64:128, :]), R=[bev], W=[B_imtmp])
                if first:
                    k.op("dve", lambda e: e.tensor_tensor(out=impacc[64:128, :], in0=pI[64:128, :], in1=imtmp[64:128, :], op=ALU.mult), R=[bI, B_imtmp], W=[B_imp])
                else:
                    k.op("dve", lambda e: e.tensor_tensor(out=imtmp[64:128, :], in0=pI[64:128, :], in1=imtmp[64:128, :], op=ALU.mult), R=[bI, B_imtmp], W=[B_imtmp])
                    k.op("dve", lambda e: e.tensor_tensor(out=impacc[64:128, :], in0=impacc[64:128, :], in1=imtmp[64:128, :], op=ALU.add), R=[B_imtmp, B_imp], W=[B_imp])
            pT, bT = PS[6], PSB[6]
            for qt in range(4):
                k.op("pe", lambda e, qt=qt: e.transpose(out=pT[:, qt * 128:(qt + 1) * 128], in_=ev[:, qt * 128:(qt + 1) * 128], identity=identf[:, :]),
                     R=[bev, B_identf], W=[bT])
            pT3 = pT[:, :].rearrange("p (t c) -> p t c", c=128)
            k.op("dve", lambda e: e.reciprocal(out=r_[:, :], in_=pT3[:, :, 64]), R=[bT], W=[br_])
            k.op("dve", lambda e: e.tensor_tensor(out=r_[:, :], in0=r_[:, :], in1=Gt[:, :, h * 3 + br], op=ALU.mult), R=[br_, B_Gt], W=[br_])
            for qt in range(4):
                if br == 0:
                    k.op("act", lambda e, qt=qt: e.activation(out=onsa[:, qt, h * 64:(h + 1) * 64], in_=pT[:, qt * 128:qt * 128 + 64], func=AF.Identity,
                                                             scale=r_[:, qt:qt + 1], bias=zcol[:, 0:1]), R=[bT, br_], W=[B_onsa[h]])
                else:
                    k.op("dve", lambda e, qt=qt: e.scalar_tensor_tensor(out=onsa[:, qt, h * 64:(h + 1) * 64], in0=pT[:, qt * 128:qt * 128 + 64], scalar=r_[:, qt:qt + 1],
                                                                       in1=onsa[:, qt, h * 64:(h + 1) * 64], op0=ALU.mult, op1=ALU.add),
                         R=[bT, br_, B_onsa[h]], W=[B_onsa[h]])

        def unit(lhs_fn, rh_ap, Rl, Rr, va_fn, Rv, chunks, maskfn, pO, bO, extra=None):
            n = len(chunks)
            for i, c in enumerate(chunks):
                pS_, bS_ = ps_s()
                k.op("pe", lambda e, c=c: e.matmul(pS_[:, :], lhsT=lhs_fn(c), rhs=rh_ap, start=True, stop=True), R=Rl + Rr, W=[bS_])
                pt, bpt = PT[cnt["pt"] % 4], B_PT[cnt["pt"] % 4]
                cnt["pt"] += 1
                k.op("act", lambda e: e.activation(out=pt[:, :], in_=pS_[:, :], func=AF.Exp), R=[bS_], W=[bpt])
                for (pat, base, cm) in maskfn(c):
                    k.op("pool", lambda e, pat=pat, base=base, cm=cm: e.affine_select(out=pt[:, :], in_=pt[:, :], pattern=[[pat, 512]], compare_op=ALU.is_ge,
                                                                                    fill=0.0, base=base, channel_multiplier=cm), R=[bpt], W=[bpt])
                k.op("pe", lambda e, c=c: e.matmul(pO[:, :], lhsT=va_fn(c), rhs=pt[:, :], start=(i == 0), stop=(i == n - 1)), R=Rv + [bpt], W=[bO])
                if extra is not None:
                    pI, bI = extra
                    k.op("pe", lambda e, c=c: e.matmul(pI[:, :], lhsT=OV[:, c, :], rhs=pt[:, :], start=(i == 0), stop=(i == n - 1)), R=[B_OV, bpt], W=[bI])

        for qb in range(int(os.environ.get("NSA_NQB", "8"))):
            q0 = qb * 512
            for c8 in range(8):
                for hf in range(2):
                    k.dma("sp", RH[0:64, 2 * c8 + hf, :], q_scr[c8][hf * 64:(hf + 1) * 64, q0:q0 + 512], R=[B_scr2], W=[B_RHq[2 * c8 + hf]])
            k.dma("sp", Gt[:, :, :], gate_scr[q0:q0 + 512, :].rearrange("(t p) d -> p t d", p=128), R=[B_scr2], W=[B_Gt])
            for g in range(4):
                ccs = [0] if qb < 4 else [0, 1]
                for hh in range(4):
                    h = 4 * g + hh
                    pO, bO = ps_o()
                    pI, bI = PS[5], PSB[5]

                    def mk(c, qb=qb):
                        base = -(2048 * c + 31 - 512 * qb)
                        if base - 16 * 127 >= 0:
                            return []
                        return [(1, base, -16)]
                    unit(lambda c, g=g: kcT[0:64, g, c * 128:(c + 1) * 128], RH[0:64, h, :], [B_kcT], [B_RHq[h]],
                         lambda c, g=g: VC[:, g, c, :], [B_VC], ccs, mk, pO, bO, extra=(pI, bI))
                    finish(h, 0, pO, bO, want_imp=(pI, bI, hh == 0))
                for qt in range(4):
                    t = qb * 4 + qt
                    pT, bT = PS[7], PSB[7]
                    k.op("pe", lambda e, qt=qt: e.transpose(out=pT[:, 0:64], in_=impacc[64:128, qt * 128:(qt + 1) * 128], identity=identf[64:128, 64:128]),
                         R=[B_imp, B_identf], W=[bT])
                    k.op("dve", lambda e: e.tensor_copy(out=impm[:, :], in_=pT[:, 0:64]), R=[bT], W=[B_impm])
                    k.op("pool", lambda e: e.memset(impm[:, 0:1], BIG), R=[B_impm], W=[B_impm])
                    if 2 * t + 2 < 64:
                        k.op("pool", lambda e, t=t: e.memset(impm[:, 2 * t + 2:64], -BIG), R=[B_impm], W=[B_impm])
                    k.op("pool", lambda e, t=t: e.memset(impm[0:64, 2 * t:2 * t + 1], BIG), R=[B_impm], W=[B_impm])
                    k.op("pool", lambda e, t=t: e.memset(impm[64:128, 2 * t + 1:2 * t + 2], BIG), R=[B_impm], W=[B_impm])
                    k.op("pool", lambda e, t=t: e.memset(impm[0:64, 2 * t + 1:2 * t + 2], -BIG), R=[B_impm], W=[B_impm])
                    k.op("dve", lambda e: e.max(out=m8[:, 0:8], in_=impm[:, :]), R=[B_impm], W=[B_m8])
                    k.op("dve", lambda e: e.match_replace(out=imp2[:, :], in_to_replace=m8[:, 0:8], in_values=impm[:, :], imm_value=-3.0e38), R=[B_impm, B_m8], W=[B_imp2])
                    k.op("dve", lambda e: e.max(out=m8[:, 8:16], in_=imp2[:, :]), R=[B_imp2, B_m8], W=[B_m8])
                    k.op("dve", lambda e: e.tensor_scalar(out=bsel[:, 64:128], in0=impm[:, :], scalar1=m8[:, 15:16], scalar2=1.0, op0=ALU.is_ge, op1=ALU.subtract),
                         R=[B_impm, B_m8], W=[B_bsel])
                    pTb = pT[:, :].bitcast(BF16)
                    k.op("pe", lambda e: e.transpose(out=pTb[:, 512:640], in_=bsel[:, :], identity=ident[:, :]), R=[B_bsel, B_ident], W=[bT])
                    for hh in range(4):
                        k.op("dve" if hh % 2 else "act", (lambda e, hh=hh, qt=qt, g=g: e.tensor_copy(out=RH[64:128, 4 * g + hh, qt * 128:(qt + 1) * 128], in_=pTb[64:128, 512:640])) if hh % 2 else
                             (lambda e, hh=hh, qt=qt, g=g: e.activation(out=RH[64:128, 4 * g + hh, qt * 128:(qt + 1) * 128], in_=pTb[64:128, 512:640], func=AF.Identity)),
                             R=[bT], W=[B_RHs[g]])
            for h in range(16):
                g = h // 4
                pO, bO = ps_o()

                def mk_s(c, qb=qb):
                    if c < 4 * qb:
                        return []
                    return [(1, 512 * qb - 128 * c, -1)]
                unit(lambda c, g=g: KE[:, g, c * 128:(c + 1) * 128], RH[:, h, :], [B_KE], [B_RHq[h], B_RHs[g]],
                     lambda c, g=g: VAs[:, g, c, :], [B_VAs], list(range(0, 4 * qb + 4)), mk_s, pO, bO)
                finish(h, 1, pO, bO)
                pO, bO = ps_o()

                def mk_w(c, qb=qb):
                    if c >= 4 * qb:
                        return [(1, 512 * qb - 128 * c, -1)]
                    return [(-1, 128 * c - 512 * qb + 511, 1)]
                unit(lambda c, g=g: KW[0:64, g, c * 128:(c + 1) * 128], RH[0:64, h, :], [B_KW], [B_RHq[h]],
                     lambda c, g=g: VAw[:, g, c, :], [B_VAw], list(range(max(0, 4 * qb - 4), 4 * qb + 4)), mk_w, pO, bO)
                finish(h, 2, pO, bO)
            for qt in range(4):
                r0 = q0 + qt * 128
                k.dma("sp", mgt[:, :], mg_scr[r0:r0 + 128, :], R=[B_scr2], W=[B_mgt])
                k.dma("sp", odt[:, :], odn_scr[r0:r0 + 128, :], R=[B_scr], W=[B_odt])
                k.op("dve", lambda e, qt=qt: e.tensor_tensor(out=mxt[:, :], in0=onsa[:, qt, :], in1=mgt[:, 0:D], op=ALU.mult), R=B_onsa + [B_mgt], W=[B_mxt])
                k.op("dve", lambda e: e.tensor_tensor(out=odt[:, :], in0=odt[:, :], in1=mgt[:, D:2 * D], op=ALU.mult), R=[B_odt, B_mgt], W=[B_odt])
                k.op("dve", lambda e: e.tensor_tensor(out=mxt[:, :], in0=mxt[:, :], in1=odt[:, :], op=ALU.add), R=[B_odt, B_mxt], W=[B_mxt])
                k.dma("sp", mix_scr[r0:r0 + 128, :], mxt[:, :], R=[B_mxt], W=[B_mix])
        if os.environ.get('DBG_MIX'):
            for i8 in range(8):
                k.dma("sp", o_yp[i8 * 512:(i8 + 1) * 512, :], mix_scr[i8 * 512:(i8 + 1) * 512, :], R=[B_mix], W=[DOUT])
        k.barrier()
    with contextlib.ExitStack() as st:
        BIG = 1.0e30
        PAST = 2048
        LP = 2112
        onesE = sb(st, "onesE", [128, 128], F32)
        B_onesE = Buf()
        k.op("pool", lambda e: e.memset(onesE[:, :], 1.0), W=[B_onesE])
        W1s = [sb(st, "W1s%d" % i, [128, 32, 128], BF16) for i in range(2)]
        w2s = [sb(st, "w2s%d" % i, [128, 64], BF16) for i in range(2)]
        pecs = sb(st, "pecs", [128, 2], F32)
        pe32s = sb(st, "pe32s", [32, 64], F32)
        peTs = sb(st, "peTs", [128, 32], BF16)
        B_W1s, B_w2s, B_pecs, B_pe32s, B_peTs = Buf(), Buf(), Buf(), Buf(), Buf()
        for kind_ in range(2):
            w1d, w2d, ped = (w1k, w2k, pek) if kind_ == 0 else (w1v, w2v, pev)
            for half in range(2):
                k.dma("pool", W1s[kind_][half * 64:(half + 1) * 64, :, :], w1d.rearrange("p d e -> d p e"), W=[B_W1s])
            k.dma("pool", w2s[kind_][:, :], w2d[:, :], W=[B_w2s])
            k.dma("sp", pe32s[:, :], ped[:, :], R=[B_pe32s], W=[B_pe32s])
            k.op("pe", lambda e: e.transpose(out=PS[0][0:64, 0:32], in_=pe32s[:, :], identity=identf[0:32, 0:32]), R=[B_pe32s, B_identf], W=[PSB[0]])
            k.op("dve", lambda e: e.tensor_copy(out=peTs[0:64, :], in_=PS[0][0:64, 0:32]), R=[PSB[0], B_peTs], W=[B_peTs])
            for p_ in range(32):
                k.op("pe", lambda e, p_=p_, kind_=kind_: e.matmul(PS[1][:, 0:1], lhsT=W1s[kind_][0:64, p_, :], rhs=peTs[0:64, p_:p_ + 1], start=(p_ == 0), stop=(p_ == 31)),
                     R=[B_W1s, B_peTs], W=[PSB[1]])
            k.op("dve", lambda e, kind_=kind_: e.tensor_copy(out=pecs[:, kind_:kind_ + 1], in_=PS[1][:, 0:1]), R=[PSB[1]], W=[B_pecs])
        OVs = sb(st, "OVs", [128, 128], BF16)
        B_OVs = Buf()
        with contextlib.ExitStack() as stc:
            ovf = sb(stc, "ovfs", [128, 2, 64], F32)
            B_ovf = Buf()
            k.op("pool", lambda e: e.memset(ovf[:, :, :], 0.5), W=[B_ovf])
            for wi_, (lo, hi) in enumerate(((-1, 3), (0, 2))):
                k.op("pool", lambda e, wi_=wi_, lo=lo: e.affine_select(out=ovf[:, wi_, :], in_=ovf[:, wi_, :], pattern=[[-4, 64]], compare_op=ALU.is_ge, fill=0.0, base=-lo, channel_multiplier=1), R=[B_ovf], W=[B_ovf])
                k.op("pool", lambda e, wi_=wi_, hi=hi: e.affine_select(out=ovf[:, wi_, :], in_=ovf[:, wi_, :], pattern=[[4, 64]], compare_op=ALU.is_ge, fill=0.0, base=hi, channel_multiplier=-1), R=[B_ovf], W=[B_ovf])
            k.op("pool", lambda e: e.memset(OVs[:, 0:64], 0.0), W=[B_OVs])
            k.op("dve", lambda e: e.tensor_tensor(out=OVs[:, 64:128], in0=ovf[:, 0, :], in1=ovf[:, 1, :], op=ALU.add), R=[B_ovf], W=[B_OVs])
            k.barrier()
        KEs = sb(st, "KEs", [128, 4, PAST], BF16)
        B_KEs = Buf()
        B_KEe = Buf()
        with contextlib.ExitStack() as stc:
            Et = sb(stc, "Ets", [64, PAST], BF16)
            B_Et = Buf()
            k.op("pool", lambda e: e.memset(Et[:, :], 30000.0), W=[B_Et])
            k.op("pool", lambda e: e.affine_select(out=Et[:, :], in_=Et[:, :], pattern=[[1, PAST]], compare_op=ALU.is_ge, fill=0.0, base=0, channel_multiplier=-64), R=[B_Et], W=[B_Et])
            k.op("pool", lambda e: e.affine_select(out=Et[:, :], in_=Et[:, :], pattern=[[-1, PAST]], compare_op=ALU.is_ge, fill=0.0, base=63, channel_multiplier=64), R=[B_Et], W=[B_Et])
            for g in range(4):
                k.op("dve", lambda e, g=g: e.tensor_copy(out=KEs[64:128, g, :], in_=Et[0:64, :]), R=[B_Et], W=[B_KEe])
            k.barrier()
        ps_tok = sb(st, "ps_tok", [NS, 2608], F32)
        B_pstok = Buf()
        k.dma("sp", ps_tok[:, :], projs_scr[:, 0:2608], R=[B_scr2], W=[B_pstok])
        RHS = sb(st, "RHS", [128, NS, 16], BF16)
        B_RHSq, B_RHSs = Buf(), Buf()
        KN = sb(st, "KN", [64, 3, 4, NS], BF16)
        B_KN = Buf()
        for c8 in range(8):
            pq, bq = PS[2 + c8 % 4], PSB[2 + c8 % 4]
            k.op("pe", lambda e, c8=c8, pq=pq: e.transpose(out=pq[:, 0:NS], in_=ps_tok[:, c8 * 128:(c8 + 1) * 128], identity=identf[0:NS, 0:NS]), R=[B_pstok, B_identf], W=[bq])
            k.op("dve", lambda e, c8=c8, pq=pq: e.tensor_scalar(out=RHS[0:64, :, 2 * c8], in0=pq[0:64, 0:NS], scalar1=0.125, scalar2=None, op0=ALU.mult), R=[bq], W=[B_RHSq])
            k.op("dve", lambda e, c8=c8, pq=pq: e.tensor_scalar(out=RHS[0:64, :, 2 * c8 + 1], in0=pq[64:128, 0:NS], scalar1=0.125, scalar2=None, op0=ALU.mult), R=[bq], W=[B_RHSq])
        for ki, c0 in enumerate((1024, 1536, 2048)):
            for c2 in range(2):
                pq, bq = PS[2 + (ki * 2 + c2) % 4], PSB[2 + (ki * 2 + c2) % 4]
                k.op("pe", lambda e, c0=c0, c2=c2, pq=pq: e.transpose(out=pq[:, 0:NS], in_=ps_tok[:, c0 + c2 * 128:c0 + (c2 + 1) * 128], identity=identf[0:NS, 0:NS]), R=[B_pstok, B_identf], W=[bq])
                k.op("dve", lambda e, ki=ki, c2=c2, pq=pq: e.tensor_copy(out=KN[0:64, ki, 2 * c2, :], in_=pq[0:64, 0:NS]), R=[bq], W=[B_KN])
                k.op("dve", lambda e, ki=ki, c2=c2, pq=pq: e.tensor_copy(out=KN[0:64, ki, 2 * c2 + 1, :], in_=pq[64:128, 0:NS]), R=[bq], W=[B_KN])
        VNc = sb(st, "VNc", [128, 2, NS], BF16)
        KNc = sb(st, "KNc", [128, 2, NS], BF16)
        B_VNc = Buf()
        for ki, (c0, dst) in enumerate(((1024, KNc), (1280, VNc))):
            for c2 in range(2):
                pq, bq = PS[6 + c2], PSB[6 + c2]
                k.op("pe", lambda e, c0=c0, c2=c2, pq=pq: e.transpose(out=pq[:, 0:NS], in_=ps_tok[:, c0 + c2 * 128:c0 + (c2 + 1) * 128], identity=identf[0:NS, 0:NS]), R=[B_pstok, B_identf], W=[bq])
                k.op("dve", lambda e, dst=dst, c2=c2, pq=pq: e.tensor_copy(out=dst[:, c2, :], in_=pq[:, 0:NS]), R=[bq], W=[B_VNc])
        gsg = sb(st, "gsg", [NS, 48], F32)
        gex = sb(st, "gex", [NS, NS, 48], F32)
        Gb = sb(st, "Gb", [128, NS, 48], F32)
        B_gsg, B_gex, B_Gb = Buf(), Buf(), Buf()
        k.op("act", lambda e: e.activation(out=gsg[:, :], in_=ps_tok[:, 2560:2608], func=AF.Sigmoid), R=[B_pstok], W=[B_gsg])
        for s_i in range(NS):
            k.op("dve", lambda e, s_i=s_i: e.tensor_scalar(out=gex[:, s_i, :], in0=gsg[:, :], scalar1=identf[0:NS, s_i:s_i + 1], scalar2=None, op0=ALU.mult), R=[B_gsg, B_identf], W=[B_gex])
        for hf in range(2):
            k.op("pe", lambda e, hf=hf: e.matmul(PS[hf][:, 0:384], lhsT=onesE[0:NS, :], rhs=gex[:, hf * 8:(hf + 1) * 8, :].rearrange("p a b -> p (a b)"), start=True, stop=True),
                 R=[B_gex, B_onesE], W=[PSB[hf]])
            k.op("dve", lambda e, hf=hf: e.tensor_copy(out=Gb[:, hf * 8:(hf + 1) * 8, :], in_=PS[hf][:, 0:384].rearrange("p (a b) -> p a b", b=48)), R=[PSB[hf]], W=[B_Gb])
        VN = sb(st, "VN", [1, NS, 2, 4, 128], BF16)
        B_VN = Buf()
        k.op("pool", lambda e: e.memset(VN[:, :, :, :, 64:128], 1.0), W=[B_VN])
        for s_i in range(NS):
            for ki, c0 in enumerate((1792, 2304)):
                k.dma("pool", VN[0:1, s_i, ki, :, 0:64], projs_scr[s_i:s_i + 1, c0:c0 + 256].rearrange("o (g d) -> o g d", d=64), R=[B_scr2], W=[B_VN])
        ptab = sb(st, "ptab", [128, NS * 16], I32)
        ioi = sb(st, "ioi", [128, 1], I32)
        iof = sb(st, "iof", [128, 1], F32)
        idxa = sb(st, "idxa", [128, NS * 16], I32)
        B_ptab, B_io = Buf(), Buf()
        k.dma("sp", ptab[:, :], ptbl[0, :].partition_broadcast(128), W=[B_ptab])
        k.op("pool", lambda e: e.iota(ioi[:, :], pattern=[[0, 1]], base=0, channel_multiplier=1), W=[B_io])
        k.op("dve", lambda e: e.tensor_copy(out=iof[:, :], in_=ioi[:, :]), R=[B_io], W=[B_io])
        k.op("dve", lambda e: e.tensor_scalar(out=idxa[:, :], in0=ptab[:, :], scalar1=128.0, scalar2=iof[:, 0:1], op0=ALU.mult, op1=ALU.add), R=[B_ptab, B_io], W=[B_ptab])
        pools2 = [p_.rearrange("n p d -> (n p) d") for p_ in (pk_cmp, pv_cmp, pk_slc, pv_slc)]
        pg = [[sb(st, "pg%d_%d" % (i, j), [128, 256], F32) for j in range(4)] for i in range(4)]
        B_pg = [[Buf() for _ in range(4)] for _ in range(4)]
        KWs = sb(st, "KWs", [128, 4, 512], BF16)
        XT2 = [sb(st, "XT2_%d" % i, [128, 2, LP], BF16) for i in range(2)]
        VAss = sb(st, "VAss", [128, 16, 4, 128], BF16)
        VAws = sb(st, "VAws", [128, 4, 4, 128], BF16)
        B_KWs, B_VAss, B_VAws = Buf(), Buf(), Buf()
        B_XT2 = [Buf(), Buf()]
        k.op("pool", lambda e: e.memset(VAss[:, :, :, 64:128], 1.0), W=[B_VAss])
        k.op("pool", lambda e: e.memset(VAws[:, :, :, 64:128], 1.0), W=[B_VAws])
        for i in range(2):
            k.op("pool", lambda e, i=i: e.memset(XT2[i][:, :, PAST:LP], 0.0), W=[B_XT2[i]])
        kcTs = sb(st, "kcTs", [128, 4, 128], BF16)
        VCs = sb(st, "VCs", [128, 4, 128], BF16)
        B_kcTs, B_VCs = Buf(), Buf()
        k.op("pool", lambda e: e.memset(VCs[:, :, 64:128], 1.0), W=[B_VCs])
        Sdz = sb(st, "Sdz", [128, 132], F32)
        xgz = sb(st, "xgz", [128, 132], F32)
        x2z = sb(st, "x2z", [128, 132], F32)
        hidz = sb(st, "hidz", [128, 128], BF16)
        B_Sdz, B_xgz, B_x2z, B_hidz = Buf(), Buf(), Buf(), Buf()
        PTs = [sb(st, "PTs%d" % i, [128, 4], BF16) for i in range(4)]
        B_PTs = [Buf() for _ in range(4)]
        pti = [0]
        onT = sb(st, "onT", [64, NS, 16], F32)
        B_onT = Buf()
        rcp = sb(st, "rcp", [64, 4], F32)
        tq = sb(st, "tq", [64, 4], F32)
        B_rcp, B_tq = Buf(), Buf()
        Oes = sb(st, "Oes", [128, 4], F32)
        B_Oes = Buf()
        impc = sb(st, "impc", [128, 4], F32)
        imt = sb(st, "imt", [128, 4], F32)
        B_impc, B_imt = Buf(), Buf()
        impr = sb(st, "impr", [4, 64], F32)
        imp2r = sb(st, "imp2r", [4, 64], F32)
        m8r = sb(st, "m8r", [4, 16], F32)
        bselr = sb(st, "bselr", [4, 128], BF16)
        B_impr, B_imp2r, B_m8r, B_bselr = Buf(), Buf(), Buf(), Buf()
        k.op("pool", lambda e: e.memset(bselr[:, :], 0.0), W=[B_bselr])
        pools = (pk_cmp, pv_cmp, pk_slc, pv_slc)
        esp = [0]

        def eps():
            i = esp[0] % 5
            esp[0] += 1
            return PS[i], PSB[i]

        def fin_branch(s_i, g, br, pO, bO, first):
            k.op("act", lambda e: e.activation(out=Oes[:, :], in_=pO[:, 0:4], func=AF.Identity), R=[bO], W=[B_Oes])
            k.op("dve", lambda e: e.tensor_scalar(out=Oes[64:128, :], in0=Oes[64:128, :], scalar1=1e-30, scalar2=None, op0=ALU.max), R=[B_Oes], W=[B_Oes])
            k.op("dve", lambda e: e.reciprocal(out=rcp[0:64, :], in_=Oes[64:128, :]), R=[B_Oes], W=[B_rcp])
            k.op("dve", lambda e: e.tensor_tensor(out=tq[0:64, :], in0=Oes[0:64, :], in1=rcp[0:64, :], op=ALU.mult), R=[B_Oes, B_rcp], W=[B_tq])
            gcol = Gb[0:64, s_i, :].rearrange("p (h b) -> p h b", b=3)[:, 4 * g:4 * g + 4, br]
            if first:
                k.op("dve", lambda e: e.tensor_tensor(out=onT[0:64, s_i, 4 * g:4 * g + 4], in0=tq[0:64, :], in1=gcol, op=ALU.mult), R=[B_tq, B_Gb], W=[B_onT])
            else:
                k.op("dve", lambda e: e.tensor_tensor(out=tq[0:64, :], in0=tq[0:64, :], in1=gcol, op=ALU.mult), R=[B_tq, B_Gb], W=[B_tq])
                k.op("dve", lambda e: e.tensor_tensor(out=onT[0:64, s_i, 4 * g:4 * g + 4], in0=onT[0:64, s_i, 4 * g:4 * g + 4], in1=tq[0:64, :], op=ALU.add), R=[B_tq, B_onT], W=[B_onT])

        def score_pv(lhsT, rhs, Rl, va, Rv, pO, bO, start, stop, np_=128, mask=None, imp=None):
            pS_, bS_ = eps()
            k.op("pe", lambda e: e.matmul(pS_[0:np_, 0:4], lhsT=lhsT, rhs=rhs, start=True, stop=True), R=Rl, W=[bS_])
            pt, bpt = PTs[pti[0] % 4], B_PTs[pti[0] % 4]
            pti[0] += 1
            k.op("act", lambda e: e.activation(out=pt[0:np_, :], in_=pS_[0:np_, 0:4], func=AF.Exp), R=[bS_], W=[bpt])
            if mask is not None:
                k.op("pool", lambda e: e.affine_select(out=pt[:, :], in_=pt[:, :], pattern=[[0, 4]], compare_op=ALU.is_ge, fill=0.0, base=mask[0], channel_multiplier=mask[1]), R=[bpt], W=[bpt])
            k.op("pe", lambda e: e.matmul(pO[:, 0:4], lhsT=va, rhs=pt[0:np_, :], start=start, stop=stop), R=Rv + [bpt], W=[bO])
            if imp is not None:
                k.op("pe", lambda e: e.matmul(imp[0][:, 0:4], lhsT=OVs[:, :], rhs=pt[:, :], start=True, stop=True), R=[B_OVs, bpt], W=[imp[1]])

        for s_i in range(int(os.environ.get("NSA_NS", str(NS)))):
            xk, bxk = XT2[0], B_XT2[0]
            xv, bxv = XT2[1], B_XT2[1]
            for j in range(16):
                pgs, bpgs = pg[j % 4], B_pg[j % 4]
                for pi_ in range(4):
                    k.op("pool", lambda e, pi_=pi_, j=j, s_i=s_i, pgs=pgs: e.indirect_dma_start(out=pgs[pi_][:, :], out_offset=None, in_=pools2[pi_],
                                                                                           in_offset=bass.IndirectOffsetOnAxis(ap=idxa[:, s_i * 16 + j:s_i * 16 + j + 1], axis=0)),
                         R=[B_ptab], W=[bpgs[pi_]], dma=True)
                for pi_, (dst, bd) in ((0, (xk, bxk)), (1, (xv, bxv))):
                    pq, bq = eps()
                    pqb = pq[:, :].bitcast(BF16)
                    for c2 in range(2):
                        k.op("pe", lambda e, c2=c2, pi_=pi_, pq=pq, pgs=pgs: e.transpose(out=pq[:, c2 * 128:(c2 + 1) * 128], in_=pgs[pi_][:, c2 * 128:(c2 + 1) * 128], identity=identf[:, :]),
                             R=[bpgs[pi_], B_identf], W=[bq])
                    k.op("act" if pi_ == 0 else "dve", (lambda e, dst=dst, j=j, pq=pq: e.activation(out=dst[:, :, j * 128:(j + 1) * 128], in_=pq[:, 0:256].rearrange("p (c t) -> p c t", t=128), func=AF.Identity)) if pi_ == 0 else
                         (lambda e, dst=dst, j=j, pq=pq: e.tensor_copy(out=dst[:, :, j * 128:(j + 1) * 128], in_=pq[:, 0:256].rearrange("p (c t) -> p c t", t=128))),
                         R=[bq], W=[bd])
                pq, bq = eps()
                for c2 in range(2):
                    k.op("pe", lambda e, c2=c2, pq=pq, pgs=pgs: e.transpose(out=pq[:, c2 * 128:(c2 + 1) * 128], in_=pgs[2][:, c2 * 128:(c2 + 1) * 128], identity=identf[:, :]),
                         R=[bpgs[2], B_identf], W=[bq])
                for g in range(4):
                    k.op("act" if g % 2 else "dve", (lambda e, g=g, j=j, pq=pq: e.activation(out=KEs[0:64, g, j * 128:(j + 1) * 128], in_=pq[(g % 2) * 64:(g % 2) * 64 + 64, (g // 2) * 128:(g // 2 + 1) * 128], func=AF.Identity)) if g % 2 else
                         (lambda e, g=g, j=j, pq=pq: e.tensor_copy(out=KEs[0:64, g, j * 128:(j + 1) * 128], in_=pq[(g % 2) * 64:(g % 2) * 64 + 64, (g // 2) * 128:(g // 2 + 1) * 128])),
                         R=[bq], W=[B_KEs])
                k.op("act", lambda e, j=j, pgs=pgs: e.activation(out=VAss[:, j, :, 0:64], in_=pgs[3][:, :].rearrange("p (g d) -> p g d", d=64), func=AF.Identity), R=[bpgs[3]], W=[B_VAss])
            for c4 in range(4):
                pgs, bpgs = pg[c4 % 4], B_pg[c4 % 4]
                k.dma("sp", pgs[0][:, :], ckwin[s_i, c4 * 128:(c4 + 1) * 128, :], W=[bpgs[0]])
                k.dma("sp", pgs[1][:, :], cvwin[s_i, c4 * 128:(c4 + 1) * 128, :], W=[bpgs[1]])
                pq, bq = eps()
                for c2 in range(2):
                    k.op("pe", lambda e, c2=c2, pq=pq, pgs=pgs: e.transpose(out=pq[:, c2 * 128:(c2 + 1) * 128], in_=pgs[0][:, c2 * 128:(c2 + 1) * 128], identity=identf[:, :]),
                         R=[bpgs[0], B_identf], W=[bq])
                for g in range(4):
                    k.op("dve", lambda e, g=g, c4=c4, pq=pq: e.tensor_copy(out=KWs[0:64, g, c4 * 128:(c4 + 1) * 128], in_=pq[(g % 2) * 64:(g % 2) * 64 + 64, (g // 2) * 128:(g // 2 + 1) * 128]),
                         R=[bq], W=[B_KWs])
                k.op("act", lambda e, c4=c4, pgs=pgs: e.activation(out=VAws[:, c4, :, 0:64], in_=pgs[1][:, :].rearrange("p (g d) -> p g d", d=64), func=AF.Identity), R=[bpgs[1]], W=[B_VAws])
            k.op("dve", lambda e, s_i=s_i: e.tensor_copy(out=xk[:, :, PAST], in_=KNc[:, :, s_i]), R=[B_VNc], W=[bxk])
            k.op("dve", lambda e, s_i=s_i: e.tensor_copy(out=xv[:, :, PAST], in_=VNc[:, :, s_i]), R=[B_VNc], W=[bxv])
            for kind_ in range(2):
                xt_, bxt = (xk, bxk) if kind_ == 0 else (xv, bxv)
                for g in range(4):
                    rs_ = slice((g % 2) * 64, (g % 2) * 64 + 64)
                    c2 = g // 2
                    pF, bF = eps()
                    pS, bS = eps()
                    for p_ in range(16):
                        k.op("pe", lambda e, p_=p_, pF=pF: e.matmul(pF[:, 0:132], lhsT=W1s[kind_][rs_, p_, :], rhs=xt_[rs_, c2, p_:LP:16], start=(p_ == 0), stop=(p_ == 15)), R=[B_W1s, bxt], W=[bF])
                    for p_ in range(16):
                        k.op("pe", lambda e, p_=p_, pS=pS: e.matmul(pS[:, 0:132], lhsT=W1s[kind_][rs_, 16 + p_, :], rhs=xt_[rs_, c2, p_:LP:16], start=(p_ == 0), stop=(p_ == 15)), R=[B_W1s, bxt], W=[bS])
                    k.op("act", lambda e, pS=pS: e.activation(out=Sdz[:, :], in_=pS[:, 0:132], func=AF.Identity), R=[bS], W=[B_Sdz])
                    k.op("dve", lambda e, pF=pF: e.scalar_tensor_tensor(out=xgz[:, 0:128], in0=pF[:, 0:128], scalar=pecs[:, kind_:kind_ + 1], in1=Sdz[:, 1:129], op0=ALU.add, op1=ALU.add), R=[bF, B_pecs, B_Sdz], W=[B_xgz])
                    k.op("act", lambda e: e.activation(out=x2z[:, 0:128], in_=xgz[:, 0:128], func=AF.Square), R=[B_xgz], W=[B_x2z])
                    k.op("dve", lambda e: e.tensor_scalar(out=x2z[:, 0:128], in0=x2z[:, 0:128], scalar1=0.044715, scalar2=1.0, op0=ALU.mult, op1=ALU.add), R=[B_x2z], W=[B_x2z])
                    k.op("dve", lambda e: e.tensor_tensor(out=x2z[:, 0:128], in0=x2z[:, 0:128], in1=xgz[:, 0:128], op=ALU.mult), R=[B_x2z, B_xgz], W=[B_x2z])
                    k.op("act", lambda e: e.activation(out=x2z[:, 0:128], in_=x2z[:, 0:128], func=AF.Sigmoid, scale=1.5957691216), R=[B_x2z], W=[B_x2z])
                    k.op("dve", lambda e: e.tensor_tensor(out=hidz[:, :], in0=x2z[:, 0:128], in1=xgz[:, 0:128], op=ALU.mult), R=[B_x2z, B_xgz], W=[B_hidz])
                    pR, bR = eps()
                    if kind_ == 0:
                        k.op("pe", lambda e, pR=pR: e.matmul(pR[0:64, 0:128], lhsT=w2s[0][:, :], rhs=hidz[:, :], start=True, stop=True), R=[B_w2s, B_hidz], W=[bR])
                        k.op("dve", lambda e, g=g, pR=pR: e.tensor_copy(out=kcTs[0:64, g, :], in_=pR[0:64, 0:128]), R=[bR], W=[B_kcTs])
                    else:
                        k.op("pe", lambda e, pR=pR: e.matmul(pR[:, 0:64], lhsT=hidz[:, :], rhs=w2s[1][:, :], start=True, stop=True), R=[B_w2s, B_hidz], W=[bR])
                        k.op("dve", lambda e, g=g, pR=pR: e.tensor_copy(out=VCs[:, g, 0:64], in_=pR[:, 0:64]), R=[bR], W=[B_VCs])
            for g in range(4):
                pO, bO = PS[5], PSB[5]
                pI, bI = PS[6], PSB[6]
                score_pv(kcTs[0:64, g, :], RHS[0:64, s_i, 4 * g:4 * g + 4], [B_kcTs, B_RHSq], VCs[:, g, :], [B_VCs], pO, bO, True, True, mask=(126, -1), imp=(pI, bI))
                fin_branch(s_i, g, 0, pO, bO, True)
                k.op("dve", lambda e: e.reciprocal(out=imt[64:128, :], in_=Oes[64:128, :]), R=[B_Oes], W=[B_imt])
                k.op("dve", lambda e, pI=pI: e.tensor_tensor(out=imt[64:128, :], in0=pI[64:128, 0:4], in1=imt[64:128, :], op=ALU.mult), R=[bI, B_imt], W=[B_imt])
                k.op("dve", lambda e, g=g: e.tensor_reduce(out=impc[64:128, g:g + 1], in_=imt[64:128, :], axis=mybir.AxisListType.X, op=ALU.add), R=[B_imt], W=[B_impc])
            pT, bT = PS[7], PSB[7]
            k.op("pe", lambda e: e.transpose(out=pT[0:4, 0:64], in_=impc[64:128, :], identity=identf[64:128, 64:128]), R=[B_impc, B_identf], W=[bT])
            k.op("dve", lambda e: e.tensor_copy(out=impr[:, :], in_=pT[0:4, 0:64]), R=[bT], W=[B_impr])
            k.op("pool", lambda e: e.memset(impr[:, 0:1], BIG), R=[B_impr], W=[B_impr])
            k.op("pool", lambda e: e.memset(impr[:, 32:33], BIG), R=[B_impr], W=[B_impr])
            k.op("pool", lambda e: e.memset(impr[:, 33:64], -BIG), R=[B_impr], W=[B_impr])
            k.op("dve", lambda e: e.max(out=m8r[:, 0:8], in_=impr[:, :]), R=[B_impr], W=[B_m8r])
            k.op("dve", lambda e: e.match_replace(out=imp2r[:, :], in_to_replace=m8r[:, 0:8], in_values=impr[:, :], imm_value=-3.0e38), R=[B_impr, B_m8r], W=[B_imp2r])
            k.op("dve", lambda e: e.max(out=m8r[:, 8:16], in_=imp2r[:, :]), R=[B_imp2r, B_m8r], W=[B_m8r])
            k.op("dve", lambda e: e.tensor_scalar(out=bselr[:, 64:128], in0=impr[:, :], scalar1=m8r[:, 15:16], scalar2=1.0, op0=ALU.is_ge, op1=ALU.subtract), R=[B_impr, B_m8r], W=[B_bselr])
            pTb = pT[:, :].bitcast(BF16)
            k.op("pe", lambda e: e.transpose(out=pTb[:, 512:516], in_=bselr[:, :], identity=ident[0:4, 0:4]), R=[B_bselr, B_ident], W=[bT])
            for hh in range(4):
                k.op("dve", lambda e, hh=hh, s_i=s_i: e.tensor_copy(out=RHS[64:128, s_i, :].rearrange("p (g h) -> p g h", h=4)[:, :, hh], in_=pTb[64:128, 512:516]), R=[bT], W=[B_RHSs])
            for g in range(4):
                pO, bO = PS[5], PSB[5]
                for c in range(16):
                    score_pv(KEs[:, g, c * 128:(c + 1) * 128], RHS[:, s_i, 4 * g:4 * g + 4], [B_KEs, B_KEe, B_RHSq, B_RHSs], VAss[:, c, g, :], [B_VAss], pO, bO, c == 0, False)
                score_pv(KN[0:64, 1, g, s_i:s_i + 1], RHS[0:64, s_i, 4 * g:4 * g + 4], [B_KN, B_RHSq], VN[0:1, s_i, 0, g, :], [B_VN], pO, bO, False, True, np_=1)
                fin_branch(s_i, g, 1, pO, bO, False)
            for g in range(4):
                pO, bO = PS[6], PSB[6]
                for c in range(4):
                    score_pv(KWs[0:64, g, c * 128:(c + 1) * 128], RHS[0:64, s_i, 4 * g:4 * g + 4], [B_KWs, B_RHSq], VAws[:, c, g, :], [B_VAws], pO, bO, c == 0, False,
                             mask=((-1, 1) if c == 0 else None))
                score_pv(KN[0:64, 2, g, s_i:s_i + 1], RHS[0:64, s_i, 4 * g:4 * g + 4], [B_KN, B_RHSq], VN[0:1, s_i, 1, g, :], [B_VN], pO, bO, False, True, np_=1)
                fin_branch(s_i, g, 2, pO, bO, False)
        ons = sb(st, "ons", [NS, D], F32)
        ods = sb(st, "ods", [NS, D], F32)
        mgs = sb(st, "mgs", [NS, 2048], F32)
        B_ons, B_ods, B_mgs = Buf(), Buf(), Buf()
        for h in range(16):
            pq, bq = PS[h // 8], PSB[h // 8]
            k.op("pe", lambda e, h=h, pq=pq: e.transpose(out=pq[0:NS, (h % 8) * 64:(h % 8 + 1) * 64], in_=onT[0:64, :, h], identity=identf[0:64, 0:64]), R=[B_onT, B_identf], W=[bq])
        for hf in range(2):
            k.op("dve", lambda e, hf=hf: e.tensor_copy(out=ons[:, hf * 512:(hf + 1) * 512], in_=PS[hf][0:NS, :]), R=[PSB[hf]], W=[B_ons])
        k.dma("sp", mgs[:, :], projs_scr[:, 6720:8768], R=[B_scr2], W=[B_mgs])
        k.dma("sp", ods[:, :], odns_scr[:, :], R=[B_scr3], W=[B_ods])
        k.op("act", lambda e: e.activation(out=mgs[:, :], in_=mgs[:, :], func=AF.Sigmoid), R=[B_mgs], W=[B_mgs])
        k.op("dve", lambda e: e.tensor_tensor(out=ons[:, :], in0=ons[:, :], in1=mgs[:, 0:D], op=ALU.mult), R=[B_ons, B_mgs], W=[B_ons])
        k.op("dve", lambda e: e.tensor_tensor(out=ods[:, :], in0=ods[:, :], in1=mgs[:, D:2 * D], op=ALU.mult), R=[B_ods, B_mgs], W=[B_ods])
        k.op("dve", lambda e: e.tensor_tensor(out=ons[:, :], in0=ons[:, :], in1=ods[:, :], op=ALU.add), R=[B_ons, B_ods], W=[B_ons])
        k.dma("sp", mixs_scr[:, :], ons[:, :], R=[B_ons], W=[B_scr3])
        k.barrier()
    with contextlib.ExitStack() as st:
        wo = sb(st, "wo", [128, 8, D], BF16)
        wu = sb(st, "wu", [128, 8, 4 * D], BF16)
        wd = sb(st, "wd", [128, 32, D], BF16)
        B_wo, B_wu, B_wd = Buf(), Buf(), Buf()
        wov = w_out.rearrange("(kc p) n -> p kc n", p=128)
        wuv = w_up.rearrange("(kc p) n -> p kc n", p=128)
        wdv = w_down.rearrange("(kc p) n -> p kc n", p=128)
        for c in range(2):
            k.dma("pool", wo[:, :, c * 512:(c + 1) * 512], wov[:, :, c * 512:(c + 1) * 512], W=[B_wo])
        for c in range(8):
            k.dma("pool", wu[:, :, c * 512:(c + 1) * 512], wuv[:, :, c * 512:(c + 1) * 512], W=[B_wu])
        for c in range(8):
            k.dma("pool", wd[:, c * 4:(c + 1) * 4, :], wdv[:, c * 4:(c + 1) * 4, :], W=[B_wd])
        onesF = sb(st, "onesF", [128, 128], F32)
        dgF = sb(st, "dgF", [128, 128], F32)
        bct = sb(st, "bct", [128, 3, D], F32)
        gm2 = sb(st, "gm2", [128, 8], F32)
        B_onesF, B_dgF, B_bct, B_gm2 = Buf(), Buf(), Buf(), Buf()
        k.op("pool", lambda e: e.memset(onesF[:, :], 1.0), W=[B_onesF])
        k.op("dve", lambda e: e.scalar_tensor_tensor(out=gm2[:, :], in0=vecT[:, 32:40], scalar=1.0, in1=vecT[:, 56:64], op0=ALU.add, op1=ALU.mult), R=[B_vecT], W=[B_gm2])
        fps = [0]

        def fp():
            i = fps[0] % 8
            fps[0] += 1
            return PS[i], PSB[i]
        for vi, c0 in enumerate((16, 40, 64)):
            for j in range(8):
                k.op("dve", lambda e, c0=c0, j=j: e.tensor_scalar(out=dgF[:, :], in0=identf[:, :], scalar1=vecT[:, c0 + j:c0 + j + 1], scalar2=None, op0=ALU.mult),
                     R=[B_vecT, B_identf, B_dgF], W=[B_dgF])
                pp, bpp = fp()
                k.op("pe", lambda e, pp=pp: e.matmul(pp[:, 0:128], lhsT=onesF[:, :], rhs=dgF[:, :], start=True, stop=True), R=[B_onesF, B_dgF], W=[bpp])
                k.op("act", lambda e, pp=pp, vi=vi, j=j: e.activation(out=bct[:, vi, j * 128:(j + 1) * 128], in_=pp[:, 0:128], func=AF.Identity), R=[bpp], W=[B_bct])
        TB = 2
        mixb = sb(st, "mixb", [128, TB, D], BF16)
        mixT = sb(st, "mixT", [128, 8, TB * 128], BF16)
        x1 = sb(st, "x1", [128, TB, D], F32)
        tmpF = sb(st, "tmpF", [128, D], F32)
        xnb = sb(st, "xnb", [128, D], BF16)
        h2T = sb(st, "h2T", [128, 8, TB * 128], BF16)
        uT = sb(st, "uT", [128, 32, TB * 128], BF16)
        rl = [sb(st, "rl%d" % i, [128, TB * 128], F32) for i in range(2)]
        sq2 = sb(st, "sq2", [128, 2], F32)
        B_mixb, B_mixT, B_x1, B_tmpF, B_xnb, B_h2T, B_sq2 = (Buf() for _ in range(7))
        yout, B_yout = tmpF, B_tmpF
        B_uT = [Buf() for _ in range(32)]
        B_rl = [Buf(), Buf()]
        for blk in range(S // (TB * 128)):
            t0 = blk * TB * 128
            k.dma("pool", mixb[:, :, :], mix_scr[t0:t0 + TB * 128, :].rearrange("(t p) d -> p t d", p=128), R=[B_mix], W=[B_mixb])
            k.dma("sp", x1[:, :, :], xp[t0:t0 + TB * 128, :].rearrange("(t p) d -> p t d", p=128), W=[B_x1])
            for tt in range(TB):
                pp, bpp = fp()
                ppb = pp[:, :].bitcast(BF16)
                for j in range(8):
                    k.op("pe", lambda e, tt=tt, j=j, ppb=ppb: e.transpose(out=ppb[:, j * 128:(j + 1) * 128], in_=mixb[:, tt, j * 128:(j + 1) * 128], identity=ident[:, :]),
                         R=[B_mixb, B_ident], W=[bpp])
                k.op("act", lambda e, tt=tt, ppb=ppb: e.activation(out=mixT[:, :, tt * 128:(tt + 1) * 128], in_=ppb[:, :].rearrange("p (j t) -> p j t", t=128), func=AF.Identity),
                     R=[bpp], W=[B_mixT])
            for tt in range(TB):
                for hf in range(2):
                    pp, bpp = fp()
                    for kc in range(8):
                        k.op("pe", lambda e, tt=tt, hf=hf, kc=kc, pp=pp: e.matmul(pp[:, :], lhsT=mixT[:, kc, tt * 128:(tt + 1) * 128], rhs=wo[:, kc, hf * 512:(hf + 1) * 512],
                                                                                 start=(kc == 0), stop=(kc == 7)), R=[B_mixT, B_wo], W=[bpp])
                    k.op("dve", lambda e, tt=tt, hf=hf, pp=pp: e.tensor_tensor(out=tmpF[:, hf * 512:(hf + 1) * 512], in0=pp[:, :], in1=bct[:, 0, hf * 512:(hf + 1) * 512], op=ALU.mult),
                         R=[bpp, B_bct], W=[B_tmpF])
                k.op("dve", lambda e, tt=tt: e.tensor_tensor(out=x1[:, tt, :], in0=x1[:, tt, :], in1=tmpF[:, :], op=ALU.add), R=[B_x1, B_tmpF], W=[B_x1])
                k.op("act", lambda e, tt=tt: e.activation(out=tmpF[:, :], in_=x1[:, tt, :], func=AF.Square, accum_out=sq2[:, 0:1]), R=[B_x1], W=[B_tmpF, B_sq2])
                k.op("act", lambda e: e.activation(out=sq2[:, 0:1], in_=sq2[:, 0:1], func=AF.Sqrt, scale=1.0 / D, bias=1e-6), R=[B_sq2], W=[B_sq2])
                k.op("dve", lambda e: e.reciprocal(out=sq2[:, 0:1], in_=sq2[:, 0:1]), R=[B_sq2], W=[B_sq2])
                k.op("dve", lambda e, tt=tt: e.tensor_scalar(out=xnb[:, :], in0=x1[:, tt, :], scalar1=sq2[:, 0:1], scalar2=None, op0=ALU.mult), R=[B_x1, B_sq2], W=[B_xnb])
                pp, bpp = fp()
                ppb = pp[:, :].bitcast(BF16)
                for j in range(8):
                    k.op("pe", lambda e, j=j, ppb=ppb: e.transpose(out=ppb[:, j * 128:(j + 1) * 128], in_=xnb[:, j * 128:(j + 1) * 128], identity=ident[:, :]),
                         R=[B_xnb, B_ident], W=[bpp])
                for j in range(8):
                    if j % 2 == 0:
                        k.op("act", lambda e, j=j, tt=tt, ppb=ppb: e.activation(out=h2T[:, j, tt * 128:(tt + 1) * 128], in_=ppb[:, j * 128:(j + 1) * 128], func=AF.Identity,
                                                                               scale=gm2[:, j:j + 1], bias=vecT[:, 24 + j:25 + j]), R=[bpp, B_gm2, B_vecT], W=[B_h2T])
                    else:
                        k.op("dve", lambda e, j=j, tt=tt, ppb=ppb: e.tensor_scalar(out=h2T[:, j, tt * 128:(tt + 1) * 128], in0=ppb[:, j * 128:(j + 1) * 128],
                                                                                  scalar1=gm2[:, j:j + 1], scalar2=vecT[:, 24 + j:25 + j], op0=ALU.mult, op1=ALU.add),
                             R=[bpp, B_gm2, B_vecT], W=[B_h2T])
            for hc in range(32):
                pp, bpp = fp()
                for kc in range(8):
                    k.op("pe", lambda e, hc=hc, kc=kc, pp=pp: e.matmul(pp[:, 0:TB * 128], lhsT=wu[:, kc, hc * 128:(hc + 1) * 128], rhs=h2T[:, kc, :], start=(kc == 0), stop=(kc == 7)),
                         R=[B_wu, B_h2T], W=[bpp])
                r_, br_ = rl[hc % 2], B_rl[hc % 2]
                k.op("act", lambda e, pp=pp, r_=r_: e.activation(out=r_[:, :], in_=pp[:, 0:TB * 128], func=AF.Relu), R=[bpp], W=[br_])
                k.op("dve" if hc % 2 else "pool", lambda e, hc=hc, r_=r_: e.tensor_tensor(out=uT[:, hc, :], in0=r_[:, :], in1=r_[:, :], op=ALU.mult), R=[br_], W=[B_uT[hc]])
            for tt in range(TB):
                for hf in range(2):
                    pp, bpp = fp()
                    for hc in range(32):
                        k.op("pe", lambda e, tt=tt, hf=hf, hc=hc, pp=pp: e.matmul(pp[:, :], lhsT=uT[:, hc, tt * 128:(tt + 1) * 128], rhs=wd[:, hc, hf * 512:(hf + 1) * 512],
                                                                                 start=(hc == 0), stop=(hc == 31)), R=[B_uT[hc], B_wd], W=[bpp])
                    k.op("dve", lambda e, tt=tt, hf=hf, pp=pp: e.tensor_tensor(out=tmpF[:, hf * 512:(hf + 1) * 512], in0=pp[:, :], in1=bct[:, 1, hf * 512:(hf + 1) * 512], op=ALU.mult),
                         R=[bpp, B_bct], W=[B_tmpF])
                k.op("dve", lambda e, tt=tt: e.tensor_tensor(out=x1[:, tt, :], in0=x1[:, tt, :], in1=tmpF[:, :], op=ALU.add), R=[B_x1, B_tmpF], W=[B_x1])
                k.op("act", lambda e, tt=tt: e.activation(out=tmpF[:, :], in_=x1[:, tt, :], func=AF.Square, accum_out=sq2[:, 1:2]), R=[B_x1], W=[B_tmpF, B_sq2])
                k.op("act", lambda e: e.activation(out=sq2[:, 1:2], in_=sq2[:, 1:2], func=AF.Sqrt, scale=1.0 / D, bias=1e-6), R=[B_sq2], W=[B_sq2])
                k.op("dve", lambda e: e.reciprocal(out=sq2[:, 1:2], in_=sq2[:, 1:2]), R=[B_sq2], W=[B_sq2])
                k.op("dve", lambda e, tt=tt: e.scalar_tensor_tensor(out=yout[:, :], in0=x1[:, tt, :], scalar=sq2[:, 1:2], in1=bct[:, 2, :], op0=ALU.mult, op1=ALU.mult),
                     R=[B_x1, B_sq2, B_bct], W=[B_yout])
                k.dma("sp", o_yp[t0 + tt * 128:t0 + (tt + 1) * 128, :], yout[:, :], R=[B_yout], W=[DOUT])
        gmT = sb(st, "gmT", [128, 8, NS], F32)
        shT = sb(st, "shT", [128, 8, NS], F32)
        B_gmT, B_shT = Buf(), Buf()
        B_m1 = B_x1
        k.dma("pool", mixb[0:NS, 0, :], mixs_scr[:, :], R=[B_scr3], W=[B_mixb])
        k.dma("sp", x1[0:NS, 0, :], xs[:, :], W=[B_x1])
        k.dma("sp", x1[0:NS, 1, :], mods_scr[:, 2 * D:3 * D], R=[B_scr2], W=[B_x1])
        pp, bpp = fp()
        ppb = pp[:, :].bitcast(BF16)
        for j in range(8):
            k.op("pe", lambda e, j=j, ppb=ppb: e.transpose(out=ppb[:, j * NS:(j + 1) * NS], in_=mixb[0:NS, 0, j * 128:(j + 1) * 128], identity=ident[0:NS, 0:NS]), R=[B_mixb, B_ident], W=[bpp])
        k.op("act", lambda e, ppb=ppb: e.activation(out=mixT[:, :, 0:NS], in_=ppb[:, 0:8 * NS].rearrange("p (j t) -> p j t", t=NS), func=AF.Identity), R=[bpp], W=[B_mixT])
        for hf in range(2):
            pp, bpp = fp()
            for kc in range(8):
                k.op("pe", lambda e, hf=hf, kc=kc, pp=pp: e.matmul(pp[0:NS, :], lhsT=mixT[:, kc, 0:NS], rhs=wo[:, kc, hf * 512:(hf + 1) * 512], start=(kc == 0), stop=(kc == 7)), R=[B_mixT, B_wo], W=[bpp])
            k.op("dve", lambda e, hf=hf, pp=pp: e.tensor_tensor(out=tmpF[0:NS, hf * 512:(hf + 1) * 512], in0=pp[0:NS, :], in1=x1[0:NS, 1, hf * 512:(hf + 1) * 512], op=ALU.mult), R=[bpp, B_x1], W=[B_tmpF])
        k.op("dve", lambda e: e.tensor_tensor(out=x1[0:NS, 0, :], in0=x1[0:NS, 0, :], in1=tmpF[0:NS, :], op=ALU.add), R=[B_x1, B_tmpF], W=[B_x1])
        for which, (c0, dstT) in enumerate(((4 * D, gmT), (3 * D, shT))):
            k.dma("sp", x1[0:NS, 1, :], mods_scr[:, c0:c0 + D], R=[B_scr2, B_x1], W=[B_x1])
            pp, bpp = fp()
            for j in range(8):
                k.op("pe", lambda e, j=j, pp=pp: e.transpose(out=pp[:, j * NS:(j + 1) * NS], in_=x1[0:NS, 1, j * 128:(j + 1) * 128], identity=identf[0:NS, 0:NS]), R=[B_x1, B_identf], W=[bpp])
            if which == 0:
                for j in range(8):
                    k.op("dve", lambda e, j=j, pp=pp: e.tensor_scalar(out=gmT[:, j, :], in0=pp[:, j * NS:(j + 1) * NS], scalar1=1.0, scalar2=vecT[:, 56 + j:57 + j], op0=ALU.add, op1=ALU.mult),
                         R=[bpp, B_vecT], W=[B_gmT])
            else:
                k.op("dve", lambda e, pp=pp: e.tensor_copy(out=shT[:, :, :], in_=pp[:, 0:8 * NS].rearrange("p (j t) -> p j t", t=NS)), R=[bpp], W=[B_shT])
        k.op("act", lambda e: e.activation(out=tmpF[0:NS, :], in_=x1[0:NS, 0, :], func=AF.Square, accum_out=sq2[0:NS, 0:1]), R=[B_x1], W=[B_tmpF, B_sq2])
        k.op("act", lambda e: e.activation(out=sq2[0:NS, 0:1], in_=sq2[0:NS, 0:1], func=AF.Sqrt, scale=1.0 / D, bias=1e-6), R=[B_sq2], W=[B_sq2])
        k.op("dve", lambda e: e.reciprocal(out=sq2[0:NS, 0:1], in_=sq2[0:NS, 0:1]), R=[B_sq2], W=[B_sq2])
        k.op("dve", lambda e: e.tensor_scalar(out=xnb[0:NS, :], in0=x1[0:NS, 0, :], scalar1=sq2[0:NS, 0:1], scalar2=None, op0=ALU.mult), R=[B_x1, B_sq2], W=[B_xnb])
        pp, bpp = fp()
        ppb = pp[:, :].bitcast(BF16)
        for j in range(8):
            k.op("pe", lambda e, j=j, ppb=ppb: e.transpose(out=ppb[:, j * NS:(j + 1) * NS], in_=xnb[0:NS, j * 128:(j + 1) * 128], identity=ident[0:NS, 0:NS]), R=[B_xnb, B_ident], W=[bpp])
        k.op("dve", lambda e, ppb=ppb: e.tensor_tensor(out=gmT[:, :, :], in0=ppb[:, 0:8 * NS].rearrange("p (j t) -> p j t", t=NS), in1=gmT[:, :, :], op=ALU.mult), R=[bpp, B_gmT], W=[B_gmT])
        k.op("dve", lambda e: e.tensor_tensor(out=h2T[:, :, 0:NS], in0=gmT[:, :, :], in1=shT[:, :, :], op=ALU.add), R=[B_gmT, B_shT], W=[B_h2T])
        for hc in range(32):
            pp, bpp = fp()
            for kc in range(8):
                k.op("pe", lambda e, hc=hc, kc=kc, pp=pp: e.matmul(pp[:, 0:NS], lhsT=wu[:, kc, hc * 128:(hc + 1) * 128], rhs=h2T[:, kc, 0:NS], start=(kc == 0), stop=(kc == 7)), R=[B_wu, B_h2T], W=[bpp])
            r_, br_ = rl[hc % 2], B_rl[hc % 2]
            k.op("act", lambda e, pp=pp, r_=r_: e.activation(out=r_[:, 0:NS], in_=pp[:, 0:NS], func=AF.Relu), R=[bpp], W=[br_])
            k.op("dve", lambda e, hc=hc, r_=r_: e.tensor_tensor(out=uT[:, hc, 0:NS], in0=r_[:, 0:NS], in1=r_[:, 0:NS], op=ALU.mult), R=[br_], W=[B_uT[hc]])
        k.dma("sp", x1[0:NS, 1, :], mods_scr[:, 5 * D:6 * D], R=[B_scr2, B_x1], W=[B_x1])
        for hf in range(2):
            pp, bpp = fp()
            for hc in range(32):
                k.op("pe", lambda e, hf=hf, hc=hc, pp=pp: e.matmul(pp[0:NS, :], lhsT=uT[:, hc, 0:NS], rhs=wd[:, hc, hf * 512:(hf + 1) * 512], start=(hc == 0), stop=(hc == 31)), R=[B_uT[hc], B_wd], W=[bpp])
            k.op("dve", lambda e, hf=hf, pp=pp: e.tensor_tensor(out=tmpF[0:NS, hf * 512:(hf + 1) * 512], in0=pp[0:NS, :], in1=x1[0:NS, 1, hf * 512:(hf + 1) * 512], op=ALU.mult), R=[bpp, B_x1], W=[B_tmpF])
        k.op("dve", lambda e: e.tensor_tensor(out=x1[0:NS, 0, :], in0=x1[0:NS, 0, :], in1=tmpF[0:NS, :], op=ALU.add), R=[B_x1, B_tmpF], W=[B_x1])
        k.op("act", lambda e: e.activation(out=tmpF[0:NS, :], in_=x1[0:NS, 0, :], func=AF.Square, accum_out=sq2[0:NS, 1:2]), R=[B_x1], W=[B_tmpF, B_sq2])
        k.op("act", lambda e: e.activation(out=sq2[0:NS, 1:2], in_=sq2[0:NS, 1:2], func=AF.Sqrt, scale=1.0 / D, bias=1e-6), R=[B_sq2], W=[B_sq2])
        k.op("dve", lambda e: e.reciprocal(out=sq2[0:NS, 1:2], in_=sq2[0:NS, 1:2]), R=[B_sq2], W=[B_sq2])
        k.op("dve", lambda e: e.scalar_tensor_tensor(out=tmpF[0:NS, :], in0=x1[0:NS, 0, :], scalar=sq2[0:NS, 1:2], in1=bct[0:NS, 2, :], op0=ALU.mult, op1=ALU.mult), R=[B_x1, B_sq2, B_bct], W=[B_tmpF])
        k.dma("sp", o_ys[:, :], tmpF[0:NS, :], R=[B_tmpF], W=[DOUT])
        k.barrier()
    st0.close()
    k.emit()
    return nc


_NC = None


def kernel(**inp):
    global _NC
    f = lambda a: np.ascontiguousarray(np.asarray(a, dtype=np.float32))
    if _NC is None:
        _NC = build_nc()
    nc = _NC
    pools_ = [f(inp[n_][0]).reshape(2560, 128, 256) for n_ in ("cache_k_cmp", "cache_v_cmp", "cache_k_slc", "cache_v_slc")]
    in_maps = []
    for i in range(8):
        b = i % 4
        s0 = i * NS
        in_maps.append({
            "xp": f(inp["x_prompt"][b]),
            "xs": f(inp["x_sample"][s0:s0 + NS, 0]),
            "cp": f(inp["c_prompt"][b:b + 1]),
            "cs": f(inp["c_sample"][s0:s0 + NS]),
            "w_ada": f(inp["w_ada"][0]),
            "b_ada": f(inp["b_ada"][0:1]),
            "g1": f(inp["norm1_g"][0:1]),
            "g2": f(inp["norm2_g"][0:1]),
            "gf": f(inp["final_g"][None, :]),
            "w_in": f(inp["w_in"][0]),
            "ckwin": f(inp["cache_k_win"][0, s0:s0 + NS]).reshape(NS, 512, 256),
            "cvwin": f(inp["cache_v_win"][0, s0:s0 + NS]).reshape(NS, 512, 256),
            "sconv": f(inp["state_conv"][0, s0:s0 + NS]),
            "conv_w": f(inp["dn_conv_w"][0]),
            "a_log": f(inp["dn_a_log"]),
            "dt_bias": f(inp["dn_dt_bias"]),
            "dn_ng": f(inp["dn_norm_g"]),
            "state_dn": f(inp["state_dn"][0, s0:s0 + NS]),
            "pk_cmp": pools_[0], "pv_cmp": pools_[1], "pk_slc": pools_[2], "pv_slc": pools_[3],
            "ptbl": np.ascontiguousarray(np.asarray(inp["page_table"][s0:s0 + NS], dtype=np.int32).reshape(1, NS * 16)),
            "w_out": f(inp["w_out"][0]), "w_up": f(inp["w_up"][0]), "w_down": f(inp["w_down"][0]),
            "w1k": f(inp["cmp_w1_k"][0]), "w2k": f(inp["cmp_w2_k"][0]), "pek": f(inp["cmp_pe_k"][0]),
            "w1v": f(inp["cmp_w1_v"][0]), "w2v": f(inp["cmp_w2_v"][0]), "pev": f(inp["cmp_pe_v"][0]),
        })
    res = run_bass_kernel_spmd(nc, in_maps, core_ids=list(range(8)))
    R = res.results
    B = 4
    y_prompt = np.stack([R[b]["o_yp"] for b in range(B)], 0)
    y_sample = np.concatenate([R[i]["o_ys"] for i in range(8)], 0)[:, None, :]
    pkv = np.stack([R[b]["o_pkv"] for b in range(B)], 0)
    p_kv = [pkv[:, j].reshape(1, B, S, 4, 64) for j in range(6)]
    p_kv[4] = p_kv[4][:, :, S - 512:]
    p_kv[5] = p_kv[5][:, :, S - 512:]
    p_conv = np.stack([R[b]["o_pconv"] for b in range(B)], 0)[None]
    p_dn = np.stack([R[b]["o_pdn"] for b in range(B)], 0)[None]
    skv = np.concatenate([R[i]["o_skv"] for i in range(8)], 0)
    s_kv = [skv[:, j * 256:(j + 1) * 256].reshape(1, 128, 1, 4, 64) for j in range(4)]
    s_kwin = np.concatenate([R[i]["o_skwin"] for i in range(8)], 0).reshape(1, 128, 512, 4, 64)
    s_vwin = np.concatenate([R[i]["o_svwin"] for i in range(8)], 0).reshape(1, 128, 512, 4, 64)
    s_conv = np.concatenate([R[i]["o_sconv"] for i in range(8)], 0)[None]
    s_dn = np.concatenate([R[i]["o_sdn"] for i in range(8)], 0)[None]
    return (y_prompt, y_sample, p_kv[0], p_kv[1], p_kv[2], p_kv[3], p_kv[4], p_kv[5], p_conv, p_dn,
            s_kv[0], s_kv[1], s_kv[2], s_kv[3], s_kwin, s_vwin, s_conv, s_dn)
```
